# Optimizing a Trainium2 kernel written in Bass

```python
import math
import jax, jax.numpy as jnp
from jax import lax
import numpy as np

D_MODEL = 1024
BATCH = 16
SEQ = 2048
DEPTH = 1
DEC_BATCH = 32
DEC_SEQ = 8
PAST_LEN = 16384
PAGE_SIZE = 128

DIFF_HEADS = 8
DIFF_QK = 32
DIFF_V = 2 * DIFF_QK
GLA_HEADS = 4
GLA_DK = 64
GLA_DV = 128
GLA_GATE_RANK = 16
GLA_TAU = 16.0
GLA_CHUNK = 64
N_MEM = 256
CROSS_HEADS = 4
CROSS_HD = D_MODEL // CROSS_HEADS
PEER_HEADS = 8
PEER_QD = 256
N_KEYS = 128
N_EXPERTS = N_KEYS * N_KEYS
PEER_TOPK = 16
PEER_BLOCK = 256
Q_BLOCK = 128
EPS = 1e-6

MIX_WIDTH = DIFF_HEADS * DIFF_V + GLA_HEADS * GLA_DV
IN_SPLIT_SIZES = (DIFF_HEADS * 2 * DIFF_QK, DIFF_HEADS * 2 * DIFF_QK, DIFF_HEADS * DIFF_V,
                  GLA_HEADS * GLA_DK, GLA_HEADS * GLA_DK, GLA_HEADS * GLA_DV,
                  GLA_GATE_RANK, GLA_HEADS * GLA_DV)
IN_WIDTH = sum(IN_SPLIT_SIZES)

kernel_name = "hymba_diffattn_gla_peer_step"


def rms_norm(x, g):
    xf = x.astype(jnp.float32)
    y = xf * lax.rsqrt(jnp.mean(xf * xf, axis=-1, keepdims=True) + EPS)
    return (y * g.astype(jnp.float32)).astype(x.dtype)


def lambda_init(layer):
    return 0.8 - 0.6 * math.exp(-0.3 * layer)


def diff_lambda(lq1, lk1, lq2, lk2, lam_init):
    f = lambda a: a.astype(jnp.float32)
    return jnp.exp(jnp.sum(f(lq1) * f(lk1))) - jnp.exp(jnp.sum(f(lq2) * f(lk2))) + lam_init


def mixer_projections(xn, w_in, q_norm, k_norm, w_gate, b_gate):
    B, T, _ = xn.shape
    offs, acc = [], 0
    for s in IN_SPLIT_SIZES[:-1]:
        acc += s
        offs.append(acc)
    dq, dk, dv, gq, gk, gv, ga, gr = jnp.split(xn @ w_in, offs, axis=-1)
    dq = rms_norm(dq.reshape(B, T, DIFF_HEADS, 2, DIFF_QK), q_norm) * (DIFF_QK ** -0.5)
    dk = rms_norm(dk.reshape(B, T, DIFF_HEADS, 2, DIFF_QK), k_norm)
    dv = dv.reshape(B, T, DIFF_HEADS, DIFF_V)
    gq = gq.reshape(B, T, GLA_HEADS, GLA_DK) * (GLA_DK ** -0.5)
    gk = gk.reshape(B, T, GLA_HEADS, GLA_DK)
    gv = gv.reshape(B, T, GLA_HEADS, GLA_DV)
    log_a = jax.nn.log_sigmoid((ga @ w_gate + b_gate).astype(jnp.float32)) / GLA_TAU
    log_a = log_a.reshape(B, T, GLA_HEADS, GLA_DK)
    return dq, dk, dv, gq, gk, gv, log_a, gr


def diff_attend(q, ks, vs, masks, lam):
    scores = []
    for k_, m in zip(ks, masks):
        s = jnp.einsum('bqhcd,bkhcd->bhcqk', q, k_.astype(q.dtype)).astype(jnp.float32)
        scores.append(s if m is None else jnp.where(m, s, -jnp.inf))
    p = jax.nn.softmax(jnp.concatenate(scores, axis=-1), axis=-1)
    p = p[:, :, 0] - lam * p[:, :, 1]
    out, off = 0.0, 0
    for k_, v_ in zip(ks, vs):
        n = k_.shape[1]
        out = out + jnp.einsum('bhqk,bkhd->bqhd', p[..., off:off + n], v_.astype(jnp.float32))
        off += n
    return out


def diff_attend_prompt(q, k, v, lam):
    B, S = q.shape[:2]
    qb = math.gcd(S, Q_BLOCK)
    n = S // qb
    q_blocks = jnp.moveaxis(q.reshape(B, n, qb, *q.shape[2:]), 1, 0)
    starts = jnp.arange(n, dtype=jnp.int32) * qb
    kpos = jnp.arange(S, dtype=jnp.int32)

    def one(args):
        qblk, start = args
        qpos = start + jnp.arange(qb, dtype=jnp.int32)
        return diff_attend(qblk, [k], [v], [kpos[None, :] <= qpos[:, None]], lam)

    o = lax.map(one, (q_blocks, starts))
    return jnp.moveaxis(o, 0, 1).reshape(B, S, DIFF_HEADS, DIFF_V)


def diff_attend_sample(q, k_new, v_new, k_past, v_past, lam):
    T = q.shape[1]
    causal = jnp.tril(jnp.ones((T, T), dtype=bool))
    return diff_attend(q, [k_past, k_new], [v_past, v_new], [None, causal], lam)


def diff_post(o, out_norm, lam_init):
    B, T = o.shape[:2]
    return (rms_norm(o, out_norm) * (1.0 - lam_init)).reshape(B, T, -1)


def gla_recurrence(q, k, v, log_a, S0):
    B, T, H, DK = q.shape
    DV = v.shape[-1]
    C = math.gcd(T, GLA_CHUNK)
    n = T // C

    def chunks(a):
        return jnp.moveaxis(a.astype(jnp.float32).reshape(B, n, C, *a.shape[2:]), 1, 0)

    tril = jnp.tril(jnp.ones((C, C), dtype=bool))[None, :, :, None, None]

    def step(S, inp):
        qc, kc, vc, ac = inp
        b = jnp.cumsum(ac, axis=1)
        rel = b[:, :, None] - b[:, None, :]
        decay = jnp.where(tril, jnp.exp(jnp.where(tril, rel, 0.0)), 0.0)
        attn = jnp.einsum('bihd,bjhd,bijhd->bhij', qc, kc, decay)
        o = (jnp.einsum('bhij,bjhe->bihe', attn, vc)
             + jnp.einsum('bihd,bhde->bihe', qc * jnp.exp(b), S))
        b_last = b[:, -1]
        S = (jnp.exp(b_last)[..., None] * S
             + jnp.einsum('bjhd,bjhe->bhde', kc * jnp.exp(b_last[:, None] - b), vc))
        return S, o

    S, o = lax.scan(step, S0.astype(jnp.float32), (chunks(q), chunks(k), chunks(v), chunks(log_a)))
    return jnp.moveaxis(o, 0, 1).reshape(B, T, H, DV), S


def gla_branch(q, k, v, log_a, r, S0, out_norm):
    o, S = gla_recurrence(q, k, v, log_a, S0)
    B, T = o.shape[:2]
    o = rms_norm(o, out_norm).reshape(B, T, -1) * jax.nn.silu(r.astype(jnp.float32))
    return o, S


def memory_kv(mem, mem_norm, w_kv, k_norm):
    B, M, _ = mem.shape
    kv = (rms_norm(mem, mem_norm) @ w_kv).reshape(B, M, 2, CROSS_HEADS, CROSS_HD)
    return rms_norm(kv[:, :, 0], k_norm), kv[:, :, 1]


def cross_attend(hn, mk, mv, w_q, q_norm, w_o):
    B, T, _ = hn.shape
    q = rms_norm((hn @ w_q).reshape(B, T, CROSS_HEADS, CROSS_HD), q_norm) * (CROSS_HD ** -0.5)
    s = jnp.einsum('bqhd,bmhd->bhqm', q, mk.astype(q.dtype)).astype(jnp.float32)
    p = jax.nn.softmax(s, axis=-1)
    o = jnp.einsum('bhqm,bmhd->bqhd', p, mv.astype(jnp.float32)).reshape(B, T, D_MODEL)
    return o.astype(hn.dtype) @ w_o


def peer_ffn(xn, w_q, q_norm, sub_keys, u, v):
    B, T, D = xn.shape
    n_tok = B * T
    blk = math.gcd(n_tok, PEER_BLOCK)
    xb = xn.reshape(n_tok // blk, blk, D)

    def one(xt):
        q = rms_norm((xt @ w_q).reshape(blk, PEER_HEADS, PEER_QD), q_norm)
        q = q.reshape(blk, PEER_HEADS, 2, PEER_QD // 2)
        s = jnp.einsum('thcd,hckd->thck', q, sub_keys).astype(jnp.float32)
        s1, i1 = lax.top_k(s[:, :, 0], PEER_TOPK)
        s2, i2 = lax.top_k(s[:, :, 1], PEER_TOPK)
        cand_s = (s1[..., :, None] + s2[..., None, :]).reshape(blk, PEER_HEADS, -1)
        cand_i = (i1[..., :, None] * N_KEYS + i2[..., None, :]).reshape(blk, PEER_HEADS, -1)
        top_s, j = lax.top_k(cand_s, PEER_TOPK)
        e = jnp.take_along_axis(cand_i, j, axis=-1).reshape(blk, -1)
        g = jax.nn.softmax(top_s, axis=-1).reshape(blk, -1)
        act = jax.nn.gelu(jnp.einsum('ted,td->te', u[e], xt).astype(jnp.float32))
        return jnp.einsum('te,ted->td', (g * act).astype(xt.dtype), v[e])

    return lax.map(one, xb).reshape(B, T, D)


def setup_inputs(seed: int = 0) -> dict:
    key = jax.random.key(seed)
    ks = iter(jax.random.split(key, 48))
    f32 = jnp.float32
    nrm = lambda shape, scale: jax.random.normal(next(ks), shape, f32) * scale
    gain = lambda shape: 1.0 + 0.02 * jax.random.normal(next(ks), shape, f32)
    n_pages = PAST_LEN // PAGE_SIZE
    n_used = DEC_BATCH * n_pages
    n_phys = n_used + max(1, n_used // 4)
    perm = jax.random.permutation(next(ks), n_phys)
    page_table = perm[:n_used].reshape(DEC_BATCH, n_pages).astype(jnp.int32)
    dsc = D_MODEL ** -0.5
    return {
        "x_prompt": nrm((BATCH, SEQ, D_MODEL), 1.0),
        "x_sample": nrm((DEC_BATCH, DEC_SEQ, D_MODEL), 1.0),
        "mem_prompt": nrm((BATCH, N_MEM, D_MODEL), 1.0),
        "cache_diff_k": nrm((DEPTH, n_phys, PAGE_SIZE, DIFF_HEADS, 2 * DIFF_QK), 1.0),
        "cache_diff_v": nrm((DEPTH, n_phys, PAGE_SIZE, DIFF_HEADS, DIFF_V), 1.0),
        "page_table": page_table,
        "state_gla": nrm((DEPTH, DEC_BATCH, GLA_HEADS, GLA_DK, GLA_DV), 1.0),
        "cache_mem_k": nrm((DEPTH, DEC_BATCH, N_MEM, CROSS_HEADS, CROSS_HD), 1.0),
        "cache_mem_v": nrm((DEPTH, DEC_BATCH, N_MEM, CROSS_HEADS, CROSS_HD), 1.0),
        "attn_norm": gain((DEPTH, D_MODEL)),
        "w_in": nrm((DEPTH, D_MODEL, IN_WIDTH), dsc),
        "diff_q_norm": gain((DEPTH, DIFF_QK)),
        "diff_k_norm": gain((DEPTH, DIFF_QK)),
        "lambda_q1": nrm((DEPTH, DIFF_QK), 0.1),
        "lambda_k1": nrm((DEPTH, DIFF_QK), 0.1),
        "lambda_q2": nrm((DEPTH, DIFF_QK), 0.1),
        "lambda_k2": nrm((DEPTH, DIFF_QK), 0.1),
        "diff_out_norm": gain((DEPTH, DIFF_V)),
        "gla_w_gate": nrm((DEPTH, GLA_GATE_RANK, GLA_HEADS * GLA_DK), GLA_GATE_RANK ** -0.5),
        "gla_b_gate": nrm((DEPTH, GLA_HEADS * GLA_DK), 0.1),
        "gla_out_norm": gain((DEPTH, GLA_DV)),
        "w_o": nrm((DEPTH, MIX_WIDTH, D_MODEL), MIX_WIDTH ** -0.5),
        "cross_norm": gain((DEPTH, D_MODEL)),
        "mem_norm": gain((DEPTH, D_MODEL)),
        "cross_w_q": nrm((DEPTH, D_MODEL, D_MODEL), dsc),
        "cross_w_kv": nrm((DEPTH, D_MODEL, 2 * D_MODEL), dsc),
        "cross_q_norm": gain((DEPTH, CROSS_HD)),
        "cross_k_norm": gain((DEPTH, CROSS_HD)),
        "cross_w_o": nrm((DEPTH, D_MODEL, D_MODEL), dsc),
        "ffn_norm": gain((DEPTH, D_MODEL)),
        "peer_w_q": nrm((DEPTH, D_MODEL, PEER_HEADS * PEER_QD), dsc),
        "peer_q_norm": gain((DEPTH, PEER_QD)),
        "peer_sub_keys": nrm((DEPTH, PEER_HEADS, 2, N_KEYS, PEER_QD // 2), (PEER_QD // 2) ** -0.5),
        "peer_u": nrm((DEPTH, N_EXPERTS, D_MODEL), dsc),
        "peer_v": nrm((DEPTH, N_EXPERTS, D_MODEL), (PEER_HEADS * PEER_TOPK) ** -0.5),
    }


def reference(x_prompt, x_sample, mem_prompt, cache_diff_k, cache_diff_v, page_table, state_gla,
              cache_mem_k, cache_mem_v, attn_norm, w_in, diff_q_norm, diff_k_norm, lambda_q1,
              lambda_k1, lambda_q2, lambda_k2, diff_out_norm, gla_w_gate, gla_b_gate, gla_out_norm,
              w_o, cross_norm, mem_norm, cross_w_q, cross_w_kv, cross_q_norm, cross_k_norm,
              cross_w_o, ffn_norm, peer_w_q, peer_q_norm, peer_sub_keys, peer_u, peer_v):
    B, S, _ = x_prompt.shape
    DB, T, _ = x_sample.shape
    n_pages = page_table.shape[1]
    past_len = n_pages * cache_diff_k.shape[2]
    hp, hs = x_prompt, x_sample
    pk, pv, pst, pmk, pmv, sk, sv, sst = [], [], [], [], [], [], [], []
    for l in range(DEPTH):
        lam_init = lambda_init(l)
        lam = diff_lambda(lambda_q1[l], lambda_k1[l], lambda_q2[l], lambda_k2[l], lam_init)

        xn = rms_norm(hp, attn_norm[l])
        dq, dk, dv, gq, gk, gv, la, gr = mixer_projections(
            xn, w_in[l], diff_q_norm[l], diff_k_norm[l], gla_w_gate[l], gla_b_gate[l])
        o_d = diff_post(diff_attend_prompt(dq, dk, dv, lam), diff_out_norm[l], lam_init)
        S0 = jnp.zeros((B, GLA_HEADS, GLA_DK, GLA_DV), jnp.float32)
        o_g, S_p = gla_branch(gq, gk, gv, la, gr, S0, gla_out_norm[l])
        hp = hp + jnp.concatenate([o_d, o_g], axis=-1).astype(hp.dtype) @ w_o[l]
        mk_p, mv_p = memory_kv(mem_prompt, mem_norm[l], cross_w_kv[l], cross_k_norm[l])
        hp = hp + cross_attend(rms_norm(hp, cross_norm[l]), mk_p, mv_p,
                               cross_w_q[l], cross_q_norm[l], cross_w_o[l])
        hp = hp + peer_ffn(rms_norm(hp, ffn_norm[l]), peer_w_q[l], peer_q_norm[l],
                           peer_sub_keys[l], peer_u[l], peer_v[l])
        pk.append(dk.reshape(B, S, DIFF_HEADS, 2 * DIFF_QK))
        pv.append(dv)
        pst.append(S_p.astype(x_prompt.dtype))
        pmk.append(mk_p)
        pmv.append(mv_p)

        xn = rms_norm(hs, attn_norm[l])
        dq, dk, dv, gq, gk, gv, la, gr = mixer_projections(
            xn, w_in[l], diff_q_norm[l], diff_k_norm[l], gla_w_gate[l], gla_b_gate[l])
        k_past = cache_diff_k[l, page_table].reshape(DB, past_len, DIFF_HEADS, 2, DIFF_QK)
        v_past = cache_diff_v[l, page_table].reshape(DB, past_len, DIFF_HEADS, DIFF_V)
        o_d = diff_post(diff_attend_sample(dq, dk, dv, k_past, v_past, lam), diff_out_norm[l], lam_init)
        o_g, S_s = gla_branch(gq, gk, gv, la, gr, state_gla[l], gla_out_norm[l])
        hs = hs + jnp.concatenate([o_d, o_g], axis=-1).astype(hs.dtype) @ w_o[l]
        hs = hs + cross_attend(rms_norm(hs, cross_norm[l]), cache_mem_k[l], cache_mem_v[l],
                               cross_w_q[l], cross_q_norm[l], cross_w_o[l])
        hs = hs + peer_ffn(rms_norm(hs, ffn_norm[l]), peer_w_q[l], peer_q_norm[l],
                           peer_sub_keys[l], peer_u[l], peer_v[l])
        sk.append(dk.reshape(DB, T, DIFF_HEADS, 2 * DIFF_QK))
        sv.append(dv)
        sst.append(S_s.astype(x_sample.dtype))

    return (hp, hs, jnp.stack(pk), jnp.stack(pv), jnp.stack(pst), jnp.stack(pmk), jnp.stack(pmv),
            jnp.stack(sk), jnp.stack(sv), jnp.stack(sst))
```

```python
from contextlib import ExitStack
import math
import numpy as np
import concourse.bass as bass
import concourse.mybir as mybir
from concourse.bass_utils import run_bass_kernel_spmd

F32 = mybir.dt.float32
BF16 = mybir.dt.bfloat16
I32 = mybir.dt.int32
U32 = mybir.dt.uint32
AF = mybir.ActivationFunctionType
ALU = mybir.AluOpType
AX = mybir.AxisListType

EPOCH = 6000
NDMA = 28
NSW = 8
EPS = 1e-6
NCORES = 8
D = 1024
SEQ = 2048
NT = SEQ // 128
NPB = 2
NSB = 4
TS = 32
NPAGES = 128
INW = 3088
LAM_INIT = 0.8 - 0.6 * math.exp(-0.3 * 0)
NTILES = NPB * NT + 1


class Prog:
    COMPUTE = ("pe", "act", "dve", "pool")
    ALL = ("pe", "act", "dve", "pool", "sp")

    def __init__(self, nc):
        self.nc = nc
        self.es = ExitStack()
        self.nsem = 0
        self.cur = {}
        self.cnt = {}
        for e in self.COMPUTE:
            self.cur[e] = self._newsem(e)
            self.cnt[e] = 0
        self.dsem = [self._newsem("d%d" % i) for i in range(NDMA)]
        self.dcnt = [0] * NDMA
        self.drr = 0
        self.wsem = [self._newsem("w%d" % i) for i in range(NSW)]
        self.wcnt = [0] * NSW
        self.wcons = [[] for _ in range(NSW)]
        self.wrr = 0
        self.wids = {id(s_): i for i, s_ in enumerate(self.wsem)}
        self.ops = {e: [] for e in self.ALL}
        self.res = {}
        self.known = {e: {} for e in self.ALL}
        self.last_tok = {}
        self.n_ops = 0

    def _newsem(self, name):
        self.nsem += 1
        return self.es.enter_context(self.nc.semaphore("s_%s_%d" % (name, self.nsem)))

    def _deps(self, eng, reads, writes):
        waits = {}

        def need(tok):
            sem, val, teng = tok
            if teng == eng and eng == "pe":
                return
            k = id(sem)
            if k not in waits or waits[k][1] < val:
                waits[k] = (sem, val)

        for r in reads:
            st = self.res.get(r)
            if st:
                for t in st["w"]:
                    need(t)
        for w in writes:
            st = self.res.get(w)
            if st:
                for t in st["w"]:
                    need(t)
                for t in st["r"]:
                    need(t)
        out = []
        kn = self.known[eng]
        for k, (sem, val) in waits.items():
            if kn.get(k, 0) >= val:
                continue
            kn[k] = val
            out.append((sem, val))
        return out

    def _commit(self, tok, reads, writes):
        for r in reads:
            st = self.res.setdefault(r, {"w": [], "r": []})
            st["r"] = [t for t in st["r"] if t[0] is not tok[0]] + [tok]
        for w in writes:
            st = self.res.setdefault(w, {"w": [], "r": []})
            if st["r"]:
                st["w"] = [tok]
                st["r"] = []
            else:
                st["w"] = [t for t in st["w"] if t[0] is not tok[0]] + [tok]
        self.last_tok[id(tok[0])] = tok

    @staticmethod
    def _excl(reads, writes):
        ex = [r for r in reads if (len(r) > 1 and r[0] == "p" and r[1].isupper()) or r.startswith("accS")]
        if ex:
            reads = [r for r in reads if r not in ex]
            writes = list(writes) + [r for r in ex if r not in writes]
        return reads, writes

    def op(self, eng, fn, reads=(), writes=()):
        reads, writes = self._excl(reads, writes)
        waits = self._deps(eng, reads, writes)
        sem = self.cur[eng]
        self.cnt[eng] += 1
        tok = (sem, self.cnt[eng], eng)
        self.ops[eng].append((waits, fn, sem, 1))
        self._note_sw(waits, tok)
        self._commit(tok, reads, writes)
        if self.cnt[eng] >= EPOCH:
            self.cur[eng] = self._newsem(eng)
            self.cnt[eng] = 0
        self.n_ops += 1
        return tok

    def dma(self, q, fn, reads=(), writes=()):
        j = self.drr
        self.drr = (self.drr + 1) % NDMA
        sem = self.dsem[j]
        waits = self._deps(q, reads, writes)
        if self.dcnt[j] > 0:
            k = id(sem)
            v = 16 * self.dcnt[j]
            if self.known[q].get(k, 0) < v:
                self.known[q][k] = v
                waits.append((sem, v))
        self.dcnt[j] += 1
        tok = (sem, 16 * self.dcnt[j], "dma")
        self.ops[q].append((waits, fn, sem, 16))
        self._note_sw(waits, tok)
        self._commit(tok, reads, writes)
        self.n_ops += 1
        return tok

    def _note_sw(self, waits, tok):
        for (s_, v_) in waits:
            i = self.wids.get(id(s_))
            if i is not None:
                self.wcons[i].append(tok)

    def dma_sw(self, fn, reads=(), writes=()):
        q = "pool"
        i = self.wrr
        self.wrr = (self.wrr + 1) % NSW
        sem = self.wsem[i]
        waits = self._deps(q, reads, writes)
        c = self.wcnt[i]
        if c > 0:
            k = id(sem)
            v = 16 * c
            if self.known[q].get(k, 0) < v:
                self.known[q][k] = v
                waits.append((sem, v))
        self.wcnt[i] = c + 1
        tok = (sem, 16 * (c + 1), "dma")
        self.ops[q].append((waits, fn, sem, 16))
        self._commit(tok, reads, writes)
        self.n_ops += 1
        return tok

    def barrier(self):
        toks = list(self.last_tok.values())
        for e in self.ALL:
            waits = []
            for (sem, val, teng) in toks:
                k = id(sem)
                if self.known[e].get(k, 0) >= val:
                    continue
                self.known[e][k] = val
                waits.append((sem, val))
            if waits:
                self.ops[e].append((waits, None, None, 0))
        self.res = {}

    def emit(self):
        nc = self.nc
        ops = self.ops

        def run(eng_obj, lst):
            for waits, fn, sem, inc in lst:
                for (s, v) in waits:
                    eng_obj.wait_ge(s, v)
                if fn is not None:
                    ins = fn(eng_obj)
                    if sem is not None:
                        ins.then_inc(sem, inc)

        with nc.Block() as blk:
            @blk.tensor
            def _(e):
                run(e, ops["pe"])

            @blk.scalar
            def _(e):
                run(e, ops["act"])

            @blk.vector
            def _(e):
                run(e, ops["dve"])

            @blk.gpsimd
            def _(e):
                run(e, ops["pool"])

            @blk.sync
            def _(e):
                run(e, ops["sp"])
        self.ops = {e: [] for e in self.ALL}

    def close(self):
        self.es.close()


def build(n_phys, cfg=None):
    cfg = cfg or {}
    NT_RUN = cfg.get("nt", NT)
    NPG_RUN = cfg.get("npages", NPAGES)
    PH = cfg.get("phases", "234")
    DBG = cfg.get("dbg", False)
    DO_S = cfg.get("sample", True)
    STOP = cfg.get("stop", 99)
    DO_P = cfg.get("prompt", True)

    nc = bass.Bass("TRN2", target_bir_lowering=False)
    di = lambda n, s, d=F32: nc.dram_tensor(n, list(s), d, kind="ExternalInput").ap()
    do = lambda n, s, d=F32: nc.dram_tensor(n, list(s), d, kind="ExternalOutput").ap()
    dscr = lambda n, s, d=F32: nc.dram_tensor(n, list(s), d, kind="Internal").ap()

    xp = di("xp", [NPB, SEQ, D])
    xs = di("xs", [TS, D])
    memp = di("memp", [NPB, 256, D])
    ck = di("ck", [n_phys * 128, 512])
    cv = di("cv", [n_phys * 128, 512])
    pt = di("pt", [NSB, NPAGES], I32)
    sgla = di("sgla", [NSB, 4, 64, 128])
    cmk = di("cmk", [NSB, 256, D])
    cmv = di("cmv", [NSB, 256, D])
    attn_norm = di("attn_norm", [D])
    w_in = di("w_in", [D, INW])
    diff_q_norm = di("diff_q_norm", [32])
    diff_k_norm = di("diff_k_norm", [32])
    lq1 = di("lambda_q1", [32]); lk1 = di("lambda_k1", [32])
    lq2 = di("lambda_q2", [32]); lk2 = di("lambda_k2", [32])
    diff_out_norm = di("diff_out_norm", [64])
    gla_w_gate = di("gla_w_gate", [16, 256])
    gla_b_gate = di("gla_b_gate", [256])
    gla_out_norm = di("gla_out_norm", [128])
    w_o = di("w_o", [D, D])
    cross_norm = di("cross_norm", [D])
    mem_norm = di("mem_norm", [D])
    cross_w_q = di("cross_w_q", [D, D])
    cross_w_kv = di("cross_w_kv", [D, 2 * D])
    cross_q_norm = di("cross_q_norm", [256])
    cross_k_norm = di("cross_k_norm", [256])
    cross_w_o = di("cross_w_o", [D, D])
    ffn_norm = di("ffn_norm", [D])
    peer_w_q = di("peer_w_q", [D, 2 * D])
    peer_q_norm = di("peer_q_norm", [256])
    peer_sk = di("peer_sub_keys", [16, 128, 128])
    peer_u = di("peer_u", [16384, D])
    peer_v = di("peer_v", [16384, D])
    c_ident = di("c_ident", [128, 128])
    c_triu = di("c_triu", [128, 128])
    c_iota = di("c_iota", [128, 128])
    c_m32 = di("c_m32", [32, 32])
    c_rowm = di("c_rowm", [32, 4])
    c_colm = di("c_colm", [128, 4 * 32])
    c_reset = di("c_reset", [128, 32])
    c_ones = di("c_ones", [128, 128])
    c_pidx = di("c_pidx", [128, 1])

    yp = do("yp", [NPB, SEQ, D])
    ys = do("ys", [TS, D])
    okp = do("okp", [NPB, SEQ, 512])
    ovp = do("ovp", [NPB, SEQ, 512])
    ostp = do("ostp", [NPB, 4, 64, 128])
    omk = do("omk", [NPB, 256, D])
    omv = do("omv", [NPB, 256, D])
    oks = do("oks", [TS, 512])
    ovs = do("ovs", [TS, 512])
    osts = do("osts", [NSB, 4, 64, 128])

    if DBG:
        h1s = do("h1s", [NTILES, 128, D])
        h2s = do("h2s", [NTILES, 128, D])
    else:
        h1s = dscr("h1s", [NTILES, 128, D])
        h2s = dscr("h2s", [NTILES, 128, D])
    hnTs = dscr("hnTs", [NTILES, 128, 8 * 128], BF16)
    r1s = dscr("r1s", [NTILES, 128, 128])
    r2s = dscr("r2s", [NTILES, 128, 128])
    rws = dscr("rws", [NTILES, 128, 128])
    uTs = dscr("uTs", [128, 128, 8 * 128], BF16)
    vbs = dscr("vbs", [128, 128, D], BF16)

    P = Prog(nc)

    def tile_of(kind, b, i):
        return NPB * NT if kind == "s" else b * NT + i

    def act_copy(out, in_, reads, writes, eng="act"):
        if eng == "act":
            P.op("act", lambda e: e.copy(out=out, in_=in_), reads, writes)
        else:
            P.op(eng, lambda e: e.tensor_copy(out=out, in_=in_), reads, writes)

    def rstd_from_ss(ss, n, T, w, rn, tmpn):
        P.op("dve", lambda e: e.tensor_scalar(out=ss, in0=ss, scalar1=1.0 / n, scalar2=EPS,
                                              op0=ALU.mult, op1=ALU.add), [rn], [rn])
        P.op("act", lambda e: e.activation(out=ss, in_=ss, func=AF.Sqrt), [rn], [rn])
        P.op("dve", lambda e: e.reciprocal(out=ss, in_=ss), [rn], [rn])

    with ExitStack() as S2:
        sb = lambda n, s, d=F32: S2.enter_context(nc.sbuf_tensor(n, list(s), d))
        ident = sb("ident", [128, 128]); ident_b = sb("ident_b", [128, 128], BF16)
        triu = sb("triu", [128, 128]); triu_b = sb("triu_b", [128, 128], BF16)
        m32 = sb("m32", [32, 32]); m32_b = sb("m32_b", [32, 32], BF16)
        rowm = sb("rowm", [32, 4]); colm = sb("colm", [128, 4, 32]); resetm = sb("resetm", [128, 32])
        ones = sb("ones", [128, 128])
        for (t_, src) in ((ident, c_ident), (triu, c_triu), (m32, c_m32), (rowm, c_rowm),
                          (resetm, c_reset), (ones, c_ones)):
            P.dma("sp", lambda e, t_=t_, src=src: e.dma_start(out=t_[:], in_=src), [], [t_.name])
        P.dma("sp", lambda e: e.dma_start(out=colm[:].rearrange("p s t -> p (s t)"), in_=c_colm), [], ["colm"])
        P.op("dve", lambda e: e.tensor_copy(out=ident_b[:], in_=ident[:]), ["ident"], ["ident_b"])
        P.op("dve", lambda e: e.tensor_copy(out=triu_b[:], in_=triu[:]), ["triu"], ["triu_b"])
        P.op("dve", lambda e: e.tensor_copy(out=m32_b[:], in_=m32[:]), ["m32"], ["m32_b"])

        iota_t = sb("iota_t", [128, 128])
        P.dma("sp", lambda e: e.dma_start(out=iota_t[:], in_=c_iota), [], ["iota_t"])
        iota_b = sb("iota_b", [128, 128], BF16)
        P.op("dve", lambda e: e.tensor_copy(out=iota_b[:], in_=iota_t[:]), ["iota_t"], ["iota_b"])
        S2p = ExitStack()
        sb = lambda n, s, d=F32: S2p.enter_context(nc.sbuf_tensor(n, list(s), d))
        if "2" in PH:
            w_in_b = sb("w_in_b", [128, 8, INW], BF16)
            w_o_b = sb("w_o_b", [128, 8, D], BF16)
            Sg = sb("Sg", [128, 2, 128])
            g_attn = sb("g_attn", [128, 8])
            qn_bc = sb("qn_bc", [128, 32]); kn_bc = sb("kn_bc", [128, 32])
            dn_bc = sb("dn_bc", [128, 64]); gn_bc = sb("gn_bc", [128, 128])
            negb = sb("negb", [128, 2]); wg = sb("wg", [16, 256])
            lamt = sb("lamt", [128, 4, 32]); lamw = sb("lamw", [128, 8]); neglam = sb("neglam", [128, 1])

            P.dma("sp", lambda e: e.dma_start(out=g_attn[:], in_=attn_norm.rearrange("(c p) -> p c", p=128), allow_slow_non_contiguous=True), [], ["g_attn"])
            P.dma("sp", lambda e: e.dma_start(out=negb[:], in_=gla_b_gate.rearrange("(c p) -> p c", p=128), allow_slow_non_contiguous=True), [], ["negb"])
            P.dma("sp", lambda e: e.dma_start(out=qn_bc[:], in_=diff_q_norm.partition_broadcast(128)), [], ["qn_bc"])
            P.dma("sp", lambda e: e.dma_start(out=kn_bc[:], in_=diff_k_norm.partition_broadcast(128)), [], ["kn_bc"])
            P.dma("sp", lambda e: e.dma_start(out=dn_bc[:], in_=diff_out_norm.partition_broadcast(128)), [], ["dn_bc"])
            P.dma("sp", lambda e: e.dma_start(out=gn_bc[:], in_=gla_out_norm.partition_broadcast(128)), [], ["gn_bc"])
            P.dma("sp", lambda e: e.dma_start(out=wg[:], in_=gla_w_gate), [], ["wg"])
            for k_, src in enumerate((lq1, lk1, lq2, lk2)):
                P.dma("sp", lambda e, k_=k_, src=src: e.dma_start(out=lamt[:, k_, :], in_=src.partition_broadcast(128)), [], ["lamt"])
            P.op("dve", lambda e: e.tensor_scalar(out=qn_bc[:], in0=qn_bc[:], scalar1=32.0 ** -0.5, scalar2=None, op0=ALU.mult), ["qn_bc"], ["qn_bc"])
            P.op("dve", lambda e: e.tensor_scalar(out=dn_bc[:], in0=dn_bc[:], scalar1=1.0 - LAM_INIT, scalar2=None, op0=ALU.mult), ["dn_bc"], ["dn_bc"])
            P.op("dve", lambda e: e.tensor_scalar(out=negb[:], in0=negb[:], scalar1=-1.0, scalar2=None, op0=ALU.mult), ["negb"], ["negb"])
            P.op("dve", lambda e: e.tensor_tensor(out=lamt[:, 0, :], in0=lamt[:, 0, :], in1=lamt[:, 1, :], op=ALU.mult), ["lamt"], ["lamt"])
            P.op("dve", lambda e: e.tensor_tensor(out=lamt[:, 2, :], in0=lamt[:, 2, :], in1=lamt[:, 3, :], op=ALU.mult), ["lamt"], ["lamt"])
            P.op("dve", lambda e: e.tensor_reduce(out=lamw[:, 0:1], in_=lamt[:, 0, :], axis=AX.X, op=ALU.add), ["lamt"], ["lamw"])
            P.op("dve", lambda e: e.tensor_reduce(out=lamw[:, 1:2], in_=lamt[:, 2, :], axis=AX.X, op=ALU.add), ["lamt"], ["lamw"])
            P.op("act", lambda e: e.activation(out=lamw[:, 2:4], in_=lamw[:, 0:2], func=AF.Exp), ["lamw"], ["lamw"])
            P.op("dve", lambda e: e.tensor_tensor(out=lamw[:, 4:5], in0=lamw[:, 3:4], in1=lamw[:, 2:3], op=ALU.subtract), ["lamw"], ["lamw"])
            P.op("dve", lambda e: e.tensor_scalar(out=neglam[:], in0=lamw[:, 4:5], scalar1=-LAM_INIT, scalar2=None, op0=ALU.add), ["lamw"], ["neglam"])

            with ExitStack() as S0:
                stg = [S0.enter_context(nc.sbuf_tensor("stg%d" % k_, [128, INW], F32)) for k_ in range(2)]
                for c in range(8):
                    s_ = stg[c % 2]
                    P.dma("sp", lambda e, c=c, s_=s_: e.dma_start(out=s_[:], in_=w_in[c * 128:(c + 1) * 128, :]), [], [s_.name])
                    if c % 2 == 0:
                        P.op("dve", lambda e, c=c, s_=s_: e.tensor_scalar(out=w_in_b[:, c, :], in0=s_[:], scalar1=g_attn[:, c:c + 1], scalar2=None, op0=ALU.mult), [s_.name, "g_attn"], ["w_in_b"])
                    else:
                        P.op("act", lambda e, c=c, s_=s_: e.activation(out=w_in_b[:, c, :], in_=s_[:], func=AF.Copy, scale=g_attn[:, c:c + 1]), [s_.name, "g_attn"], ["w_in_b"])
                for c in range(8):
                    s_ = stg[c % 2]
                    P.dma("sp", lambda e, c=c, s_=s_: e.dma_start(out=s_[:, 0:D], in_=w_o[c * 128:(c + 1) * 128, :]), [], [s_.name])
                    act_copy(w_o_b[:, c, :], s_[:, 0:D], [s_.name], ["w_o_b"], eng=("dve" if c % 2 == 0 else "act"))
                P.barrier()
                P.emit()

            with ExitStack() as SW:
                sw = lambda n, s, d=F32: SW.enter_context(nc.sbuf_tensor(n, list(s), d))
                xt = [sw("xt%d" % k_, [128, D]) for k_ in range(2)]
                xs_b = sw("xs_b", [128, D], BF16)
                xnT = sw("xnT", [128, 8, 128], BF16)
                sq = sw("sq", [128, D])
                tmp = sw("tmp", [128, 512])
                st = sw("st", [128, 64])
                qn_b = sw("qn_b", [128, 512], BF16)
                kn = [sw("kn%d" % k_, [128, 512]) for k_ in range(2)]
                kn_b = sw("kn_b", [128, 512], BF16)
                QTb = sw("QTb", [64, 8, 2, 128], BF16)
                dv_sb = [sw("dv%d" % k_, [128, 512]) for k_ in range(2)]
                gv_sb = sw("gv_sb", [128, 512])
                sr = sw("sr", [128, 512])
                gaT = sw("gaT", [16, 128])
                e_sb = sw("e_sb", [128, 2, 128]); sp_sb = sw("sp_sb", [128, 2, 128])
                bT = sw("bT", [128, 2, 128]); eb = sw("eb", [128, 2, 128]); enb = sw("enb", [128, 2, 128])
                tmpE = sw("tmpE", [128, 2, 128])
                qdz = sw("qdz", [128, 2, 2, 128]); kd = sw("kd", [128, 2, 128])
                kd2T = sw("kd2T", [128, 2, 128]); kd2 = sw("kd2", [128, 2, 128])
                AT = sw("AT", [128, 4, 128])
                o_all = sw("o_all", [128, 8, 64])
                PT = [sw("PT%d" % k_, [128, 2, 128], BF16) for k_ in range(3)]
                cat = sw("cat", [128, D], BF16)
                catT = sw("catT", [128, 8, 128], BF16)
                h1 = [sw("h1_%d" % k_, [128, D]) for k_ in range(2)]
                qdp = sw("qdp", [128, 4, 2, 2, 32])
                S0s = sw("S0s", [128, 4, 2, 128])
                Sns = sw("Sns", [128, 4, 2, 128])
                bl = sw("bl", [128, 8]); ebl = sw("ebl", [128, 8])
                Vnew = sw("Vnew", [32, 8, 65], BF16)
                KTn = sw("KTn", [64, 8, 32], BF16)

                P.op("pool", lambda e: e.memset(QTb[:], 0.0), [], ["QTb"])
                P.op("pool", lambda e: e.memset(qdz[:], 0.0), [], ["qdz"])
                P.op("pool", lambda e: e.memset(Vnew[:], 1.0), [], ["Vnew"])

                with ExitStack() as SA:
                    KT = SA.enter_context(nc.sbuf_tensor("KT", [64, 8, SEQ], BF16))
                    Vaug = SA.enter_context(nc.sbuf_tensor("Vaug", [128, NT, 8, 65], BF16))
                    P.op("pool", lambda e: e.memset(Vaug[:], 1.0), [], ["Vaug"])
                    pA = SA.enter_context(nc.psum_tensor("pA", [128, 1024], F32))
                    pT = SA.enter_context(nc.psum_tensor("pT", [128, 8, 128], BF16))
                    pSb = [SA.enter_context(nc.psum_tensor("pS%d" % k_, [128, 512], F32)) for k_ in range(2)]
                    pAccF = [SA.enter_context(nc.psum_tensor("pAcc%d" % k_, [128, 512], F32)) for k_ in range(2)]
                    pAcc = [t_[:, 0:130].rearrange("p (c d) -> p c d", d=65) for t_ in pAccF]
                    pF = SA.enter_context(nc.psum_tensor("pF", [128, 512], F32))
                    pG = pA[:, 0:512]

                    cnt = {"s": 0, "pt": 0, "x": 0, "kn": 0, "dv": 0, "h1": 0}

                    def load_x(kind, b, i):
                        buf = xt[cnt["x"] % 2]; cnt["x"] += 1
                        T = 128 if kind == "p" else TS
                        src = xp[b, i * 128:(i + 1) * 128, :] if kind == "p" else xs
                        P.dma("sp", lambda e: e.dma_start(out=buf[:T, :], in_=src), [], [buf.name])
                        return buf

                    def transposes_bf(src_tile, T, dstT, n, src_res, dst_res, width=128):
                        for k_ in range(n):
                            P.op("pe", lambda e, k_=k_: e.transpose(out=pT[:width, k_, :T], in_=src_tile[:T, k_ * width:(k_ + 1) * width], identity=ident_b[:T, :T]),
                                 [src_res, "ident_b"], ["pT"])

                    def mixer_front(kind, b, i, xbuf):
                        T = 128 if kind == "p" else TS
                        xr = xbuf.name
                        P.op("act", lambda e: e.activation(out=sq[:T, :], in_=xbuf[:T, :], func=AF.Square, accum_out=st[:T, 0:1]), [xr], ["sq", "st0"])
                        rstd_from_ss(st[:T, 0:1], D, T, 1, "st0", None)
                        P.op("act", lambda e: e.activation(out=xs_b[:T, :], in_=xbuf[:T, :], func=AF.Copy, scale=st[:T, 0:1]), [xr, "st0"], ["xs_b"])
                        transposes_bf(xs_b, T, xnT, 8, "xs_b", "xnT")
                        P.op("dve", lambda e: e.tensor_copy(out=xnT[:, :, :T], in_=pT[:, :, :T]), ["pT"], ["xnT"])

                        if STOP <= 1:
                            return
                        def proj_tok(col0, ncols, half):
                            for c in range(8):
                                P.op("pe", lambda e, c=c: e.matmul(pA[:T, half * 512:half * 512 + ncols], lhsT=xnT[:, c, :T], rhs=w_in_b[:, c, col0:col0 + ncols],
                                                                   start=(c == 0), stop=(c == 7)), ["xnT", "w_in_b"], ["pA%d" % half])
                            return pA[:T, half * 512:half * 512 + ncols], "pA%d" % half

                        def qknorm(src, sres, gain, out_ap, out_res):
                            P.op("act", lambda e: e.activation(out=sq[:T, 0:512], in_=src, func=AF.Square), [sres], ["sq"])
                            P.op("dve", lambda e: e.tensor_reduce(out=st[:T, 8:24], in_=sq[:T, 0:512].rearrange("p (g d) -> p g d", d=32), axis=AX.X, op=ALU.add), ["sq"], ["st8"])
                            rstd_from_ss(st[:T, 8:24], 32, T, 16, "st8", None)
                            P.op("dve", lambda e: e.tensor_tensor(out=tmp[:T, :].rearrange("p (g d) -> p g d", d=32), in0=src.rearrange("p (g d) -> p g d", d=32),
                                                                  in1=st[:T, 8:24].unsqueeze(2).to_broadcast([T, 16, 32]), op=ALU.mult), [sres, "st8"], ["tmp"])
                            P.op("dve", lambda e: e.tensor_tensor(out=out_ap.rearrange("p (g d) -> p g d", d=32), in0=tmp[:T, :].rearrange("p (g d) -> p g d", d=32),
                                                                  in1=gain[:T, :].unsqueeze(1).to_broadcast([T, 16, 32]), op=ALU.mult), ["tmp", gain.name], [out_res])

                        src, sres = proj_tok(0, 512, 0)
                        qknorm(src, sres, qn_bc, qn_b[:T, :], "qn_b")
                        if STOP <= 2:
                            return
                        src, sres = proj_tok(512, 512, 1)
                        knb = kn[cnt["kn"] % 2]; cnt["kn"] += 1
                        qknorm(src, sres, kn_bc, knb[:T, :], knb.name)
                        dstk = okp[b, i * 128:(i + 1) * 128, :] if kind == "p" else oks
                        P.dma("sp", lambda e: e.dma_start(out=dstk, in_=knb[:T, :]), [knb.name], [])
                        P.op("act", lambda e: e.copy(out=kn_b[:T, :], in_=knb[:T, :]), [knb.name], ["kn_b"])
                        transposes_bf(qn_b, T, None, 8, "qn_b", None, width=64)
                        P.op("dve", lambda e: e.tensor_copy(out=QTb[0:32, :, 0, :T], in_=pT[0:32, :, :T]), ["pT"], ["QTb"])
                        P.op("dve", lambda e: e.tensor_copy(out=QTb[32:64, :, 1, :T], in_=pT[32:64, :, :T]), ["pT"], ["QTb"])
                        transposes_bf(kn_b, T, None, 8, "kn_b", None, width=64)
                        if kind == "p":
                            P.op("dve", lambda e: e.tensor_copy(out=KT[:, :, i * 128:(i + 1) * 128], in_=pT[0:64, :, :]), ["pT"], ["KT%d" % i])
                        else:
                            P.op("dve", lambda e: e.tensor_copy(out=KTn[:, :, :], in_=pT[0:64, :, :T]), ["pT"], ["KTn"])
                        if STOP <= 3:
                            return
                        src, sres = proj_tok(1024, 512, 0)
                        dvb = dv_sb[cnt["dv"] % 2]; cnt["dv"] += 1
                        P.op("act", lambda e, src=src: e.copy(out=dvb[:T, :], in_=src), [sres], [dvb.name])
                        dstv = ovp[b, i * 128:(i + 1) * 128, :] if kind == "p" else ovs
                        P.dma("sp", lambda e: e.dma_start(out=dstv, in_=dvb[:T, :]), [dvb.name], [])
                        if kind == "p":
                            P.op("pool", lambda e: e.tensor_copy(out=Vaug[:, i, :, 0:64], in_=dvb[:, :].rearrange("p (h d) -> p h d", d=64)), [dvb.name, "Vaug"], ["Vaug%d" % i])
                        else:
                            P.op("pool", lambda e: e.tensor_copy(out=Vnew[:, :, 0:64], in_=dvb[:T, :].rearrange("p (h d) -> p h d", d=64)), [dvb.name], ["Vnew"])
                        src, sres = proj_tok(2048, 512, 1)
                        P.op("act", lambda e, src=src: e.copy(out=gv_sb[:T, :], in_=src), [sres], ["gv_sb"])
                        src, sres = proj_tok(2576, 512, 0)
                        P.op("act", lambda e, src=src: e.activation(out=sr[:T, :], in_=src, func=AF.Silu), [sres], ["sr"])
                        if STOP <= 4:
                            return
                        for k_, col0 in enumerate((1536, 1664, 1792, 1920)):
                            for c in range(8):
                                P.op("pe", lambda e, c=c, k_=k_, col0=col0: e.matmul(pF[:, k_ * 128:k_ * 128 + T], lhsT=w_in_b[:, c, col0:col0 + 128], rhs=xnT[:, c, :T],
                                                                                     start=(c == 0), stop=(c == 7)), ["xnT", "w_in_b"], ["pF"])
                        for c in range(8):
                            P.op("pe", lambda e, c=c: e.matmul(pG[0:16, 0:T], lhsT=w_in_b[:, c, 2560:2576], rhs=xnT[:, c, :T], start=(c == 0), stop=(c == 7)),
                                 ["xnT", "w_in_b"], ["pA0"])
                        P.op("act", lambda e: e.copy(out=gaT[:, :T], in_=pG[0:16, 0:T]), ["pA0"], ["gaT"])
                        if STOP <= 5:
                            return
                        for m in range(2):
                            P.op("pe", lambda e, m=m: e.matmul(pG[:, 128 + m * 128:128 + m * 128 + T], lhsT=wg[:, m * 128:(m + 1) * 128], rhs=gaT[:, :T], start=True, stop=True),
                                 ["wg", "gaT"], ["pA0"])
                        for m in range(2):
                            P.op("act", lambda e, m=m: e.activation(out=e_sb[:, m, :T], in_=pG[:, 128 + m * 128:128 + m * 128 + T], func=AF.Exp, scale=-1.0, bias=negb[:, m:m + 1]),
                                 ["pA0", "negb"], ["e_sb"])
                        P.op("act", lambda e: e.activation(out=sp_sb[:, :, :T], in_=e_sb[:, :, :T], func=AF.Ln, bias=1.0), ["e_sb"], ["sp_sb"])
                        for m in range(2):
                            d0 = ones[:, :T] if kind == "p" else resetm[:, :T]
                            P.op("dve", lambda e, m=m, d0=d0: e.tensor_tensor_scan(out=bT[:, m, :T], data0=d0, data1=sp_sb[:, m, :T], initial=0.0, op0=ALU.mult, op1=ALU.subtract),
                                 ["sp_sb", "ones", "resetm"], ["bT"])
                        P.op("act", lambda e: e.activation(out=eb[:, :, :T], in_=bT[:, :, :T], func=AF.Exp, scale=1.0 / 16), ["bT"], ["eb"])
                        P.op("act", lambda e: e.activation(out=enb[:, :, :T], in_=bT[:, :, :T], func=AF.Exp, scale=-1.0 / 16), ["bT"], ["enb"])
                        for m in range(2):
                            for hh in range(2):
                                rws = slice(hh * 64, (hh + 1) * 64)
                                P.op("dve", lambda e, m=m, hh=hh, rws=rws: e.scalar_tensor_tensor(out=qdz[rws, m, hh, :T], in0=pF[rws, m * 128:m * 128 + T], scalar=0.125, in1=eb[rws, m, :T], op0=ALU.mult, op1=ALU.mult),
                                     ["pF", "eb"], ["qdz"])
                            P.op("dve", lambda e, m=m: e.tensor_tensor(out=kd[:, m, :T], in0=pF[:, 256 + m * 128:256 + m * 128 + T], in1=enb[:, m, :T], op=ALU.mult),
                                 ["pF", "enb"], ["kd"])
                        if STOP <= 6:
                            return
                        for h in range(4):
                            m, hh = h // 2, h % 2
                            rows = slice(hh * 64, (hh + 1) * 64)
                            P.op("pe", lambda e, h=h, m=m, hh=hh: e.matmul(pG[:T, h * 128:h * 128 + T], lhsT=kd[:, m, :T], rhs=qdz[:, m, hh, :T], start=True, stop=True),
                                 ["kd", "qdz"], ["pA0"])
                        msk = triu if kind == "p" else m32
                        P.op("dve", lambda e: e.tensor_tensor(out=AT[:T, :, :T], in0=pG[:T, :].rearrange("p (h t) -> p h t", t=128)[:, :, :T],
                                                              in1=msk[:T, :T].unsqueeze(1).to_broadcast([T, 4, T]), op=ALU.mult), ["pA0", msk.name], ["AT"])
                        if STOP <= 7:
                            return
                        if kind == "s":
                            for s_ in range(NSB):
                                P.op("dve", lambda e, s_=s_: e.tensor_tensor(out=qdp[:, s_, :, :, :].rearrange("p m h t -> p (m h) t"), in0=qdz[:, :, :, :T].rearrange("p m h t -> p (m h) t"),
                                                                             in1=colm[:, s_, :].unsqueeze(1).to_broadcast([128, 4, TS]), op=ALU.mult),
                                     ["qdz", "colm"], ["qdp"])
                        for h in range(4):
                            m, hh = h // 2, h % 2
                            rows = slice(hh * 64, (hh + 1) * 64)
                            oo = pA[:T, 512 + h * 128:512 + (h + 1) * 128]
                            P.op("pe", lambda e, h=h, oo=oo: e.matmul(oo, lhsT=AT[:T, h, :T], rhs=gv_sb[:T, h * 128:(h + 1) * 128], start=True, stop=False),
                                 ["AT", "gv_sb"], ["pA1"])
                            if kind == "p":
                                P.op("pe", lambda e, m=m, hh=hh, oo=oo: e.matmul(oo, lhsT=qdz[:, m, hh, :T], rhs=Sg[:, m, :], start=False, stop=True),
                                     ["qdz", "Sg"], ["pA1"])
                            else:
                                for s_ in range(NSB):
                                    P.op("pe", lambda e, m=m, hh=hh, oo=oo, s_=s_: e.matmul(oo, lhsT=qdp[:, s_, m, hh, :], rhs=S0s[:, s_, m, :], start=False, stop=(s_ == NSB - 1)),
                                         ["qdp", "S0s"], ["pA1"])
                        if STOP <= 8:
                            return
                        nseq = 1 if kind == "p" else NSB
                        for s_ in range(nseq):
                            last = T - 1 if kind == "p" else 8 * s_ + 7
                            P.op("dve", lambda e, s_=s_, last=last: e.tensor_scalar(out=bl[:, 2 * s_:2 * s_ + 2], in0=bT[:, :, last], scalar1=1.0 / 16, scalar2=None, op0=ALU.mult),
                                 ["bT"], ["bl"])
                            P.op("act", lambda e, s_=s_: e.activation(out=ebl[:, 2 * s_:2 * s_ + 2], in_=bl[:, 2 * s_:2 * s_ + 2], func=AF.Exp), ["bl"], ["ebl"])
                            for m in range(2):
                                P.op("act", lambda e, m=m, s_=s_: e.activation(out=tmpE[:, m, :T], in_=bT[:, m, :T], func=AF.Exp, scale=-1.0 / 16, bias=bl[:, 2 * s_ + m:2 * s_ + m + 1]),
                                     ["bT", "bl"], ["tmpE"])
                                P.op("dve", lambda e, m=m: e.tensor_tensor(out=kd2T[:, m, :T], in0=pF[:, 256 + m * 128:256 + m * 128 + T], in1=tmpE[:, m, :T], op=ALU.mult),
                                     ["pF", "tmpE"], ["kd2T"])
                                P.op("pe", lambda e, m=m: e.transpose(out=pG[:T, m * 128:(m + 1) * 128], in_=kd2T[:, m, :T], identity=ident[:, :]), ["kd2T", "ident"], ["pA0"])
                            if kind == "p":
                                P.op("dve", lambda e: e.tensor_copy(out=kd2[:T, :, :], in_=pG[:T, 0:256].rearrange("p (m f) -> p m f", m=2)), ["pA0"], ["kd2"])
                            else:
                                P.op("dve", lambda e, s_=s_: e.tensor_scalar(out=kd2[:T, :, :], in0=pG[:T, 0:256].rearrange("p (m f) -> p m f", m=2), scalar1=rowm[:T, s_:s_ + 1], scalar2=None, op0=ALU.mult),
                                     ["pA0", "rowm"], ["kd2"])
                            for m in range(2):
                                P.op("pe", lambda e, m=m: e.matmul(pG[:, 256:512], lhsT=kd2[:T, m, :], rhs=gv_sb[:T, m * 256:(m + 1) * 256], start=True, stop=True),
                                     ["kd2", "gv_sb"], ["pA0"])
                                for hh in range(2):
                                    rows = slice(hh * 64, (hh + 1) * 64)
                                    if kind == "p":
                                        P.op("dve", lambda e, m=m, hh=hh, rows=rows: e.scalar_tensor_tensor(out=Sg[rows, m, :], in0=Sg[rows, m, :], scalar=ebl[rows, m:m + 1],
                                                                                                         in1=pG[rows, 256 + hh * 128:256 + (hh + 1) * 128], op0=ALU.mult, op1=ALU.add),
                                             ["Sg", "ebl", "pA0"], ["Sg"])
                                    else:
                                        P.op("dve", lambda e, m=m, hh=hh, rows=rows, s_=s_: e.scalar_tensor_tensor(out=Sns[rows, s_, m, :], in0=S0s[rows, s_, m, :], scalar=ebl[rows, 2 * s_ + m:2 * s_ + m + 1],
                                                                                                                  in1=pG[rows, 256 + hh * 128:256 + (hh + 1) * 128], op0=ALU.mult, op1=ALU.add),
                                             ["S0s", "ebl", "pA0"], ["Sns"])
                        if STOP <= 9:
                            return
                        og = pA[:T, 512:1024]
                        P.op("act", lambda e: e.activation(out=sq[:T, 0:512], in_=og, func=AF.Square), ["pA1"], ["sq"])
                        P.op("dve", lambda e: e.tensor_reduce(out=st[:T, 24:28], in_=sq[:T, 0:512].rearrange("p (g d) -> p g d", d=128), axis=AX.X, op=ALU.add), ["sq"], ["st24"])
                        rstd_from_ss(st[:T, 24:28], 128, T, 4, "st24", None)
                        P.op("dve", lambda e: e.tensor_tensor(out=tmp[:T, :].rearrange("p (g d) -> p g d", d=128), in0=og.rearrange("p (g d) -> p g d", d=128),
                                                              in1=st[:T, 24:28].unsqueeze(2).to_broadcast([T, 4, 128]), op=ALU.mult), ["pA1", "st24"], ["tmp"])
                        P.op("dve", lambda e: e.tensor_tensor(out=tmp[:T, :].rearrange("p (g d) -> p g d", d=128), in0=tmp[:T, :].rearrange("p (g d) -> p g d", d=128),
                                                              in1=gn_bc[:T, :].unsqueeze(1).to_broadcast([T, 4, 128]), op=ALU.mult), ["tmp", "gn_bc"], ["tmp"])
                        P.op("dve", lambda e: e.tensor_tensor(out=cat[:T, 512:1024], in0=tmp[:T, :], in1=sr[:T, :], op=ALU.mult), ["tmp", "sr"], ["cat_g"])

                    def attn_normalize(T, acc, h):
                        P.op("dve", lambda e: e.reciprocal(out=st[:T, 32:34], in_=acc[:T, :, 64]), [acc.name], ["st32"])
                        P.op("dve", lambda e: e.tensor_tensor(out=st[:T, 34:35], in0=st[:T, 33:34], in1=neglam[:T, :], op=ALU.mult), ["st32", "neglam"], ["st34"])
                        P.op("dve", lambda e: e.tensor_scalar(out=o_all[:T, h, :], in0=acc[:T, 0, 0:64], scalar1=st[:T, 32:33], scalar2=None, op0=ALU.mult), [acc.name, "st32"], ["o_all"])
                        P.op("dve", lambda e: e.scalar_tensor_tensor(out=o_all[:T, h, :], in0=acc[:T, 1, 0:64], scalar=st[:T, 34:35], in1=o_all[:T, h, :], op0=ALU.mult, op1=ALU.add),
                             [acc.name, "st34", "o_all"], ["o_all"])

                    def diff_post_and_out(kind, b, i, xbuf, pAo, pTt):
                        T = 128 if kind == "p" else TS
                        P.op("act", lambda e: e.activation(out=sq[:T, 0:512], in_=o_all[:T, :, :].rearrange("p h d -> p (h d)"), func=AF.Square), ["o_all"], ["sq"])
                        P.op("dve", lambda e: e.tensor_reduce(out=st[:T, 36:44], in_=sq[:T, 0:512].rearrange("p (g d) -> p g d", d=64), axis=AX.X, op=ALU.add), ["sq"], ["st36"])
                        rstd_from_ss(st[:T, 36:44], 64, T, 8, "st36", None)
                        P.op("dve", lambda e: e.tensor_tensor(out=o_all[:T, :, :], in0=o_all[:T, :, :], in1=st[:T, 36:44].unsqueeze(2).to_broadcast([T, 8, 64]), op=ALU.mult), ["o_all", "st36"], ["o_all"])
                        P.op("dve", lambda e: e.tensor_tensor(out=cat[:T, 0:512].rearrange("p (h d) -> p h d", d=64), in0=o_all[:T, :, :], in1=dn_bc[:T, :].unsqueeze(1).to_broadcast([T, 8, 64]), op=ALU.mult),
                             ["o_all", "dn_bc"], ["cat_d"])
                        for k_ in range(8):
                            P.op("pe", lambda e, k_=k_: e.transpose(out=pTt[:, k_, :T], in_=cat[:T, k_ * 128:(k_ + 1) * 128], identity=ident_b[:T, :T]), ["cat_d", "cat_g", "ident_b"], ["pT"])
                        P.op("dve", lambda e: e.tensor_copy(out=catT[:, :, :T], in_=pTt[:, :, :T]), ["pT"], ["catT"])
                        hb = h1[cnt["h1"] % 2]; cnt["h1"] += 1
                        for half in range(2):
                            po = pAo[half]
                            for c in range(8):
                                P.op("pe", lambda e, c=c, half=half, po=po: e.matmul(po[0][:T, po[1]:po[1] + 512], lhsT=catT[:, c, :T], rhs=w_o_b[:, c, half * 512:(half + 1) * 512], start=(c == 0), stop=(c == 7)),
                                     ["catT", "w_o_b"], [po[2]])
                            P.op("dve", lambda e, half=half, po=po: e.tensor_tensor(out=hb[:T, half * 512:(half + 1) * 512], in0=po[0][:T, po[1]:po[1] + 512], in1=xbuf[:T, half * 512:(half + 1) * 512], op=ALU.add),
                                 [po[2], xbuf.name], [hb.name])
                        ti = tile_of(kind, b, i)
                        P.dma("sp", lambda e: e.dma_start(out=h1s[ti, :T, :], in_=hb[:T, :]), [hb.name], [])

                    def prompt_attention(i):
                        for h in range(8):
                            acc = pAcc[h % 2]
                            for j in range(i + 1):
                                sl = cnt["s"] % 2; cnt["s"] += 1
                                ptb = PT[cnt["pt"] % 3]; cnt["pt"] += 1
                                P.op("pe", lambda e, sl=sl, j=j, h=h: e.matmul(pSb[sl][:, 0:256], lhsT=KT[:, h, j * 128:(j + 1) * 128], rhs=QTb[:, h, :, :], start=True, stop=True),
                                     ["KT%d" % j, "QTb"], ["pS%d" % sl])
                                P.op("act", lambda e, sl=sl, ptb=ptb: e.activation(out=ptb[:, :, :].rearrange("p c q -> p (c q)"), in_=pSb[sl][:, 0:256], func=AF.Exp), ["pS%d" % sl], [ptb.name])
                                if j == i:
                                    P.op("pool", lambda e, ptb=ptb: e.tensor_tensor(out=ptb[:, :, :], in0=ptb[:, :, :], in1=triu_b[:, :].unsqueeze(1).to_broadcast([128, 2, 128]), op=ALU.mult),
                                         [ptb.name, "triu_b"], [ptb.name])
                                for c in range(2):
                                    P.op("pe", lambda e, c=c, j=j, h=h, acc=acc, ptb=ptb: e.matmul(acc[:, c, :], lhsT=ptb[:, c, :], rhs=Vaug[:, j, h, :], start=(j == 0 and c == 0), stop=(j == i),
                                                                                            skip_group_check=True), [ptb.name, "Vaug%d" % j], [acc.name])
                            attn_normalize(128, acc, h)

                    pAo = [(pA, 0, "pA0"), (pA, 512, "pA1")]
                    for b in range(NPB if DO_P else 0):
                        P.op("pool", lambda e: e.memset(Sg[:], 0.0), ["Sg"], ["Sg"])
                        nxt = load_x("p", b, 0)
                        for i in range(NT_RUN):
                            xbuf = nxt
                            if i + 1 < NT_RUN:
                                nxt = load_x("p", b, i + 1)
                            mixer_front("p", b, i, xbuf)
                            if STOP > 10:
                                prompt_attention(i)
                            if STOP > 11:
                                diff_post_and_out("p", b, i, xbuf, pAo, pT)
                        P.dma("sp", lambda e, b=b: e.dma_start(out=ostp[b].rearrange("(m hh) d v -> (hh d) m v", hh=2), in_=Sg[:]), ["Sg"], [])
                    if DO_S:
                        P.dma("sp", lambda e: e.dma_start(out=S0s[:], in_=sgla.rearrange("s (m hh) d v -> (hh d) s m v", hh=2)), [], ["S0s"])
                        xsb = load_x("s", 0, 0)
                        mixer_front("s", 0, 0, xsb)
                        P.dma("sp", lambda e: e.dma_start(out=osts.rearrange("s (m hh) d v -> (hh d) s m v", hh=2), in_=Sns[:]), ["Sns"], [])
                    P.barrier()
                    P.emit()

                with ExitStack() as SBk:
                  if DO_S:
                    sbk = lambda n, s, d=F32: SBk.enter_context(nc.sbuf_tensor(n, list(s), d))
                    pKT = SBk.enter_context(nc.psum_tensor("pKT", [128, 4, 128], F32))
                    pSsF = SBk.enter_context(nc.psum_tensor("pSs", [128, 512], F32))
                    pSs = pSsF[:, 0:128].rearrange("p (g q) -> p g q", q=8)
                    accSF = [SBk.enter_context(nc.psum_tensor("accS%d" % k_, [128, 512], F32)) for k_ in range(4)]
                    accS = [t_[0:32, 0:260].rearrange("p (a b) -> p a b", b=65) for t_ in accSF]
                    pT2 = SBk.enter_context(nc.psum_tensor("pT2", [128, 8, 128], BF16))
                    pA2 = SBk.enter_context(nc.psum_tensor("pA2", [128, 512], F32))
                    NB = 3
                    kpg = [sbk("kpg%d" % k_, [128, 512]) for k_ in range(NB)]
                    vpg = [sbk("vpg%d" % k_, [128, 512]) for k_ in range(NB)]
                    kTp = [sbk("kTp%d" % k_, [128, 4, 128], BF16) for k_ in range(2)]
                    vpa = [sbk("vpa%d" % k_, [128, 8, 65], BF16) for k_ in range(2)]
                    PTp = [[sbk("PTp%d_%d" % (s_, k_), [128, 16, 32], BF16) for k_ in range(2)] for s_ in range(NSB)]
                    Qb2 = sbk("Qb2", [128, 4, NSB, 4, 8], BF16)
                    ptf = sbk("ptf", [128, NSB * NPAGES])
                    pti = sbk("pti", [128, NSB * NPAGES], I32)
                    pidx = sbk("pidx", [128, 1])
                    zz = sbk("zz", [1, 512], BF16)
                    for s_ in range(NSB):
                        for k_ in range(2):
                            P.op("pool", lambda e, t_=PTp[s_][k_]: e.memset(t_[:], 0.0), [], [PTp[s_][k_].name])
                    for k_ in range(2):
                        P.op("pool", lambda e, k_=k_: e.memset(vpa[k_][:], 1.0), [], [vpa[k_].name])
                    P.op("pool", lambda e: e.memset(Qb2[:], 0.0), [], ["Qb2"])
                    P.op("pool", lambda e: e.memset(zz[:], 0.0), [], ["zz"])
                    for pr in range(4):
                        for hh in range(2):
                            h = 2 * pr + hh
                            for c in range(2):
                                rows_src = slice(c * 32, (c + 1) * 32)
                                P.dma("sp", lambda e, pr=pr, hh=hh, h=h, c=c, rows_src=rows_src: e.dma_start(
                                    out=Qb2[hh * 64 + c * 32:hh * 64 + (c + 1) * 32, pr, :, 2 * hh + c, :],
                                    in_=QTb[rows_src, h, c, 0:TS].rearrange("p (s q) -> p s q", q=8)), ["QTb"], ["Qb2"])
                    P.dma("sp", lambda e: e.dma_start(out=pti[:], in_=pt.rearrange("s n -> (s n)").partition_broadcast(128)), [], ["pti"])
                    P.op("dve", lambda e: e.tensor_copy(out=ptf[:], in_=pti[:]), ["pti"], ["ptf"])
                    P.dma("sp", lambda e: e.dma_start(out=pidx[:], in_=c_pidx), [], ["pidx"])
                    P.op("dve", lambda e: e.tensor_scalar(out=ptf[:], in0=ptf[:], scalar1=128.0, scalar2=pidx[:, 0:1], op0=ALU.mult, op1=ALU.add), ["ptf", "pidx"], ["ptf"])
                    P.op("dve", lambda e: e.tensor_copy(out=pti[:], in_=ptf[:]), ["ptf"], ["pti"])
                    for k_ in range(4):
                        P.op("pe", lambda e, k_=k_: e.matmul(accS[k_][:, :, :].rearrange("p a b -> p (a b)"), lhsT=zz[0:1, 0:32], rhs=zz[0:1, 0:260], start=True, stop=False, skip_group_check=True),
                             ["zz"], ["accS"])
                    pages = [(s_, n_) for s_ in range(NSB) for n_ in range(NPG_RUN)]

                    def issue_page(k_):
                        s_, n_ = pages[k_]
                        kb, vb = kpg[k_ % NB], vpg[k_ % NB]
                        col = s_ * NPAGES + n_
                        P.dma_sw(lambda e: e.indirect_dma_start(out=kb[:], out_offset=None, in_=ck, in_offset=bass.IndirectOffsetOnAxis(ap=pti[:, col:col + 1], axis=0)), ["pti"], [kb.name])
                        P.dma_sw(lambda e: e.indirect_dma_start(out=vb[:], out_offset=None, in_=cv, in_offset=bass.IndirectOffsetOnAxis(ap=pti[:, col:col + 1], axis=0)), ["pti"], [vb.name])

                    for k_ in range(min(NB - 1, len(pages))):
                        issue_page(k_)
                    for k_, (s_, n_) in enumerate(pages):
                        if k_ + NB - 1 < len(pages):
                            issue_page(k_ + NB - 1)
                        kb, vb = kpg[k_ % NB], vpg[k_ % NB]
                        ktb, vab, ptp = kTp[k_ % 2], vpa[k_ % 2], PTp[s_][k_ % 2]
                        for pr in range(4):
                            P.op("pe", lambda e, pr=pr, kb=kb: e.transpose(out=pKT[:, pr, :], in_=kb[:, pr * 128:(pr + 1) * 128], identity=ident[:, :]), [kb.name, "ident"], ["pKT"])
                        P.op("dve", lambda e, ktb=ktb: e.tensor_copy(out=ktb[:], in_=pKT[:]), ["pKT"], [ktb.name])
                        P.op("act", lambda e, vab=vab, vb=vb: e.copy(out=vab[:, :, 0:64], in_=vb[:, :].rearrange("p (h d) -> p h d", d=64)), [vb.name], [vab.name])
                        for pr in range(4):
                            P.op("pe", lambda e, pr=pr, ktb=ktb, s_=s_: e.matmul(pSs[:, 4 * pr:4 * pr + 4, :], lhsT=ktb[:, pr, :], rhs=Qb2[:, pr, s_, :, :], start=True, stop=True),
                                 [ktb.name, "Qb2"], ["pSs"])
                        P.op("act", lambda e, ptp=ptp, s_=s_: e.activation(out=ptp[:, :, 8 * s_:8 * s_ + 8], in_=pSs[:, :, :], func=AF.Exp), ["pSs"], [ptp.name])
                        for g in range(16):
                            P.op("pe", lambda e, g=g, ptp=ptp, vab=vab: e.matmul(accS[g // 4][:, g % 4, :], lhsT=ptp[:, g, :], rhs=vab[:, g // 2, :], start=False, stop=False, skip_group_check=True),
                                 [ptp.name, vab.name], ["accS"])
                    PTn = sbk("PTn", [32, 8, 2, 32], BF16)
                    for h in range(8):
                        P.op("pe", lambda e, h=h: e.matmul(pA2[0:32, h * 64:(h + 1) * 64], lhsT=KTn[:, h, :], rhs=QTb[:, h, :, 0:TS], start=True, stop=True), ["KTn", "QTb"], ["pA2"])
                    P.op("act", lambda e: e.activation(out=PTn[:, :, :, :].rearrange("p h c q -> p (h c q)"), in_=pA2[0:32, 0:512], func=AF.Exp), ["pA2"], ["PTn"])
                    P.op("dve", lambda e: e.tensor_tensor(out=PTn[:, :, :, :].rearrange("p h c q -> p (h c) q"), in0=PTn[:, :, :, :].rearrange("p h c q -> p (h c) q"),
                                                          in1=m32_b[:, :].unsqueeze(1).to_broadcast([32, 16, 32]), op=ALU.mult), ["PTn", "m32_b"], ["PTn"])
                    for g in range(16):
                        P.op("pe", lambda e, g=g: e.matmul(accS[g // 4][:, g % 4, :], lhsT=PTn[:, g // 2, g % 2, :], rhs=Vnew[:, g // 2, :], start=False, stop=(g % 4 == 3), skip_group_check=True),
                             ["PTn", "Vnew"], ["accS"])
                    for h in range(8):
                        a_ = accS[h // 2]
                        base = 2 * (h % 2)
                        P.op("dve", lambda e, a_=a_, base=base: e.reciprocal(out=st[:TS, 32:34], in_=a_[:, base:base + 2, 64]), ["accS"], ["st32"])
                        P.op("dve", lambda e: e.tensor_tensor(out=st[:TS, 34:35], in0=st[:TS, 33:34], in1=neglam[:TS, :], op=ALU.mult), ["st32", "neglam"], ["st34"])
                        P.op("dve", lambda e, a_=a_, base=base, h=h: e.tensor_scalar(out=o_all[:TS, h, :], in0=a_[:, base, 0:64], scalar1=st[:TS, 32:33], scalar2=None, op0=ALU.mult), ["accS", "st32"], ["o_all"])
                        P.op("dve", lambda e, a_=a_, base=base, h=h: e.scalar_tensor_tensor(out=o_all[:TS, h, :], in0=a_[:, base + 1, 0:64], scalar=st[:TS, 34:35], in1=o_all[:TS, h, :], op0=ALU.mult, op1=ALU.add),
                             ["accS", "st34", "o_all"], ["o_all"])
                    diff_post_and_out("s", 0, 0, xsb, [(pA2, 0, "pA2"), (pA2, 0, "pA2")], pT2)
                    P.barrier()
                    P.emit()

        S2p.close()
        if "3" in PH:
            S3 = ExitStack()
            s3 = lambda n, s, d=F32: S3.enter_context(nc.sbuf_tensor(n, list(s), d))
            cwq_b = s3("cwq_b", [128, 8, D], BF16)
            cwo_b = s3("cwo_b", [128, 8, D], BF16)
            pwq_b = s3("pwq_b", [128, 8, 2 * D], BF16)
            skT_b = s3("skT_b", [128, 16, 128], BF16)
            mkT = s3("mkT", [128, 4, 8, 256], BF16)
            mva = s3("mva", [128, 4, 2, 4, 257], BF16)
            g_cross = s3("g_cross", [128, 8]); g_mem = s3("g_mem", [128, 8]); g_ffn = s3("g_ffn", [128, 8])
            cqn_bc = s3("cqn_bc", [128, 256]); ckn_bc = s3("ckn_bc", [128, 256]); pqn_bc = s3("pqn_bc", [128, 256])
            for (t_, src) in ((g_cross, cross_norm), (g_mem, mem_norm), (g_ffn, ffn_norm)):
                P.dma("sp", lambda e, t_=t_, src=src: e.dma_start(out=t_[:], in_=src.rearrange("(c p) -> p c", p=128), allow_slow_non_contiguous=True), [], [t_.name])
            for (t_, src) in ((cqn_bc, cross_q_norm), (ckn_bc, cross_k_norm), (pqn_bc, peer_q_norm)):
                P.dma("sp", lambda e, t_=t_, src=src: e.dma_start(out=t_[:], in_=src.partition_broadcast(128)), [], [t_.name])
            P.op("dve", lambda e: e.tensor_scalar(out=cqn_bc[:], in0=cqn_bc[:], scalar1=1.0 / 16.0, scalar2=None, op0=ALU.mult), ["cqn_bc"], ["cqn_bc"])
            P.op("pool", lambda e: e.memset(mva[:], 1.0), [], ["mva"])

            with ExitStack() as SP1:
                s1 = lambda n, s, d=F32: SP1.enter_context(nc.sbuf_tensor(n, list(s), d))
                wkv_b = s1("wkv_b", [128, 8, 2 * D], BF16)
                stg = [s1("stg3_%d" % k_, [128, 2 * D]) for k_ in range(2)]
                xm = [s1("xm%d" % k_, [128, D]) for k_ in range(2)]
                xms_b = s1("xms_b", [128, D], BF16)
                xmT = s1("xmT", [128, 8, 128], BF16)
                sq1 = s1("sq1", [128, D])
                tq1 = s1("tq1", [128, D])
                mkf = [s1("mkf%d" % k_, [128, D]) for k_ in range(2)]
                mk_b = s1("mk_b", [128, D], BF16)
                mvf = [s1("mvf%d" % k_, [128, D]) for k_ in range(2)]
                st1 = s1("st1", [128, 16])
                pA = SP1.enter_context(nc.psum_tensor("pA3", [128, 1024], F32))
                pB = SP1.enter_context(nc.psum_tensor("pB3", [128, 1024], F32))
                pT = SP1.enter_context(nc.psum_tensor("pT3", [128, 8, 128], BF16))
                pTf = SP1.enter_context(nc.psum_tensor("pTf3", [128, 512], F32))
                wl = [(cross_w_kv, 2 * D, wkv_b, g_mem), (cross_w_q, D, cwq_b, g_cross), (cross_w_o, D, cwo_b, None), (peer_w_q, 2 * D, pwq_b, g_ffn)]
                k2 = 0
                for (wsrc, wn, wdst, gg) in wl:
                    for c in range(8):
                        s_ = stg[k2 % 2]; k2 += 1
                        P.dma("sp", lambda e, c=c, s_=s_, wsrc=wsrc, wn=wn: e.dma_start(out=s_[:, 0:wn], in_=wsrc[c * 128:(c + 1) * 128, :]), [], [s_.name])
                        if gg is None:
                            act_copy(wdst[:, c, :], s_[:, 0:wn], [s_.name], [wdst.name], eng=("dve" if c % 2 == 0 else "act"))
                        elif c % 2 == 0:
                            P.op("dve", lambda e, c=c, s_=s_, wn=wn, wdst=wdst, gg=gg: e.tensor_scalar(out=wdst[:, c, :], in0=s_[:, 0:wn], scalar1=gg[:, c:c + 1], scalar2=None, op0=ALU.mult),
                                 [s_.name, gg.name], [wdst.name])
                        else:
                            P.op("act", lambda e, c=c, s_=s_, wn=wn, wdst=wdst, gg=gg: e.activation(out=wdst[:, c, :], in_=s_[:, 0:wn], func=AF.Copy, scale=gg[:, c:c + 1]),
                                 [s_.name, gg.name], [wdst.name])
                for g4 in range(4):
                    s_ = stg[k2 % 2]; k2 += 1
                    P.dma("sp", lambda e, g4=g4, s_=s_: e.dma_start(out=s_[:, 0:512].rearrange("p (g d) -> p g d", d=128), in_=peer_sk[4 * g4:4 * g4 + 4].rearrange("g k d -> k g d")), [], [s_.name])
                    for g_ in range(4):
                        P.op("pe", lambda e, g_=g_, s_=s_: e.transpose(out=pTf[:, g_ * 128:(g_ + 1) * 128], in_=s_[:, g_ * 128:(g_ + 1) * 128], identity=ident[:, :]), [s_.name, "ident"], ["pTf"])
                    P.op("dve", lambda e, g4=g4: e.tensor_copy(out=skT_b[:, 4 * g4:4 * g4 + 4, :], in_=pTf[:, :].rearrange("p (g k) -> p g k", k=128)), ["pTf"], ["skT_b"])

                def mem_set(kind, b):
                    slot = b
                    for mt in range(2):
                        rows = slice(mt * 128, (mt + 1) * 128)
                        mkb = mkf[mt]; mvb = mvf[mt]
                        if kind == "p":
                            xb = xm[mt]
                            P.dma("sp", lambda e, xb=xb, rows=rows: e.dma_start(out=xb[:], in_=memp[b, rows, :]), [], [xb.name])
                            P.op("act", lambda e, xb=xb: e.activation(out=sq1[:], in_=xb[:], func=AF.Square, accum_out=st1[:, 0:1]), [xb.name], ["sq1", "st1a"])
                            rstd_from_ss(st1[:, 0:1], D, 128, 1, "st1a", None)
                            P.op("act", lambda e, xb=xb: e.activation(out=xms_b[:], in_=xb[:], func=AF.Copy, scale=st1[:, 0:1]), [xb.name, "st1a"], ["xms_b"])
                            for k_ in range(8):
                                P.op("pe", lambda e, k_=k_: e.transpose(out=pT[:, k_, :], in_=xms_b[:, k_ * 128:(k_ + 1) * 128], identity=ident_b[:, :]), ["xms_b", "ident_b"], ["pT3"])
                            P.op("dve", lambda e: e.tensor_copy(out=xmT[:], in_=pT[:]), ["pT3"], ["xmT"])
                            for grp in range(4):
                                po = pA if grp < 2 else pB
                                pon = "pA3" if grp < 2 else "pB3"
                                for c in range(8):
                                    P.op("pe", lambda e, c=c, grp=grp, po=po: e.matmul(po[:, (grp % 2) * 512:(grp % 2 + 1) * 512], lhsT=xmT[:, c, :], rhs=wkv_b[:, c, grp * 512:(grp + 1) * 512], start=(c == 0), stop=(c == 7)),
                                         ["xmT", "wkv_b"], [pon])
                            P.op("act", lambda e: e.activation(out=sq1[:], in_=pA[:], func=AF.Square), ["pA3"], ["sq1"])
                            P.op("dve", lambda e: e.tensor_reduce(out=st1[:, 4:8], in_=sq1[:].rearrange("p (g d) -> p g d", d=256), axis=AX.X, op=ALU.add), ["sq1"], ["st1b"])
                            rstd_from_ss(st1[:, 4:8], 256, 128, 4, "st1b", None)
                            P.op("dve", lambda e: e.tensor_tensor(out=tq1[:].rearrange("p (g d) -> p g d", d=256), in0=pA[:].rearrange("p (g d) -> p g d", d=256),
                                                                  in1=st1[:, 4:8].unsqueeze(2).to_broadcast([128, 4, 256]), op=ALU.mult), ["pA3", "st1b"], ["tq1"])
                            P.op("dve", lambda e, mkb=mkb: e.tensor_tensor(out=mkb[:].rearrange("p (g d) -> p g d", d=256), in0=tq1[:].rearrange("p (g d) -> p g d", d=256),
                                                                           in1=ckn_bc[:, :].unsqueeze(1).to_broadcast([128, 4, 256]), op=ALU.mult), ["tq1", "ckn_bc"], [mkb.name])
                            P.dma("sp", lambda e, mkb=mkb, rows=rows: e.dma_start(out=omk[b, rows, :], in_=mkb[:]), [mkb.name], [])
                            P.op("act", lambda e, mvb=mvb: e.copy(out=mvb[:], in_=pB[:]), ["pB3"], [mvb.name])
                            P.dma("sp", lambda e, mvb=mvb, rows=rows: e.dma_start(out=omv[b, rows, :], in_=mvb[:]), [mvb.name], [])
                        else:
                            P.dma("sp", lambda e, mkb=mkb, rows=rows: e.dma_start(out=mkb[:], in_=cmk[b, rows, :]), [], [mkb.name])
                            P.dma("sp", lambda e, mvb=mvb, rows=rows: e.dma_start(out=mvb[:], in_=cmv[b, rows, :]), [], [mvb.name])
                        P.op("act", lambda e, mkb=mkb: e.copy(out=mk_b[:], in_=mkb[:]), [mkb.name], ["mk_b"])
                        for k_ in range(8):
                            P.op("pe", lambda e, k_=k_: e.transpose(out=pT[:, k_, :], in_=mk_b[:, k_ * 128:(k_ + 1) * 128], identity=ident_b[:, :]), ["mk_b", "ident_b"], ["pT3"])
                        P.op("dve", lambda e, rows=rows: e.tensor_copy(out=mkT[:, slot, :, rows], in_=pT[:]), ["pT3"], ["mkT%d" % slot])
                        P.op("pool", lambda e, mvb=mvb, mt=mt: e.tensor_copy(out=mva[:, slot, mt, :, 0:256], in_=mvb[:].rearrange("p (h d) -> p h d", d=256)), [mvb.name, "mva"], ["mva%d" % slot])

                for b in range(NPB):
                    mem_set("p", b)
                P.barrier()
                P.emit()

            with ExitStack() as SW3:
                sw = lambda n, s, d=F32: SW3.enter_context(nc.sbuf_tensor(n, list(s), d))
                hx = [sw("hx%d" % k_, [128, D]) for k_ in range(2)]
                hs_b = sw("hs_b", [128, D], BF16)
                hnT = sw("hnT", [128, 8, 128], BF16)
                sq = sw("sq3", [128, D])
                tq = sw("tq3", [128, D])
                qn_b = sw("qn3_b", [128, D], BF16)
                qnT = sw("qnT", [128, 8, 128], BF16)
                PTc = [sw("PTc%d" % k_, [128, 2, 128], BF16) for k_ in range(2)]
                PTz = [sw("PTz%d" % k_, [128, 2, 32], BF16) for k_ in range(NSB)]
                co = sw("co", [128, D], BF16)
                coT = sw("coT", [128, 8, 128], BF16)
                h2b = [sw("h2b%d" % k_, [128, D]) for k_ in range(2)]
                h2s_b = sw("h2s_b", [128, D], BF16)
                hn2T = [sw("hn2T%d" % k_, [128, 8, 128], BF16) for k_ in range(2)]
                pqn_b = sw("pqn_b", [128, 2 * D], BF16)
                pqT = sw("pqT", [128, 16, 128], BF16)
                sc = sw("sc", [128, 16, 128])
                m16 = sw("m16", [128, 16, 16]); ix = sw("ix", [128, 16, 16], U32); ixf = sw("ixf", [128, 16, 16])
                wk = sw("wk", [128, 128]); wk2 = sw("wk2", [128, 256])
                cand = sw("cand", [128, 8, 256])
                cm = sw("cm", [128, 8, 16]); cj = sw("cj", [128, 8, 16], U32)
                ca = sw("ca", [128, 8, 16], U32); cb = sw("cb", [128, 8, 16], U32)
                caf = sw("caf", [128, 8, 16]); cbf = sw("cbf", [128, 8, 16])
                ce = sw("ce", [128, 8, 16])
                oh = sw("oh", [128, 8, 256])
                rtk = sw("rtk", [128, 3, 128])
                rt = [sw("rt%d" % k_, [128, 3, 128]) for k_ in range(2)]
                st = sw("st3", [128, 64])
                mkf2 = [sw("mkg%d" % k_, [128, D]) for k_ in range(2)]
                mk_b2 = sw("mk_b2", [128, D], BF16)
                for k_ in range(NSB):
                    P.op("pool", lambda e, k_=k_: e.memset(PTz[k_][:], 0.0), [], [PTz[k_].name])

                with ExitStack() as SA3:
                    pA = SA3.enter_context(nc.psum_tensor("pA4", [128, 1024], F32))
                    pB = SA3.enter_context(nc.psum_tensor("pB4", [128, 1024], F32))
                    pT = SA3.enter_context(nc.psum_tensor("pT4", [128, 8, 128], BF16))
                    pS = SA3.enter_context(nc.psum_tensor("pS4", [128, 2, 256], F32))
                    pO = [SA3.enter_context(nc.psum_tensor("pO4_%d" % k_, [128, 512], F32)) for k_ in range(2)]
                    c3 = {"hx": 0, "h2": 0, "hn": 0, "pt": 0, "rt": 0}

                    def load_h1(ti, T):
                        buf = hx[c3["hx"] % 2]; c3["hx"] += 1
                        P.dma("sp", lambda e: e.dma_start(out=buf[:T, :], in_=h1s[ti, :T, :]), [], [buf.name])
                        return buf

                    def norm_T(src, srcn, T, dst_b, dstT, dstn):
                        P.op("act", lambda e: e.activation(out=sq[:T, :], in_=src[:T, :], func=AF.Square, accum_out=st[:T, 0:1]), [srcn], ["sq3", "st0"])
                        rstd_from_ss(st[:T, 0:1], D, T, 1, "st0", None)
                        P.op("act", lambda e: e.activation(out=dst_b[:T, :], in_=src[:T, :], func=AF.Copy, scale=st[:T, 0:1]), [srcn, "st0"], [dst_b.name])
                        tr8(dst_b, T, dstT, dstn, 0)

                    def tr8(src_b, T, dstT, dstn, k0):
                        for k_ in range(8):
                            P.op("pe", lambda e, k_=k_: e.transpose(out=pT[:, k_, :T], in_=src_b[:T, (k0 + k_) * 128:(k0 + k_ + 1) * 128], identity=ident_b[:T, :T]), [src_b.name, "ident_b"], ["pT4"])
                        P.op("dve", lambda e: e.tensor_copy(out=dstT[:, k0:k0 + 8, :T], in_=pT[:, :, :T]), ["pT4"], [dstn])

                    def headnorm(T, src, srcn, nh, gain, out_ap, outn, stoff):
                        w = nh * 256
                        P.op("act", lambda e: e.activation(out=sq[:T, 0:w], in_=src, func=AF.Square), [srcn], ["sq3"])
                        P.op("dve", lambda e: e.tensor_reduce(out=st[:T, stoff:stoff + nh], in_=sq[:T, 0:w].rearrange("p (g d) -> p g d", d=256), axis=AX.X, op=ALU.add), ["sq3"], ["st%d" % stoff])
                        rstd_from_ss(st[:T, stoff:stoff + nh], 256, T, nh, "st%d" % stoff, None)
                        P.op("dve", lambda e: e.tensor_tensor(out=tq[:T, 0:w].rearrange("p (g d) -> p g d", d=256), in0=src.rearrange("p (g d) -> p g d", d=256),
                                                              in1=st[:T, stoff:stoff + nh].unsqueeze(2).to_broadcast([T, nh, 256]), op=ALU.mult), [srcn, "st%d" % stoff], ["tq3"])
                        P.op("dve", lambda e: e.tensor_tensor(out=out_ap.rearrange("p (g d) -> p g d", d=256), in0=tq[:T, 0:w].rearrange("p (g d) -> p g d", d=256),
                                                              in1=gain[:T, :].unsqueeze(1).to_broadcast([T, nh, 256]), op=ALU.mult), ["tq3", gain.name], [outn])

                    def tile3(kind, ti, T, sets, hxb):
                        hn_ = hxb.name
                        norm_T(hxb, hn_, T, hs_b, hnT, "hnT")
                        for half in range(2):
                            for c in range(8):
                                P.op("pe", lambda e, c=c, half=half: e.matmul(pA[:T, half * 512:(half + 1) * 512], lhsT=hnT[:, c, :T], rhs=cwq_b[:, c, half * 512:(half + 1) * 512], start=(c == 0), stop=(c == 7)),
                                     ["hnT", "cwq_b"], ["pA4"])
                        headnorm(T, pA[:T, :], "pA4", 4, cqn_bc, qn_b[:T, :], "qn3_b", 4)
                        tr8(qn_b, T, qnT, "qnT", 0)
                        for h in range(4):
                            po = pO[h % 2]
                            pon = "pO4_%d" % (h % 2)
                            nmm = 2 * len(sets)
                            kmm = 0
                            for (slot, c0, c1) in sets:
                                for mb in range(2):
                                    for k2_ in range(2):
                                        P.op("pe", lambda e, mb=mb, k2_=k2_, slot=slot, h=h: e.matmul(pS[:, mb, :T], lhsT=mkT[:, slot, 2 * h + k2_, mb * 128:(mb + 1) * 128], rhs=qnT[:, 2 * h + k2_, :T],
                                                                                                   start=(k2_ == 0), stop=(k2_ == 1)), ["mkT%d" % slot, "qnT"], ["pS4"])
                                if kind == "p":
                                    ptb = PTc[c3["pt"] % 2]; c3["pt"] += 1
                                    P.op("act", lambda e, ptb=ptb: e.activation(out=ptb[:, :, :T], in_=pS[:, :, :T], func=AF.Exp), ["pS4"], [ptb.name])
                                else:
                                    ptb = PTz[slot]
                                    P.op("act", lambda e, ptb=ptb, c0=c0, c1=c1: e.activation(out=ptb[:, :, c0:c1], in_=pS[:, :, c0:c1], func=AF.Exp), ["pS4"], [ptb.name])
                                for mb in range(2):
                                    P.op("pe", lambda e, mb=mb, slot=slot, h=h, ptb=ptb, po=po, kmm=kmm, nmm=nmm: e.matmul(po[:T, 0:257], lhsT=ptb[:, mb, :T], rhs=mva[:, slot, mb, h, :],
                                                                                                                       start=(kmm == 0), stop=(kmm == nmm - 1)), [ptb.name, "mva%d" % slot], [pon])
                                    kmm += 1
                            P.op("dve", lambda e, po=po: e.reciprocal(out=st[:T, 12:13], in_=po[:T, 256:257]), [pon], ["st12"])
                            P.op("dve", lambda e, po=po, h=h: e.tensor_scalar(out=co[:T, h * 256:(h + 1) * 256], in0=po[:T, 0:256], scalar1=st[:T, 12:13], scalar2=None, op0=ALU.mult), [pon, "st12"], ["co"])
                        tr8(co, T, coT, "coT", 0)
                        h2 = h2b[c3["h2"] % 2]; c3["h2"] += 1
                        for half in range(2):
                            for c in range(8):
                                P.op("pe", lambda e, c=c, half=half: e.matmul(pA[:T, half * 512:(half + 1) * 512], lhsT=coT[:, c, :T], rhs=cwo_b[:, c, half * 512:(half + 1) * 512], start=(c == 0), stop=(c == 7)),
                                     ["coT", "cwo_b"], ["pA4"])
                        P.op("dve", lambda e: e.tensor_tensor(out=h2[:T, :], in0=pA[:T, :], in1=hxb[:T, :], op=ALU.add), ["pA4", hn_], [h2.name])
                        P.dma("sp", lambda e: e.dma_start(out=h2s[ti, :T, :], in_=h2[:T, :]), [h2.name], [])
                        h2T = hn2T[c3["hn"] % 2]; c3["hn"] += 1
                        norm_T(h2, h2.name, T, h2s_b, h2T, h2T.name)
                        P.dma("sp", lambda e: e.dma_start(out=hnTs[ti].rearrange("p (c t) -> p c t", t=128)[:, :, :T], in_=h2T[:, :, :T]), [h2T.name], [])
                        for grp in range(4):
                            po2 = pA if grp < 2 else pB
                            pon2 = "pA4" if grp < 2 else "pB4"
                            for c in range(8):
                                P.op("pe", lambda e, c=c, grp=grp, po2=po2: e.matmul(po2[:T, (grp % 2) * 512:(grp % 2 + 1) * 512], lhsT=h2T[:, c, :T], rhs=pwq_b[:, c, grp * 512:(grp + 1) * 512], start=(c == 0), stop=(c == 7)),
                                     [h2T.name, "pwq_b"], [pon2])
                        headnorm(T, pA[:T, :], "pA4", 4, pqn_bc, pqn_b[:T, 0:D], "pqn_b", 16)
                        headnorm(T, pB[:T, :], "pB4", 4, pqn_bc, pqn_b[:T, D:2 * D], "pqn_b", 20)
                        tr8(pqn_b, T, pqT, "pqT", 0)
                        tr8(pqn_b, T, pqT, "pqT", 8)
                        for g_ in range(16):
                            po2 = pA if g_ < 8 else pB
                            pon2 = "pA4" if g_ < 8 else "pB4"
                            P.op("pe", lambda e, g_=g_, po2=po2: e.matmul(po2[:T, (g_ % 8) * 128:(g_ % 8 + 1) * 128], lhsT=pqT[:, g_, :T], rhs=skT_b[:, g_, :], start=True, stop=True), ["pqT", "skT_b"], [pon2])
                        P.op("act", lambda e: e.copy(out=sc[:T, 0:8, :], in_=pA[:T, :].rearrange("p (g k) -> p g k", k=128)), ["pA4"], ["sc"])
                        P.op("act", lambda e: e.copy(out=sc[:T, 8:16, :], in_=pB[:T, :].rearrange("p (g k) -> p g k", k=128)), ["pB4"], ["sc"])
                        for g_ in range(16):
                            P.op("dve", lambda e, g_=g_: e.max(out=m16[:T, g_, 0:8], in_=sc[:T, g_, :]), ["sc"], ["m16"])
                            P.op("dve", lambda e, g_=g_: e.max_index(out=ix[:T, g_, 0:8], in_max=m16[:T, g_, 0:8], in_values=sc[:T, g_, :]), ["sc", "m16"], ["ix"])
                            P.op("dve", lambda e, g_=g_: e.match_replace(out=wk[:T, :], in_to_replace=m16[:T, g_, 0:8], in_values=sc[:T, g_, :], imm_value=-1e30), ["sc", "m16"], ["wk"])
                            P.op("dve", lambda e, g_=g_: e.max(out=m16[:T, g_, 8:16], in_=wk[:T, :]), ["wk"], ["m16"])
                            P.op("dve", lambda e, g_=g_: e.max_index(out=ix[:T, g_, 8:16], in_max=m16[:T, g_, 8:16], in_values=wk[:T, :]), ["wk", "m16"], ["ix"])
                        m16v = m16[:T, :, :].rearrange("p (h c) k -> p h c k", c=2)
                        P.op("dve", lambda e: e.tensor_tensor(out=cand[:T, :, :].rearrange("p h (a b) -> p h a b", b=16), in0=m16v[:, :, 0, :].unsqueeze(3).to_broadcast([T, 8, 16, 16]),
                                                              in1=m16v[:, :, 1, :].unsqueeze(2).to_broadcast([T, 8, 16, 16]), op=ALU.add), ["m16"], ["cand"])
                        for h in range(8):
                            P.op("dve", lambda e, h=h: e.max(out=cm[:T, h, 0:8], in_=cand[:T, h, :]), ["cand"], ["cm"])
                            P.op("dve", lambda e, h=h: e.max_index(out=cj[:T, h, 0:8], in_max=cm[:T, h, 0:8], in_values=cand[:T, h, :]), ["cand", "cm"], ["cj"])
                            P.op("dve", lambda e, h=h: e.match_replace(out=wk2[:T, :], in_to_replace=cm[:T, h, 0:8], in_values=cand[:T, h, :], imm_value=-1e30), ["cand", "cm"], ["wk2"])
                            P.op("dve", lambda e, h=h: e.max(out=cm[:T, h, 8:16], in_=wk2[:T, :]), ["wk2"], ["cm"])
                            P.op("dve", lambda e, h=h: e.max_index(out=cj[:T, h, 8:16], in_max=cm[:T, h, 8:16], in_values=wk2[:T, :]), ["wk2", "cm"], ["cj"])
                        P.op("dve", lambda e: e.tensor_tensor(out=ce[:T, :, :], in0=cm[:T, :, :], in1=cm[:T, :, 0:1].to_broadcast([T, 8, 16]), op=ALU.subtract), ["cm"], ["ce"])
                        P.op("act", lambda e: e.activation(out=ce[:T, :, :], in_=ce[:T, :, :], func=AF.Exp), ["ce"], ["ce"])
                        P.op("dve", lambda e: e.tensor_reduce(out=st[:T, 24:32], in_=ce[:T, :, :], axis=AX.X, op=ALU.add), ["ce"], ["st24"])
                        P.op("dve", lambda e: e.reciprocal(out=st[:T, 24:32], in_=st[:T, 24:32]), ["st24"], ["st24"])
                        P.op("dve", lambda e: e.tensor_tensor(out=rtk[:T, 2, :].rearrange("p (h k) -> p h k", k=16), in0=ce[:T, :, :], in1=st[:T, 24:32].unsqueeze(2).to_broadcast([T, 8, 16]), op=ALU.mult),
                             ["ce", "st24"], ["rtk"])
                        P.op("dve", lambda e: e.tensor_single_scalar(out=ca[:T, :, :], in_=cj[:T, :, :], scalar=4, op=ALU.logical_shift_right), ["cj"], ["ca"])
                        P.op("dve", lambda e: e.tensor_single_scalar(out=cb[:T, :, :], in_=cj[:T, :, :], scalar=15, op=ALU.bitwise_and), ["cj"], ["cb"])
                        P.op("dve", lambda e: e.tensor_copy(out=caf[:T, :, :], in_=ca[:T, :, :]), ["ca"], ["caf"])
                        P.op("dve", lambda e: e.tensor_copy(out=cbf[:T, :, :], in_=cb[:T, :, :]), ["cb"], ["cbf"])
                        P.op("dve", lambda e: e.tensor_copy(out=ixf[:T, :, :], in_=ix[:T, :, :]), ["ix"], ["ixf"])
                        ixv = ixf[:T, :, :].rearrange("p (h c) k -> p h c k", c=2)
                        io16 = iota_t[:T, 0:16].unsqueeze(1).unsqueeze(1).to_broadcast([T, 8, 16, 16])
                        for (sel, cxf, dsti) in ((0, caf, 0), (1, cbf, 1)):
                            P.op("dve", lambda e, cxf=cxf: e.tensor_tensor(out=oh[:T, :, :].rearrange("p h (k a) -> p h k a", a=16), in0=cxf[:T, :, :].unsqueeze(3).to_broadcast([T, 8, 16, 16]), in1=io16, op=ALU.is_equal),
                                 [cxf.name, "iota_t"], ["oh"])
                            P.op("dve", lambda e, sel=sel: e.tensor_tensor(out=oh[:T, :, :].rearrange("p h (k a) -> p h k a", a=16), in0=oh[:T, :, :].rearrange("p h (k a) -> p h k a", a=16),
                                                                           in1=ixv[:, :, sel, :].unsqueeze(2).to_broadcast([T, 8, 16, 16]), op=ALU.mult), ["oh", "ixf"], ["oh"])
                            P.op("dve", lambda e, dsti=dsti: e.tensor_reduce(out=rtk[:T, dsti, :], in_=oh[:T, :, :].rearrange("p h (k a) -> p (h k) a", a=16), axis=AX.X, op=ALU.add), ["oh"], ["rtk"])
                        rtb = rt[c3["rt"] % 2]; c3["rt"] += 1
                        pR = pO[0]
                        for k_ in range(3):
                            P.op("pe", lambda e, k_=k_: e.transpose(out=pR[:, k_ * 128:k_ * 128 + T], in_=rtk[:T, k_, :], identity=ident[:T, :T]), ["rtk", "ident"], ["pO4_0"])
                        P.op("dve", lambda e, rtb=rtb: e.tensor_copy(out=rtb[:, :, :T], in_=pR[:, 0:384].rearrange("p (k t) -> p k t", t=128)[:, :, :T]), ["pO4_0"], [rtb.name])
                        for k_, dst in enumerate((r1s, r2s, rws)):
                            P.dma("sp", lambda e, k_=k_, dst=dst, rtb=rtb: e.dma_start(out=dst[ti, :, :T], in_=rtb[:, k_, :T]), [rtb.name], [])

                    tiles = [("p", b, i) for b in range(NPB if DO_P else 0) for i in range(NT_RUN)]
                    nxt = load_h1(tile_of(*tiles[0]), 128) if tiles else None
                    for n_, (kind, b, i) in enumerate(tiles):
                        cur = nxt
                        if n_ + 1 < len(tiles):
                            nxt = load_h1(tile_of(*tiles[n_ + 1]), 128)
                        tile3("p", tile_of(kind, b, i), 128, [(b, 0, 128)], cur)
                    if DO_S:
                        for s_ in range(NSB):
                            for mt in range(2):
                                rows = slice(mt * 128, (mt + 1) * 128)
                                mkb = mkf2[mt]
                                P.dma("sp", lambda e, mkb=mkb, rows=rows, s_=s_: e.dma_start(out=mkb[:], in_=cmk[s_, rows, :]), [], [mkb.name])
                                P.op("act", lambda e, mkb=mkb: e.copy(out=mk_b2[:], in_=mkb[:]), [mkb.name], ["mk_b2"])
                                for k_ in range(8):
                                    P.op("pe", lambda e, k_=k_: e.transpose(out=pT[:, k_, :], in_=mk_b2[:, k_ * 128:(k_ + 1) * 128], identity=ident_b[:, :]), ["mk_b2", "ident_b"], ["pT4"])
                                P.op("dve", lambda e, rows=rows, s_=s_: e.tensor_copy(out=mkT[:, s_, :, rows], in_=pT[:]), ["pT4"], ["mkT%d" % s_])
                                mvb = hx[mt]
                                P.dma("sp", lambda e, mvb=mvb, rows=rows, s_=s_: e.dma_start(out=mvb[:], in_=cmv[s_, rows, :]), [], [mvb.name])
                                P.op("pool", lambda e, mvb=mvb, mt=mt, s_=s_: e.tensor_copy(out=mva[:, s_, mt, :, 0:256], in_=mvb[:].rearrange("p (h d) -> p h d", d=256)), [mvb.name, "mva"], ["mva%d" % s_])
                        c3["hx"] = 0
                        cur = load_h1(NPB * NT, TS)
                        tile3("s", NPB * NT, TS, [(s_, 8 * s_, 8 * s_ + 8) for s_ in range(NSB)], cur)
                    P.barrier()
                    P.emit()
            S3.close()

        if "4" in PH:
            with ExitStack() as S4:
                s4 = lambda n, s, d=F32: S4.enter_context(nc.sbuf_tensor(n, list(s), d))
                g_ffn4 = s4("g_ffn4", [128, 8])
                P.dma("sp", lambda e: e.dma_start(out=g_ffn4[:], in_=ffn_norm.rearrange("(c p) -> p c", p=128), allow_slow_non_contiguous=True), [], ["g_ffn4"])
                NJ = cfg.get("nj", 128)
                with ExitStack() as S4a:
                    s4a = lambda n, s, d=F32: S4a.enter_context(nc.sbuf_tensor(n, list(s), d))
                    ub = [s4a("ub%d" % k_, [128, D]) for k_ in range(3)]
                    vb4 = [s4a("vb4_%d" % k_, [128, D]) for k_ in range(3)]
                    uTb = [s4a("uTb%d" % k_, [128, 8, 128], BF16) for k_ in range(2)]
                    vbb = [s4a("vbb%d" % k_, [128, D], BF16) for k_ in range(2)]
                    pU = [S4a.enter_context(nc.psum_tensor("pU%d" % k_, [128, 1024], F32)) for k_ in range(2)]

                    def ld4a(j):
                        P.dma("sp", lambda e: e.dma_start(out=ub[j % 3][:], in_=peer_u[j * 128:(j + 1) * 128, :]), [], [ub[j % 3].name])
                        P.dma("sp", lambda e: e.dma_start(out=vb4[j % 3][:], in_=peer_v[j * 128:(j + 1) * 128, :]), [], [vb4[j % 3].name])

                    for j in range(min(2, NJ)):
                        ld4a(j)
                    for j in range(NJ):
                        if j + 2 < NJ:
                            ld4a(j + 2)
                        u_ = ub[j % 3]; v_ = vb4[j % 3]; pu = pU[j % 2]; ut = uTb[j % 2]; vt = vbb[j % 2]
                        for c in range(8):
                            P.op("pe", lambda e, c=c, u_=u_, pu=pu: e.transpose(out=pu[:, c * 128:(c + 1) * 128], in_=u_[:, c * 128:(c + 1) * 128], identity=ident[:, :]), [u_.name, "ident"], [pu.name])
                        P.op("dve", lambda e, pu=pu, ut=ut: e.tensor_tensor(out=ut[:, :, :], in0=pu[:, :].rearrange("p (c e) -> p c e", e=128), in1=g_ffn4[:, :].unsqueeze(2).to_broadcast([128, 8, 128]), op=ALU.mult),
                             [pu.name, "g_ffn4"], [ut.name])
                        P.dma("sp", lambda e, ut=ut, j=j: e.dma_start(out=uTs[j], in_=ut[:, :, :].rearrange("p c e -> p (c e)")), [ut.name], [])
                        P.op("act", lambda e, v_=v_, vt=vt: e.copy(out=vt[:], in_=v_[:]), [v_.name], [vt.name])
                        P.dma("sp", lambda e, vt=vt, j=j: e.dma_start(out=vbs[j], in_=vt[:]), [vt.name], [])
                    P.barrier()
                    P.emit()

                with ExitStack() as S4b:
                    s4b = lambda n, s, d=F32: S4b.enter_context(nc.sbuf_tensor(n, list(s), d))
                    TB = 256
                    GT = [s4b("GT%d" % k_, [128, 128, TB], BF16) for k_ in range(2)]
                    NUV = 5
                    uj = [s4b("uj%d" % k_, [128, 8, 128], BF16) for k_ in range(NUV)]
                    vj = [s4b("vj%d" % k_, [128, D], BF16) for k_ in range(NUV)]
                    hT = [s4b("hT%d" % k_, [128, 8, TB], BF16) for k_ in range(2)]
                    rr = [s4b("rr%d" % k_, [128, 3, TB]) for k_ in range(2)]
                    O1 = [s4b("O1_%d" % k_, [128, 128], BF16) for k_ in range(4)]
                    O2 = [s4b("O2_%d" % k_, [128, 128], BF16) for k_ in range(4)]
                    Hg = [s4b("Hg%d" % k_, [128, TB], BF16) for k_ in range(3)]
                    Wb = [s4b("Wb%d" % k_, [128, TB], BF16) for k_ in range(3)]
                    h2t = [s4b("h2t%d" % k_, [128, D]) for k_ in range(2)]
                    ybuf = [s4b("ybuf%d" % k_, [128, D]) for k_ in range(2)]
                    pY = [S4b.enter_context(nc.psum_tensor("pY%d" % k_, [128, 1024], F32)) for k_ in range(2)]
                    pH = [S4b.enter_context(nc.psum_tensor("pH%d" % k_, [128, 512], F32)) for k_ in range(3)]
                    pGt = [S4b.enter_context(nc.psum_tensor("pGt%d" % k_, [128, 4, 128], F32)) for k_ in range(1)]
                    c4 = {"o": 0, "g": 0, "uv": 0, "hw": 0, "y": 0}
                    blocks = []
                    if DO_P:
                        for b in range(NPB):
                            for i in range(0, NT_RUN, 2):
                                tl = [(b * NT + i, 128)]
                                if i + 1 < NT_RUN:
                                    tl.append((b * NT + i + 1, 128))
                                blocks.append(tl)
                    if DO_S:
                        blocks.append([(NPB * NT, TS)])

                    def load_block(bi):
                        tl = blocks[bi]
                        hb = hT[bi % 2]; rb = rr[bi % 2]
                        for k_, (ti, T) in enumerate(tl):
                            P.dma("sp", lambda e, k_=k_, ti=ti, T=T: e.dma_start(out=hb[:, :, k_ * 128:k_ * 128 + T], in_=hnTs[ti].rearrange("p (c t) -> p c t", t=128)[:, :, :T]), [], [hb.name])
                            for q_, src in enumerate((r1s, r2s, rws)):
                                P.dma("sp", lambda e, k_=k_, ti=ti, T=T, q_=q_, src=src: e.dma_start(out=rb[:, q_, k_ * 128:k_ * 128 + T], in_=src[ti, :, :T]), [], [rb.name])

                    def g_tokens(bi, t0, t1):
                        rb = rr[bi % 2]; gt = GT[bi % 2]
                        nt_ = sum(T for (_, T) in blocks[bi])
                        t1 = min(t1, nt_)
                        t = t0
                        while t < t1:
                            n4 = min(4, t1 - t)
                            pg = pGt[0]
                            for q_ in range(n4):
                                o1 = O1[c4["o"] % 4]; o2 = O2[c4["o"] % 4]; c4["o"] += 1
                                tt = t + q_
                                P.op("dve", lambda e, o1=o1, tt=tt: e.tensor_scalar(out=o1[:], in0=iota_b[:], scalar1=rb[:, 0, tt:tt + 1], scalar2=None, op0=ALU.is_equal), ["iota_b", rb.name], [o1.name])
                                P.op("dve", lambda e, o2=o2, tt=tt: e.tensor_scalar(out=o2[:], in0=iota_b[:], scalar1=rb[:, 1, tt:tt + 1], scalar2=rb[:, 2, tt:tt + 1], op0=ALU.is_equal, op1=ALU.mult), ["iota_b", rb.name], [o2.name])
                                P.op("pe", lambda e, o1=o1, o2=o2, pg=pg, q_=q_: e.matmul(pg[:, q_, :], lhsT=o2[:], rhs=o1[:], start=True, stop=True), [o1.name, o2.name], [pg.name])
                            P.op("act", lambda e, pg=pg, t=t, n4=n4: e.copy(out=gt[:, :, t:t + n4], in_=pg[:, 0:n4, :].rearrange("p q i -> p i q")), [pg.name], [gt.name])
                            t += n4

                    def ld_uv(j):
                        k_ = c4["uv"] % NUV; c4["uv"] += 1
                        P.dma("sp", lambda e: e.dma_start(out=uj[k_][:, :, :].rearrange("p c e -> p (c e)"), in_=uTs[j]), [], [uj[k_].name])
                        P.dma("sp", lambda e: e.dma_start(out=vj[k_][:], in_=vbs[j]), [], [vj[k_].name])
                        return k_

                    if blocks:
                        load_block(0)
                        g_tokens(0, 0, TB)
                    for bi, tl in enumerate(blocks):
                        nt_ = sum(T for (_, T) in tl)
                        hb = hT[bi % 2]; gt = GT[bi % 2]
                        if bi + 1 < len(blocks):
                            load_block(bi + 1)
                        q = [ld_uv(j_) for j_ in range(min(NUV - 1, NJ))]
                        hst = {}

                        def do_H(j, nt_=nt_, hb=hb):
                            k_ = q[j]
                            sel = c4["hw"] % 3; c4["hw"] += 1
                            ph, hg, wb = pH[sel], Hg[sel], Wb[sel]
                            for c in range(8):
                                P.op("pe", lambda e, c=c, k_=k_, ph=ph: e.matmul(ph[:, 0:nt_], lhsT=uj[k_][:, c, :], rhs=hb[:, c, 0:nt_], start=(c == 0), stop=(c == 7)), [uj[k_].name, hb.name], [ph.name])
                            hst[j] = (k_, ph, hg, wb)

                        do_H(0)
                        if NJ > 1:
                            do_H(1)
                        for j in range(NJ):
                            if j + NUV - 1 < NJ:
                                q.append(ld_uv(j + NUV - 1))
                            if j + 2 < NJ:
                                do_H(j + 2)
                            k_, ph, hg, wb = hst.pop(j)
                            P.op("act", lambda e, ph=ph, hg=hg, nt_=nt_: e.activation(out=hg[:, 0:nt_], in_=ph[:, 0:nt_], func=AF.Gelu_apprx_tanh), [ph.name], [hg.name])
                            P.op("dve", lambda e, hg=hg, wb=wb, j=j, nt_=nt_, gt=gt: e.tensor_tensor(out=wb[:, 0:nt_], in0=hg[:, 0:nt_], in1=gt[:, j, 0:nt_], op=ALU.mult), [hg.name, gt.name], [wb.name])
                            for k2_, (ti, T) in enumerate(tl):
                                for half in range(2):
                                    P.op("pe", lambda e, k2_=k2_, T=T, half=half, wb=wb, k_=k_, j=j: e.matmul(pY[k2_][:T, half * 512:(half + 1) * 512], lhsT=wb[:, k2_ * 128:k2_ * 128 + T], rhs=vj[k_][:, half * 512:(half + 1) * 512],
                                                                                                       start=(j == 0), stop=(j == NJ - 1)), [wb.name, vj[k_].name], ["pY%d" % k2_])
                            if bi + 1 < len(blocks):
                                if NJ >= 128:
                                    if j % 2 == 0:
                                        g_tokens(bi + 1, 2 * j, 2 * j + 4)
                                elif j == 0:
                                    g_tokens(bi + 1, 0, TB)
                        for k2_, (ti, T) in enumerate(tl):
                            hh_ = h2t[c4["y"] % 2]; yb = ybuf[c4["y"] % 2]; c4["y"] += 1
                            P.dma("sp", lambda e, hh_=hh_, ti=ti, T=T: e.dma_start(out=hh_[:T, :], in_=h2s[ti, :T, :]), [], [hh_.name])
                            P.op("dve", lambda e, hh_=hh_, yb=yb, k2_=k2_, T=T: e.tensor_tensor(out=yb[:T, :], in0=pY[k2_][:T, :], in1=hh_[:T, :], op=ALU.add), ["pY%d" % k2_, hh_.name], [yb.name])
                            if ti == NPB * NT:
                                P.dma("sp", lambda e, yb=yb, T=T: e.dma_start(out=ys, in_=yb[:T, :]), [yb.name], [])
                            else:
                                b_, i_ = ti // NT, ti % NT
                                P.dma("sp", lambda e, yb=yb, b_=b_, i_=i_: e.dma_start(out=yp[b_, i_ * 128:(i_ + 1) * 128, :], in_=yb[:, :]), [yb.name], [])
                    P.barrier()
                    P.emit()
    P.close()
    return nc


def _consts():
    j = np.arange(128)
    triu = (j[:, None] <= j[None, :]).astype(np.float32)
    iota = np.tile(np.arange(128, dtype=np.float32)[None, :], (128, 1))
    t = np.arange(32)
    m32 = ((t[:, None] // 8 == t[None, :] // 8) & (t[:, None] <= t[None, :])).astype(np.float32)
    rowm = (t[:, None] // 8 == np.arange(4)[None, :]).astype(np.float32)
    colm = np.tile((np.arange(4)[:, None] == (t[None, :] // 8)).astype(np.float32).reshape(1, 128), (128, 1))
    reset = np.tile((t % 8 != 0).astype(np.float32)[None, :], (128, 1))
    return dict(c_ident=np.eye(128, dtype=np.float32), c_triu=triu, c_iota=iota, c_m32=m32, c_rowm=rowm,
                c_colm=colm, c_reset=reset, c_ones=np.ones((128, 128), np.float32),
                c_pidx=np.arange(128, dtype=np.float32).reshape(128, 1))


def make_in_maps(inp, cores):
    f = lambda a: np.ascontiguousarray(np.asarray(a))
    n_phys = inp["cache_diff_k"].shape[1]
    ckf = f(inp["cache_diff_k"]).reshape(n_phys * 128, 512)
    cvf = f(inp["cache_diff_v"]).reshape(n_phys * 128, 512)
    shared = dict(ck=ckf, cv=cvf)
    for k in ("attn_norm", "w_in", "diff_q_norm", "diff_k_norm", "lambda_q1", "lambda_k1", "lambda_q2", "lambda_k2",
              "diff_out_norm", "gla_w_gate", "gla_b_gate", "gla_out_norm", "w_o", "cross_norm", "mem_norm", "cross_w_q",
              "cross_w_kv", "cross_q_norm", "cross_k_norm", "cross_w_o", "ffn_norm", "peer_w_q", "peer_q_norm", "peer_u", "peer_v"):
        shared[k] = f(inp[k])[0]
    shared["peer_sub_keys"] = f(inp["peer_sub_keys"])[0].reshape(16, 128, 128)
    shared.update(_consts())
    maps = []
    for c in cores:
        m = dict(shared)
        m["xp"] = f(inp["x_prompt"][NPB * c:NPB * (c + 1)])
        m["xs"] = f(inp["x_sample"][NSB * c:NSB * (c + 1)]).reshape(TS, D)
        m["memp"] = f(inp["mem_prompt"][NPB * c:NPB * (c + 1)])
        m["pt"] = f(inp["page_table"][NSB * c:NSB * (c + 1)]).astype(np.int32)
        m["sgla"] = f(inp["state_gla"][0, NSB * c:NSB * (c + 1)])
        m["cmk"] = f(inp["cache_mem_k"][0, NSB * c:NSB * (c + 1)]).reshape(NSB, 256, D)
        m["cmv"] = f(inp["cache_mem_v"][0, NSB * c:NSB * (c + 1)]).reshape(NSB, 256, D)
        maps.append(m)
    return maps, n_phys


def kernel(**inp):
    cores = list(range(NCORES))
    maps, n_phys = make_in_maps(inp, cores)
    nc = build(n_phys)
    used = set()
    for alloc_name in maps[0]:
        used.add(alloc_name)
    res = run_bass_kernel_spmd(nc, maps, core_ids=cores)
    R = res.results
    cat = lambda k: np.concatenate([r[k] for r in R], axis=0)
    B = NPB * NCORES
    DB = NSB * NCORES
    y_p = cat("yp")
    y_s = cat("ys").reshape(DB, 8, D)
    k_p = cat("okp").reshape(1, B, SEQ, 8, 64)
    v_p = cat("ovp").reshape(1, B, SEQ, 8, 64)
    st_p = cat("ostp").reshape(1, B, 4, 64, 128)
    mk_p = cat("omk").reshape(1, B, 256, 4, 256)
    mv_p = cat("omv").reshape(1, B, 256, 4, 256)
    k_s = cat("oks").reshape(1, DB, 8, 8, 64)
    v_s = cat("ovs").reshape(1, DB, 8, 8, 64)
    st_s = cat("osts").reshape(1, DB, 4, 64, 128)
    return (y_p, y_s, k_p, v_p, st_p, mk_p, mv_p, k_s, v_s, st_s)
```

```python
from contextlib import ExitStack
import math
import numpy as np
import concourse.bass as bass
import concourse.mybir as mybir
from concourse.bass_utils import run_bass_kernel_spmd

F32 = mybir.dt.float32
BF16 = mybir.dt.bfloat16
I32 = mybir.dt.int32
U32 = mybir.dt.uint32
AF = mybir.ActivationFunctionType
ALU = mybir.AluOpType
AX = mybir.AxisListType

EPOCH = 6000
NDMA = 28
NSW = 8
EPS = 1e-6
NCORES = 8
D = 1024
SEQ = 2048
NT = SEQ // 128
NPB = 2
NSB = 4
TS = 32
NPAGES = 128
INW = 3088
LAM_INIT = 0.8 - 0.6 * math.exp(-0.3 * 0)
NTILES = NPB * NT + 1


class Prog:
    COMPUTE = ("pe", "act", "dve", "pool")
    ALL = ("pe", "act", "dve", "pool", "sp")

    def __init__(self, nc):
        self.nc = nc
        self.es = ExitStack()
        self.nsem = 0
        self.cur = {}
        self.cnt = {}
        for e in self.COMPUTE:
            self.cur[e] = self._newsem(e)
            self.cnt[e] = 0
        self.dsem = [self._newsem("d%d" % i) for i in range(NDMA)]
        self.dcnt = [0] * NDMA
        self.drr = 0
        self.wsem = [self._newsem("w%d" % i) for i in range(NSW)]
        self.wcnt = [0] * NSW
        self.wcons = [[] for _ in range(NSW)]
        self.wrr = 0
        self.wids = {id(s_): i for i, s_ in enumerate(self.wsem)}
        self.ops = {e: [] for e in self.ALL}
        self.res = {}
        self.known = {e: {} for e in self.ALL}
        self.last_tok = {}
        self.n_ops = 0

    def _newsem(self, name):
        self.nsem += 1
        return self.es.enter_context(self.nc.semaphore("s_%s_%d" % (name, self.nsem)))

    def _deps(self, eng, reads, writes):
        waits = {}

        def need(tok):
            sem, val, teng = tok
            if teng == eng and eng == "pe":
                return
            k = id(sem)
            if k not in waits or waits[k][1] < val:
                waits[k] = (sem, val)

        for r in reads:
            st = self.res.get(r)
            if st:
                for t in st["w"]:
                    need(t)
        for w in writes:
            st = self.res.get(w)
            if st:
                for t in st["w"]:
                    need(t)
                for t in st["r"]:
                    need(t)
        out = []
        kn = self.known[eng]
        for k, (sem, val) in waits.items():
            if kn.get(k, 0) >= val:
                continue
            kn[k] = val
            out.append((sem, val))
        return out

    def _commit(self, tok, reads, writes):
        for r in reads:
            st = self.res.setdefault(r, {"w": [], "r": []})
            st["r"] = [t for t in st["r"] if t[0] is not tok[0]] + [tok]
        for w in writes:
            st = self.res.setdefault(w, {"w": [], "r": []})
            if st["r"]:
                st["w"] = [tok]
                st["r"] = []
            else:
                st["w"] = [t for t in st["w"] if t[0] is not tok[0]] + [tok]
        self.last_tok[id(tok[0])] = tok

    @staticmethod
    def _excl(reads, writes):
        ex = [r for r in reads if (len(r) > 1 and r[0] == "p" and r[1].isupper()) or r.startswith("accS")]
        if ex:
            reads = [r for r in reads if r not in ex]
            writes = list(writes) + [r for r in ex if r not in writes]
        return reads, writes

    def op(self, eng, fn, reads=(), writes=()):
        reads, writes = self._excl(reads, writes)
        waits = self._deps(eng, reads, writes)
        sem = self.cur[eng]
        self.cnt[eng] += 1
        tok = (sem, self.cnt[eng], eng)
        self.ops[eng].append((waits, fn, sem, 1))
        self._note_sw(waits, tok)
        self._commit(tok, reads, writes)
        if self.cnt[eng] >= EPOCH:
            self.cur[eng] = self._newsem(eng)
            self.cnt[eng] = 0
        self.n_ops += 1
        return tok

    def dma(self, q, fn, reads=(), writes=()):
        j = self.drr
        self.drr = (self.drr + 1) % NDMA
        sem = self.dsem[j]
        waits = self._deps(q, reads, writes)
        if self.dcnt[j] > 0:
            k = id(sem)
            v = 16 * self.dcnt[j]
            if self.known[q].get(k, 0) < v:
                self.known[q][k] = v
                waits.append((sem, v))
        self.dcnt[j] += 1
        tok = (sem, 16 * self.dcnt[j], "dma")
        self.ops[q].append((waits, fn, sem, 16))
        self._note_sw(waits, tok)
        self._commit(tok, reads, writes)
        self.n_ops += 1
        return tok

    def _note_sw(self, waits, tok):
        for (s_, v_) in waits:
            i = self.wids.get(id(s_))
            if i is not None:
                self.wcons[i].append(tok)

    def dma_sw(self, fn, reads=(), writes=()):
        q = "pool"
        i = self.wrr
        self.wrr = (self.wrr + 1) % NSW
        sem = self.wsem[i]
        waits = self._deps(q, reads, writes)
        c = self.wcnt[i]
        if c > 0:
            k = id(sem)
            v = 16 * c
            if self.known[q].get(k, 0) < v:
                self.known[q][k] = v
                waits.append((sem, v))
        self.wcnt[i] = c + 1
        tok = (sem, 16 * (c + 1), "dma")
        self.ops[q].append((waits, fn, sem, 16))
        self._commit(tok, reads, writes)
        self.n_ops += 1
        return tok

    def barrier(self):
        toks = list(self.last_tok.values())
        for e in self.ALL:
            waits = []
            for (sem, val, teng) in toks:
                k = id(sem)
                if self.known[e].get(k, 0) >= val:
                    continue
                self.known[e][k] = val
                waits.append((sem, val))
            if waits:
                self.ops[e].append((waits, None, None, 0))
        self.res = {}

    def emit(self):
        nc = self.nc
        ops = self.ops

        def run(eng_obj, lst):
            for waits, fn, sem, inc in lst:
                for (s, v) in waits:
                    eng_obj.wait_ge(s, v)
                if fn is not None:
                    ins = fn(eng_obj)
                    if sem is not None:
                        ins.then_inc(sem, inc)

        with nc.Block() as blk:
            @blk.tensor
            def _(e):
                run(e, ops["pe"])

            @blk.scalar
            def _(e):
                run(e, ops["act"])

            @blk.vector
            def _(e):
                run(e, ops["dve"])

            @blk.gpsimd
            def _(e):
                run(e, ops["pool"])

            @blk.sync
            def _(e):
                run(e, ops["sp"])
        self.ops = {e: [] for e in self.ALL}

    def close(self):
        self.es.close()


def build(n_phys, cfg=None):
    cfg = cfg or {}
    NT_RUN = cfg.get("nt", NT)
    NPG_RUN = cfg.get("npages", NPAGES)
    PH = cfg.get("phases", "234")
    DBG = cfg.get("dbg", False)
    DO_S = cfg.get("sample", True)
    STOP = cfg.get("stop", 99)
    DO_P = cfg.get("prompt", True)

    nc = bass.Bass("TRN2", target_bir_lowering=False)
    di = lambda n, s, d=F32: nc.dram_tensor(n, list(s), d, kind="ExternalInput").ap()
    do = lambda n, s, d=F32: nc.dram_tensor(n, list(s), d, kind="ExternalOutput").ap()
    dscr = lambda n, s, d=F32: nc.dram_tensor(n, list(s), d, kind="Internal").ap()

    xp = di("xp", [NPB, SEQ, D])
    xs = di("xs", [TS, D])
    memp = di("memp", [NPB, 256, D])
    ck = di("ck", [n_phys * 128, 512])
    cv = di("cv", [n_phys * 128, 512])
    pt = di("pt", [NSB, NPAGES], I32)
    sgla = di("sgla", [NSB, 4, 64, 128])
    cmk = di("cmk", [NSB, 256, D])
    cmv = di("cmv", [NSB, 256, D])
    attn_norm = di("attn_norm", [D])
    w_in = di("w_in", [D, INW])
    diff_q_norm = di("diff_q_norm", [32])
    diff_k_norm = di("diff_k_norm", [32])
    lq1 = di("lambda_q1", [32]); lk1 = di("lambda_k1", [32])
    lq2 = di("lambda_q2", [32]); lk2 = di("lambda_k2", [32])
    diff_out_norm = di("diff_out_norm", [64])
    gla_w_gate = di("gla_w_gate", [16, 256])
    gla_b_gate = di("gla_b_gate", [256])
    gla_out_norm = di("gla_out_norm", [128])
    w_o = di("w_o", [D, D])
    cross_norm = di("cross_norm", [D])
    mem_norm = di("mem_norm", [D])
    cross_w_q = di("cross_w_q", [D, D])
    cross_w_kv = di("cross_w_kv", [D, 2 * D])
    cross_q_norm = di("cross_q_norm", [256])
    cross_k_norm = di("cross_k_norm", [256])
    cross_w_o = di("cross_w_o", [D, D])
    ffn_norm = di("ffn_norm", [D])
    peer_w_q = di("peer_w_q", [D, 2 * D])
    peer_q_norm = di("peer_q_norm", [256])
    peer_sk = di("peer_sub_keys", [16, 128, 128])
    peer_u = di("peer_u", [16384, D])
    peer_v = di("peer_v", [16384, D])
    c_ident = di("c_ident", [128, 128])
    c_triu = di("c_triu", [128, 128])
    c_iota = di("c_iota", [128, 128])
    c_m32 = di("c_m32", [32, 32])
    c_rowm = di("c_rowm", [32, 4])
    c_colm = di("c_colm", [128, 4 * 32])
    c_reset = di("c_reset", [128, 32])
    c_ones = di("c_ones", [128, 128])
    c_pidx = di("c_pidx", [128, 1])

    yp = do("yp", [NPB, SEQ, D])
    ys = do("ys", [TS, D])
    okp = do("okp", [NPB, SEQ, 512])
    ovp = do("ovp", [NPB, SEQ, 512])
    ostp = do("ostp", [NPB, 4, 64, 128])
    omk = do("omk", [NPB, 256, D])
    omv = do("omv", [NPB, 256, D])
    oks = do("oks", [TS, 512])
    ovs = do("ovs", [TS, 512])
    osts = do("osts", [NSB, 4, 64, 128])

    if DBG:
        h1s = do("h1s", [NTILES, 128, D])
        h2s = do("h2s", [NTILES, 128, D])
    else:
        h1s = dscr("h1s", [NTILES, 128, D])
        h2s = dscr("h2s", [NTILES, 128, D])
    hnTs = dscr("hnTs", [NTILES, 128, 8 * 128], BF16)
    r1s = dscr("r1s", [NTILES, 128, 128])
    r2s = dscr("r2s", [NTILES, 128, 128])
    rws = dscr("rws", [NTILES, 128, 128])
    uTs = dscr("uTs", [128, 128, 8 * 128], BF16)
    vbs = dscr("vbs", [128, 128, D], BF16)

    P = Prog(nc)

    def tile_of(kind, b, i):
        return NPB * NT if kind == "s" else b * NT + i

    def act_copy(out, in_, reads, writes, eng="act"):
        if eng == "act":
            P.op("act", lambda e: e.copy(out=out, in_=in_), reads, writes)
        else:
            P.op(eng, lambda e: e.tensor_copy(out=out, in_=in_), reads, writes)

    def rstd_from_ss(ss, n, T, w, rn, tmpn):
        P.op("dve", lambda e: e.tensor_scalar(out=ss, in0=ss, scalar1=1.0 / n, scalar2=EPS,
                                              op0=ALU.mult, op1=ALU.add), [rn], [rn])
        P.op("act", lambda e: e.activation(out=ss, in_=ss, func=AF.Sqrt), [rn], [rn])
        P.op("dve", lambda e: e.reciprocal(out=ss, in_=ss), [rn], [rn])

    with ExitStack() as S2:
        sb = lambda n, s, d=F32: S2.enter_context(nc.sbuf_tensor(n, list(s), d))
        ident = sb("ident", [128, 128]); ident_b = sb("ident_b", [128, 128], BF16)
        triu = sb("triu", [128, 128]); triu_b = sb("triu_b", [128, 128], BF16)
        m32 = sb("m32", [32, 32]); m32_b = sb("m32_b", [32, 32], BF16)
        rowm = sb("rowm", [32, 4]); colm = sb("colm", [128, 4, 32]); resetm = sb("resetm", [128, 32])
        ones = sb("ones", [128, 128])
        for (t_, src) in ((ident, c_ident), (triu, c_triu), (m32, c_m32), (rowm, c_rowm),
                          (resetm, c_reset), (ones, c_ones)):
            P.dma("sp", lambda e, t_=t_, src=src: e.dma_start(out=t_[:], in_=src), [], [t_.name])
        P.dma("sp", lambda e: e.dma_start(out=colm[:].rearrange("p s t -> p (s t)"), in_=c_colm), [], ["colm"])
        P.op("dve", lambda e: e.tensor_copy(out=ident_b[:], in_=ident[:]), ["ident"], ["ident_b"])
        P.op("dve", lambda e: e.tensor_copy(out=triu_b[:], in_=triu[:]), ["triu"], ["triu_b"])
        P.op("dve", lambda e: e.tensor_copy(out=m32_b[:], in_=m32[:]), ["m32"], ["m32_b"])

        iota_t = sb("iota_t", [128, 128])
        P.dma("sp", lambda e: e.dma_start(out=iota_t[:], in_=c_iota), [], ["iota_t"])
        iota_b = sb("iota_b", [128, 128], BF16)
        P.op("dve", lambda e: e.tensor_copy(out=iota_b[:], in_=iota_t[:]), ["iota_t"], ["iota_b"])
        S2p = ExitStack()
        sb = lambda n, s, d=F32: S2p.enter_context(nc.sbuf_tensor(n, list(s), d))
        if "2" in PH:
            w_in_b = sb("w_in_b", [128, 8, INW], BF16)
            w_o_b = sb("w_o_b", [128, 8, D], BF16)
            Sg = sb("Sg", [128, 2, 128])
            g_attn = sb("g_attn", [128, 8])
            qn_bc = sb("qn_bc", [128, 32]); kn_bc = sb("kn_bc", [128, 32])
            dn_bc = sb("dn_bc", [128, 64]); gn_bc = sb("gn_bc", [128, 128])
            negb = sb("negb", [128, 2]); wg = sb("wg", [16, 256])
            lamt = sb("lamt", [128, 4, 32]); lamw = sb("lamw", [128, 8]); neglam = sb("neglam", [128, 1])

            P.dma("sp", lambda e: e.dma_start(out=g_attn[:], in_=attn_norm.rearrange("(c p) -> p c", p=128), allow_slow_non_contiguous=True), [], ["g_attn"])
            P.dma("sp", lambda e: e.dma_start(out=negb[:], in_=gla_b_gate.rearrange("(c p) -> p c", p=128), allow_slow_non_contiguous=True), [], ["negb"])
            P.dma("sp", lambda e: e.dma_start(out=qn_bc[:], in_=diff_q_norm.partition_broadcast(128)), [], ["qn_bc"])
            P.dma("sp", lambda e: e.dma_start(out=kn_bc[:], in_=diff_k_norm.partition_broadcast(128)), [], ["kn_bc"])
            P.dma("sp", lambda e: e.dma_start(out=dn_bc[:], in_=diff_out_norm.partition_broadcast(128)), [], ["dn_bc"])
            P.dma("sp", lambda e: e.dma_start(out=gn_bc[:], in_=gla_out_norm.partition_broadcast(128)), [], ["gn_bc"])
            P.dma("sp", lambda e: e.dma_start(out=wg[:], in_=gla_w_gate), [], ["wg"])
            for k_, src in enumerate((lq1, lk1, lq2, lk2)):
                P.dma("sp", lambda e, k_=k_, src=src: e.dma_start(out=lamt[:, k_, :], in_=src.partition_broadcast(128)), [], ["lamt"])
            P.op("dve", lambda e: e.tensor_scalar(out=qn_bc[:], in0=qn_bc[:], scalar1=32.0 ** -0.5, scalar2=None, op0=ALU.mult), ["qn_bc"], ["qn_bc"])
            P.op("dve", lambda e: e.tensor_scalar(out=dn_bc[:], in0=dn_bc[:], scalar1=1.0 - LAM_INIT, scalar2=None, op0=ALU.mult), ["dn_bc"], ["dn_bc"])
            P.op("dve", lambda e: e.tensor_scalar(out=negb[:], in0=negb[:], scalar1=-1.0, scalar2=None, op0=ALU.mult), ["negb"], ["negb"])
            P.op("dve", lambda e: e.tensor_tensor(out=lamt[:, 0, :], in0=lamt[:, 0, :], in1=lamt[:, 1, :], op=ALU.mult), ["lamt"], ["lamt"])
            P.op("dve", lambda e: e.tensor_tensor(out=lamt[:, 2, :], in0=lamt[:, 2, :], in1=lamt[:, 3, :], op=ALU.mult), ["lamt"], ["lamt"])
            P.op("dve", lambda e: e.tensor_reduce(out=lamw[:, 0:1], in_=lamt[:, 0, :], axis=AX.X, op=ALU.add), ["lamt"], ["lamw"])
            P.op("dve", lambda e: e.tensor_reduce(out=lamw[:, 1:2], in_=lamt[:, 2, :], axis=AX.X, op=ALU.add), ["lamt"], ["lamw"])
            P.op("act", lambda e: e.activation(out=lamw[:, 2:4], in_=lamw[:, 0:2], func=AF.Exp), ["lamw"], ["lamw"])
            P.op("dve", lambda e: e.tensor_tensor(out=lamw[:, 4:5], in0=lamw[:, 3:4], in1=lamw[:, 2:3], op=ALU.subtract), ["lamw"], ["lamw"])
            P.op("dve", lambda e: e.tensor_scalar(out=neglam[:], in0=lamw[:, 4:5], scalar1=-LAM_INIT, scalar2=None, op0=ALU.add), ["lamw"], ["neglam"])

            with ExitStack() as S0:
                stg = [S0.enter_context(nc.sbuf_tensor("stg%d" % k_, [128, INW], F32)) for k_ in range(2)]
                for c in range(8):
                    s_ = stg[c % 2]
                    P.dma("sp", lambda e, c=c, s_=s_: e.dma_start(out=s_[:], in_=w_in[c * 128:(c + 1) * 128, :]), [], [s_.name])
                    if c % 2 == 0:
                        P.op("dve", lambda e, c=c, s_=s_: e.tensor_scalar(out=w_in_b[:, c, :], in0=s_[:], scalar1=g_attn[:, c:c + 1], scalar2=None, op0=ALU.mult), [s_.name, "g_attn"], ["w_in_b"])
                    else:
                        P.op("act", lambda e, c=c, s_=s_: e.activation(out=w_in_b[:, c, :], in_=s_[:], func=AF.Copy, scale=g_attn[:, c:c + 1]), [s_.name, "g_attn"], ["w_in_b"])
                for c in range(8):
                    s_ = stg[c % 2]
                    P.dma("sp", lambda e, c=c, s_=s_: e.dma_start(out=s_[:, 0:D], in_=w_o[c * 128:(c + 1) * 128, :]), [], [s_.name])
                    act_copy(w_o_b[:, c, :], s_[:, 0:D], [s_.name], ["w_o_b"], eng=("dve" if c % 2 == 0 else "act"))
                P.barrier()
                P.emit()

            with ExitStack() as SW:
                sw = lambda n, s, d=F32: SW.enter_context(nc.sbuf_tensor(n, list(s), d))
                xt = [sw("xt%d" % k_, [128, D]) for k_ in range(2)]
                xs_b = sw("xs_b", [128, D], BF16)
                xnT = sw("xnT", [128, 8, 128], BF16)
                sq = sw("sq", [128, D])
                tmp = sw("tmp", [128, 512])
                st = sw("st", [128, 64])
                qn_b = sw("qn_b", [128, 512], BF16)
                kn = [sw("kn%d" % k_, [128, 512]) for k_ in range(2)]
                kn_b = sw("kn_b", [128, 512], BF16)
                QTb = sw("QTb", [64, 8, 2, 128], BF16)
                dv_sb = [sw("dv%d" % k_, [128, 512]) for k_ in range(2)]
                gv_sb = sw("gv_sb", [128, 512])
                sr = sw("sr", [128, 512])
                gaT = sw("gaT", [16, 128])
                e_sb = sw("e_sb", [128, 2, 128]); sp_sb = sw("sp_sb", [128, 2, 128])
                bT = sw("bT", [128, 2, 128]); eb = sw("eb", [128, 2, 128]); enb = sw("enb", [128, 2, 128])
                tmpE = sw("tmpE", [128, 2, 128])
                qdz = sw("qdz", [128, 2, 2, 128]); kd = sw("kd", [128, 2, 128])
                kd2T = sw("kd2T", [128, 2, 128]); kd2 = sw("kd2", [128, 2, 128])
                AT = sw("AT", [128, 4, 128])
                o_all = sw("o_all", [128, 8, 64])
                PT = [sw("PT%d" % k_, [128, 2, 128], BF16) for k_ in range(3)]
                cat = sw("cat", [128, D], BF16)
                catT = sw("catT", [128, 8, 128], BF16)
                h1 = [sw("h1_%d" % k_, [128, D]) for k_ in range(2)]
                qdp = sw("qdp", [128, 4, 2, 2, 32])
                S0s = sw("S0s", [128, 4, 2, 128])
                Sns = sw("Sns", [128, 4, 2, 128])
                bl = sw("bl", [128, 8]); ebl = sw("ebl", [128, 8])
                Vnew = sw("Vnew", [32, 8, 65], BF16)
                KTn = sw("KTn", [64, 8, 32], BF16)

                P.op("pool", lambda e: e.memset(QTb[:], 0.0), [], ["QTb"])
                P.op("pool", lambda e: e.memset(qdz[:], 0.0), [], ["qdz"])
                P.op("pool", lambda e: e.memset(Vnew[:], 1.0), [], ["Vnew"])

                with ExitStack() as SA:
                    KT = SA.enter_context(nc.sbuf_tensor("KT", [64, 8, SEQ], BF16))
                    Vaug = SA.enter_context(nc.sbuf_tensor("Vaug", [128, NT, 8, 65], BF16))
                    P.op("pool", lambda e: e.memset(Vaug[:], 1.0), [], ["Vaug"])
                    pA = SA.enter_context(nc.psum_tensor("pA", [128, 1024], F32))
                    pT = SA.enter_context(nc.psum_tensor("pT", [128, 8, 128], BF16))
                    pSb = [SA.enter_context(nc.psum_tensor("pS%d" % k_, [128, 512], F32)) for k_ in range(2)]
                    pAccF = [SA.enter_context(nc.psum_tensor("pAcc%d" % k_, [128, 512], F32)) for k_ in range(2)]
                    pAcc = [t_[:, 0:130].rearrange("p (c d) -> p c d", d=65) for t_ in pAccF]
                    pF = SA.enter_context(nc.psum_tensor("pF", [128, 512], F32))
                    pG = pA[:, 0:512]

                    cnt = {"s": 0, "pt": 0, "x": 0, "kn": 0, "dv": 0, "h1": 0}

                    def load_x(kind, b, i):
                        buf = xt[cnt["x"] % 2]; cnt["x"] += 1
                        T = 128 if kind == "p" else TS
                        src = xp[b, i * 128:(i + 1) * 128, :] if kind == "p" else xs
                        P.dma("sp", lambda e: e.dma_start(out=buf[:T, :], in_=src), [], [buf.name])
                        return buf

                    def transposes_bf(src_tile, T, dstT, n, src_res, dst_res, width=128):
                        for k_ in range(n):
                            P.op("pe", lambda e, k_=k_: e.transpose(out=pT[:width, k_, :T], in_=src_tile[:T, k_ * width:(k_ + 1) * width], identity=ident_b[:T, :T]),
                                 [src_res, "ident_b"], ["pT"])

                    def mixer_front(kind, b, i, xbuf):
                        T = 128 if kind == "p" else TS
                        xr = xbuf.name
                        P.op("act", lambda e: e.activation(out=sq[:T, :], in_=xbuf[:T, :], func=AF.Square, accum_out=st[:T, 0:1]), [xr], ["sq", "st0"])
                        rstd_from_ss(st[:T, 0:1], D, T, 1, "st0", None)
                        P.op("act", lambda e: e.activation(out=xs_b[:T, :], in_=xbuf[:T, :], func=AF.Copy, scale=st[:T, 0:1]), [xr, "st0"], ["xs_b"])
                        transposes_bf(xs_b, T, xnT, 8, "xs_b", "xnT")
                        P.op("dve", lambda e: e.tensor_copy(out=xnT[:, :, :T], in_=pT[:, :, :T]), ["pT"], ["xnT"])

                        if STOP <= 1:
                            return
                        def proj_tok(col0, ncols, half):
                            for c in range(8):
                                P.op("pe", lambda e, c=c: e.matmul(pA[:T, half * 512:half * 512 + ncols], lhsT=xnT[:, c, :T], rhs=w_in_b[:, c, col0:col0 + ncols],
                                                                   start=(c == 0), stop=(c == 7)), ["xnT", "w_in_b"], ["pA%d" % half])
                            return pA[:T, half * 512:half * 512 + ncols], "pA%d" % half

                        def qknorm(src, sres, gain, out_ap, out_res):
                            P.op("act", lambda e: e.activation(out=sq[:T, 0:512], in_=src, func=AF.Square), [sres], ["sq"])
                            P.op("dve", lambda e: e.tensor_reduce(out=st[:T, 8:24], in_=sq[:T, 0:512].rearrange("p (g d) -> p g d", d=32), axis=AX.X, op=ALU.add), ["sq"], ["st8"])
                            rstd_from_ss(st[:T, 8:24], 32, T, 16, "st8", None)
                            P.op("dve", lambda e: e.tensor_tensor(out=tmp[:T, :].rearrange("p (g d) -> p g d", d=32), in0=src.rearrange("p (g d) -> p g d", d=32),
                                                                  in1=st[:T, 8:24].unsqueeze(2).to_broadcast([T, 16, 32]), op=ALU.mult), [sres, "st8"], ["tmp"])
                            P.op("dve", lambda e: e.tensor_tensor(out=out_ap.rearrange("p (g d) -> p g d", d=32), in0=tmp[:T, :].rearrange("p (g d) -> p g d", d=32),
                                                                  in1=gain[:T, :].unsqueeze(1).to_broadcast([T, 16, 32]), op=ALU.mult), ["tmp", gain.name], [out_res])

                        src, sres = proj_tok(0, 512, 0)
                        qknorm(src, sres, qn_bc, qn_b[:T, :], "qn_b")
                        if STOP <= 2:
                            return
                        src, sres = proj_tok(512, 512, 1)
                        knb = kn[cnt["kn"] % 2]; cnt["kn"] += 1
                        qknorm(src, sres, kn_bc, knb[:T, :], knb.name)
                        dstk = okp[b, i * 128:(i + 1) * 128, :] if kind == "p" else oks
                        P.dma("sp", lambda e: e.dma_start(out=dstk, in_=knb[:T, :]), [knb.name], [])
                        P.op("act", lambda e: e.copy(out=kn_b[:T, :], in_=knb[:T, :]), [knb.name], ["kn_b"])
                        transposes_bf(qn_b, T, None, 8, "qn_b", None, width=64)
                        P.op("dve", lambda e: e.tensor_copy(out=QTb[0:32, :, 0, :T], in_=pT[0:32, :, :T]), ["pT"], ["QTb"])
                        P.op("dve", lambda e: e.tensor_copy(out=QTb[32:64, :, 1, :T], in_=pT[32:64, :, :T]), ["pT"], ["QTb"])
                        transposes_bf(kn_b, T, None, 8, "kn_b", None, width=64)
                        if kind == "p":
                            P.op("dve", lambda e: e.tensor_copy(out=KT[:, :, i * 128:(i + 1) * 128], in_=pT[0:64, :, :]), ["pT"], ["KT%d" % i])
                        else:
                            P.op("dve", lambda e: e.tensor_copy(out=KTn[:, :, :], in_=pT[0:64, :, :T]), ["pT"], ["KTn"])
                        if STOP <= 3:
                            return
                        src, sres = proj_tok(1024, 512, 0)
                        dvb = dv_sb[cnt["dv"] % 2]; cnt["dv"] += 1
                        P.op("act", lambda e, src=src: e.copy(out=dvb[:T, :], in_=src), [sres], [dvb.name])
                        dstv = ovp[b, i * 128:(i + 1) * 128, :] if kind == "p" else ovs
                        P.dma("sp", lambda e: e.dma_start(out=dstv, in_=dvb[:T, :]), [dvb.name], [])
                        if kind == "p":
                            P.op("pool", lambda e: e.tensor_copy(out=Vaug[:, i, :, 0:64], in_=dvb[:, :].rearrange("p (h d) -> p h d", d=64)), [dvb.name, "Vaug"], ["Vaug%d" % i])
                        else:
                            P.op("pool", lambda e: e.tensor_copy(out=Vnew[:, :, 0:64], in_=dvb[:T, :].rearrange("p (h d) -> p h d", d=64)), [dvb.name], ["Vnew"])
                        src, sres = proj_tok(2048, 512, 1)
                        P.op("act", lambda e, src=src: e.copy(out=gv_sb[:T, :], in_=src), [sres], ["gv_sb"])
                        src, sres = proj_tok(2576, 512, 0)
                        P.op("act", lambda e, src=src: e.activation(out=sr[:T, :], in_=src, func=AF.Silu), [sres], ["sr"])
                        if STOP <= 4:
                            return
                        for k_, col0 in enumerate((1536, 1664, 1792, 1920)):
                            for c in range(8):
                                P.op("pe", lambda e, c=c, k_=k_, col0=col0: e.matmul(pF[:, k_ * 128:k_ * 128 + T], lhsT=w_in_b[:, c, col0:col0 + 128], rhs=xnT[:, c, :T],
                                                                                     start=(c == 0), stop=(c == 7)), ["xnT", "w_in_b"], ["pF"])
                        for c in range(8):
                            P.op("pe", lambda e, c=c: e.matmul(pG[0:16, 0:T], lhsT=w_in_b[:, c, 2560:2576], rhs=xnT[:, c, :T], start=(c == 0), stop=(c == 7)),
                                 ["xnT", "w_in_b"], ["pA0"])
                        P.op("act", lambda e: e.copy(out=gaT[:, :T], in_=pG[0:16, 0:T]), ["pA0"], ["gaT"])
                        if STOP <= 5:
                            return
                        yield
                        for m in range(2):
                            P.op("pe", lambda e, m=m: e.matmul(pG[:, 128 + m * 128:128 + m * 128 + T], lhsT=wg[:, m * 128:(m + 1) * 128], rhs=gaT[:, :T], start=True, stop=True),
                                 ["wg", "gaT"], ["pA0"])
                        for m in range(2):
                            P.op("act", lambda e, m=m: e.activation(out=e_sb[:, m, :T], in_=pG[:, 128 + m * 128:128 + m * 128 + T], func=AF.Exp, scale=-1.0, bias=negb[:, m:m + 1]),
                                 ["pA0", "negb"], ["e_sb"])
                        P.op("act", lambda e: e.activation(out=sp_sb[:, :, :T], in_=e_sb[:, :, :T], func=AF.Ln, bias=1.0), ["e_sb"], ["sp_sb"])
                        for m in range(2):
                            d0 = ones[:, :T] if kind == "p" else resetm[:, :T]
                            P.op("dve", lambda e, m=m, d0=d0: e.tensor_tensor_scan(out=bT[:, m, :T], data0=d0, data1=sp_sb[:, m, :T], initial=0.0, op0=ALU.mult, op1=ALU.subtract),
                                 ["sp_sb", "ones", "resetm"], ["bT"])
                        P.op("act", lambda e: e.activation(out=eb[:, :, :T], in_=bT[:, :, :T], func=AF.Exp, scale=1.0 / 16), ["bT"], ["eb"])
                        P.op("act", lambda e: e.activation(out=enb[:, :, :T], in_=bT[:, :, :T], func=AF.Exp, scale=-1.0 / 16), ["bT"], ["enb"])
                        for m in range(2):
                            for hh in range(2):
                                rws = slice(hh * 64, (hh + 1) * 64)
                                P.op("dve", lambda e, m=m, hh=hh, rws=rws: e.scalar_tensor_tensor(out=qdz[rws, m, hh, :T], in0=pF[rws, m * 128:m * 128 + T], scalar=0.125, in1=eb[rws, m, :T], op0=ALU.mult, op1=ALU.mult),
                                     ["pF", "eb"], ["qdz"])
                            P.op("dve", lambda e, m=m: e.tensor_tensor(out=kd[:, m, :T], in0=pF[:, 256 + m * 128:256 + m * 128 + T], in1=enb[:, m, :T], op=ALU.mult),
                                 ["pF", "enb"], ["kd"])
                        if STOP <= 6:
                            return
                        yield
                        for h in range(4):
                            m, hh = h // 2, h % 2
                            rows = slice(hh * 64, (hh + 1) * 64)
                            P.op("pe", lambda e, h=h, m=m, hh=hh: e.matmul(pG[:T, h * 128:h * 128 + T], lhsT=kd[:, m, :T], rhs=qdz[:, m, hh, :T], start=True, stop=True),
                                 ["kd", "qdz"], ["pA0"])
                        msk = triu if kind == "p" else m32
                        P.op("dve", lambda e: e.tensor_tensor(out=AT[:T, :, :T], in0=pG[:T, :].rearrange("p (h t) -> p h t", t=128)[:, :, :T],
                                                              in1=msk[:T, :T].unsqueeze(1).to_broadcast([T, 4, T]), op=ALU.mult), ["pA0", msk.name], ["AT"])
                        if STOP <= 7:
                            return
                        yield
                        if kind == "s":
                            for s_ in range(NSB):
                                P.op("dve", lambda e, s_=s_: e.tensor_tensor(out=qdp[:, s_, :, :, :].rearrange("p m h t -> p (m h) t"), in0=qdz[:, :, :, :T].rearrange("p m h t -> p (m h) t"),
                                                                             in1=colm[:, s_, :].unsqueeze(1).to_broadcast([128, 4, TS]), op=ALU.mult),
                                     ["qdz", "colm"], ["qdp"])
                        for h in range(4):
                            m, hh = h // 2, h % 2
                            rows = slice(hh * 64, (hh + 1) * 64)
                            oo = pA[:T, 512 + h * 128:512 + (h + 1) * 128]
                            P.op("pe", lambda e, h=h, oo=oo: e.matmul(oo, lhsT=AT[:T, h, :T], rhs=gv_sb[:T, h * 128:(h + 1) * 128], start=True, stop=False),
                                 ["AT", "gv_sb"], ["pA1"])
                            if kind == "p":
                                P.op("pe", lambda e, m=m, hh=hh, oo=oo: e.matmul(oo, lhsT=qdz[:, m, hh, :T], rhs=Sg[:, m, :], start=False, stop=True),
                                     ["qdz", "Sg"], ["pA1"])
                            else:
                                for s_ in range(NSB):
                                    P.op("pe", lambda e, m=m, hh=hh, oo=oo, s_=s_: e.matmul(oo, lhsT=qdp[:, s_, m, hh, :], rhs=S0s[:, s_, m, :], start=False, stop=(s_ == NSB - 1)),
                                         ["qdp", "S0s"], ["pA1"])
                        if STOP <= 8:
                            return
                        yield
                        nseq = 1 if kind == "p" else NSB
                        for s_ in range(nseq):
                            last = T - 1 if kind == "p" else 8 * s_ + 7
                            P.op("dve", lambda e, s_=s_, last=last: e.tensor_scalar(out=bl[:, 2 * s_:2 * s_ + 2], in0=bT[:, :, last], scalar1=1.0 / 16, scalar2=None, op0=ALU.mult),
                                 ["bT"], ["bl"])
                            P.op("act", lambda e, s_=s_: e.activation(out=ebl[:, 2 * s_:2 * s_ + 2], in_=bl[:, 2 * s_:2 * s_ + 2], func=AF.Exp), ["bl"], ["ebl"])
                            for m in range(2):
                                P.op("act", lambda e, m=m, s_=s_: e.activation(out=tmpE[:, m, :T], in_=bT[:, m, :T], func=AF.Exp, scale=-1.0 / 16, bias=bl[:, 2 * s_ + m:2 * s_ + m + 1]),
                                     ["bT", "bl"], ["tmpE"])
                                P.op("dve", lambda e, m=m: e.tensor_tensor(out=kd2T[:, m, :T], in0=pF[:, 256 + m * 128:256 + m * 128 + T], in1=tmpE[:, m, :T], op=ALU.mult),
                                     ["pF", "tmpE"], ["kd2T"])
                                P.op("pe", lambda e, m=m: e.transpose(out=pG[:T, m * 128:(m + 1) * 128], in_=kd2T[:, m, :T], identity=ident[:, :]), ["kd2T", "ident"], ["pA0"])
                            if kind == "p":
                                P.op("dve", lambda e: e.tensor_copy(out=kd2[:T, :, :], in_=pG[:T, 0:256].rearrange("p (m f) -> p m f", m=2)), ["pA0"], ["kd2"])
                            else:
                                P.op("dve", lambda e, s_=s_: e.tensor_scalar(out=kd2[:T, :, :], in0=pG[:T, 0:256].rearrange("p (m f) -> p m f", m=2), scalar1=rowm[:T, s_:s_ + 1], scalar2=None, op0=ALU.mult),
                                     ["pA0", "rowm"], ["kd2"])
                            for m in range(2):
                                yield
                                P.op("pe", lambda e, m=m: e.matmul(pG[:, 256:512], lhsT=kd2[:T, m, :], rhs=gv_sb[:T, m * 256:(m + 1) * 256], start=True, stop=True),
                                     ["kd2", "gv_sb"], ["pA0"])
                                for hh in range(2):
                                    rows = slice(hh * 64, (hh + 1) * 64)
                                    if kind == "p":
                                        P.op("dve", lambda e, m=m, hh=hh, rows=rows: e.scalar_tensor_tensor(out=Sg[rows, m, :], in0=Sg[rows, m, :], scalar=ebl[rows, m:m + 1],
                                                                                                         in1=pG[rows, 256 + hh * 128:256 + (hh + 1) * 128], op0=ALU.mult, op1=ALU.add),
                                             ["Sg", "ebl", "pA0"], ["Sg"])
                                    else:
                                        P.op("dve", lambda e, m=m, hh=hh, rows=rows, s_=s_: e.scalar_tensor_tensor(out=Sns[rows, s_, m, :], in0=S0s[rows, s_, m, :], scalar=ebl[rows, 2 * s_ + m:2 * s_ + m + 1],
                                                                                                                  in1=pG[rows, 256 + hh * 128:256 + (hh + 1) * 128], op0=ALU.mult, op1=ALU.add),
                                             ["S0s", "ebl", "pA0"], ["Sns"])
                        if STOP <= 9:
                            return
                        yield
                        og = pA[:T, 512:1024]
                        P.op("act", lambda e: e.activation(out=sq[:T, 0:512], in_=og, func=AF.Square), ["pA1"], ["sq"])
                        P.op("dve", lambda e: e.tensor_reduce(out=st[:T, 24:28], in_=sq[:T, 0:512].rearrange("p (g d) -> p g d", d=128), axis=AX.X, op=ALU.add), ["sq"], ["st24"])
                        rstd_from_ss(st[:T, 24:28], 128, T, 4, "st24", None)
                        P.op("dve", lambda e: e.tensor_tensor(out=tmp[:T, :].rearrange("p (g d) -> p g d", d=128), in0=og.rearrange("p (g d) -> p g d", d=128),
                                                              in1=st[:T, 24:28].unsqueeze(2).to_broadcast([T, 4, 128]), op=ALU.mult), ["pA1", "st24"], ["tmp"])
                        P.op("dve", lambda e: e.tensor_tensor(out=tmp[:T, :].rearrange("p (g d) -> p g d", d=128), in0=tmp[:T, :].rearrange("p (g d) -> p g d", d=128),
                                                              in1=gn_bc[:T, :].unsqueeze(1).to_broadcast([T, 4, 128]), op=ALU.mult), ["tmp", "gn_bc"], ["tmp"])
                        P.op("dve", lambda e: e.tensor_tensor(out=cat[:T, 512:1024], in0=tmp[:T, :], in1=sr[:T, :], op=ALU.mult), ["tmp", "sr"], ["cat_g"])

                    def attn_normalize(T, acc, h):
                        P.op("dve", lambda e: e.reciprocal(out=st[:T, 32:34], in_=acc[:T, :, 64]), [acc.name], ["st32"])
                        P.op("dve", lambda e: e.tensor_tensor(out=st[:T, 34:35], in0=st[:T, 33:34], in1=neglam[:T, :], op=ALU.mult), ["st32", "neglam"], ["st34"])
                        P.op("dve", lambda e: e.tensor_scalar(out=o_all[:T, h, :], in0=acc[:T, 0, 0:64], scalar1=st[:T, 32:33], scalar2=None, op0=ALU.mult), [acc.name, "st32"], ["o_all"])
                        P.op("dve", lambda e: e.scalar_tensor_tensor(out=o_all[:T, h, :], in0=acc[:T, 1, 0:64], scalar=st[:T, 34:35], in1=o_all[:T, h, :], op0=ALU.mult, op1=ALU.add),
                             [acc.name, "st34", "o_all"], ["o_all"])

                    def diff_post_and_out(kind, b, i, xbuf, pAo, pTt):
                        T = 128 if kind == "p" else TS
                        P.op("act", lambda e: e.activation(out=sq[:T, 0:512], in_=o_all[:T, :, :].rearrange("p h d -> p (h d)"), func=AF.Square), ["o_all"], ["sq"])
                        P.op("dve", lambda e: e.tensor_reduce(out=st[:T, 36:44], in_=sq[:T, 0:512].rearrange("p (g d) -> p g d", d=64), axis=AX.X, op=ALU.add), ["sq"], ["st36"])
                        rstd_from_ss(st[:T, 36:44], 64, T, 8, "st36", None)
                        P.op("dve", lambda e: e.tensor_tensor(out=o_all[:T, :, :], in0=o_all[:T, :, :], in1=st[:T, 36:44].unsqueeze(2).to_broadcast([T, 8, 64]), op=ALU.mult), ["o_all", "st36"], ["o_all"])
                        P.op("dve", lambda e: e.tensor_tensor(out=cat[:T, 0:512].rearrange("p (h d) -> p h d", d=64), in0=o_all[:T, :, :], in1=dn_bc[:T, :].unsqueeze(1).to_broadcast([T, 8, 64]), op=ALU.mult),
                             ["o_all", "dn_bc"], ["cat_d"])
                        for k_ in range(8):
                            P.op("pe", lambda e, k_=k_: e.transpose(out=pTt[:, k_, :T], in_=cat[:T, k_ * 128:(k_ + 1) * 128], identity=ident_b[:T, :T]), ["cat_d", "cat_g", "ident_b"], ["pT"])
                        P.op("dve", lambda e: e.tensor_copy(out=catT[:, :, :T], in_=pTt[:, :, :T]), ["pT"], ["catT"])
                        hb = h1[cnt["h1"] % 2]; cnt["h1"] += 1
                        for half in range(2):
                            po = pAo[half]
                            for c in range(8):
                                P.op("pe", lambda e, c=c, half=half, po=po: e.matmul(po[0][:T, po[1]:po[1] + 512], lhsT=catT[:, c, :T], rhs=w_o_b[:, c, half * 512:(half + 1) * 512], start=(c == 0), stop=(c == 7)),
                                     ["catT", "w_o_b"], [po[2]])
                            P.op("dve", lambda e, half=half, po=po: e.tensor_tensor(out=hb[:T, half * 512:(half + 1) * 512], in0=po[0][:T, po[1]:po[1] + 512], in1=xbuf[:T, half * 512:(half + 1) * 512], op=ALU.add),
                                 [po[2], xbuf.name], [hb.name])
                        ti = tile_of(kind, b, i)
                        P.dma("sp", lambda e: e.dma_start(out=h1s[ti, :T, :], in_=hb[:T, :]), [hb.name], [])

                    def prompt_attention(i):
                        steps = [(h, j) for h in range(8) for j in range(i + 1)]

                        def issue_S(n):
                            h, j = steps[n]
                            sl = cnt["s"] % 2; cnt["s"] += 1
                            P.op("pe", lambda e, sl=sl, j=j, h=h: e.matmul(pSb[sl][:, 0:256], lhsT=KT[:, h, j * 128:(j + 1) * 128], rhs=QTb[:, h, :, :], start=True, stop=True),
                                 ["KT%d" % j, "QTb"], ["pS%d" % sl])
                            return sl

                        sls = {0: issue_S(0)}
                        for n, (h, j) in enumerate(steps):
                            if n + 1 < len(steps):
                                sls[n + 1] = issue_S(n + 1)
                            sl = sls.pop(n)
                            acc = pAcc[h % 2]
                            ptb = PT[cnt["pt"] % 3]; cnt["pt"] += 1
                            P.op("act", lambda e, sl=sl, ptb=ptb: e.activation(out=ptb[:, :, :].rearrange("p c q -> p (c q)"), in_=pSb[sl][:, 0:256], func=AF.Exp), ["pS%d" % sl], [ptb.name])
                            if j == i:
                                P.op("pool", lambda e, ptb=ptb: e.tensor_tensor(out=ptb[:, :, :], in0=ptb[:, :, :], in1=triu_b[:, :].unsqueeze(1).to_broadcast([128, 2, 128]), op=ALU.mult),
                                     [ptb.name, "triu_b"], [ptb.name])
                            for c in range(2):
                                P.op("pe", lambda e, c=c, j=j, h=h, acc=acc, ptb=ptb: e.matmul(acc[:, c, :], lhsT=ptb[:, c, :], rhs=Vaug[:, j, h, :], start=(j == 0 and c == 0), stop=(j == i),
                                                                                        skip_group_check=True), [ptb.name, "Vaug%d" % j], [acc.name])
                            if j == i:
                                attn_normalize(128, acc, h)
                                yield

                    pAo = [(pA, 0, "pA0"), (pA, 512, "pA1")]
                    for b in range(NPB if DO_P else 0):
                        P.op("pool", lambda e: e.memset(Sg[:], 0.0), ["Sg"], ["Sg"])
                        nxt = load_x("p", b, 0)
                        for i in range(NT_RUN):
                            xbuf = nxt
                            if i + 1 < NT_RUN:
                                nxt = load_x("p", b, i + 1)
                            gm = mixer_front("p", b, i, xbuf)
                            try:
                                next(gm)
                            except StopIteration:
                                gm = None
                            live = [g_ for g_ in (gm, prompt_attention(i) if STOP > 10 else None) if g_ is not None]
                            while live:
                                for g_ in list(live):
                                    try:
                                        next(g_)
                                    except StopIteration:
                                        live.remove(g_)
                            if STOP > 11:
                                diff_post_and_out("p", b, i, xbuf, pAo, pT)
                        P.dma("sp", lambda e, b=b: e.dma_start(out=ostp[b].rearrange("(m hh) d v -> (hh d) m v", hh=2), in_=Sg[:]), ["Sg"], [])
                    if DO_S:
                        P.dma("sp", lambda e: e.dma_start(out=S0s[:], in_=sgla.rearrange("s (m hh) d v -> (hh d) s m v", hh=2)), [], ["S0s"])
                        xsb = load_x("s", 0, 0)
                        for _ in mixer_front("s", 0, 0, xsb):
                            pass
                        P.dma("sp", lambda e: e.dma_start(out=osts.rearrange("s (m hh) d v -> (hh d) s m v", hh=2), in_=Sns[:]), ["Sns"], [])
                    P.barrier()
                    P.emit()

                with ExitStack() as SBk:
                  if DO_S:
                    sbk = lambda n, s, d=F32: SBk.enter_context(nc.sbuf_tensor(n, list(s), d))
                    pKT = SBk.enter_context(nc.psum_tensor("pKT", [128, 4, 128], F32))
                    pSsF = SBk.enter_context(nc.psum_tensor("pSs", [128, 512], F32))
                    pSs = pSsF[:, 0:128].rearrange("p (g q) -> p g q", q=8)
                    accSF = [SBk.enter_context(nc.psum_tensor("accS%d" % k_, [128, 512], F32)) for k_ in range(4)]
                    accS = [t_[0:32, 0:260].rearrange("p (a b) -> p a b", b=65) for t_ in accSF]
                    pT2 = SBk.enter_context(nc.psum_tensor("pT2", [128, 8, 128], BF16))
                    pA2 = SBk.enter_context(nc.psum_tensor("pA2", [128, 512], F32))
                    NB = 3
                    kpg = [sbk("kpg%d" % k_, [128, 512]) for k_ in range(NB)]
                    vpg = [sbk("vpg%d" % k_, [128, 512]) for k_ in range(NB)]
                    kTp = [sbk("kTp%d" % k_, [128, 4, 128], BF16) for k_ in range(2)]
                    vpa = [sbk("vpa%d" % k_, [128, 8, 65], BF16) for k_ in range(2)]
                    PTp = [[sbk("PTp%d_%d" % (s_, k_), [128, 16, 32], BF16) for k_ in range(2)] for s_ in range(NSB)]
                    Qb2 = sbk("Qb2", [128, 4, NSB, 4, 8], BF16)
                    ptf = sbk("ptf", [128, NSB * NPAGES])
                    pti = sbk("pti", [128, NSB * NPAGES], I32)
                    pidx = sbk("pidx", [128, 1])
                    zz = sbk("zz", [1, 512], BF16)
                    for s_ in range(NSB):
                        for k_ in range(2):
                            P.op("pool", lambda e, t_=PTp[s_][k_]: e.memset(t_[:], 0.0), [], [PTp[s_][k_].name])
                    for k_ in range(2):
                        P.op("pool", lambda e, k_=k_: e.memset(vpa[k_][:], 1.0), [], [vpa[k_].name])
                    P.op("pool", lambda e: e.memset(Qb2[:], 0.0), [], ["Qb2"])
                    P.op("pool", lambda e: e.memset(zz[:], 0.0), [], ["zz"])
                    for pr in range(4):
                        for hh in range(2):
                            h = 2 * pr + hh
                            for c in range(2):
                                rows_src = slice(c * 32, (c + 1) * 32)
                                P.dma("sp", lambda e, pr=pr, hh=hh, h=h, c=c, rows_src=rows_src: e.dma_start(
                                    out=Qb2[hh * 64 + c * 32:hh * 64 + (c + 1) * 32, pr, :, 2 * hh + c, :],
                                    in_=QTb[rows_src, h, c, 0:TS].rearrange("p (s q) -> p s q", q=8)), ["QTb"], ["Qb2"])
                    P.dma("sp", lambda e: e.dma_start(out=pti[:], in_=pt.rearrange("s n -> (s n)").partition_broadcast(128)), [], ["pti"])
                    P.op("dve", lambda e: e.tensor_copy(out=ptf[:], in_=pti[:]), ["pti"], ["ptf"])
                    P.dma("sp", lambda e: e.dma_start(out=pidx[:], in_=c_pidx), [], ["pidx"])
                    P.op("dve", lambda e: e.tensor_scalar(out=ptf[:], in0=ptf[:], scalar1=128.0, scalar2=pidx[:, 0:1], op0=ALU.mult, op1=ALU.add), ["ptf", "pidx"], ["ptf"])
                    P.op("dve", lambda e: e.tensor_copy(out=pti[:], in_=ptf[:]), ["ptf"], ["pti"])
                    for k_ in range(4):
                        P.op("pe", lambda e, k_=k_: e.matmul(accS[k_][:, :, :].rearrange("p a b -> p (a b)"), lhsT=zz[0:1, 0:32], rhs=zz[0:1, 0:260], start=True, stop=False, skip_group_check=True),
                             ["zz"], ["accS"])
                    pages = [(s_, n_) for s_ in range(NSB) for n_ in range(NPG_RUN)]

                    def issue_page(k_):
                        s_, n_ = pages[k_]
                        kb, vb = kpg[k_ % NB], vpg[k_ % NB]
                        col = s_ * NPAGES + n_
                        P.dma_sw(lambda e: e.indirect_dma_start(out=kb[:], out_offset=None, in_=ck, in_offset=bass.IndirectOffsetOnAxis(ap=pti[:, col:col + 1], axis=0)), ["pti"], [kb.name])
                        P.dma_sw(lambda e: e.indirect_dma_start(out=vb[:], out_offset=None, in_=cv, in_offset=bass.IndirectOffsetOnAxis(ap=pti[:, col:col + 1], axis=0)), ["pti"], [vb.name])

                    for k_ in range(min(NB - 1, len(pages))):
                        issue_page(k_)
                    for k_, (s_, n_) in enumerate(pages):
                        if k_ + NB - 1 < len(pages):
                            issue_page(k_ + NB - 1)
                        kb, vb = kpg[k_ % NB], vpg[k_ % NB]
                        ktb, vab, ptp = kTp[k_ % 2], vpa[k_ % 2], PTp[s_][k_ % 2]
                        for pr in range(4):
                            P.op("pe", lambda e, pr=pr, kb=kb: e.transpose(out=pKT[:, pr, :], in_=kb[:, pr * 128:(pr + 1) * 128], identity=ident[:, :]), [kb.name, "ident"], ["pKT"])
                        P.op("dve", lambda e, ktb=ktb: e.tensor_copy(out=ktb[:], in_=pKT[:]), ["pKT"], [ktb.name])
                        P.op("act", lambda e, vab=vab, vb=vb: e.copy(out=vab[:, :, 0:64], in_=vb[:, :].rearrange("p (h d) -> p h d", d=64)), [vb.name], [vab.name])
                        for pr in range(4):
                            P.op("pe", lambda e, pr=pr, ktb=ktb, s_=s_: e.matmul(pSs[:, 4 * pr:4 * pr + 4, :], lhsT=ktb[:, pr, :], rhs=Qb2[:, pr, s_, :, :], start=True, stop=True),
                                 [ktb.name, "Qb2"], ["pSs"])
                        P.op("act", lambda e, ptp=ptp, s_=s_: e.activation(out=ptp[:, :, 8 * s_:8 * s_ + 8], in_=pSs[:, :, :], func=AF.Exp), ["pSs"], [ptp.name])
                        for g in range(16):
                            P.op("pe", lambda e, g=g, ptp=ptp, vab=vab: e.matmul(accS[g // 4][:, g % 4, :], lhsT=ptp[:, g, :], rhs=vab[:, g // 2, :], start=False, stop=False, skip_group_check=True),
                                 [ptp.name, vab.name], ["accS"])
                    PTn = sbk("PTn", [32, 8, 2, 32], BF16)
                    for h in range(8):
                        P.op("pe", lambda e, h=h: e.matmul(pA2[0:32, h * 64:(h + 1) * 64], lhsT=KTn[:, h, :], rhs=QTb[:, h, :, 0:TS], start=True, stop=True), ["KTn", "QTb"], ["pA2"])
                    P.op("act", lambda e: e.activation(out=PTn[:, :, :, :].rearrange("p h c q -> p (h c q)"), in_=pA2[0:32, 0:512], func=AF.Exp), ["pA2"], ["PTn"])
                    P.op("dve", lambda e: e.tensor_tensor(out=PTn[:, :, :, :].rearrange("p h c q -> p (h c) q"), in0=PTn[:, :, :, :].rearrange("p h c q -> p (h c) q"),
                                                          in1=m32_b[:, :].unsqueeze(1).to_broadcast([32, 16, 32]), op=ALU.mult), ["PTn", "m32_b"], ["PTn"])
                    for g in range(16):
                        P.op("pe", lambda e, g=g: e.matmul(accS[g // 4][:, g % 4, :], lhsT=PTn[:, g // 2, g % 2, :], rhs=Vnew[:, g // 2, :], start=False, stop=(g % 4 == 3), skip_group_check=True),
                             ["PTn", "Vnew"], ["accS"])
                    for h in range(8):
                        a_ = accS[h // 2]
                        base = 2 * (h % 2)
                        P.op("dve", lambda e, a_=a_, base=base: e.reciprocal(out=st[:TS, 32:34], in_=a_[:, base:base + 2, 64]), ["accS"], ["st32"])
                        P.op("dve", lambda e: e.tensor_tensor(out=st[:TS, 34:35], in0=st[:TS, 33:34], in1=neglam[:TS, :], op=ALU.mult), ["st32", "neglam"], ["st34"])
                        P.op("dve", lambda e, a_=a_, base=base, h=h: e.tensor_scalar(out=o_all[:TS, h, :], in0=a_[:, base, 0:64], scalar1=st[:TS, 32:33], scalar2=None, op0=ALU.mult), ["accS", "st32"], ["o_all"])
                        P.op("dve", lambda e, a_=a_, base=base, h=h: e.scalar_tensor_tensor(out=o_all[:TS, h, :], in0=a_[:, base + 1, 0:64], scalar=st[:TS, 34:35], in1=o_all[:TS, h, :], op0=ALU.mult, op1=ALU.add),
                             ["accS", "st34", "o_all"], ["o_all"])
                    diff_post_and_out("s", 0, 0, xsb, [(pA2, 0, "pA2"), (pA2, 0, "pA2")], pT2)
                    P.barrier()
                    P.emit()

        S2p.close()
        if "3" in PH:
            S3 = ExitStack()
            s3 = lambda n, s, d=F32: S3.enter_context(nc.sbuf_tensor(n, list(s), d))
            cwq_b = s3("cwq_b", [128, 8, D], BF16)
            cwo_b = s3("cwo_b", [128, 8, D], BF16)
            pwq_b = s3("pwq_b", [128, 8, 2 * D], BF16)
            skT_b = s3("skT_b", [128, 16, 128], BF16)
            mkT = s3("mkT", [128, 4, 8, 256], BF16)
            mva = s3("mva", [128, 4, 2, 4, 257], BF16)
            g_cross = s3("g_cross", [128, 8]); g_mem = s3("g_mem", [128, 8]); g_ffn = s3("g_ffn", [128, 8])
            cqn_bc = s3("cqn_bc", [128, 256]); ckn_bc = s3("ckn_bc", [128, 256]); pqn_bc = s3("pqn_bc", [128, 256])
            for (t_, src) in ((g_cross, cross_norm), (g_mem, mem_norm), (g_ffn, ffn_norm)):
                P.dma("sp", lambda e, t_=t_, src=src: e.dma_start(out=t_[:], in_=src.rearrange("(c p) -> p c", p=128), allow_slow_non_contiguous=True), [], [t_.name])
            for (t_, src) in ((cqn_bc, cross_q_norm), (ckn_bc, cross_k_norm), (pqn_bc, peer_q_norm)):
                P.dma("sp", lambda e, t_=t_, src=src: e.dma_start(out=t_[:], in_=src.partition_broadcast(128)), [], [t_.name])
            P.op("dve", lambda e: e.tensor_scalar(out=cqn_bc[:], in0=cqn_bc[:], scalar1=1.0 / 16.0, scalar2=None, op0=ALU.mult), ["cqn_bc"], ["cqn_bc"])
            P.op("pool", lambda e: e.memset(mva[:], 1.0), [], ["mva"])

            with ExitStack() as SP1:
                s1 = lambda n, s, d=F32: SP1.enter_context(nc.sbuf_tensor(n, list(s), d))
                wkv_b = s1("wkv_b", [128, 8, 2 * D], BF16)
                stg = [s1("stg3_%d" % k_, [128, 2 * D]) for k_ in range(2)]
                xm = [s1("xm%d" % k_, [128, D]) for k_ in range(2)]
                xms_b = s1("xms_b", [128, D], BF16)
                xmT = s1("xmT", [128, 8, 128], BF16)
                sq1 = s1("sq1", [128, D])
                tq1 = s1("tq1", [128, D])
                mkf = [s1("mkf%d" % k_, [128, D]) for k_ in range(2)]
                mk_b = s1("mk_b", [128, D], BF16)
                mvf = [s1("mvf%d" % k_, [128, D]) for k_ in range(2)]
                st1 = s1("st1", [128, 16])
                pA = SP1.enter_context(nc.psum_tensor("pA3", [128, 1024], F32))
                pB = SP1.enter_context(nc.psum_tensor("pB3", [128, 1024], F32))
                pT = SP1.enter_context(nc.psum_tensor("pT3", [128, 8, 128], BF16))
                pTf = SP1.enter_context(nc.psum_tensor("pTf3", [128, 512], F32))
                wl = [(cross_w_kv, 2 * D, wkv_b, g_mem), (cross_w_q, D, cwq_b, g_cross), (cross_w_o, D, cwo_b, None), (peer_w_q, 2 * D, pwq_b, g_ffn)]
                k2 = 0
                for (wsrc, wn, wdst, gg) in wl:
                    for c in range(8):
                        s_ = stg[k2 % 2]; k2 += 1
                        P.dma("sp", lambda e, c=c, s_=s_, wsrc=wsrc, wn=wn: e.dma_start(out=s_[:, 0:wn], in_=wsrc[c * 128:(c + 1) * 128, :]), [], [s_.name])
                        if gg is None:
                            act_copy(wdst[:, c, :], s_[:, 0:wn], [s_.name], [wdst.name], eng=("dve" if c % 2 == 0 else "act"))
                        elif c % 2 == 0:
                            P.op("dve", lambda e, c=c, s_=s_, wn=wn, wdst=wdst, gg=gg: e.tensor_scalar(out=wdst[:, c, :], in0=s_[:, 0:wn], scalar1=gg[:, c:c + 1], scalar2=None, op0=ALU.mult),
                                 [s_.name, gg.name], [wdst.name])
                        else:
                            P.op("act", lambda e, c=c, s_=s_, wn=wn, wdst=wdst, gg=gg: e.activation(out=wdst[:, c, :], in_=s_[:, 0:wn], func=AF.Copy, scale=gg[:, c:c + 1]),
                                 [s_.name, gg.name], [wdst.name])
                for g4 in range(4):
                    s_ = stg[k2 % 2]; k2 += 1
                    P.dma("sp", lambda e, g4=g4, s_=s_: e.dma_start(out=s_[:, 0:512].rearrange("p (g d) -> p g d", d=128), in_=peer_sk[4 * g4:4 * g4 + 4].rearrange("g k d -> k g d")), [], [s_.name])
                    for g_ in range(4):
                        P.op("pe", lambda e, g_=g_, s_=s_: e.transpose(out=pTf[:, g_ * 128:(g_ + 1) * 128], in_=s_[:, g_ * 128:(g_ + 1) * 128], identity=ident[:, :]), [s_.name, "ident"], ["pTf"])
                    P.op("dve", lambda e, g4=g4: e.tensor_copy(out=skT_b[:, 4 * g4:4 * g4 + 4, :], in_=pTf[:, :].rearrange("p (g k) -> p g k", k=128)), ["pTf"], ["skT_b"])

                def mem_set(kind, b):
                    slot = b
                    for mt in range(2):
                        rows = slice(mt * 128, (mt + 1) * 128)
                        mkb = mkf[mt]; mvb = mvf[mt]
                        if kind == "p":
                            xb = xm[mt]
                            P.dma("sp", lambda e, xb=xb, rows=rows: e.dma_start(out=xb[:], in_=memp[b, rows, :]), [], [xb.name])
                            P.op("act", lambda e, xb=xb: e.activation(out=sq1[:], in_=xb[:], func=AF.Square, accum_out=st1[:, 0:1]), [xb.name], ["sq1", "st1a"])
                            rstd_from_ss(st1[:, 0:1], D, 128, 1, "st1a", None)
                            P.op("act", lambda e, xb=xb: e.activation(out=xms_b[:], in_=xb[:], func=AF.Copy, scale=st1[:, 0:1]), [xb.name, "st1a"], ["xms_b"])
                            for k_ in range(8):
                                P.op("pe", lambda e, k_=k_: e.transpose(out=pT[:, k_, :], in_=xms_b[:, k_ * 128:(k_ + 1) * 128], identity=ident_b[:, :]), ["xms_b", "ident_b"], ["pT3"])
                            P.op("dve", lambda e: e.tensor_copy(out=xmT[:], in_=pT[:]), ["pT3"], ["xmT"])
                            for grp in range(4):
                                po = pA if grp < 2 else pB
                                pon = "pA3" if grp < 2 else "pB3"
                                for c in range(8):
                                    P.op("pe", lambda e, c=c, grp=grp, po=po: e.matmul(po[:, (grp % 2) * 512:(grp % 2 + 1) * 512], lhsT=xmT[:, c, :], rhs=wkv_b[:, c, grp * 512:(grp + 1) * 512], start=(c == 0), stop=(c == 7)),
                                         ["xmT", "wkv_b"], [pon])
                            P.op("act", lambda e: e.activation(out=sq1[:], in_=pA[:], func=AF.Square), ["pA3"], ["sq1"])
                            P.op("dve", lambda e: e.tensor_reduce(out=st1[:, 4:8], in_=sq1[:].rearrange("p (g d) -> p g d", d=256), axis=AX.X, op=ALU.add), ["sq1"], ["st1b"])
                            rstd_from_ss(st1[:, 4:8], 256, 128, 4, "st1b", None)
                            P.op("dve", lambda e: e.tensor_tensor(out=tq1[:].rearrange("p (g d) -> p g d", d=256), in0=pA[:].rearrange("p (g d) -> p g d", d=256),
                                                                  in1=st1[:, 4:8].unsqueeze(2).to_broadcast([128, 4, 256]), op=ALU.mult), ["pA3", "st1b"], ["tq1"])
                            P.op("dve", lambda e, mkb=mkb: e.tensor_tensor(out=mkb[:].rearrange("p (g d) -> p g d", d=256), in0=tq1[:].rearrange("p (g d) -> p g d", d=256),
                                                                           in1=ckn_bc[:, :].unsqueeze(1).to_broadcast([128, 4, 256]), op=ALU.mult), ["tq1", "ckn_bc"], [mkb.name])
                            P.dma("sp", lambda e, mkb=mkb, rows=rows: e.dma_start(out=omk[b, rows, :], in_=mkb[:]), [mkb.name], [])
                            P.op("act", lambda e, mvb=mvb: e.copy(out=mvb[:], in_=pB[:]), ["pB3"], [mvb.name])
                            P.dma("sp", lambda e, mvb=mvb, rows=rows: e.dma_start(out=omv[b, rows, :], in_=mvb[:]), [mvb.name], [])
                        else:
                            P.dma("sp", lambda e, mkb=mkb, rows=rows: e.dma_start(out=mkb[:], in_=cmk[b, rows, :]), [], [mkb.name])
                            P.dma("sp", lambda e, mvb=mvb, rows=rows: e.dma_start(out=mvb[:], in_=cmv[b, rows, :]), [], [mvb.name])
                        P.op("act", lambda e, mkb=mkb: e.copy(out=mk_b[:], in_=mkb[:]), [mkb.name], ["mk_b"])
                        for k_ in range(8):
                            P.op("pe", lambda e, k_=k_: e.transpose(out=pT[:, k_, :], in_=mk_b[:, k_ * 128:(k_ + 1) * 128], identity=ident_b[:, :]), ["mk_b", "ident_b"], ["pT3"])
                        P.op("dve", lambda e, rows=rows: e.tensor_copy(out=mkT[:, slot, :, rows], in_=pT[:]), ["pT3"], ["mkT%d" % slot])
                        P.op("pool", lambda e, mvb=mvb, mt=mt: e.tensor_copy(out=mva[:, slot, mt, :, 0:256], in_=mvb[:].rearrange("p (h d) -> p h d", d=256)), [mvb.name, "mva"], ["mva%d" % slot])

                for b in range(NPB):
                    mem_set("p", b)
                P.barrier()
                P.emit()

            with ExitStack() as SW3:
                sw = lambda n, s, d=F32: SW3.enter_context(nc.sbuf_tensor(n, list(s), d))
                hx = [sw("hx%d" % k_, [128, D]) for k_ in range(2)]
                hs_b = sw("hs_b", [128, D], BF16)
                hnT = sw("hnT", [128, 8, 128], BF16)
                sq = sw("sq3", [128, D])
                tq = sw("tq3", [128, D])
                qn_b = sw("qn3_b", [128, D], BF16)
                qnT = sw("qnT", [128, 8, 128], BF16)
                PTc = [sw("PTc%d" % k_, [128, 2, 128], BF16) for k_ in range(2)]
                PTz = [sw("PTz%d" % k_, [128, 2, 32], BF16) for k_ in range(NSB)]
                co = sw("co", [128, D], BF16)
                coT = sw("coT", [128, 8, 128], BF16)
                h2b = [sw("h2b%d" % k_, [128, D]) for k_ in range(2)]
                h2s_b = sw("h2s_b", [128, D], BF16)
                hn2T = [sw("hn2T%d" % k_, [128, 8, 128], BF16) for k_ in range(2)]
                pqn_b = sw("pqn_b", [128, 2 * D], BF16)
                pqT = sw("pqT", [128, 16, 128], BF16)
                scs = [sw("sc%d" % k_, [128, 16, 128]) for k_ in range(2)]
                m16 = sw("m16", [128, 16, 16]); ix = sw("ix", [128, 16, 16], U32); ixf = sw("ixf", [128, 16, 16])
                wk = sw("wk", [128, 128]); wk2 = sw("wk2", [128, 256])
                cand = sw("cand", [128, 8, 256])
                cm = sw("cm", [128, 8, 16]); cj = sw("cj", [128, 8, 16], U32)
                ca = sw("ca", [128, 8, 16], U32); cb = sw("cb", [128, 8, 16], U32)
                caf = sw("caf", [128, 8, 16]); cbf = sw("cbf", [128, 8, 16])
                ce = sw("ce", [128, 8, 16])
                oh = sw("oh", [128, 8, 256])
                rtk = sw("rtk", [128, 3, 128])
                rt = [sw("rt%d" % k_, [128, 3, 128]) for k_ in range(2)]
                st = sw("st3", [128, 64])
                mkf2 = [sq, tq]
                mk_b2 = sw("mk_b2", [128, D], BF16)
                for k_ in range(NSB):
                    P.op("pool", lambda e, k_=k_: e.memset(PTz[k_][:], 0.0), [], [PTz[k_].name])

                with ExitStack() as SA3:
                    pA = SA3.enter_context(nc.psum_tensor("pA4", [128, 1024], F32))
                    pB = SA3.enter_context(nc.psum_tensor("pB4", [128, 1024], F32))
                    pT = SA3.enter_context(nc.psum_tensor("pT4", [128, 8, 128], BF16))
                    pS = SA3.enter_context(nc.psum_tensor("pS4", [128, 2, 256], F32))
                    pO = [SA3.enter_context(nc.psum_tensor("pO4_%d" % k_, [128, 512], F32)) for k_ in range(2)]
                    c3 = {"hx": 0, "h2": 0, "hn": 0, "pt": 0, "rt": 0}

                    def load_h1(ti, T):
                        buf = hx[c3["hx"] % 2]; c3["hx"] += 1
                        P.dma("sp", lambda e: e.dma_start(out=buf[:T, :], in_=h1s[ti, :T, :]), [], [buf.name])
                        return buf

                    def norm_T(src, srcn, T, dst_b, dstT, dstn):
                        P.op("act", lambda e: e.activation(out=sq[:T, :], in_=src[:T, :], func=AF.Square, accum_out=st[:T, 0:1]), [srcn], ["sq3", "st0"])
                        rstd_from_ss(st[:T, 0:1], D, T, 1, "st0", None)
                        P.op("act", lambda e: e.activation(out=dst_b[:T, :], in_=src[:T, :], func=AF.Copy, scale=st[:T, 0:1]), [srcn, "st0"], [dst_b.name])
                        tr8(dst_b, T, dstT, dstn, 0)

                    def tr8(src_b, T, dstT, dstn, k0):
                        for k_ in range(8):
                            P.op("pe", lambda e, k_=k_: e.transpose(out=pT[:, k_, :T], in_=src_b[:T, (k0 + k_) * 128:(k0 + k_ + 1) * 128], identity=ident_b[:T, :T]), [src_b.name, "ident_b"], ["pT4"])
                        P.op("dve", lambda e: e.tensor_copy(out=dstT[:, k0:k0 + 8, :T], in_=pT[:, :, :T]), ["pT4"], [dstn])

                    def headnorm(T, src, srcn, nh, gain, out_ap, outn, stoff):
                        w = nh * 256
                        P.op("act", lambda e: e.activation(out=sq[:T, 0:w], in_=src, func=AF.Square), [srcn], ["sq3"])
                        P.op("dve", lambda e: e.tensor_reduce(out=st[:T, stoff:stoff + nh], in_=sq[:T, 0:w].rearrange("p (g d) -> p g d", d=256), axis=AX.X, op=ALU.add), ["sq3"], ["st%d" % stoff])
                        rstd_from_ss(st[:T, stoff:stoff + nh], 256, T, nh, "st%d" % stoff, None)
                        P.op("dve", lambda e: e.tensor_tensor(out=tq[:T, 0:w].rearrange("p (g d) -> p g d", d=256), in0=src.rearrange("p (g d) -> p g d", d=256),
                                                              in1=st[:T, stoff:stoff + nh].unsqueeze(2).to_broadcast([T, nh, 256]), op=ALU.mult), [srcn, "st%d" % stoff], ["tq3"])
                        P.op("dve", lambda e: e.tensor_tensor(out=out_ap.rearrange("p (g d) -> p g d", d=256), in0=tq[:T, 0:w].rearrange("p (g d) -> p g d", d=256),
                                                              in1=gain[:T, :].unsqueeze(1).to_broadcast([T, nh, 256]), op=ALU.mult), ["tq3", gain.name], [outn])

                    def tile3A(kind, ti, T, sets, hxb, sc):
                        hn_ = hxb.name
                        norm_T(hxb, hn_, T, hs_b, hnT, "hnT")
                        yield
                        for half in range(2):
                            for c in range(8):
                                P.op("pe", lambda e, c=c, half=half: e.matmul(pA[:T, half * 512:(half + 1) * 512], lhsT=hnT[:, c, :T], rhs=cwq_b[:, c, half * 512:(half + 1) * 512], start=(c == 0), stop=(c == 7)),
                                     ["hnT", "cwq_b"], ["pA4"])
                        headnorm(T, pA[:T, :], "pA4", 4, cqn_bc, qn_b[:T, :], "qn3_b", 4)
                        tr8(qn_b, T, qnT, "qnT", 0)
                        yield
                        for h in range(4):
                            po = pO[h % 2]
                            pon = "pO4_%d" % (h % 2)
                            nmm = 2 * len(sets)
                            kmm = 0
                            for (slot, c0, c1) in sets:
                                for mb in range(2):
                                    for k2_ in range(2):
                                        P.op("pe", lambda e, mb=mb, k2_=k2_, slot=slot, h=h: e.matmul(pS[:, mb, :T], lhsT=mkT[:, slot, 2 * h + k2_, mb * 128:(mb + 1) * 128], rhs=qnT[:, 2 * h + k2_, :T],
                                                                                                   start=(k2_ == 0), stop=(k2_ == 1)), ["mkT%d" % slot, "qnT"], ["pS4"])
                                if kind == "p":
                                    ptb = PTc[c3["pt"] % 2]; c3["pt"] += 1
                                    P.op("act", lambda e, ptb=ptb: e.activation(out=ptb[:, :, :T], in_=pS[:, :, :T], func=AF.Exp), ["pS4"], [ptb.name])
                                else:
                                    ptb = PTz[slot]
                                    P.op("act", lambda e, ptb=ptb, c0=c0, c1=c1: e.activation(out=ptb[:, :, c0:c1], in_=pS[:, :, c0:c1], func=AF.Exp), ["pS4"], [ptb.name])
                                for mb in range(2):
                                    P.op("pe", lambda e, mb=mb, slot=slot, h=h, ptb=ptb, po=po, kmm=kmm, nmm=nmm: e.matmul(po[:T, 0:257], lhsT=ptb[:, mb, :T], rhs=mva[:, slot, mb, h, :],
                                                                                                                       start=(kmm == 0), stop=(kmm == nmm - 1)), [ptb.name, "mva%d" % slot], [pon])
                                    kmm += 1
                            P.op("dve", lambda e, po=po: e.reciprocal(out=st[:T, 12:13], in_=po[:T, 256:257]), [pon], ["st12"])
                            P.op("dve", lambda e, po=po, h=h: e.tensor_scalar(out=co[:T, h * 256:(h + 1) * 256], in0=po[:T, 0:256], scalar1=st[:T, 12:13], scalar2=None, op0=ALU.mult), [pon, "st12"], ["co"])
                            yield
                        tr8(co, T, coT, "coT", 0)
                        h2 = h2b[c3["h2"] % 2]; c3["h2"] += 1
                        for half in range(2):
                            for c in range(8):
                                P.op("pe", lambda e, c=c, half=half: e.matmul(pA[:T, half * 512:(half + 1) * 512], lhsT=coT[:, c, :T], rhs=cwo_b[:, c, half * 512:(half + 1) * 512], start=(c == 0), stop=(c == 7)),
                                     ["coT", "cwo_b"], ["pA4"])
                        P.op("dve", lambda e: e.tensor_tensor(out=h2[:T, :], in0=pA[:T, :], in1=hxb[:T, :], op=ALU.add), ["pA4", hn_], [h2.name])
                        P.dma("sp", lambda e: e.dma_start(out=h2s[ti, :T, :], in_=h2[:T, :]), [h2.name], [])
                        yield
                        h2T = hn2T[c3["hn"] % 2]; c3["hn"] += 1
                        norm_T(h2, h2.name, T, h2s_b, h2T, h2T.name)
                        P.dma("sp", lambda e: e.dma_start(out=hnTs[ti].rearrange("p (c t) -> p c t", t=128)[:, :, :T], in_=h2T[:, :, :T]), [h2T.name], [])
                        yield
                        for grp in range(4):
                            po2 = pA if grp < 2 else pB
                            pon2 = "pA4" if grp < 2 else "pB4"
                            for c in range(8):
                                P.op("pe", lambda e, c=c, grp=grp, po2=po2: e.matmul(po2[:T, (grp % 2) * 512:(grp % 2 + 1) * 512], lhsT=h2T[:, c, :T], rhs=pwq_b[:, c, grp * 512:(grp + 1) * 512], start=(c == 0), stop=(c == 7)),
                                     [h2T.name, "pwq_b"], [pon2])
                        yield
                        headnorm(T, pA[:T, :], "pA4", 4, pqn_bc, pqn_b[:T, 0:D], "pqn_b", 16)
                        yield
                        headnorm(T, pB[:T, :], "pB4", 4, pqn_bc, pqn_b[:T, D:2 * D], "pqn_b", 20)
                        yield
                        tr8(pqn_b, T, pqT, "pqT", 0)
                        tr8(pqn_b, T, pqT, "pqT", 8)
                        for g_ in range(16):
                            po2 = pA if g_ < 8 else pB
                            pon2 = "pA4" if g_ < 8 else "pB4"
                            P.op("pe", lambda e, g_=g_, po2=po2: e.matmul(po2[:T, (g_ % 8) * 128:(g_ % 8 + 1) * 128], lhsT=pqT[:, g_, :T], rhs=skT_b[:, g_, :], start=True, stop=True), ["pqT", "skT_b"], [pon2])
                        P.op("act", lambda e: e.copy(out=sc[:T, 0:8, :], in_=pA[:T, :].rearrange("p (g k) -> p g k", k=128)), ["pA4"], [sc.name])
                        P.op("act", lambda e: e.copy(out=sc[:T, 8:16, :], in_=pB[:T, :].rearrange("p (g k) -> p g k", k=128)), ["pB4"], [sc.name])
                        yield

                    def tile3B(ti, T, sc):
                        for g_ in range(16):
                            if g_ % 2 == 0 and g_ > 0:
                                yield
                            P.op("dve", lambda e, g_=g_: e.max(out=m16[:T, g_, 0:8], in_=sc[:T, g_, :]), [sc.name], ["m16"])
                            P.op("dve", lambda e, g_=g_: e.max_index(out=ix[:T, g_, 0:8], in_max=m16[:T, g_, 0:8], in_values=sc[:T, g_, :]), [sc.name, "m16"], ["ix"])
                            P.op("dve", lambda e, g_=g_: e.match_replace(out=wk[:T, :], in_to_replace=m16[:T, g_, 0:8], in_values=sc[:T, g_, :], imm_value=-1e30), [sc.name, "m16"], ["wk"])
                            P.op("dve", lambda e, g_=g_: e.max(out=m16[:T, g_, 8:16], in_=wk[:T, :]), ["wk"], ["m16"])
                            P.op("dve", lambda e, g_=g_: e.max_index(out=ix[:T, g_, 8:16], in_max=m16[:T, g_, 8:16], in_values=wk[:T, :]), ["wk", "m16"], ["ix"])
                        yield
                        m16v = m16[:T, :, :].rearrange("p (h c) k -> p h c k", c=2)
                        P.op("dve", lambda e: e.tensor_tensor(out=cand[:T, :, :].rearrange("p h (a b) -> p h a b", b=16), in0=m16v[:, :, 0, :].unsqueeze(3).to_broadcast([T, 8, 16, 16]),
                                                              in1=m16v[:, :, 1, :].unsqueeze(2).to_broadcast([T, 8, 16, 16]), op=ALU.add), ["m16"], ["cand"])
                        for h in range(8):
                            if h % 2 == 0:
                                yield
                            P.op("dve", lambda e, h=h: e.max(out=cm[:T, h, 0:8], in_=cand[:T, h, :]), ["cand"], ["cm"])
                            P.op("dve", lambda e, h=h: e.max_index(out=cj[:T, h, 0:8], in_max=cm[:T, h, 0:8], in_values=cand[:T, h, :]), ["cand", "cm"], ["cj"])
                            P.op("dve", lambda e, h=h: e.match_replace(out=wk2[:T, :], in_to_replace=cm[:T, h, 0:8], in_values=cand[:T, h, :], imm_value=-1e30), ["cand", "cm"], ["wk2"])
                            P.op("dve", lambda e, h=h: e.max(out=cm[:T, h, 8:16], in_=wk2[:T, :]), ["wk2"], ["cm"])
                            P.op("dve", lambda e, h=h: e.max_index(out=cj[:T, h, 8:16], in_max=cm[:T, h, 8:16], in_values=wk2[:T, :]), ["wk2", "cm"], ["cj"])
                        yield
                        P.op("dve", lambda e: e.tensor_tensor(out=ce[:T, :, :], in0=cm[:T, :, :], in1=cm[:T, :, 0:1].to_broadcast([T, 8, 16]), op=ALU.subtract), ["cm"], ["ce"])
                        P.op("act", lambda e: e.activation(out=ce[:T, :, :], in_=ce[:T, :, :], func=AF.Exp), ["ce"], ["ce"])
                        P.op("dve", lambda e: e.tensor_reduce(out=st[:T, 24:32], in_=ce[:T, :, :], axis=AX.X, op=ALU.add), ["ce"], ["st24"])
                        P.op("dve", lambda e: e.reciprocal(out=st[:T, 24:32], in_=st[:T, 24:32]), ["st24"], ["st24"])
                        P.op("dve", lambda e: e.tensor_tensor(out=rtk[:T, 2, :].rearrange("p (h k) -> p h k", k=16), in0=ce[:T, :, :], in1=st[:T, 24:32].unsqueeze(2).to_broadcast([T, 8, 16]), op=ALU.mult),
                             ["ce", "st24"], ["rtk"])
                        yield
                        P.op("dve", lambda e: e.tensor_single_scalar(out=ca[:T, :, :], in_=cj[:T, :, :], scalar=4, op=ALU.logical_shift_right), ["cj"], ["ca"])
                        P.op("dve", lambda e: e.tensor_single_scalar(out=cb[:T, :, :], in_=cj[:T, :, :], scalar=15, op=ALU.bitwise_and), ["cj"], ["cb"])
                        P.op("dve", lambda e: e.tensor_copy(out=caf[:T, :, :], in_=ca[:T, :, :]), ["ca"], ["caf"])
                        P.op("dve", lambda e: e.tensor_copy(out=cbf[:T, :, :], in_=cb[:T, :, :]), ["cb"], ["cbf"])
                        P.op("dve", lambda e: e.tensor_copy(out=ixf[:T, :, :], in_=ix[:T, :, :]), ["ix"], ["ixf"])
                        ixv = ixf[:T, :, :].rearrange("p (h c) k -> p h c k", c=2)
                        io16 = iota_t[:T, 0:16].unsqueeze(1).unsqueeze(1).to_broadcast([T, 8, 16, 16])
                        for (sel, cxf, dsti) in ((0, caf, 0), (1, cbf, 1)):
                            yield
                            P.op("dve", lambda e, cxf=cxf: e.tensor_tensor(out=oh[:T, :, :].rearrange("p h (k a) -> p h k a", a=16), in0=cxf[:T, :, :].unsqueeze(3).to_broadcast([T, 8, 16, 16]), in1=io16, op=ALU.is_equal),
                                 [cxf.name, "iota_t"], ["oh"])
                            P.op("dve", lambda e, sel=sel: e.tensor_tensor(out=oh[:T, :, :].rearrange("p h (k a) -> p h k a", a=16), in0=oh[:T, :, :].rearrange("p h (k a) -> p h k a", a=16),
                                                                           in1=ixv[:, :, sel, :].unsqueeze(2).to_broadcast([T, 8, 16, 16]), op=ALU.mult), ["oh", "ixf"], ["oh"])
                            P.op("dve", lambda e, dsti=dsti: e.tensor_reduce(out=rtk[:T, dsti, :], in_=oh[:T, :, :].rearrange("p h (k a) -> p (h k) a", a=16), axis=AX.X, op=ALU.add), ["oh"], ["rtk"])
                        yield
                        rtb = rt[c3["rt"] % 2]; c3["rt"] += 1
                        pR = pS
                        for k_ in range(3):
                            P.op("pe", lambda e, k_=k_: e.transpose(out=pR[:, :, :].rearrange("p a b -> p (a b)")[:, k_ * 128:k_ * 128 + T], in_=rtk[:T, k_, :], identity=ident[:T, :T]), ["rtk", "ident"], ["pS4"])
                        P.op("dve", lambda e, rtb=rtb: e.tensor_copy(out=rtb[:, :, :T], in_=pR[:, :, :].rearrange("p a b -> p (a b)")[:, 0:384].rearrange("p (k t) -> p k t", t=128)[:, :, :T]), ["pS4"], [rtb.name])
                        for k_, dst in enumerate((r1s, r2s, rws)):
                            P.dma("sp", lambda e, k_=k_, dst=dst, rtb=rtb: e.dma_start(out=dst[ti, :, :T], in_=rtb[:, k_, :T]), [rtb.name], [])

                    tiles = [("p", b, i) for b in range(NPB if DO_P else 0) for i in range(NT_RUN)]
                    nxt = load_h1(tile_of(*tiles[0]), 128) if tiles else None
                    def interleave(ga, gb):
                        live = [g for g in (ga, gb) if g is not None]
                        while live:
                            for g in list(live):
                                try:
                                    next(g)
                                except StopIteration:
                                    live.remove(g)

                    prevB = None
                    nsc = 0
                    for n_, (kind, b, i) in enumerate(tiles):
                        cur = nxt
                        if n_ + 1 < len(tiles):
                            nxt = load_h1(tile_of(*tiles[n_ + 1]), 128)
                        scb = scs[nsc % 2]; nsc += 1
                        interleave(tile3A("p", tile_of(kind, b, i), 128, [(b, 0, 128)], cur, scb), tile3B(*prevB) if prevB else None)
                        prevB = (tile_of(kind, b, i), 128, scb)
                    if DO_S:
                        for s_ in range(NSB):
                            for mt in range(2):
                                rows = slice(mt * 128, (mt + 1) * 128)
                                mkb = mkf2[mt]
                                P.dma("sp", lambda e, mkb=mkb, rows=rows, s_=s_: e.dma_start(out=mkb[:], in_=cmk[s_, rows, :]), [], [mkb.name])
                                P.op("act", lambda e, mkb=mkb: e.copy(out=mk_b2[:], in_=mkb[:]), [mkb.name], ["mk_b2"])
                                for k_ in range(8):
                                    P.op("pe", lambda e, k_=k_: e.transpose(out=pT[:, k_, :], in_=mk_b2[:, k_ * 128:(k_ + 1) * 128], identity=ident_b[:, :]), ["mk_b2", "ident_b"], ["pT4"])
                                P.op("dve", lambda e, rows=rows, s_=s_: e.tensor_copy(out=mkT[:, s_, :, rows], in_=pT[:]), ["pT4"], ["mkT%d" % s_])
                                mvb = hx[mt]
                                P.dma("sp", lambda e, mvb=mvb, rows=rows, s_=s_: e.dma_start(out=mvb[:], in_=cmv[s_, rows, :]), [], [mvb.name])
                                P.op("pool", lambda e, mvb=mvb, mt=mt, s_=s_: e.tensor_copy(out=mva[:, s_, mt, :, 0:256], in_=mvb[:].rearrange("p (h d) -> p h d", d=256)), [mvb.name, "mva"], ["mva%d" % s_])
                        c3["hx"] = 0
                        cur = load_h1(NPB * NT, TS)
                        scb = scs[nsc % 2]; nsc += 1
                        interleave(tile3A("s", NPB * NT, TS, [(s_, 8 * s_, 8 * s_ + 8) for s_ in range(NSB)], cur, scb), tile3B(*prevB) if prevB else None)
                        prevB = (NPB * NT, TS, scb)
                    if prevB:
                        interleave(tile3B(*prevB), None)
                    P.barrier()
                    P.emit()
            S3.close()

        if "4" in PH:
            with ExitStack() as S4:
                s4 = lambda n, s, d=F32: S4.enter_context(nc.sbuf_tensor(n, list(s), d))
                g_ffn4 = s4("g_ffn4", [128, 8])
                P.dma("sp", lambda e: e.dma_start(out=g_ffn4[:], in_=ffn_norm.rearrange("(c p) -> p c", p=128), allow_slow_non_contiguous=True), [], ["g_ffn4"])
                NJ = cfg.get("nj", 128)
                with ExitStack() as S4a:
                    s4a = lambda n, s, d=F32: S4a.enter_context(nc.sbuf_tensor(n, list(s), d))
                    ub = [s4a("ub%d" % k_, [128, D]) for k_ in range(3)]
                    vb4 = [s4a("vb4_%d" % k_, [128, D]) for k_ in range(3)]
                    uTb = [s4a("uTb%d" % k_, [128, 8, 128], BF16) for k_ in range(2)]
                    vbb = [s4a("vbb%d" % k_, [128, D], BF16) for k_ in range(2)]
                    pU = [S4a.enter_context(nc.psum_tensor("pU%d" % k_, [128, 1024], F32)) for k_ in range(2)]

                    def ld4a(j):
                        P.dma("sp", lambda e: e.dma_start(out=ub[j % 3][:], in_=peer_u[j * 128:(j + 1) * 128, :]), [], [ub[j % 3].name])
                        P.dma("sp", lambda e: e.dma_start(out=vb4[j % 3][:], in_=peer_v[j * 128:(j + 1) * 128, :]), [], [vb4[j % 3].name])

                    for j in range(min(2, NJ)):
                        ld4a(j)
                    for j in range(NJ):
                        if j + 2 < NJ:
                            ld4a(j + 2)
                        u_ = ub[j % 3]; v_ = vb4[j % 3]; pu = pU[j % 2]; ut = uTb[j % 2]; vt = vbb[j % 2]
                        for c in range(8):
                            P.op("pe", lambda e, c=c, u_=u_, pu=pu: e.transpose(out=pu[:, c * 128:(c + 1) * 128], in_=u_[:, c * 128:(c + 1) * 128], identity=ident[:, :]), [u_.name, "ident"], [pu.name])
                        P.op("dve", lambda e, pu=pu, ut=ut: e.tensor_tensor(out=ut[:, :, :], in0=pu[:, :].rearrange("p (c e) -> p c e", e=128), in1=g_ffn4[:, :].unsqueeze(2).to_broadcast([128, 8, 128]), op=ALU.mult),
                             [pu.name, "g_ffn4"], [ut.name])
                        P.dma("sp", lambda e, ut=ut, j=j: e.dma_start(out=uTs[j], in_=ut[:, :, :].rearrange("p c e -> p (c e)")), [ut.name], [])
                        P.op("act", lambda e, v_=v_, vt=vt: e.copy(out=vt[:], in_=v_[:]), [v_.name], [vt.name])
                        P.dma("sp", lambda e, vt=vt, j=j: e.dma_start(out=vbs[j], in_=vt[:]), [vt.name], [])
                    P.barrier()
                    P.emit()

                with ExitStack() as S4b:
                    s4b = lambda n, s, d=F32: S4b.enter_context(nc.sbuf_tensor(n, list(s), d))
                    TB = 256
                    GT = [s4b("GT%d" % k_, [128, 128, TB], BF16) for k_ in range(2)]
                    NUV = 5
                    uj = [s4b("uj%d" % k_, [128, 8, 128], BF16) for k_ in range(NUV)]
                    vj = [s4b("vj%d" % k_, [128, D], BF16) for k_ in range(NUV)]
                    hT = [s4b("hT%d" % k_, [128, 8, TB], BF16) for k_ in range(2)]
                    rr = [s4b("rr%d" % k_, [128, 3, TB]) for k_ in range(2)]
                    O1 = [s4b("O1_%d" % k_, [128, 128], BF16) for k_ in range(4)]
                    O2 = [s4b("O2_%d" % k_, [128, 128], BF16) for k_ in range(4)]
                    Hg = [s4b("Hg%d" % k_, [128, TB], BF16) for k_ in range(3)]
                    Wb = [s4b("Wb%d" % k_, [128, TB], BF16) for k_ in range(3)]
                    h2t = [s4b("h2t%d" % k_, [128, D]) for k_ in range(2)]
                    ybuf = [s4b("ybuf%d" % k_, [128, D]) for k_ in range(2)]
                    pY = [S4b.enter_context(nc.psum_tensor("pY%d" % k_, [128, 1024], F32)) for k_ in range(2)]
                    pH = [S4b.enter_context(nc.psum_tensor("pH%d" % k_, [128, 512], F32)) for k_ in range(3)]
                    pGt = [S4b.enter_context(nc.psum_tensor("pGt%d" % k_, [128, 4, 128], F32)) for k_ in range(1)]
                    c4 = {"o": 0, "g": 0, "uv": 0, "hw": 0, "y": 0}
                    blocks = []
                    if DO_P:
                        for b in range(NPB):
                            for i in range(0, NT_RUN, 2):
                                tl = [(b * NT + i, 128)]
                                if i + 1 < NT_RUN:
                                    tl.append((b * NT + i + 1, 128))
                                blocks.append(tl)
                    if DO_S:
                        blocks.append([(NPB * NT, TS)])

                    def load_block(bi):
                        tl = blocks[bi]
                        hb = hT[bi % 2]; rb = rr[bi % 2]
                        for k_, (ti, T) in enumerate(tl):
                            P.dma("sp", lambda e, k_=k_, ti=ti, T=T: e.dma_start(out=hb[:, :, k_ * 128:k_ * 128 + T], in_=hnTs[ti].rearrange("p (c t) -> p c t", t=128)[:, :, :T]), [], [hb.name])
                            for q_, src in enumerate((r1s, r2s, rws)):
                                P.dma("sp", lambda e, k_=k_, ti=ti, T=T, q_=q_, src=src: e.dma_start(out=rb[:, q_, k_ * 128:k_ * 128 + T], in_=src[ti, :, :T]), [], [rb.name])

                    def g_tokens(bi, t0, t1):
                        rb = rr[bi % 2]; gt = GT[bi % 2]
                        nt_ = sum(T for (_, T) in blocks[bi])
                        t1 = min(t1, nt_)
                        t = t0
                        while t < t1:
                            n4 = min(4, t1 - t)
                            pg = pGt[0]
                            for q_ in range(n4):
                                o1 = O1[c4["o"] % 4]; o2 = O2[c4["o"] % 4]; c4["o"] += 1
                                tt = t + q_
                                P.op("dve", lambda e, o1=o1, tt=tt: e.tensor_scalar(out=o1[:], in0=iota_b[:], scalar1=rb[:, 0, tt:tt + 1], scalar2=None, op0=ALU.is_equal), ["iota_b", rb.name], [o1.name])
                                P.op("dve", lambda e, o2=o2, tt=tt: e.tensor_scalar(out=o2[:], in0=iota_b[:], scalar1=rb[:, 1, tt:tt + 1], scalar2=rb[:, 2, tt:tt + 1], op0=ALU.is_equal, op1=ALU.mult), ["iota_b", rb.name], [o2.name])
                                P.op("pe", lambda e, o1=o1, o2=o2, pg=pg, q_=q_: e.matmul(pg[:, q_, :], lhsT=o2[:], rhs=o1[:], start=True, stop=True), [o1.name, o2.name], [pg.name])
                            P.op("act", lambda e, pg=pg, t=t, n4=n4: e.copy(out=gt[:, :, t:t + n4], in_=pg[:, 0:n4, :].rearrange("p q i -> p i q")), [pg.name], [gt.name])
                            t += n4

                    def ld_uv(j):
                        k_ = c4["uv"] % NUV; c4["uv"] += 1
                        P.dma("sp", lambda e: e.dma_start(out=uj[k_][:, :, :].rearrange("p c e -> p (c e)"), in_=uTs[j]), [], [uj[k_].name])
                        P.dma("sp", lambda e: e.dma_start(out=vj[k_][:], in_=vbs[j]), [], [vj[k_].name])
                        return k_

                    if blocks:
                        load_block(0)
                        g_tokens(0, 0, TB)
                    for bi, tl in enumerate(blocks):
                        nt_ = sum(T for (_, T) in tl)
                        hb = hT[bi % 2]; gt = GT[bi % 2]
                        if bi + 1 < len(blocks):
                            load_block(bi + 1)
                        q = [ld_uv(j_) for j_ in range(min(NUV - 1, NJ))]
                        hst = {}

                        def do_H(j, nt_=nt_, hb=hb):
                            k_ = q[j]
                            sel = c4["hw"] % 3; c4["hw"] += 1
                            ph, hg, wb = pH[sel], Hg[sel], Wb[sel]
                            for c in range(8):
                                P.op("pe", lambda e, c=c, k_=k_, ph=ph: e.matmul(ph[:, 0:nt_], lhsT=uj[k_][:, c, :], rhs=hb[:, c, 0:nt_], start=(c == 0), stop=(c == 7)), [uj[k_].name, hb.name], [ph.name])
                            hst[j] = (k_, ph, hg, wb)

                        do_H(0)
                        if NJ > 1:
                            do_H(1)
                        for j in range(NJ):
                            if j + NUV - 1 < NJ:
                                q.append(ld_uv(j + NUV - 1))
                            if j + 2 < NJ:
                                do_H(j + 2)
                            k_, ph, hg, wb = hst.pop(j)
                            P.op("act", lambda e, ph=ph, hg=hg, nt_=nt_: e.activation(out=hg[:, 0:nt_], in_=ph[:, 0:nt_], func=AF.Gelu_apprx_tanh), [ph.name], [hg.name])
                            P.op("dve", lambda e, hg=hg, wb=wb, j=j, nt_=nt_, gt=gt: e.tensor_tensor(out=wb[:, 0:nt_], in0=hg[:, 0:nt_], in1=gt[:, j, 0:nt_], op=ALU.mult), [hg.name, gt.name], [wb.name])
                            for k2_, (ti, T) in enumerate(tl):
                                for half in range(2):
                                    P.op("pe", lambda e, k2_=k2_, T=T, half=half, wb=wb, k_=k_, j=j: e.matmul(pY[k2_][:T, half * 512:(half + 1) * 512], lhsT=wb[:, k2_ * 128:k2_ * 128 + T], rhs=vj[k_][:, half * 512:(half + 1) * 512],
                                                                                                       start=(j == 0), stop=(j == NJ - 1)), [wb.name, vj[k_].name], ["pY%d" % k2_])
                            if bi + 1 < len(blocks):
                                if NJ >= 128:
                                    if j % 2 == 0:
                                        g_tokens(bi + 1, 2 * j, 2 * j + 4)
                                elif j == 0:
                                    g_tokens(bi + 1, 0, TB)
                        for k2_, (ti, T) in enumerate(tl):
                            hh_ = h2t[c4["y"] % 2]; yb = ybuf[c4["y"] % 2]; c4["y"] += 1
                            P.dma("sp", lambda e, hh_=hh_, ti=ti, T=T: e.dma_start(out=hh_[:T, :], in_=h2s[ti, :T, :]), [], [hh_.name])
                            P.op("dve", lambda e, hh_=hh_, yb=yb, k2_=k2_, T=T: e.tensor_tensor(out=yb[:T, :], in0=pY[k2_][:T, :], in1=hh_[:T, :], op=ALU.add), ["pY%d" % k2_, hh_.name], [yb.name])
                            if ti == NPB * NT:
                                P.dma("sp", lambda e, yb=yb, T=T: e.dma_start(out=ys, in_=yb[:T, :]), [yb.name], [])
                            else:
                                b_, i_ = ti // NT, ti % NT
                                P.dma("sp", lambda e, yb=yb, b_=b_, i_=i_: e.dma_start(out=yp[b_, i_ * 128:(i_ + 1) * 128, :], in_=yb[:, :]), [yb.name], [])
                    P.barrier()
                    P.emit()
    P.close()
    return nc


def _consts():
    j = np.arange(128)
    triu = (j[:, None] <= j[None, :]).astype(np.float32)
    iota = np.tile(np.arange(128, dtype=np.float32)[None, :], (128, 1))
    t = np.arange(32)
    m32 = ((t[:, None] // 8 == t[None, :] // 8) & (t[:, None] <= t[None, :])).astype(np.float32)
    rowm = (t[:, None] // 8 == np.arange(4)[None, :]).astype(np.float32)
    colm = np.tile((np.arange(4)[:, None] == (t[None, :] // 8)).astype(np.float32).reshape(1, 128), (128, 1))
    reset = np.tile((t % 8 != 0).astype(np.float32)[None, :], (128, 1))
    return dict(c_ident=np.eye(128, dtype=np.float32), c_triu=triu, c_iota=iota, c_m32=m32, c_rowm=rowm,
                c_colm=colm, c_reset=reset, c_ones=np.ones((128, 128), np.float32),
                c_pidx=np.arange(128, dtype=np.float32).reshape(128, 1))


def make_in_maps(inp, cores):
    f = lambda a: np.ascontiguousarray(np.asarray(a))
    n_phys = inp["cache_diff_k"].shape[1]
    ckf = f(inp["cache_diff_k"]).reshape(n_phys * 128, 512)
    cvf = f(inp["cache_diff_v"]).reshape(n_phys * 128, 512)
    shared = dict(ck=ckf, cv=cvf)
    for k in ("attn_norm", "w_in", "diff_q_norm", "diff_k_norm", "lambda_q1", "lambda_k1", "lambda_q2", "lambda_k2",
              "diff_out_norm", "gla_w_gate", "gla_b_gate", "gla_out_norm", "w_o", "cross_norm", "mem_norm", "cross_w_q",
              "cross_w_kv", "cross_q_norm", "cross_k_norm", "cross_w_o", "ffn_norm", "peer_w_q", "peer_q_norm", "peer_u", "peer_v"):
        shared[k] = f(inp[k])[0]
    shared["peer_sub_keys"] = f(inp["peer_sub_keys"])[0].reshape(16, 128, 128)
    shared.update(_consts())
    maps = []
    for c in cores:
        m = dict(shared)
        m["xp"] = f(inp["x_prompt"][NPB * c:NPB * (c + 1)])
        m["xs"] = f(inp["x_sample"][NSB * c:NSB * (c + 1)]).reshape(TS, D)
        m["memp"] = f(inp["mem_prompt"][NPB * c:NPB * (c + 1)])
        m["pt"] = f(inp["page_table"][NSB * c:NSB * (c + 1)]).astype(np.int32)
        m["sgla"] = f(inp["state_gla"][0, NSB * c:NSB * (c + 1)])
        m["cmk"] = f(inp["cache_mem_k"][0, NSB * c:NSB * (c + 1)]).reshape(NSB, 256, D)
        m["cmv"] = f(inp["cache_mem_v"][0, NSB * c:NSB * (c + 1)]).reshape(NSB, 256, D)
        maps.append(m)
    return maps, n_phys


def kernel(**inp):
    cores = list(range(NCORES))
    maps, n_phys = make_in_maps(inp, cores)
    nc = build(n_phys)
    used = set()
    for alloc_name in maps[0]:
        used.add(alloc_name)
    res = run_bass_kernel_spmd(nc, maps, core_ids=cores)
    R = res.results
    cat = lambda k: np.concatenate([r[k] for r in R], axis=0)
    B = NPB * NCORES
    DB = NSB * NCORES
    y_p = cat("yp")
    y_s = cat("ys").reshape(DB, 8, D)
    k_p = cat("okp").reshape(1, B, SEQ, 8, 64)
    v_p = cat("ovp").reshape(1, B, SEQ, 8, 64)
    st_p = cat("ostp").reshape(1, B, 4, 64, 128)
    mk_p = cat("omk").reshape(1, B, 256, 4, 256)
    mv_p = cat("omv").reshape(1, B, 256, 4, 256)
    k_s = cat("oks").reshape(1, DB, 8, 8, 64)
    v_s = cat("ovs").reshape(1, DB, 8, 8, 64)
    st_s = cat("osts").reshape(1, DB, 4, 64, 128)
    return (y_p, y_s, k_p, v_p, st_p, mk_p, mv_p, k_s, v_s, st_s)
```

```python
from contextlib import ExitStack
import math
import numpy as np
import concourse.bass as bass
import concourse.mybir as mybir
from concourse.bass_utils import run_bass_kernel_spmd

F32 = mybir.dt.float32
BF16 = mybir.dt.bfloat16
I32 = mybir.dt.int32
U32 = mybir.dt.uint32
AF = mybir.ActivationFunctionType
ALU = mybir.AluOpType
AX = mybir.AxisListType

EPOCH = 6000
NDMA = 28
NSW = 8
EPS = 1e-6
NCORES = 8
D = 1024
SEQ = 2048
NT = SEQ // 128
NPB = 2
NSB = 4
TS = 32
NPAGES = 128
INW = 3088
LAM_INIT = 0.8 - 0.6 * math.exp(-0.3 * 0)
NTILES = NPB * NT + 1


class Prog:
    COMPUTE = ("pe", "act", "dve", "pool")
    ALL = ("pe", "act", "dve", "pool", "sp")

    def __init__(self, nc):
        self.nc = nc
        self.es = ExitStack()
        self.nsem = 0
        self.cur = {}
        self.cnt = {}
        for e in self.COMPUTE:
            self.cur[e] = self._newsem(e)
            self.cnt[e] = 0
        self.dsem = [self._newsem("d%d" % i) for i in range(NDMA)]
        self.dcnt = [0] * NDMA
        self.drr = 0
        self.wsem = [self._newsem("w%d" % i) for i in range(NSW)]
        self.wcnt = [0] * NSW
        self.wcons = [[] for _ in range(NSW)]
        self.wrr = 0
        self.wids = {id(s_): i for i, s_ in enumerate(self.wsem)}
        self.ops = {e: [] for e in self.ALL}
        self.res = {}
        self.known = {e: {} for e in self.ALL}
        self.last_tok = {}
        self.n_ops = 0

    def _newsem(self, name):
        self.nsem += 1
        return self.es.enter_context(self.nc.semaphore("s_%s_%d" % (name, self.nsem)))

    def _deps(self, eng, reads, writes):
        waits = {}

        def need(tok):
            sem, val, teng = tok
            if teng == eng and eng == "pe":
                return
            k = id(sem)
            if k not in waits or waits[k][1] < val:
                waits[k] = (sem, val)

        for r in reads:
            st = self.res.get(r)
            if st:
                for t in st["w"]:
                    need(t)
        for w in writes:
            st = self.res.get(w)
            if st:
                for t in st["w"]:
                    need(t)
                for t in st["r"]:
                    need(t)
        out = []
        kn = self.known[eng]
        for k, (sem, val) in waits.items():
            if kn.get(k, 0) >= val:
                continue
            kn[k] = val
            out.append((sem, val))
        return out

    def _commit(self, tok, reads, writes):
        for r in reads:
            st = self.res.setdefault(r, {"w": [], "r": []})
            st["r"] = [t for t in st["r"] if t[0] is not tok[0]] + [tok]
        for w in writes:
            st = self.res.setdefault(w, {"w": [], "r": []})
            if st["r"]:
                st["w"] = [tok]
                st["r"] = []
            else:
                st["w"] = [t for t in st["w"] if t[0] is not tok[0]] + [tok]
        self.last_tok[id(tok[0])] = tok

    @staticmethod
    def _excl(reads, writes):
        ex = [r for r in reads if (len(r) > 1 and r[0] == "p" and r[1].isupper()) or r.startswith("accS")]
        if ex:
            reads = [r for r in reads if r not in ex]
            writes = list(writes) + [r for r in ex if r not in writes]
        return reads, writes

    def op(self, eng, fn, reads=(), writes=()):
        reads, writes = self._excl(reads, writes)
        waits = self._deps(eng, reads, writes)
        sem = self.cur[eng]
        self.cnt[eng] += 1
        tok = (sem, self.cnt[eng], eng)
        self.ops[eng].append((waits, fn, sem, 1))
        self._note_sw(waits, tok)
        self._commit(tok, reads, writes)
        if self.cnt[eng] >= EPOCH:
            self.cur[eng] = self._newsem(eng)
            self.cnt[eng] = 0
        self.n_ops += 1
        return tok

    def dma(self, q, fn, reads=(), writes=()):
        j = self.drr
        self.drr = (self.drr + 1) % NDMA
        sem = self.dsem[j]
        waits = self._deps(q, reads, writes)
        if self.dcnt[j] > 0:
            k = id(sem)
            v = 16 * self.dcnt[j]
            if self.known[q].get(k, 0) < v:
                self.known[q][k] = v
                waits.append((sem, v))
        self.dcnt[j] += 1
        tok = (sem, 16 * self.dcnt[j], "dma")
        self.ops[q].append((waits, fn, sem, 16))
        self._note_sw(waits, tok)
        self._commit(tok, reads, writes)
        self.n_ops += 1
        return tok

    def _note_sw(self, waits, tok):
        for (s_, v_) in waits:
            i = self.wids.get(id(s_))
            if i is not None:
                self.wcons[i].append(tok)

    def dma_sw(self, fn, reads=(), writes=()):
        q = "pool"
        i = self.wrr
        self.wrr = (self.wrr + 1) % NSW
        sem = self.wsem[i]
        waits = self._deps(q, reads, writes)
        c = self.wcnt[i]
        if c > 0:
            k = id(sem)
            v = 16 * c
            if self.known[q].get(k, 0) < v:
                self.known[q][k] = v
                waits.append((sem, v))
        self.wcnt[i] = c + 1
        tok = (sem, 16 * (c + 1), "dma")
        self.ops[q].append((waits, fn, sem, 16))
        self._commit(tok, reads, writes)
        self.n_ops += 1
        return tok

    def barrier(self):
        toks = list(self.last_tok.values())
        for e in self.ALL:
            waits = []
            for (sem, val, teng) in toks:
                k = id(sem)
                if self.known[e].get(k, 0) >= val:
                    continue
                self.known[e][k] = val
                waits.append((sem, val))
            if waits:
                self.ops[e].append((waits, None, None, 0))
        self.res = {}

    def emit(self):
        nc = self.nc
        ops = self.ops

        def run(eng_obj, lst):
            for waits, fn, sem, inc in lst:
                for (s, v) in waits:
                    eng_obj.wait_ge(s, v)
                if fn is not None:
                    ins = fn(eng_obj)
                    if sem is not None:
                        ins.then_inc(sem, inc)

        with nc.Block() as blk:
            @blk.tensor
            def _(e):
                run(e, ops["pe"])

            @blk.scalar
            def _(e):
                run(e, ops["act"])

            @blk.vector
            def _(e):
                run(e, ops["dve"])

            @blk.gpsimd
            def _(e):
                run(e, ops["pool"])

            @blk.sync
            def _(e):
                run(e, ops["sp"])
        self.ops = {e: [] for e in self.ALL}

    def close(self):
        self.es.close()


def build(n_phys, cfg=None):
    cfg = cfg or {}
    NT_RUN = cfg.get("nt", NT)
    NPG_RUN = cfg.get("npages", NPAGES)
    PH = cfg.get("phases", "234")
    DBG = cfg.get("dbg", False)
    DO_S = cfg.get("sample", True)
    STOP = cfg.get("stop", 99)
    DO_P = cfg.get("prompt", True)

    nc = bass.Bass("TRN2", target_bir_lowering=False)
    di = lambda n, s, d=F32: nc.dram_tensor(n, list(s), d, kind="ExternalInput").ap()
    do = lambda n, s, d=F32: nc.dram_tensor(n, list(s), d, kind="ExternalOutput").ap()
    dscr = lambda n, s, d=F32: nc.dram_tensor(n, list(s), d, kind="Internal").ap()

    xp = di("xp", [NPB, SEQ, D])
    xs = di("xs", [TS, D])
    memp = di("memp", [NPB, 256, D])
    ck = di("ck", [n_phys * 128, 512])
    cv = di("cv", [n_phys * 128, 512])
    pt = di("pt", [NSB, NPAGES], I32)
    sgla = di("sgla", [NSB, 4, 64, 128])
    cmk = di("cmk", [NSB, 256, D])
    cmv = di("cmv", [NSB, 256, D])
    attn_norm = di("attn_norm", [D])
    w_in = di("w_in", [D, INW])
    diff_q_norm = di("diff_q_norm", [32])
    diff_k_norm = di("diff_k_norm", [32])
    lq1 = di("lambda_q1", [32]); lk1 = di("lambda_k1", [32])
    lq2 = di("lambda_q2", [32]); lk2 = di("lambda_k2", [32])
    diff_out_norm = di("diff_out_norm", [64])
    gla_w_gate = di("gla_w_gate", [16, 256])
    gla_b_gate = di("gla_b_gate", [256])
    gla_out_norm = di("gla_out_norm", [128])
    w_o = di("w_o", [D, D])
    cross_norm = di("cross_norm", [D])
    mem_norm = di("mem_norm", [D])
    cross_w_q = di("cross_w_q", [D, D])
    cross_w_kv = di("cross_w_kv", [D, 2 * D])
    cross_q_norm = di("cross_q_norm", [256])
    cross_k_norm = di("cross_k_norm", [256])
    cross_w_o = di("cross_w_o", [D, D])
    ffn_norm = di("ffn_norm", [D])
    peer_w_q = di("peer_w_q", [D, 2 * D])
    peer_q_norm = di("peer_q_norm", [256])
    peer_sk = di("peer_sub_keys", [16, 128, 128])
    peer_u = di("peer_u", [16384, D])
    peer_v = di("peer_v", [16384, D])
    c_ident = di("c_ident", [128, 128])
    c_triu = di("c_triu", [128, 128])
    c_iota = di("c_iota", [128, 128])
    c_m32 = di("c_m32", [32, 32])
    c_rowm = di("c_rowm", [32, 4])
    c_colm = di("c_colm", [128, 4 * 32])
    c_reset = di("c_reset", [128, 32])
    c_ones = di("c_ones", [128, 128])
    c_pidx = di("c_pidx", [128, 1])

    yp = do("yp", [NPB, SEQ, D])
    ys = do("ys", [TS, D])
    okp = do("okp", [NPB, SEQ, 512])
    ovp = do("ovp", [NPB, SEQ, 512])
    ostp = do("ostp", [NPB, 4, 64, 128])
    omk = do("omk", [NPB, 256, D])
    omv = do("omv", [NPB, 256, D])
    oks = do("oks", [TS, 512])
    ovs = do("ovs", [TS, 512])
    osts = do("osts", [NSB, 4, 64, 128])

    if DBG:
        h1s = do("h1s", [NTILES, 128, D])
        h2s = do("h2s", [NTILES, 128, D])
    else:
        h1s = dscr("h1s", [NTILES, 128, D])
        h2s = dscr("h2s", [NTILES, 128, D])
    hnTs = dscr("hnTs", [NTILES, 128, 8 * 128], BF16)
    r1s = dscr("r1s", [NTILES, 128, 128])
    r2s = dscr("r2s", [NTILES, 128, 128])
    rws = dscr("rws", [NTILES, 128, 128])
    uTs = dscr("uTs", [128, 128, 8 * 128], BF16)
    vbs = dscr("vbs", [128, 128, D], BF16)

    P = Prog(nc)

    def tile_of(kind, b, i):
        return NPB * NT if kind == "s" else b * NT + i

    def act_copy(out, in_, reads, writes, eng="act"):
        if eng == "act":
            P.op("act", lambda e: e.copy(out=out, in_=in_), reads, writes)
        else:
            P.op(eng, lambda e: e.tensor_copy(out=out, in_=in_), reads, writes)

    def rstd_from_ss(ss, n, T, w, rn, tmpn):
        P.op("dve", lambda e: e.tensor_scalar(out=ss, in0=ss, scalar1=1.0 / n, scalar2=EPS,
                                              op0=ALU.mult, op1=ALU.add), [rn], [rn])
        P.op("act", lambda e: e.activation(out=ss, in_=ss, func=AF.Sqrt), [rn], [rn])
        P.op("dve", lambda e: e.reciprocal(out=ss, in_=ss), [rn], [rn])

    with ExitStack() as S2:
        sb = lambda n, s, d=F32: S2.enter_context(nc.sbuf_tensor(n, list(s), d))
        ident = sb("ident", [128, 128]); ident_b = sb("ident_b", [128, 128], BF16)
        triu = sb("triu", [128, 128]); triu_b = sb("triu_b", [128, 128], BF16)
        m32 = sb("m32", [32, 32]); m32_b = sb("m32_b", [32, 32], BF16)
        rowm = sb("rowm", [32, 4]); colm = sb("colm", [128, 4, 32]); resetm = sb("resetm", [128, 32])
        ones = sb("ones", [128, 128])
        for (t_, src) in ((ident, c_ident), (triu, c_triu), (m32, c_m32), (rowm, c_rowm),
                          (resetm, c_reset), (ones, c_ones)):
            P.dma("sp", lambda e, t_=t_, src=src: e.dma_start(out=t_[:], in_=src), [], [t_.name])
        P.dma("sp", lambda e: e.dma_start(out=colm[:].rearrange("p s t -> p (s t)"), in_=c_colm), [], ["colm"])
        P.op("dve", lambda e: e.tensor_copy(out=ident_b[:], in_=ident[:]), ["ident"], ["ident_b"])
        P.op("dve", lambda e: e.tensor_copy(out=triu_b[:], in_=triu[:]), ["triu"], ["triu_b"])
        P.op("dve", lambda e: e.tensor_copy(out=m32_b[:], in_=m32[:]), ["m32"], ["m32_b"])

        iota_t = sb("iota_t", [128, 128])
        P.dma("sp", lambda e: e.dma_start(out=iota_t[:], in_=c_iota), [], ["iota_t"])
        iota_b = sb("iota_b", [128, 128], BF16)
        P.op("dve", lambda e: e.tensor_copy(out=iota_b[:], in_=iota_t[:]), ["iota_t"], ["iota_b"])
        S2p = ExitStack()
        sb = lambda n, s, d=F32: S2p.enter_context(nc.sbuf_tensor(n, list(s), d))
        if "2" in PH:
            w_in_b = sb("w_in_b", [128, 8, INW], BF16)
            w_o_b = sb("w_o_b", [128, 8, D], BF16)
            Sg = sb("Sg", [128, 2, 128])
            g_attn = sb("g_attn", [128, 8])
            qn_bc = sb("qn_bc", [128, 32]); kn_bc = sb("kn_bc", [128, 32])
            dn_bc = sb("dn_bc", [128, 64]); gn_bc = sb("gn_bc", [128, 128])
            negb = sb("negb", [128, 2]); wg = sb("wg", [16, 256])
            lamt = sb("lamt", [128, 4, 32]); lamw = sb("lamw", [128, 8]); neglam = sb("neglam", [128, 1])

            P.dma("sp", lambda e: e.dma_start(out=g_attn[:], in_=attn_norm.rearrange("(c p) -> p c", p=128), allow_slow_non_contiguous=True), [], ["g_attn"])
            P.dma("sp", lambda e: e.dma_start(out=negb[:], in_=gla_b_gate.rearrange("(c p) -> p c", p=128), allow_slow_non_contiguous=True), [], ["negb"])
            P.dma("sp", lambda e: e.dma_start(out=qn_bc[:], in_=diff_q_norm.partition_broadcast(128)), [], ["qn_bc"])
            P.dma("sp", lambda e: e.dma_start(out=kn_bc[:], in_=diff_k_norm.partition_broadcast(128)), [], ["kn_bc"])
            P.dma("sp", lambda e: e.dma_start(out=dn_bc[:], in_=diff_out_norm.partition_broadcast(128)), [], ["dn_bc"])
            P.dma("sp", lambda e: e.dma_start(out=gn_bc[:], in_=gla_out_norm.partition_broadcast(128)), [], ["gn_bc"])
            P.dma("sp", lambda e: e.dma_start(out=wg[:], in_=gla_w_gate), [], ["wg"])
            for k_, src in enumerate((lq1, lk1, lq2, lk2)):
                P.dma("sp", lambda e, k_=k_, src=src: e.dma_start(out=lamt[:, k_, :], in_=src.partition_broadcast(128)), [], ["lamt"])
            P.op("dve", lambda e: e.tensor_scalar(out=qn_bc[:], in0=qn_bc[:], scalar1=32.0 ** -0.5, scalar2=None, op0=ALU.mult), ["qn_bc"], ["qn_bc"])
            P.op("dve", lambda e: e.tensor_scalar(out=dn_bc[:], in0=dn_bc[:], scalar1=1.0 - LAM_INIT, scalar2=None, op0=ALU.mult), ["dn_bc"], ["dn_bc"])
            P.op("dve", lambda e: e.tensor_scalar(out=negb[:], in0=negb[:], scalar1=-1.0, scalar2=None, op0=ALU.mult), ["negb"], ["negb"])
            P.op("dve", lambda e: e.tensor_tensor(out=lamt[:, 0, :], in0=lamt[:, 0, :], in1=lamt[:, 1, :], op=ALU.mult), ["lamt"], ["lamt"])
            P.op("dve", lambda e: e.tensor_tensor(out=lamt[:, 2, :], in0=lamt[:, 2, :], in1=lamt[:, 3, :], op=ALU.mult), ["lamt"], ["lamt"])
            P.op("dve", lambda e: e.tensor_reduce(out=lamw[:, 0:1], in_=lamt[:, 0, :], axis=AX.X, op=ALU.add), ["lamt"], ["lamw"])
            P.op("dve", lambda e: e.tensor_reduce(out=lamw[:, 1:2], in_=lamt[:, 2, :], axis=AX.X, op=ALU.add), ["lamt"], ["lamw"])
            P.op("act", lambda e: e.activation(out=lamw[:, 2:4], in_=lamw[:, 0:2], func=AF.Exp), ["lamw"], ["lamw"])
            P.op("dve", lambda e: e.tensor_tensor(out=lamw[:, 4:5], in0=lamw[:, 3:4], in1=lamw[:, 2:3], op=ALU.subtract), ["lamw"], ["lamw"])
            P.op("dve", lambda e: e.tensor_scalar(out=neglam[:], in0=lamw[:, 4:5], scalar1=-LAM_INIT, scalar2=None, op0=ALU.add), ["lamw"], ["neglam"])

            with ExitStack() as S0:
                stg = [S0.enter_context(nc.sbuf_tensor("stg%d" % k_, [128, INW], F32)) for k_ in range(2)]
                for c in range(8):
                    s_ = stg[c % 2]
                    P.dma("sp", lambda e, c=c, s_=s_: e.dma_start(out=s_[:], in_=w_in[c * 128:(c + 1) * 128, :]), [], [s_.name])
                    if c % 2 == 0:
                        P.op("dve", lambda e, c=c, s_=s_: e.tensor_scalar(out=w_in_b[:, c, :], in0=s_[:], scalar1=g_attn[:, c:c + 1], scalar2=None, op0=ALU.mult), [s_.name, "g_attn"], ["w_in_b"])
                    else:
                        P.op("act", lambda e, c=c, s_=s_: e.activation(out=w_in_b[:, c, :], in_=s_[:], func=AF.Copy, scale=g_attn[:, c:c + 1]), [s_.name, "g_attn"], ["w_in_b"])
                for c in range(8):
                    s_ = stg[c % 2]
                    P.dma("sp", lambda e, c=c, s_=s_: e.dma_start(out=s_[:, 0:D], in_=w_o[c * 128:(c + 1) * 128, :]), [], [s_.name])
                    act_copy(w_o_b[:, c, :], s_[:, 0:D], [s_.name], ["w_o_b"], eng=("dve" if c % 2 == 0 else "act"))
                P.barrier()
                P.emit()

            with ExitStack() as SW:
                sw = lambda n, s, d=F32: SW.enter_context(nc.sbuf_tensor(n, list(s), d))
                xt = [sw("xt%d" % k_, [128, D]) for k_ in range(2)]
                xs_b = sw("xs_b", [128, D], BF16)
                xnT = sw("xnT", [128, 8, 128], BF16)
                sq = sw("sq", [128, D])
                tmp = sw("tmp", [128, 512])
                st = sw("st", [128, 64])
                qn_b = sw("qn_b", [128, 512], BF16)
                kn = [sw("kn%d" % k_, [128, 512]) for k_ in range(2)]
                kn_b = sw("kn_b", [128, 512], BF16)
                QTb = sw("QTb", [64, 8, 2, 128], BF16)
                dv_sb = [sw("dv%d" % k_, [128, 512]) for k_ in range(2)]
                gv_sb = sw("gv_sb", [128, 512])
                sr = sw("sr", [128, 512])
                gaT = sw("gaT", [16, 128])
                e_sb = sw("e_sb", [128, 2, 128]); sp_sb = sw("sp_sb", [128, 2, 128])
                bT = sw("bT", [128, 2, 128]); eb = sw("eb", [128, 2, 128]); enb = sw("enb", [128, 2, 128])
                tmpE = sw("tmpE", [128, 2, 128])
                qdz = sw("qdz", [128, 2, 2, 128]); kd = sw("kd", [128, 2, 128])
                kd2T = sw("kd2T", [128, 2, 128]); kd2 = sw("kd2", [128, 2, 128])
                AT = sw("AT", [128, 4, 128])
                o_all = sw("o_all", [128, 8, 64])
                PT = [sw("PT%d" % k_, [128, 2, 128], BF16) for k_ in range(3)]
                cat = sw("cat", [128, D], BF16)
                catT = sw("catT", [128, 8, 128], BF16)
                h1 = [sw("h1_%d" % k_, [128, D]) for k_ in range(2)]
                qdp = sw("qdp", [128, 4, 2, 2, 32])
                S0s = sw("S0s", [128, 4, 2, 128])
                Sns = sw("Sns", [128, 4, 2, 128])
                bl = sw("bl", [128, 8]); ebl = sw("ebl", [128, 8])
                Vnew = sw("Vnew", [32, 8, 65], BF16)
                KTn = sw("KTn", [64, 8, 32], BF16)

                P.op("pool", lambda e: e.memset(QTb[:], 0.0), [], ["QTb"])
                P.op("pool", lambda e: e.memset(qdz[:], 0.0), [], ["qdz"])
                P.op("pool", lambda e: e.memset(Vnew[:], 1.0), [], ["Vnew"])

                with ExitStack() as SA:
                    KT = SA.enter_context(nc.sbuf_tensor("KT", [64, 8, SEQ], BF16))
                    Vaug = SA.enter_context(nc.sbuf_tensor("Vaug", [128, NT, 8, 65], BF16))
                    P.op("pool", lambda e: e.memset(Vaug[:], 1.0), [], ["Vaug"])
                    pA = SA.enter_context(nc.psum_tensor("pA", [128, 1024], F32))
                    pT = SA.enter_context(nc.psum_tensor("pT", [128, 8, 128], BF16))
                    pSb = [SA.enter_context(nc.psum_tensor("pS%d" % k_, [128, 512], F32)) for k_ in range(2)]
                    pAccF = [SA.enter_context(nc.psum_tensor("pAcc%d" % k_, [128, 512], F32)) for k_ in range(2)]
                    pAcc = [t_[:, 0:130].rearrange("p (c d) -> p c d", d=65) for t_ in pAccF]
                    pF = SA.enter_context(nc.psum_tensor("pF", [128, 512], F32))
                    pG = pA[:, 0:512]

                    cnt = {"s": 0, "pt": 0, "x": 0, "kn": 0, "dv": 0, "h1": 0}

                    def load_x(kind, b, i):
                        buf = xt[cnt["x"] % 2]; cnt["x"] += 1
                        T = 128 if kind == "p" else TS
                        src = xp[b, i * 128:(i + 1) * 128, :] if kind == "p" else xs
                        P.dma("sp", lambda e: e.dma_start(out=buf[:T, :], in_=src), [], [buf.name])
                        return buf

                    def transposes_bf(src_tile, T, dstT, n, src_res, dst_res, width=128):
                        for k_ in range(n):
                            P.op("pe", lambda e, k_=k_: e.transpose(out=pT[:width, k_, :T], in_=src_tile[:T, k_ * width:(k_ + 1) * width], identity=ident_b[:T, :T]),
                                 [src_res, "ident_b"], ["pT"])

                    def mixer_front(kind, b, i, xbuf):
                        T = 128 if kind == "p" else TS
                        xr = xbuf.name
                        P.op("act", lambda e: e.activation(out=sq[:T, :], in_=xbuf[:T, :], func=AF.Square, accum_out=st[:T, 0:1]), [xr], ["sq", "st0"])
                        rstd_from_ss(st[:T, 0:1], D, T, 1, "st0", None)
                        P.op("act", lambda e: e.activation(out=xs_b[:T, :], in_=xbuf[:T, :], func=AF.Copy, scale=st[:T, 0:1]), [xr, "st0"], ["xs_b"])
                        transposes_bf(xs_b, T, xnT, 8, "xs_b", "xnT")
                        P.op("dve", lambda e: e.tensor_copy(out=xnT[:, :, :T], in_=pT[:, :, :T]), ["pT"], ["xnT"])

                        if STOP <= 1:
                            return
                        def proj_tok(col0, ncols, half):
                            for c in range(8):
                                P.op("pe", lambda e, c=c: e.matmul(pA[:T, half * 512:half * 512 + ncols], lhsT=xnT[:, c, :T], rhs=w_in_b[:, c, col0:col0 + ncols],
                                                                   start=(c == 0), stop=(c == 7)), ["xnT", "w_in_b"], ["pA%d" % half])
                            return pA[:T, half * 512:half * 512 + ncols], "pA%d" % half

                        def qknorm(src, sres, gain, out_ap, out_res):
                            P.op("act", lambda e: e.activation(out=sq[:T, 0:512], in_=src, func=AF.Square), [sres], ["sq"])
                            P.op("dve", lambda e: e.tensor_reduce(out=st[:T, 8:24], in_=sq[:T, 0:512].rearrange("p (g d) -> p g d", d=32), axis=AX.X, op=ALU.add), ["sq"], ["st8"])
                            rstd_from_ss(st[:T, 8:24], 32, T, 16, "st8", None)
                            P.op("dve", lambda e: e.tensor_tensor(out=tmp[:T, :].rearrange("p (g d) -> p g d", d=32), in0=src.rearrange("p (g d) -> p g d", d=32),
                                                                  in1=st[:T, 8:24].unsqueeze(2).to_broadcast([T, 16, 32]), op=ALU.mult), [sres, "st8"], ["tmp"])
                            P.op("dve", lambda e: e.tensor_tensor(out=out_ap.rearrange("p (g d) -> p g d", d=32), in0=tmp[:T, :].rearrange("p (g d) -> p g d", d=32),
                                                                  in1=gain[:T, :].unsqueeze(1).to_broadcast([T, 16, 32]), op=ALU.mult), ["tmp", gain.name], [out_res])

                        src, sres = proj_tok(0, 512, 0)
                        qknorm(src, sres, qn_bc, qn_b[:T, :], "qn_b")
                        if STOP <= 2:
                            return
                        src, sres = proj_tok(512, 512, 1)
                        knb = kn[cnt["kn"] % 2]; cnt["kn"] += 1
                        qknorm(src, sres, kn_bc, knb[:T, :], knb.name)
                        dstk = okp[b, i * 128:(i + 1) * 128, :] if kind == "p" else oks
                        P.dma("sp", lambda e: e.dma_start(out=dstk, in_=knb[:T, :]), [knb.name], [])
                        P.op("act", lambda e: e.copy(out=kn_b[:T, :], in_=knb[:T, :]), [knb.name], ["kn_b"])
                        transposes_bf(qn_b, T, None, 8, "qn_b", None, width=64)
                        P.op("dve", lambda e: e.tensor_copy(out=QTb[0:32, :, 0, :T], in_=pT[0:32, :, :T]), ["pT"], ["QTb"])
                        P.op("dve", lambda e: e.tensor_copy(out=QTb[32:64, :, 1, :T], in_=pT[32:64, :, :T]), ["pT"], ["QTb"])
                        transposes_bf(kn_b, T, None, 8, "kn_b", None, width=64)
                        if kind == "p":
                            P.op("dve", lambda e: e.tensor_copy(out=KT[:, :, i * 128:(i + 1) * 128], in_=pT[0:64, :, :]), ["pT"], ["KT%d" % i])
                        else:
                            P.op("dve", lambda e: e.tensor_copy(out=KTn[:, :, :], in_=pT[0:64, :, :T]), ["pT"], ["KTn"])
                        if STOP <= 3:
                            return
                        src, sres = proj_tok(1024, 512, 0)
                        dvb = dv_sb[cnt["dv"] % 2]; cnt["dv"] += 1
                        P.op("act", lambda e, src=src: e.copy(out=dvb[:T, :], in_=src), [sres], [dvb.name])
                        dstv = ovp[b, i * 128:(i + 1) * 128, :] if kind == "p" else ovs
                        P.dma("sp", lambda e: e.dma_start(out=dstv, in_=dvb[:T, :]), [dvb.name], [])
                        if kind == "p":
                            P.op("pool", lambda e: e.tensor_copy(out=Vaug[:, i, :, 0:64], in_=dvb[:, :].rearrange("p (h d) -> p h d", d=64)), [dvb.name, "Vaug"], ["Vaug%d" % i])
                        else:
                            P.op("pool", lambda e: e.tensor_copy(out=Vnew[:, :, 0:64], in_=dvb[:T, :].rearrange("p (h d) -> p h d", d=64)), [dvb.name], ["Vnew"])
                        src, sres = proj_tok(2048, 512, 1)
                        P.op("act", lambda e, src=src: e.copy(out=gv_sb[:T, :], in_=src), [sres], ["gv_sb"])
                        src, sres = proj_tok(2576, 512, 0)
                        P.op("act", lambda e, src=src: e.activation(out=sr[:T, :], in_=src, func=AF.Silu), [sres], ["sr"])
                        if STOP <= 4:
                            return
                        for k_, col0 in enumerate((1536, 1664, 1792, 1920)):
                            for c in range(8):
                                P.op("pe", lambda e, c=c, k_=k_, col0=col0: e.matmul(pF[:, k_ * 128:k_ * 128 + T], lhsT=w_in_b[:, c, col0:col0 + 128], rhs=xnT[:, c, :T],
                                                                                     start=(c == 0), stop=(c == 7)), ["xnT", "w_in_b"], ["pF"])
                        for c in range(8):
                            P.op("pe", lambda e, c=c: e.matmul(pG[0:16, 0:T], lhsT=w_in_b[:, c, 2560:2576], rhs=xnT[:, c, :T], start=(c == 0), stop=(c == 7)),
                                 ["xnT", "w_in_b"], ["pA0"])
                        P.op("act", lambda e: e.copy(out=gaT[:, :T], in_=pG[0:16, 0:T]), ["pA0"], ["gaT"])
                        if STOP <= 5:
                            return
                        yield
                        for m in range(2):
                            P.op("pe", lambda e, m=m: e.matmul(pG[:, 128 + m * 128:128 + m * 128 + T], lhsT=wg[:, m * 128:(m + 1) * 128], rhs=gaT[:, :T], start=True, stop=True),
                                 ["wg", "gaT"], ["pA0"])
                        for m in range(2):
                            P.op("act", lambda e, m=m: e.activation(out=e_sb[:, m, :T], in_=pG[:, 128 + m * 128:128 + m * 128 + T], func=AF.Exp, scale=-1.0, bias=negb[:, m:m + 1]),
                                 ["pA0", "negb"], ["e_sb"])
                        P.op("act", lambda e: e.activation(out=sp_sb[:, :, :T], in_=e_sb[:, :, :T], func=AF.Ln, bias=1.0), ["e_sb"], ["sp_sb"])
                        for m in range(2):
                            d0 = ones[:, :T] if kind == "p" else resetm[:, :T]
                            P.op("dve", lambda e, m=m, d0=d0: e.tensor_tensor_scan(out=bT[:, m, :T], data0=d0, data1=sp_sb[:, m, :T], initial=0.0, op0=ALU.mult, op1=ALU.subtract),
                                 ["sp_sb", "ones", "resetm"], ["bT"])
                        P.op("act", lambda e: e.activation(out=eb[:, :, :T], in_=bT[:, :, :T], func=AF.Exp, scale=1.0 / 16), ["bT"], ["eb"])
                        P.op("act", lambda e: e.activation(out=enb[:, :, :T], in_=bT[:, :, :T], func=AF.Exp, scale=-1.0 / 16), ["bT"], ["enb"])
                        for m in range(2):
                            for hh in range(2):
                                rws = slice(hh * 64, (hh + 1) * 64)
                                P.op("dve", lambda e, m=m, hh=hh, rws=rws: e.scalar_tensor_tensor(out=qdz[rws, m, hh, :T], in0=pF[rws, m * 128:m * 128 + T], scalar=0.125, in1=eb[rws, m, :T], op0=ALU.mult, op1=ALU.mult),
                                     ["pF", "eb"], ["qdz"])
                            P.op("dve", lambda e, m=m: e.tensor_tensor(out=kd[:, m, :T], in0=pF[:, 256 + m * 128:256 + m * 128 + T], in1=enb[:, m, :T], op=ALU.mult),
                                 ["pF", "enb"], ["kd"])
                        if STOP <= 6:
                            return
                        yield
                        for h in range(4):
                            m, hh = h // 2, h % 2
                            rows = slice(hh * 64, (hh + 1) * 64)
                            P.op("pe", lambda e, h=h, m=m, hh=hh: e.matmul(pG[:T, h * 128:h * 128 + T], lhsT=kd[:, m, :T], rhs=qdz[:, m, hh, :T], start=True, stop=True),
                                 ["kd", "qdz"], ["pA0"])
                        msk = triu if kind == "p" else m32
                        P.op("dve", lambda e: e.tensor_tensor(out=AT[:T, :, :T], in0=pG[:T, :].rearrange("p (h t) -> p h t", t=128)[:, :, :T],
                                                              in1=msk[:T, :T].unsqueeze(1).to_broadcast([T, 4, T]), op=ALU.mult), ["pA0", msk.name], ["AT"])
                        if STOP <= 7:
                            return
                        yield
                        if kind == "s":
                            for s_ in range(NSB):
                                P.op("dve", lambda e, s_=s_: e.tensor_tensor(out=qdp[:, s_, :, :, :].rearrange("p m h t -> p (m h) t"), in0=qdz[:, :, :, :T].rearrange("p m h t -> p (m h) t"),
                                                                             in1=colm[:, s_, :].unsqueeze(1).to_broadcast([128, 4, TS]), op=ALU.mult),
                                     ["qdz", "colm"], ["qdp"])
                        for h in range(4):
                            m, hh = h // 2, h % 2
                            rows = slice(hh * 64, (hh + 1) * 64)
                            oo = pA[:T, 512 + h * 128:512 + (h + 1) * 128]
                            P.op("pe", lambda e, h=h, oo=oo: e.matmul(oo, lhsT=AT[:T, h, :T], rhs=gv_sb[:T, h * 128:(h + 1) * 128], start=True, stop=False),
                                 ["AT", "gv_sb"], ["pA1"])
                            if kind == "p":
                                P.op("pe", lambda e, m=m, hh=hh, oo=oo: e.matmul(oo, lhsT=qdz[:, m, hh, :T], rhs=Sg[:, m, :], start=False, stop=True),
                                     ["qdz", "Sg"], ["pA1"])
                            else:
                                for s_ in range(NSB):
                                    P.op("pe", lambda e, m=m, hh=hh, oo=oo, s_=s_: e.matmul(oo, lhsT=qdp[:, s_, m, hh, :], rhs=S0s[:, s_, m, :], start=False, stop=(s_ == NSB - 1)),
                                         ["qdp", "S0s"], ["pA1"])
                        if STOP <= 8:
                            return
                        yield
                        nseq = 1 if kind == "p" else NSB
                        for s_ in range(nseq):
                            last = T - 1 if kind == "p" else 8 * s_ + 7
                            P.op("dve", lambda e, s_=s_, last=last: e.tensor_scalar(out=bl[:, 2 * s_:2 * s_ + 2], in0=bT[:, :, last], scalar1=1.0 / 16, scalar2=None, op0=ALU.mult),
                                 ["bT"], ["bl"])
                            P.op("act", lambda e, s_=s_: e.activation(out=ebl[:, 2 * s_:2 * s_ + 2], in_=bl[:, 2 * s_:2 * s_ + 2], func=AF.Exp), ["bl"], ["ebl"])
                            for m in range(2):
                                P.op("act", lambda e, m=m, s_=s_: e.activation(out=tmpE[:, m, :T], in_=bT[:, m, :T], func=AF.Exp, scale=-1.0 / 16, bias=bl[:, 2 * s_ + m:2 * s_ + m + 1]),
                                     ["bT", "bl"], ["tmpE"])
                                P.op("dve", lambda e, m=m: e.tensor_tensor(out=kd2T[:, m, :T], in0=pF[:, 256 + m * 128:256 + m * 128 + T], in1=tmpE[:, m, :T], op=ALU.mult),
                                     ["pF", "tmpE"], ["kd2T"])
                                P.op("pe", lambda e, m=m: e.transpose(out=pG[:T, m * 128:(m + 1) * 128], in_=kd2T[:, m, :T], identity=ident[:, :]), ["kd2T", "ident"], ["pA0"])
                            if kind == "p":
                                P.op("dve", lambda e: e.tensor_copy(out=kd2[:T, :, :], in_=pG[:T, 0:256].rearrange("p (m f) -> p m f", m=2)), ["pA0"], ["kd2"])
                            else:
                                P.op("dve", lambda e, s_=s_: e.tensor_scalar(out=kd2[:T, :, :], in0=pG[:T, 0:256].rearrange("p (m f) -> p m f", m=2), scalar1=rowm[:T, s_:s_ + 1], scalar2=None, op0=ALU.mult),
                                     ["pA0", "rowm"], ["kd2"])
                            for m in range(2):
                                yield
                                P.op("pe", lambda e, m=m: e.matmul(pG[:, 256:512], lhsT=kd2[:T, m, :], rhs=gv_sb[:T, m * 256:(m + 1) * 256], start=True, stop=True),
                                     ["kd2", "gv_sb"], ["pA0"])
                                for hh in range(2):
                                    rows = slice(hh * 64, (hh + 1) * 64)
                                    if kind == "p":
                                        P.op("dve", lambda e, m=m, hh=hh, rows=rows: e.scalar_tensor_tensor(out=Sg[rows, m, :], in0=Sg[rows, m, :], scalar=ebl[rows, m:m + 1],
                                                                                                         in1=pG[rows, 256 + hh * 128:256 + (hh + 1) * 128], op0=ALU.mult, op1=ALU.add),
                                             ["Sg", "ebl", "pA0"], ["Sg"])
                                    else:
                                        P.op("dve", lambda e, m=m, hh=hh, rows=rows, s_=s_: e.scalar_tensor_tensor(out=Sns[rows, s_, m, :], in0=S0s[rows, s_, m, :], scalar=ebl[rows, 2 * s_ + m:2 * s_ + m + 1],
                                                                                                                  in1=pG[rows, 256 + hh * 128:256 + (hh + 1) * 128], op0=ALU.mult, op1=ALU.add),
                                             ["S0s", "ebl", "pA0"], ["Sns"])
                        if STOP <= 9:
                            return
                        yield
                        og = pA[:T, 512:1024]
                        P.op("act", lambda e: e.activation(out=sq[:T, 0:512], in_=og, func=AF.Square), ["pA1"], ["sq"])
                        P.op("dve", lambda e: e.tensor_reduce(out=st[:T, 24:28], in_=sq[:T, 0:512].rearrange("p (g d) -> p g d", d=128), axis=AX.X, op=ALU.add), ["sq"], ["st24"])
                        rstd_from_ss(st[:T, 24:28], 128, T, 4, "st24", None)
                        P.op("dve", lambda e: e.tensor_tensor(out=tmp[:T, :].rearrange("p (g d) -> p g d", d=128), in0=og.rearrange("p (g d) -> p g d", d=128),
                                                              in1=st[:T, 24:28].unsqueeze(2).to_broadcast([T, 4, 128]), op=ALU.mult), ["pA1", "st24"], ["tmp"])
                        P.op("dve", lambda e: e.tensor_tensor(out=tmp[:T, :].rearrange("p (g d) -> p g d", d=128), in0=tmp[:T, :].rearrange("p (g d) -> p g d", d=128),
                                                              in1=gn_bc[:T, :].unsqueeze(1).to_broadcast([T, 4, 128]), op=ALU.mult), ["tmp", "gn_bc"], ["tmp"])
                        P.op("dve", lambda e: e.tensor_tensor(out=cat[:T, 512:1024], in0=tmp[:T, :], in1=sr[:T, :], op=ALU.mult), ["tmp", "sr"], ["cat_g"])

                    def attn_normalize(T, acc, h):
                        P.op("dve", lambda e: e.reciprocal(out=st[:T, 32:34], in_=acc[:T, :, 64]), [acc.name], ["st32"])
                        P.op("dve", lambda e: e.tensor_tensor(out=st[:T, 34:35], in0=st[:T, 33:34], in1=neglam[:T, :], op=ALU.mult), ["st32", "neglam"], ["st34"])
                        P.op("dve", lambda e: e.tensor_scalar(out=o_all[:T, h, :], in0=acc[:T, 0, 0:64], scalar1=st[:T, 32:33], scalar2=None, op0=ALU.mult), [acc.name, "st32"], ["o_all"])
                        P.op("dve", lambda e: e.scalar_tensor_tensor(out=o_all[:T, h, :], in0=acc[:T, 1, 0:64], scalar=st[:T, 34:35], in1=o_all[:T, h, :], op0=ALU.mult, op1=ALU.add),
                             [acc.name, "st34", "o_all"], ["o_all"])

                    def diff_post_and_out(kind, b, i, xbuf, pAo, pTt):
                        T = 128 if kind == "p" else TS
                        P.op("act", lambda e: e.activation(out=sq[:T, 0:512], in_=o_all[:T, :, :].rearrange("p h d -> p (h d)"), func=AF.Square), ["o_all"], ["sq"])
                        P.op("dve", lambda e: e.tensor_reduce(out=st[:T, 36:44], in_=sq[:T, 0:512].rearrange("p (g d) -> p g d", d=64), axis=AX.X, op=ALU.add), ["sq"], ["st36"])
                        rstd_from_ss(st[:T, 36:44], 64, T, 8, "st36", None)
                        P.op("dve", lambda e: e.tensor_tensor(out=o_all[:T, :, :], in0=o_all[:T, :, :], in1=st[:T, 36:44].unsqueeze(2).to_broadcast([T, 8, 64]), op=ALU.mult), ["o_all", "st36"], ["o_all"])
                        P.op("dve", lambda e: e.tensor_tensor(out=cat[:T, 0:512].rearrange("p (h d) -> p h d", d=64), in0=o_all[:T, :, :], in1=dn_bc[:T, :].unsqueeze(1).to_broadcast([T, 8, 64]), op=ALU.mult),
                             ["o_all", "dn_bc"], ["cat_d"])
                        for k_ in range(8):
                            P.op("pe", lambda e, k_=k_: e.transpose(out=pTt[:, k_, :T], in_=cat[:T, k_ * 128:(k_ + 1) * 128], identity=ident_b[:T, :T]), ["cat_d", "cat_g", "ident_b"], ["pT"])
                        P.op("dve", lambda e: e.tensor_copy(out=catT[:, :, :T], in_=pTt[:, :, :T]), ["pT"], ["catT"])
                        hb = h1[cnt["h1"] % 2]; cnt["h1"] += 1
                        for half in range(2):
                            po = pAo[half]
                            for c in range(8):
                                P.op("pe", lambda e, c=c, half=half, po=po: e.matmul(po[0][:T, po[1]:po[1] + 512], lhsT=catT[:, c, :T], rhs=w_o_b[:, c, half * 512:(half + 1) * 512], start=(c == 0), stop=(c == 7)),
                                     ["catT", "w_o_b"], [po[2]])
                            P.op("dve", lambda e, half=half, po=po: e.tensor_tensor(out=hb[:T, half * 512:(half + 1) * 512], in0=po[0][:T, po[1]:po[1] + 512], in1=xbuf[:T, half * 512:(half + 1) * 512], op=ALU.add),
                                 [po[2], xbuf.name], [hb.name])
                        ti = tile_of(kind, b, i)
                        P.dma("sp", lambda e: e.dma_start(out=h1s[ti, :T, :], in_=hb[:T, :]), [hb.name], [])

                    def prompt_attention(i):
                        steps = [(h, j) for h in range(8) for j in range(i + 1)]

                        def issue_S(n):
                            h, j = steps[n]
                            sl = cnt["s"] % 2; cnt["s"] += 1
                            P.op("pe", lambda e, sl=sl, j=j, h=h: e.matmul(pSb[sl][:, 0:256], lhsT=KT[:, h, j * 128:(j + 1) * 128], rhs=QTb[:, h, :, :], start=True, stop=True),
                                 ["KT%d" % j, "QTb"], ["pS%d" % sl])
                            return sl

                        sls = {0: issue_S(0)}
                        for n, (h, j) in enumerate(steps):
                            if n + 1 < len(steps):
                                sls[n + 1] = issue_S(n + 1)
                            sl = sls.pop(n)
                            acc = pAcc[h % 2]
                            ptb = PT[cnt["pt"] % 3]; cnt["pt"] += 1
                            P.op("act", lambda e, sl=sl, ptb=ptb: e.activation(out=ptb[:, :, :].rearrange("p c q -> p (c q)"), in_=pSb[sl][:, 0:256], func=AF.Exp), ["pS%d" % sl], [ptb.name])
                            if j == i:
                                P.op("pool", lambda e, ptb=ptb: e.tensor_tensor(out=ptb[:, :, :], in0=ptb[:, :, :], in1=triu_b[:, :].unsqueeze(1).to_broadcast([128, 2, 128]), op=ALU.mult),
                                     [ptb.name, "triu_b"], [ptb.name])
                            for c in range(2):
                                P.op("pe", lambda e, c=c, j=j, h=h, acc=acc, ptb=ptb: e.matmul(acc[:, c, :], lhsT=ptb[:, c, :], rhs=Vaug[:, j, h, :], start=(j == 0 and c == 0), stop=(j == i),
                                                                                        skip_group_check=True), [ptb.name, "Vaug%d" % j], [acc.name])
                            if j == i:
                                attn_normalize(128, acc, h)
                                yield

                    pAo = [(pA, 0, "pA0"), (pA, 512, "pA1")]
                    for b in range(NPB if DO_P else 0):
                        P.op("pool", lambda e: e.memset(Sg[:], 0.0), ["Sg"], ["Sg"])
                        nxt = load_x("p", b, 0)
                        for i in range(NT_RUN):
                            xbuf = nxt
                            if i + 1 < NT_RUN:
                                nxt = load_x("p", b, i + 1)
                            gm = mixer_front("p", b, i, xbuf)
                            try:
                                next(gm)
                            except StopIteration:
                                gm = None
                            live = [g_ for g_ in (gm, prompt_attention(i) if STOP > 10 else None) if g_ is not None]
                            while live:
                                for g_ in list(live):
                                    try:
                                        next(g_)
                                    except StopIteration:
                                        live.remove(g_)
                            if STOP > 11:
                                diff_post_and_out("p", b, i, xbuf, pAo, pT)
                        P.dma("sp", lambda e, b=b: e.dma_start(out=ostp[b].rearrange("(m hh) d v -> (hh d) m v", hh=2), in_=Sg[:]), ["Sg"], [])
                    if DO_S:
                        P.dma("sp", lambda e: e.dma_start(out=S0s[:], in_=sgla.rearrange("s (m hh) d v -> (hh d) s m v", hh=2)), [], ["S0s"])
                        xsb = load_x("s", 0, 0)
                        for _ in mixer_front("s", 0, 0, xsb):
                            pass
                        P.dma("sp", lambda e: e.dma_start(out=osts.rearrange("s (m hh) d v -> (hh d) s m v", hh=2), in_=Sns[:]), ["Sns"], [])
                    P.barrier()
                    P.emit()

                with ExitStack() as SBk:
                  if DO_S:
                    sbk = lambda n, s, d=F32: SBk.enter_context(nc.sbuf_tensor(n, list(s), d))
                    pKT = SBk.enter_context(nc.psum_tensor("pKT", [128, 4, 128], F32))
                    pSsF = SBk.enter_context(nc.psum_tensor("pSs", [128, 512], F32))
                    pSs = pSsF[:, 0:128].rearrange("p (g q) -> p g q", q=8)
                    accSF = [SBk.enter_context(nc.psum_tensor("accS%d" % k_, [128, 512], F32)) for k_ in range(4)]
                    accS = [t_[0:32, 0:260].rearrange("p (a b) -> p a b", b=65) for t_ in accSF]
                    pT2 = SBk.enter_context(nc.psum_tensor("pT2", [128, 8, 128], BF16))
                    pA2 = SBk.enter_context(nc.psum_tensor("pA2", [128, 512], F32))
                    NB = 4
                    kpg = [sbk("kpg%d" % k_, [128, 512]) for k_ in range(NB)]
                    vpg = [sbk("vpg%d" % k_, [128, 512]) for k_ in range(NB)]
                    kTp = [sbk("kTp%d" % k_, [128, 4, 128], BF16) for k_ in range(3)]
                    vpa = [sbk("vpa%d" % k_, [128, 8, 65], BF16) for k_ in range(3)]
                    PTp = [[sbk("PTp%d_%d" % (s_, k_), [128, 16, 32], BF16) for k_ in range(2)] for s_ in range(NSB)]
                    Qb2 = sbk("Qb2", [128, 4, NSB, 4, 8], BF16)
                    ptf = sbk("ptf", [128, NSB * NPAGES])
                    pti = sbk("pti", [128, NSB * NPAGES], I32)
                    pidx = sbk("pidx", [128, 1])
                    zz = sbk("zz", [1, 512], BF16)
                    for s_ in range(NSB):
                        for k_ in range(2):
                            P.op("pool", lambda e, t_=PTp[s_][k_]: e.memset(t_[:], 0.0), [], [PTp[s_][k_].name])
                    for k_ in range(3):
                        P.op("pool", lambda e, k_=k_: e.memset(vpa[k_][:], 1.0), [], [vpa[k_].name])
                    P.op("pool", lambda e: e.memset(Qb2[:], 0.0), [], ["Qb2"])
                    P.op("pool", lambda e: e.memset(zz[:], 0.0), [], ["zz"])
                    for pr in range(4):
                        for hh in range(2):
                            h = 2 * pr + hh
                            for c in range(2):
                                rows_src = slice(c * 32, (c + 1) * 32)
                                P.dma("sp", lambda e, pr=pr, hh=hh, h=h, c=c, rows_src=rows_src: e.dma_start(
                                    out=Qb2[hh * 64 + c * 32:hh * 64 + (c + 1) * 32, pr, :, 2 * hh + c, :],
                                    in_=QTb[rows_src, h, c, 0:TS].rearrange("p (s q) -> p s q", q=8)), ["QTb"], ["Qb2"])
                    P.dma("sp", lambda e: e.dma_start(out=pti[:], in_=pt.rearrange("s n -> (s n)").partition_broadcast(128)), [], ["pti"])
                    P.op("dve", lambda e: e.tensor_copy(out=ptf[:], in_=pti[:]), ["pti"], ["ptf"])
                    P.dma("sp", lambda e: e.dma_start(out=pidx[:], in_=c_pidx), [], ["pidx"])
                    P.op("dve", lambda e: e.tensor_scalar(out=ptf[:], in0=ptf[:], scalar1=128.0, scalar2=pidx[:, 0:1], op0=ALU.mult, op1=ALU.add), ["ptf", "pidx"], ["ptf"])
                    P.op("dve", lambda e: e.tensor_copy(out=pti[:], in_=ptf[:]), ["ptf"], ["pti"])
                    for k_ in range(4):
                        P.op("pe", lambda e, k_=k_: e.matmul(accSF[k_][:, 0:130], lhsT=zz[0:1, 0:128], rhs=zz[0:1, 0:130], start=True, stop=False, skip_group_check=True),
                             ["zz"], ["accS"])
                    pages = [(s_, n_) for s_ in range(NSB) for n_ in range(NPG_RUN)]

                    def issue_page(k_):
                        s_, n_ = pages[k_]
                        kb, vb = kpg[k_ % NB], vpg[k_ % NB]
                        col = s_ * NPAGES + n_
                        P.dma_sw(lambda e: e.indirect_dma_start(out=kb[:], out_offset=None, in_=ck, in_offset=bass.IndirectOffsetOnAxis(ap=pti[:, col:col + 1], axis=0)), ["pti"], [kb.name])
                        P.dma_sw(lambda e: e.indirect_dma_start(out=vb[:], out_offset=None, in_=cv, in_offset=bass.IndirectOffsetOnAxis(ap=pti[:, col:col + 1], axis=0)), ["pti"], [vb.name])

                    for k_ in range(min(NB - 1, len(pages))):
                        issue_page(k_)
                    for k_, (s_, n_) in enumerate(pages):
                        if k_ + NB - 1 < len(pages):
                            issue_page(k_ + NB - 1)
                        kb, vb = kpg[k_ % NB], vpg[k_ % NB]
                        ktb, vab, ptp = kTp[k_ % 3], vpa[k_ % 3], PTp[s_][k_ % 2]
                        for pr in range(4):
                            P.op("pe", lambda e, pr=pr, kb=kb: e.transpose(out=pKT[:, pr, :], in_=kb[:, pr * 128:(pr + 1) * 128], identity=ident[:, :]), [kb.name, "ident"], ["pKT"])
                        P.op("dve", lambda e, ktb=ktb: e.tensor_copy(out=ktb[:], in_=pKT[:]), ["pKT"], [ktb.name])
                        P.op("act", lambda e, vab=vab, vb=vb: e.copy(out=vab[:, :, 0:64], in_=vb[:, :].rearrange("p (h d) -> p h d", d=64)), [vb.name], [vab.name])
                        for pr in range(4):
                            P.op("pe", lambda e, pr=pr, ktb=ktb, s_=s_: e.matmul(pSs[:, 4 * pr:4 * pr + 4, :], lhsT=ktb[:, pr, :], rhs=Qb2[:, pr, s_, :, :], start=True, stop=True),
                                 [ktb.name, "Qb2"], ["pSs"])
                        P.op("act", lambda e, ptp=ptp, s_=s_: e.activation(out=ptp[:, :, 8 * s_:8 * s_ + 8], in_=pSs[:, :, :], func=AF.Exp), ["pSs"], [ptp.name])
                        for pr in range(4):
                            P.op("pe", lambda e, pr=pr, ptp=ptp, vab=vab: e.matmul(accSF[pr][:, 0:130], lhsT=ptp[:, 4 * pr:4 * pr + 4, :].rearrange("p a b -> p (a b)"),
                                                                                   rhs=vab[:, 2 * pr:2 * pr + 2, :].rearrange("p a b -> p (a b)"), start=False, stop=False, skip_group_check=True),
                                 [ptp.name, vab.name], ["accS"])
                    PTn = sbk("PTn", [32, 8, 2, 32], BF16)
                    for h in range(8):
                        P.op("pe", lambda e, h=h: e.matmul(pA2[0:32, h * 64:(h + 1) * 64], lhsT=KTn[:, h, :], rhs=QTb[:, h, :, 0:TS], start=True, stop=True), ["KTn", "QTb"], ["pA2"])
                    P.op("act", lambda e: e.activation(out=PTn[:, :, :, :].rearrange("p h c q -> p (h c q)"), in_=pA2[0:32, 0:512], func=AF.Exp), ["pA2"], ["PTn"])
                    P.op("dve", lambda e: e.tensor_tensor(out=PTn[:, :, :, :].rearrange("p h c q -> p (h c) q"), in0=PTn[:, :, :, :].rearrange("p h c q -> p (h c) q"),
                                                          in1=m32_b[:, :].unsqueeze(1).to_broadcast([32, 16, 32]), op=ALU.mult), ["PTn", "m32_b"], ["PTn"])
                    for pr in range(4):
                        P.op("pe", lambda e, pr=pr: e.matmul(accSF[pr][:, 0:130], lhsT=PTn[:, 2 * pr:2 * pr + 2, :, :].rearrange("p h c q -> p (h c q)"),
                                                             rhs=Vnew[:, 2 * pr:2 * pr + 2, :].rearrange("p a b -> p (a b)"), start=False, stop=True, skip_group_check=True),
                             ["PTn", "Vnew"], ["accS"])
                    accsb = [sbk("accsb%d" % k_, [128, 130]) for k_ in range(4)]
                    for pr in range(4):
                        P.op("dve", lambda e, pr=pr: e.tensor_copy(out=accsb[pr][:], in_=accSF[pr][:, 0:130]), ["accS"], [accsb[pr].name])
                    for pr in range(4):
                        for k_ in range(4):
                            hh_ = k_ // 2
                            P.op("pe", lambda e, pr=pr, k_=k_, hh_=hh_: e.matmul(accSF[pr][0:32, k_ * 65:(k_ + 1) * 65], lhsT=ident[:, 32 * k_:32 * k_ + 32], rhs=accsb[pr][:, hh_ * 65:(hh_ + 1) * 65], start=True, stop=True,
                                                                             skip_group_check=True), ["ident", accsb[pr].name], ["accS"])
                    for h in range(8):
                        a_ = accS[h // 2]
                        base = 2 * (h % 2)
                        P.op("dve", lambda e, a_=a_, base=base: e.reciprocal(out=st[:TS, 32:34], in_=a_[:, base:base + 2, 64]), ["accS"], ["st32"])
                        P.op("dve", lambda e: e.tensor_tensor(out=st[:TS, 34:35], in0=st[:TS, 33:34], in1=neglam[:TS, :], op=ALU.mult), ["st32", "neglam"], ["st34"])
                        P.op("dve", lambda e, a_=a_, base=base, h=h: e.tensor_scalar(out=o_all[:TS, h, :], in0=a_[:, base, 0:64], scalar1=st[:TS, 32:33], scalar2=None, op0=ALU.mult), ["accS", "st32"], ["o_all"])
                        P.op("dve", lambda e, a_=a_, base=base, h=h: e.scalar_tensor_tensor(out=o_all[:TS, h, :], in0=a_[:, base + 1, 0:64], scalar=st[:TS, 34:35], in1=o_all[:TS, h, :], op0=ALU.mult, op1=ALU.add),
                             ["accS", "st34", "o_all"], ["o_all"])
                    diff_post_and_out("s", 0, 0, xsb, [(pA2, 0, "pA2"), (pA2, 0, "pA2")], pT2)
                    P.barrier()
                    P.emit()

        S2p.close()
        if "3" in PH:
            S3 = ExitStack()
            s3 = lambda n, s, d=F32: S3.enter_context(nc.sbuf_tensor(n, list(s), d))
            cwq_b = s3("cwq_b", [128, 8, D], BF16)
            cwo_b = s3("cwo_b", [128, 8, D], BF16)
            pwq_b = s3("pwq_b", [128, 8, 2 * D], BF16)
            skT_b = s3("skT_b", [128, 16, 128], BF16)
            mkT = s3("mkT", [128, 4, 8, 256], BF16)
            mva = s3("mva", [128, 4, 2, 4, 257], BF16)
            g_cross = s3("g_cross", [128, 8]); g_mem = s3("g_mem", [128, 8]); g_ffn = s3("g_ffn", [128, 8])
            cqn_bc = s3("cqn_bc", [128, 256]); ckn_bc = s3("ckn_bc", [128, 256]); pqn_bc = s3("pqn_bc", [128, 256])
            for (t_, src) in ((g_cross, cross_norm), (g_mem, mem_norm), (g_ffn, ffn_norm)):
                P.dma("sp", lambda e, t_=t_, src=src: e.dma_start(out=t_[:], in_=src.rearrange("(c p) -> p c", p=128), allow_slow_non_contiguous=True), [], [t_.name])
            for (t_, src) in ((cqn_bc, cross_q_norm), (ckn_bc, cross_k_norm), (pqn_bc, peer_q_norm)):
                P.dma("sp", lambda e, t_=t_, src=src: e.dma_start(out=t_[:], in_=src.partition_broadcast(128)), [], [t_.name])
            P.op("dve", lambda e: e.tensor_scalar(out=cqn_bc[:], in0=cqn_bc[:], scalar1=1.0 / 16.0, scalar2=None, op0=ALU.mult), ["cqn_bc"], ["cqn_bc"])
            P.op("pool", lambda e: e.memset(mva[:], 1.0), [], ["mva"])

            with ExitStack() as SP1:
                s1 = lambda n, s, d=F32: SP1.enter_context(nc.sbuf_tensor(n, list(s), d))
                wkv_b = s1("wkv_b", [128, 8, 2 * D], BF16)
                stg = [s1("stg3_%d" % k_, [128, 2 * D]) for k_ in range(2)]
                xm = [s1("xm%d" % k_, [128, D]) for k_ in range(2)]
                xms_b = s1("xms_b", [128, D], BF16)
                xmT = s1("xmT", [128, 8, 128], BF16)
                sq1 = s1("sq1", [128, D])
                tq1 = s1("tq1", [128, D])
                mkf = [s1("mkf%d" % k_, [128, D]) for k_ in range(2)]
                mk_b = s1("mk_b", [128, D], BF16)
                mvf = [s1("mvf%d" % k_, [128, D]) for k_ in range(2)]
                st1 = s1("st1", [128, 16])
                pA = SP1.enter_context(nc.psum_tensor("pA3", [128, 1024], F32))
                pB = SP1.enter_context(nc.psum_tensor("pB3", [128, 1024], F32))
                pT = SP1.enter_context(nc.psum_tensor("pT3", [128, 8, 128], BF16))
                pTf = SP1.enter_context(nc.psum_tensor("pTf3", [128, 512], F32))
                wl = [(cross_w_kv, 2 * D, wkv_b, g_mem), (cross_w_q, D, cwq_b, g_cross), (cross_w_o, D, cwo_b, None), (peer_w_q, 2 * D, pwq_b, g_ffn)]
                k2 = 0
                for (wsrc, wn, wdst, gg) in wl:
                    for c in range(8):
                        s_ = stg[k2 % 2]; k2 += 1
                        P.dma("sp", lambda e, c=c, s_=s_, wsrc=wsrc, wn=wn: e.dma_start(out=s_[:, 0:wn], in_=wsrc[c * 128:(c + 1) * 128, :]), [], [s_.name])
                        if gg is None:
                            act_copy(wdst[:, c, :], s_[:, 0:wn], [s_.name], [wdst.name], eng=("dve" if c % 2 == 0 else "act"))
                        elif c % 2 == 0:
                            P.op("dve", lambda e, c=c, s_=s_, wn=wn, wdst=wdst, gg=gg: e.tensor_scalar(out=wdst[:, c, :], in0=s_[:, 0:wn], scalar1=gg[:, c:c + 1], scalar2=None, op0=ALU.mult),
                                 [s_.name, gg.name], [wdst.name])
                        else:
                            P.op("act", lambda e, c=c, s_=s_, wn=wn, wdst=wdst, gg=gg: e.activation(out=wdst[:, c, :], in_=s_[:, 0:wn], func=AF.Copy, scale=gg[:, c:c + 1]),
                                 [s_.name, gg.name], [wdst.name])
                for g4 in range(4):
                    s_ = stg[k2 % 2]; k2 += 1
                    P.dma("sp", lambda e, g4=g4, s_=s_: e.dma_start(out=s_[:, 0:512].rearrange("p (g d) -> p g d", d=128), in_=peer_sk[4 * g4:4 * g4 + 4].rearrange("g k d -> k g d")), [], [s_.name])
                    for g_ in range(4):
                        P.op("pe", lambda e, g_=g_, s_=s_: e.transpose(out=pTf[:, g_ * 128:(g_ + 1) * 128], in_=s_[:, g_ * 128:(g_ + 1) * 128], identity=ident[:, :]), [s_.name, "ident"], ["pTf"])
                    P.op("dve", lambda e, g4=g4: e.tensor_copy(out=skT_b[:, 4 * g4:4 * g4 + 4, :], in_=pTf[:, :].rearrange("p (g k) -> p g k", k=128)), ["pTf"], ["skT_b"])

                def mem_set(kind, b):
                    slot = b
                    for mt in range(2):
                        rows = slice(mt * 128, (mt + 1) * 128)
                        mkb = mkf[mt]; mvb = mvf[mt]
                        if kind == "p":
                            xb = xm[mt]
                            P.dma("sp", lambda e, xb=xb, rows=rows: e.dma_start(out=xb[:], in_=memp[b, rows, :]), [], [xb.name])
                            P.op("act", lambda e, xb=xb: e.activation(out=sq1[:], in_=xb[:], func=AF.Square, accum_out=st1[:, 0:1]), [xb.name], ["sq1", "st1a"])
                            rstd_from_ss(st1[:, 0:1], D, 128, 1, "st1a", None)
                            P.op("act", lambda e, xb=xb: e.activation(out=xms_b[:], in_=xb[:], func=AF.Copy, scale=st1[:, 0:1]), [xb.name, "st1a"], ["xms_b"])
                            for k_ in range(8):
                                P.op("pe", lambda e, k_=k_: e.transpose(out=pT[:, k_, :], in_=xms_b[:, k_ * 128:(k_ + 1) * 128], identity=ident_b[:, :]), ["xms_b", "ident_b"], ["pT3"])
                            P.op("dve", lambda e: e.tensor_copy(out=xmT[:], in_=pT[:]), ["pT3"], ["xmT"])
                            for grp in range(4):
                                po = pA if grp < 2 else pB
                                pon = "pA3" if grp < 2 else "pB3"
                                for c in range(8):
                                    P.op("pe", lambda e, c=c, grp=grp, po=po: e.matmul(po[:, (grp % 2) * 512:(grp % 2 + 1) * 512], lhsT=xmT[:, c, :], rhs=wkv_b[:, c, grp * 512:(grp + 1) * 512], start=(c == 0), stop=(c == 7)),
                                         ["xmT", "wkv_b"], [pon])
                            P.op("act", lambda e: e.activation(out=sq1[:], in_=pA[:], func=AF.Square), ["pA3"], ["sq1"])
                            P.op("dve", lambda e: e.tensor_reduce(out=st1[:, 4:8], in_=sq1[:].rearrange("p (g d) -> p g d", d=256), axis=AX.X, op=ALU.add), ["sq1"], ["st1b"])
                            rstd_from_ss(st1[:, 4:8], 256, 128, 4, "st1b", None)
                            P.op("dve", lambda e: e.tensor_tensor(out=tq1[:].rearrange("p (g d) -> p g d", d=256), in0=pA[:].rearrange("p (g d) -> p g d", d=256),
                                                                  in1=st1[:, 4:8].unsqueeze(2).to_broadcast([128, 4, 256]), op=ALU.mult), ["pA3", "st1b"], ["tq1"])
                            P.op("dve", lambda e, mkb=mkb: e.tensor_tensor(out=mkb[:].rearrange("p (g d) -> p g d", d=256), in0=tq1[:].rearrange("p (g d) -> p g d", d=256),
                                                                           in1=ckn_bc[:, :].unsqueeze(1).to_broadcast([128, 4, 256]), op=ALU.mult), ["tq1", "ckn_bc"], [mkb.name])
                            P.dma("sp", lambda e, mkb=mkb, rows=rows: e.dma_start(out=omk[b, rows, :], in_=mkb[:]), [mkb.name], [])
                            P.op("act", lambda e, mvb=mvb: e.copy(out=mvb[:], in_=pB[:]), ["pB3"], [mvb.name])
                            P.dma("sp", lambda e, mvb=mvb, rows=rows: e.dma_start(out=omv[b, rows, :], in_=mvb[:]), [mvb.name], [])
                        else:
                            P.dma("sp", lambda e, mkb=mkb, rows=rows: e.dma_start(out=mkb[:], in_=cmk[b, rows, :]), [], [mkb.name])
                            P.dma("sp", lambda e, mvb=mvb, rows=rows: e.dma_start(out=mvb[:], in_=cmv[b, rows, :]), [], [mvb.name])
                        P.op("act", lambda e, mkb=mkb: e.copy(out=mk_b[:], in_=mkb[:]), [mkb.name], ["mk_b"])
                        for k_ in range(8):
                            P.op("pe", lambda e, k_=k_: e.transpose(out=pT[:, k_, :], in_=mk_b[:, k_ * 128:(k_ + 1) * 128], identity=ident_b[:, :]), ["mk_b", "ident_b"], ["pT3"])
                        P.op("dve", lambda e, rows=rows: e.tensor_copy(out=mkT[:, slot, :, rows], in_=pT[:]), ["pT3"], ["mkT%d" % slot])
                        P.op("pool", lambda e, mvb=mvb, mt=mt: e.tensor_copy(out=mva[:, slot, mt, :, 0:256], in_=mvb[:].rearrange("p (h d) -> p h d", d=256)), [mvb.name, "mva"], ["mva%d" % slot])

                for b in range(NPB):
                    mem_set("p", b)
                P.barrier()
                P.emit()

            with ExitStack() as SW3:
                sw = lambda n, s, d=F32: SW3.enter_context(nc.sbuf_tensor(n, list(s), d))
                hx = [sw("hx%d" % k_, [128, D]) for k_ in range(2)]
                hs_b = sw("hs_b", [128, D], BF16)
                hnT = sw("hnT", [128, 8, 128], BF16)
                sq = sw("sq3", [128, D])
                tq = sw("tq3", [128, D])
                qn_b = sw("qn3_b", [128, D], BF16)
                qnT = sw("qnT", [128, 8, 128], BF16)
                PTc = [sw("PTc%d" % k_, [128, 2, 128], BF16) for k_ in range(2)]
                PTz = [sw("PTz%d" % k_, [128, 2, 32], BF16) for k_ in range(NSB)]
                co = sw("co", [128, D], BF16)
                coT = sw("coT", [128, 8, 128], BF16)
                h2b = [sw("h2b%d" % k_, [128, D]) for k_ in range(2)]
                h2s_b = sw("h2s_b", [128, D], BF16)
                hn2T = [sw("hn2T%d" % k_, [128, 8, 128], BF16) for k_ in range(2)]
                pqn_b = sw("pqn_b", [128, 2 * D], BF16)
                pqT = sw("pqT", [128, 16, 128], BF16)
                scs = [sw("sc%d" % k_, [128, 16, 128]) for k_ in range(2)]
                m16 = sw("m16", [128, 16, 16]); ix = sw("ix", [128, 16, 16], U32); ixf = sw("ixf", [128, 16, 16])
                wk = sw("wk", [128, 128]); wk2 = sw("wk2", [128, 256])
                cand = sw("cand", [128, 8, 256])
                cm = sw("cm", [128, 8, 16]); cj = sw("cj", [128, 8, 16], U32)
                ca = sw("ca", [128, 8, 16], U32); cb = sw("cb", [128, 8, 16], U32)
                caf = sw("caf", [128, 8, 16]); cbf = sw("cbf", [128, 8, 16])
                ce = sw("ce", [128, 8, 16])
                oh = sw("oh", [128, 8, 256])
                rtk = sw("rtk", [128, 3, 128])
                rt = [sw("rt%d" % k_, [128, 3, 128]) for k_ in range(2)]
                st = sw("st3", [128, 64])
                mkf2 = [sq, tq]
                mk_b2 = sw("mk_b2", [128, D], BF16)
                for k_ in range(NSB):
                    P.op("pool", lambda e, k_=k_: e.memset(PTz[k_][:], 0.0), [], [PTz[k_].name])

                with ExitStack() as SA3:
                    pA = SA3.enter_context(nc.psum_tensor("pA4", [128, 1024], F32))
                    pB = SA3.enter_context(nc.psum_tensor("pB4", [128, 1024], F32))
                    pT = SA3.enter_context(nc.psum_tensor("pT4", [128, 8, 128], BF16))
                    pS = SA3.enter_context(nc.psum_tensor("pS4", [128, 2, 256], F32))
                    pO = [SA3.enter_context(nc.psum_tensor("pO4_%d" % k_, [128, 512], F32)) for k_ in range(2)]
                    c3 = {"hx": 0, "h2": 0, "hn": 0, "pt": 0, "rt": 0}

                    def load_h1(ti, T):
                        buf = hx[c3["hx"] % 2]; c3["hx"] += 1
                        P.dma("sp", lambda e: e.dma_start(out=buf[:T, :], in_=h1s[ti, :T, :]), [], [buf.name])
                        return buf

                    def norm_T(src, srcn, T, dst_b, dstT, dstn):
                        P.op("act", lambda e: e.activation(out=sq[:T, :], in_=src[:T, :], func=AF.Square, accum_out=st[:T, 0:1]), [srcn], ["sq3", "st0"])
                        rstd_from_ss(st[:T, 0:1], D, T, 1, "st0", None)
                        P.op("act", lambda e: e.activation(out=dst_b[:T, :], in_=src[:T, :], func=AF.Copy, scale=st[:T, 0:1]), [srcn, "st0"], [dst_b.name])
                        tr8(dst_b, T, dstT, dstn, 0)

                    def tr8(src_b, T, dstT, dstn, k0):
                        for k_ in range(8):
                            P.op("pe", lambda e, k_=k_: e.transpose(out=pT[:, k_, :T], in_=src_b[:T, (k0 + k_) * 128:(k0 + k_ + 1) * 128], identity=ident_b[:T, :T]), [src_b.name, "ident_b"], ["pT4"])
                        P.op("dve", lambda e: e.tensor_copy(out=dstT[:, k0:k0 + 8, :T], in_=pT[:, :, :T]), ["pT4"], [dstn])

                    def headnorm(T, src, srcn, nh, gain, out_ap, outn, stoff):
                        w = nh * 256
                        P.op("act", lambda e: e.activation(out=sq[:T, 0:w], in_=src, func=AF.Square), [srcn], ["sq3"])
                        P.op("dve", lambda e: e.tensor_reduce(out=st[:T, stoff:stoff + nh], in_=sq[:T, 0:w].rearrange("p (g d) -> p g d", d=256), axis=AX.X, op=ALU.add), ["sq3"], ["st%d" % stoff])
                        rstd_from_ss(st[:T, stoff:stoff + nh], 256, T, nh, "st%d" % stoff, None)
                        P.op("dve", lambda e: e.tensor_tensor(out=tq[:T, 0:w].rearrange("p (g d) -> p g d", d=256), in0=src.rearrange("p (g d) -> p g d", d=256),
                                                              in1=st[:T, stoff:stoff + nh].unsqueeze(2).to_broadcast([T, nh, 256]), op=ALU.mult), [srcn, "st%d" % stoff], ["tq3"])
                        P.op("dve", lambda e: e.tensor_tensor(out=out_ap.rearrange("p (g d) -> p g d", d=256), in0=tq[:T, 0:w].rearrange("p (g d) -> p g d", d=256),
                                                              in1=gain[:T, :].unsqueeze(1).to_broadcast([T, nh, 256]), op=ALU.mult), ["tq3", gain.name], [outn])

                    def tile3A(kind, ti, T, sets, hxb, sc):
                        hn_ = hxb.name
                        norm_T(hxb, hn_, T, hs_b, hnT, "hnT")
                        yield
                        for half in range(2):
                            for c in range(8):
                                P.op("pe", lambda e, c=c, half=half: e.matmul(pA[:T, half * 512:(half + 1) * 512], lhsT=hnT[:, c, :T], rhs=cwq_b[:, c, half * 512:(half + 1) * 512], start=(c == 0), stop=(c == 7)),
                                     ["hnT", "cwq_b"], ["pA4"])
                        headnorm(T, pA[:T, :], "pA4", 4, cqn_bc, qn_b[:T, :], "qn3_b", 4)
                        tr8(qn_b, T, qnT, "qnT", 0)
                        yield
                        for h in range(4):
                            po = pO[h % 2]
                            pon = "pO4_%d" % (h % 2)
                            nmm = 2 * len(sets)
                            kmm = 0
                            for (slot, c0, c1) in sets:
                                for mb in range(2):
                                    for k2_ in range(2):
                                        P.op("pe", lambda e, mb=mb, k2_=k2_, slot=slot, h=h: e.matmul(pS[:, mb, :T], lhsT=mkT[:, slot, 2 * h + k2_, mb * 128:(mb + 1) * 128], rhs=qnT[:, 2 * h + k2_, :T],
                                                                                                   start=(k2_ == 0), stop=(k2_ == 1)), ["mkT%d" % slot, "qnT"], ["pS4"])
                                if kind == "p":
                                    ptb = PTc[c3["pt"] % 2]; c3["pt"] += 1
                                    P.op("act", lambda e, ptb=ptb: e.activation(out=ptb[:, :, :T], in_=pS[:, :, :T], func=AF.Exp), ["pS4"], [ptb.name])
                                else:
                                    ptb = PTz[slot]
                                    P.op("act", lambda e, ptb=ptb, c0=c0, c1=c1: e.activation(out=ptb[:, :, c0:c1], in_=pS[:, :, c0:c1], func=AF.Exp), ["pS4"], [ptb.name])
                                for mb in range(2):
                                    P.op("pe", lambda e, mb=mb, slot=slot, h=h, ptb=ptb, po=po, kmm=kmm, nmm=nmm: e.matmul(po[:T, 0:257], lhsT=ptb[:, mb, :T], rhs=mva[:, slot, mb, h, :],
                                                                                                                       start=(kmm == 0), stop=(kmm == nmm - 1)), [ptb.name, "mva%d" % slot], [pon])
                                    kmm += 1
                            P.op("dve", lambda e, po=po: e.reciprocal(out=st[:T, 12:13], in_=po[:T, 256:257]), [pon], ["st12"])
                            P.op("dve", lambda e, po=po, h=h: e.tensor_scalar(out=co[:T, h * 256:(h + 1) * 256], in0=po[:T, 0:256], scalar1=st[:T, 12:13], scalar2=None, op0=ALU.mult), [pon, "st12"], ["co"])
                            yield
                        tr8(co, T, coT, "coT", 0)
                        h2 = h2b[c3["h2"] % 2]; c3["h2"] += 1
                        for half in range(2):
                            for c in range(8):
                                P.op("pe", lambda e, c=c, half=half: e.matmul(pA[:T, half * 512:(half + 1) * 512], lhsT=coT[:, c, :T], rhs=cwo_b[:, c, half * 512:(half + 1) * 512], start=(c == 0), stop=(c == 7)),
                                     ["coT", "cwo_b"], ["pA4"])
                        P.op("dve", lambda e: e.tensor_tensor(out=h2[:T, :], in0=pA[:T, :], in1=hxb[:T, :], op=ALU.add), ["pA4", hn_], [h2.name])
                        P.dma("sp", lambda e: e.dma_start(out=h2s[ti, :T, :], in_=h2[:T, :]), [h2.name], [])
                        yield
                        h2T = hn2T[c3["hn"] % 2]; c3["hn"] += 1
                        norm_T(h2, h2.name, T, h2s_b, h2T, h2T.name)
                        P.dma("sp", lambda e: e.dma_start(out=hnTs[ti].rearrange("p (c t) -> p c t", t=128)[:, :, :T], in_=h2T[:, :, :T]), [h2T.name], [])
                        yield
                        for grp in range(4):
                            po2 = pA if grp < 2 else pB
                            pon2 = "pA4" if grp < 2 else "pB4"
                            for c in range(8):
                                P.op("pe", lambda e, c=c, grp=grp, po2=po2: e.matmul(po2[:T, (grp % 2) * 512:(grp % 2 + 1) * 512], lhsT=h2T[:, c, :T], rhs=pwq_b[:, c, grp * 512:(grp + 1) * 512], start=(c == 0), stop=(c == 7)),
                                     [h2T.name, "pwq_b"], [pon2])
                        yield
                        headnorm(T, pA[:T, :], "pA4", 4, pqn_bc, pqn_b[:T, 0:D], "pqn_b", 16)
                        yield
                        headnorm(T, pB[:T, :], "pB4", 4, pqn_bc, pqn_b[:T, D:2 * D], "pqn_b", 20)
                        yield
                        tr8(pqn_b, T, pqT, "pqT", 0)
                        tr8(pqn_b, T, pqT, "pqT", 8)
                        for g_ in range(16):
                            po2 = pA if g_ < 8 else pB
                            pon2 = "pA4" if g_ < 8 else "pB4"
                            P.op("pe", lambda e, g_=g_, po2=po2: e.matmul(po2[:T, (g_ % 8) * 128:(g_ % 8 + 1) * 128], lhsT=pqT[:, g_, :T], rhs=skT_b[:, g_, :], start=True, stop=True), ["pqT", "skT_b"], [pon2])
                        P.op("act", lambda e: e.copy(out=sc[:T, 0:8, :], in_=pA[:T, :].rearrange("p (g k) -> p g k", k=128)), ["pA4"], [sc.name])
                        P.op("act", lambda e: e.copy(out=sc[:T, 8:16, :], in_=pB[:T, :].rearrange("p (g k) -> p g k", k=128)), ["pB4"], [sc.name])
                        yield

                    def tile3B(ti, T, sc):
                        for g_ in range(16):
                            if g_ % 2 == 0 and g_ > 0:
                                yield
                            P.op("dve", lambda e, g_=g_: e.max(out=m16[:T, g_, 0:8], in_=sc[:T, g_, :]), [sc.name], ["m16"])
                            P.op("dve", lambda e, g_=g_: e.max_index(out=ix[:T, g_, 0:8], in_max=m16[:T, g_, 0:8], in_values=sc[:T, g_, :]), [sc.name, "m16"], ["ix"])
                            P.op("dve", lambda e, g_=g_: e.match_replace(out=wk[:T, :], in_to_replace=m16[:T, g_, 0:8], in_values=sc[:T, g_, :], imm_value=-1e30), [sc.name, "m16"], ["wk"])
                            P.op("dve", lambda e, g_=g_: e.max(out=m16[:T, g_, 8:16], in_=wk[:T, :]), ["wk"], ["m16"])
                            P.op("dve", lambda e, g_=g_: e.max_index(out=ix[:T, g_, 8:16], in_max=m16[:T, g_, 8:16], in_values=wk[:T, :]), ["wk", "m16"], ["ix"])
                        yield
                        m16v = m16[:T, :, :].rearrange("p (h c) k -> p h c k", c=2)
                        P.op("dve", lambda e: e.tensor_tensor(out=cand[:T, :, :].rearrange("p h (a b) -> p h a b", b=16), in0=m16v[:, :, 0, :].unsqueeze(3).to_broadcast([T, 8, 16, 16]),
                                                              in1=m16v[:, :, 1, :].unsqueeze(2).to_broadcast([T, 8, 16, 16]), op=ALU.add), ["m16"], ["cand"])
                        for h in range(8):
                            if h % 2 == 0:
                                yield
                            P.op("dve", lambda e, h=h: e.max(out=cm[:T, h, 0:8], in_=cand[:T, h, :]), ["cand"], ["cm"])
                            P.op("dve", lambda e, h=h: e.max_index(out=cj[:T, h, 0:8], in_max=cm[:T, h, 0:8], in_values=cand[:T, h, :]), ["cand", "cm"], ["cj"])
                            P.op("dve", lambda e, h=h: e.match_replace(out=wk2[:T, :], in_to_replace=cm[:T, h, 0:8], in_values=cand[:T, h, :], imm_value=-1e30), ["cand", "cm"], ["wk2"])
                            P.op("dve", lambda e, h=h: e.max(out=cm[:T, h, 8:16], in_=wk2[:T, :]), ["wk2"], ["cm"])
                            P.op("dve", lambda e, h=h: e.max_index(out=cj[:T, h, 8:16], in_max=cm[:T, h, 8:16], in_values=wk2[:T, :]), ["wk2", "cm"], ["cj"])
                        yield
                        P.op("dve", lambda e: e.tensor_tensor(out=ce[:T, :, :], in0=cm[:T, :, :], in1=cm[:T, :, 0:1].to_broadcast([T, 8, 16]), op=ALU.subtract), ["cm"], ["ce"])
                        P.op("act", lambda e: e.activation(out=ce[:T, :, :], in_=ce[:T, :, :], func=AF.Exp), ["ce"], ["ce"])
                        P.op("dve", lambda e: e.tensor_reduce(out=st[:T, 24:32], in_=ce[:T, :, :], axis=AX.X, op=ALU.add), ["ce"], ["st24"])
                        P.op("dve", lambda e: e.reciprocal(out=st[:T, 24:32], in_=st[:T, 24:32]), ["st24"], ["st24"])
                        P.op("dve", lambda e: e.tensor_tensor(out=rtk[:T, 2, :].rearrange("p (h k) -> p h k", k=16), in0=ce[:T, :, :], in1=st[:T, 24:32].unsqueeze(2).to_broadcast([T, 8, 16]), op=ALU.mult),
                             ["ce", "st24"], ["rtk"])
                        yield
                        P.op("dve", lambda e: e.tensor_single_scalar(out=ca[:T, :, :], in_=cj[:T, :, :], scalar=4, op=ALU.logical_shift_right), ["cj"], ["ca"])
                        P.op("dve", lambda e: e.tensor_single_scalar(out=cb[:T, :, :], in_=cj[:T, :, :], scalar=15, op=ALU.bitwise_and), ["cj"], ["cb"])
                        P.op("dve", lambda e: e.tensor_copy(out=caf[:T, :, :], in_=ca[:T, :, :]), ["ca"], ["caf"])
                        P.op("dve", lambda e: e.tensor_copy(out=cbf[:T, :, :], in_=cb[:T, :, :]), ["cb"], ["cbf"])
                        P.op("dve", lambda e: e.tensor_copy(out=ixf[:T, :, :], in_=ix[:T, :, :]), ["ix"], ["ixf"])
                        ixv = ixf[:T, :, :].rearrange("p (h c) k -> p h c k", c=2)
                        io16 = iota_t[:T, 0:16].unsqueeze(1).unsqueeze(1).to_broadcast([T, 8, 16, 16])
                        for (sel, cxf, dsti) in ((0, caf, 0), (1, cbf, 1)):
                            yield
                            P.op("dve", lambda e, cxf=cxf: e.tensor_tensor(out=oh[:T, :, :].rearrange("p h (k a) -> p h k a", a=16), in0=cxf[:T, :, :].unsqueeze(3).to_broadcast([T, 8, 16, 16]), in1=io16, op=ALU.is_equal),
                                 [cxf.name, "iota_t"], ["oh"])
                            P.op("dve", lambda e, sel=sel: e.tensor_tensor(out=oh[:T, :, :].rearrange("p h (k a) -> p h k a", a=16), in0=oh[:T, :, :].rearrange("p h (k a) -> p h k a", a=16),
                                                                           in1=ixv[:, :, sel, :].unsqueeze(2).to_broadcast([T, 8, 16, 16]), op=ALU.mult), ["oh", "ixf"], ["oh"])
                            P.op("dve", lambda e, dsti=dsti: e.tensor_reduce(out=rtk[:T, dsti, :], in_=oh[:T, :, :].rearrange("p h (k a) -> p (h k) a", a=16), axis=AX.X, op=ALU.add), ["oh"], ["rtk"])
                        yield
                        rtb = rt[c3["rt"] % 2]; c3["rt"] += 1
                        pR = pS
                        for k_ in range(3):
                            P.op("pe", lambda e, k_=k_: e.transpose(out=pR[:, :, :].rearrange("p a b -> p (a b)")[:, k_ * 128:k_ * 128 + T], in_=rtk[:T, k_, :], identity=ident[:T, :T]), ["rtk", "ident"], ["pS4"])
                        P.op("dve", lambda e, rtb=rtb: e.tensor_copy(out=rtb[:, :, :T], in_=pR[:, :, :].rearrange("p a b -> p (a b)")[:, 0:384].rearrange("p (k t) -> p k t", t=128)[:, :, :T]), ["pS4"], [rtb.name])
                        for k_, dst in enumerate((r1s, r2s, rws)):
                            P.dma("sp", lambda e, k_=k_, dst=dst, rtb=rtb: e.dma_start(out=dst[ti, :, :T], in_=rtb[:, k_, :T]), [rtb.name], [])

                    tiles = [("p", b, i) for b in range(NPB if DO_P else 0) for i in range(NT_RUN)]
                    nxt = load_h1(tile_of(*tiles[0]), 128) if tiles else None
                    def interleave(ga, gb):
                        live = [g for g in (ga, gb) if g is not None]
                        while live:
                            for g in list(live):
                                try:
                                    next(g)
                                except StopIteration:
                                    live.remove(g)

                    prevB = None
                    nsc = 0
                    for n_, (kind, b, i) in enumerate(tiles):
                        cur = nxt
                        if n_ + 1 < len(tiles):
                            nxt = load_h1(tile_of(*tiles[n_ + 1]), 128)
                        scb = scs[nsc % 2]; nsc += 1
                        interleave(tile3A("p", tile_of(kind, b, i), 128, [(b, 0, 128)], cur, scb), tile3B(*prevB) if prevB else None)
                        prevB = (tile_of(kind, b, i), 128, scb)
                    if DO_S:
                        for s_ in range(NSB):
                            for mt in range(2):
                                rows = slice(mt * 128, (mt + 1) * 128)
                                mkb = mkf2[mt]
                                P.dma("sp", lambda e, mkb=mkb, rows=rows, s_=s_: e.dma_start(out=mkb[:], in_=cmk[s_, rows, :]), [], [mkb.name])
                                P.op("act", lambda e, mkb=mkb: e.copy(out=mk_b2[:], in_=mkb[:]), [mkb.name], ["mk_b2"])
                                for k_ in range(8):
                                    P.op("pe", lambda e, k_=k_: e.transpose(out=pT[:, k_, :], in_=mk_b2[:, k_ * 128:(k_ + 1) * 128], identity=ident_b[:, :]), ["mk_b2", "ident_b"], ["pT4"])
                                P.op("dve", lambda e, rows=rows, s_=s_: e.tensor_copy(out=mkT[:, s_, :, rows], in_=pT[:]), ["pT4"], ["mkT%d" % s_])
                                mvb = hx[mt]
                                P.dma("sp", lambda e, mvb=mvb, rows=rows, s_=s_: e.dma_start(out=mvb[:], in_=cmv[s_, rows, :]), [], [mvb.name])
                                P.op("pool", lambda e, mvb=mvb, mt=mt, s_=s_: e.tensor_copy(out=mva[:, s_, mt, :, 0:256], in_=mvb[:].rearrange("p (h d) -> p h d", d=256)), [mvb.name, "mva"], ["mva%d" % s_])
                        c3["hx"] = 0
                        cur = load_h1(NPB * NT, TS)
                        scb = scs[nsc % 2]; nsc += 1
                        interleave(tile3A("s", NPB * NT, TS, [(s_, 8 * s_, 8 * s_ + 8) for s_ in range(NSB)], cur, scb), tile3B(*prevB) if prevB else None)
                        prevB = (NPB * NT, TS, scb)
                    if prevB:
                        interleave(tile3B(*prevB), None)
                    P.barrier()
                    P.emit()
            S3.close()

        if "4" in PH:
            with ExitStack() as S4:
                s4 = lambda n, s, d=F32: S4.enter_context(nc.sbuf_tensor(n, list(s), d))
                g_ffn4 = s4("g_ffn4", [128, 8])
                P.dma("sp", lambda e: e.dma_start(out=g_ffn4[:], in_=ffn_norm.rearrange("(c p) -> p c", p=128), allow_slow_non_contiguous=True), [], ["g_ffn4"])
                NJ = cfg.get("nj", 128)
                with ExitStack() as S4a:
                    s4a = lambda n, s, d=F32: S4a.enter_context(nc.sbuf_tensor(n, list(s), d))
                    ub = [s4a("ub%d" % k_, [128, D]) for k_ in range(4)]
                    vb4 = [s4a("vb4_%d" % k_, [128, D]) for k_ in range(4)]
                    uTb = [s4a("uTb%d" % k_, [128, 8, 128], BF16) for k_ in range(3)]
                    vbb = [s4a("vbb%d" % k_, [128, D], BF16) for k_ in range(3)]
                    pU = [S4a.enter_context(nc.psum_tensor("pU%d" % k_, [128, 1024], F32)) for k_ in range(3)]

                    def ld4a(j):
                        P.dma("sp", lambda e: e.dma_start(out=ub[j % 4][:], in_=peer_u[j * 128:(j + 1) * 128, :]), [], [ub[j % 4].name])
                        P.dma("sp", lambda e: e.dma_start(out=vb4[j % 4][:], in_=peer_v[j * 128:(j + 1) * 128, :]), [], [vb4[j % 4].name])

                    for j in range(min(3, NJ)):
                        ld4a(j)
                    for j in range(NJ):
                        if j + 3 < NJ:
                            ld4a(j + 3)
                        u_ = ub[j % 4]; v_ = vb4[j % 4]; pu = pU[j % 3]; ut = uTb[j % 3]; vt = vbb[j % 3]
                        for c in range(8):
                            P.op("pe", lambda e, c=c, u_=u_, pu=pu: e.transpose(out=pu[:, c * 128:(c + 1) * 128], in_=u_[:, c * 128:(c + 1) * 128], identity=ident[:, :]), [u_.name, "ident"], [pu.name])
                        P.op("dve", lambda e, pu=pu, ut=ut: e.tensor_tensor(out=ut[:, :, :], in0=pu[:, :].rearrange("p (c e) -> p c e", e=128), in1=g_ffn4[:, :].unsqueeze(2).to_broadcast([128, 8, 128]), op=ALU.mult),
                             [pu.name, "g_ffn4"], [ut.name])
                        P.dma("sp", lambda e, ut=ut, j=j: e.dma_start(out=uTs[j], in_=ut[:, :, :].rearrange("p c e -> p (c e)")), [ut.name], [])
                        P.op("act", lambda e, v_=v_, vt=vt: e.copy(out=vt[:], in_=v_[:]), [v_.name], [vt.name])
                        P.dma("sp", lambda e, vt=vt, j=j: e.dma_start(out=vbs[j], in_=vt[:]), [vt.name], [])
                    P.barrier()
                    P.emit()

                with ExitStack() as S4b:
                    s4b = lambda n, s, d=F32: S4b.enter_context(nc.sbuf_tensor(n, list(s), d))
                    TB = 256
                    GT = [s4b("GT%d" % k_, [128, 128, TB], BF16) for k_ in range(2)]
                    NUV = 5
                    uj = [s4b("uj%d" % k_, [128, 8, 128], BF16) for k_ in range(NUV)]
                    vj = [s4b("vj%d" % k_, [128, D], BF16) for k_ in range(NUV)]
                    hT = [s4b("hT%d" % k_, [128, 8, TB], BF16) for k_ in range(2)]
                    rr = [s4b("rr%d" % k_, [128, 3, TB]) for k_ in range(2)]
                    O1 = [s4b("O1_%d" % k_, [128, 128], BF16) for k_ in range(4)]
                    O2 = [s4b("O2_%d" % k_, [128, 128], BF16) for k_ in range(4)]
                    Hg = [s4b("Hg%d" % k_, [128, TB], BF16) for k_ in range(3)]
                    Wb = [s4b("Wb%d" % k_, [128, TB], BF16) for k_ in range(3)]
                    h2t = [s4b("h2t%d" % k_, [128, D]) for k_ in range(2)]
                    ybuf = [s4b("ybuf%d" % k_, [128, D]) for k_ in range(2)]
                    pY = [S4b.enter_context(nc.psum_tensor("pY%d" % k_, [128, 1024], F32)) for k_ in range(2)]
                    pH = [S4b.enter_context(nc.psum_tensor("pH%d" % k_, [128, 512], F32)) for k_ in range(3)]
                    pGt = [S4b.enter_context(nc.psum_tensor("pGt%d" % k_, [128, 4, 128], F32)) for k_ in range(1)]
                    c4 = {"o": 0, "g": 0, "uv": 0, "hw": 0, "y": 0}
                    blocks = []
                    if DO_P:
                        for b in range(NPB):
                            for i in range(0, NT_RUN, 2):
                                tl = [(b * NT + i, 128)]
                                if i + 1 < NT_RUN:
                                    tl.append((b * NT + i + 1, 128))
                                blocks.append(tl)
                    if DO_S:
                        blocks.append([(NPB * NT, TS)])

                    def load_block(bi):
                        tl = blocks[bi]
                        hb = hT[bi % 2]; rb = rr[bi % 2]
                        for k_, (ti, T) in enumerate(tl):
                            P.dma("sp", lambda e, k_=k_, ti=ti, T=T: e.dma_start(out=hb[:, :, k_ * 128:k_ * 128 + T], in_=hnTs[ti].rearrange("p (c t) -> p c t", t=128)[:, :, :T]), [], [hb.name])
                            for q_, src in enumerate((r1s, r2s, rws)):
                                P.dma("sp", lambda e, k_=k_, ti=ti, T=T, q_=q_, src=src: e.dma_start(out=rb[:, q_, k_ * 128:k_ * 128 + T], in_=src[ti, :, :T]), [], [rb.name])

                    def g_tokens(bi, t0, t1):
                        rb = rr[bi % 2]; gt = GT[bi % 2]
                        nt_ = sum(T for (_, T) in blocks[bi])
                        t1 = min(t1, nt_)
                        t = t0
                        while t < t1:
                            n4 = min(4, t1 - t)
                            pg = pGt[0]
                            for q_ in range(n4):
                                o1 = O1[c4["o"] % 4]; o2 = O2[c4["o"] % 4]; c4["o"] += 1
                                tt = t + q_
                                P.op("dve", lambda e, o1=o1, tt=tt: e.tensor_scalar(out=o1[:], in0=iota_b[:], scalar1=rb[:, 0, tt:tt + 1], scalar2=None, op0=ALU.is_equal), ["iota_b", rb.name], [o1.name])
                                P.op("dve", lambda e, o2=o2, tt=tt: e.tensor_scalar(out=o2[:], in0=iota_b[:], scalar1=rb[:, 1, tt:tt + 1], scalar2=rb[:, 2, tt:tt + 1], op0=ALU.is_equal, op1=ALU.mult), ["iota_b", rb.name], [o2.name])
                                P.op("pe", lambda e, o1=o1, o2=o2, pg=pg, q_=q_: e.matmul(pg[:, q_, :], lhsT=o2[:], rhs=o1[:], start=True, stop=True), [o1.name, o2.name], [pg.name])
                            P.op("act", lambda e, pg=pg, t=t, n4=n4: e.copy(out=gt[:, :, t:t + n4], in_=pg[:, 0:n4, :].rearrange("p q i -> p i q")), [pg.name], [gt.name])
                            t += n4

                    def ld_uv(j):
                        k_ = c4["uv"] % NUV; c4["uv"] += 1
                        P.dma("sp", lambda e: e.dma_start(out=uj[k_][:, :, :].rearrange("p c e -> p (c e)"), in_=uTs[j]), [], [uj[k_].name])
                        P.dma("sp", lambda e: e.dma_start(out=vj[k_][:], in_=vbs[j]), [], [vj[k_].name])
                        return k_

                    if blocks:
                        load_block(0)
                        g_tokens(0, 0, TB)
                    for bi, tl in enumerate(blocks):
                        nt_ = sum(T for (_, T) in tl)
                        hb = hT[bi % 2]; gt = GT[bi % 2]
                        if bi + 1 < len(blocks):
                            load_block(bi + 1)
                        q = [ld_uv(j_) for j_ in range(min(NUV - 1, NJ))]
                        hst = {}

                        def do_H(j, nt_=nt_, hb=hb):
                            k_ = q[j]
                            sel = c4["hw"] % 3; c4["hw"] += 1
                            ph, hg, wb = pH[sel], Hg[sel], Wb[sel]
                            for c in range(8):
                                P.op("pe", lambda e, c=c, k_=k_, ph=ph: e.matmul(ph[:, 0:nt_], lhsT=uj[k_][:, c, :], rhs=hb[:, c, 0:nt_], start=(c == 0), stop=(c == 7)), [uj[k_].name, hb.name], [ph.name])
                            hst[j] = (k_, ph, hg, wb)

                        do_H(0)
                        if NJ > 1:
                            do_H(1)
                        for j in range(NJ):
                            if j + NUV - 1 < NJ:
                                q.append(ld_uv(j + NUV - 1))
                            if j + 2 < NJ:
                                do_H(j + 2)
                            k_, ph, hg, wb = hst.pop(j)
                            P.op("act", lambda e, ph=ph, hg=hg, nt_=nt_: e.activation(out=hg[:, 0:nt_], in_=ph[:, 0:nt_], func=AF.Gelu_apprx_tanh), [ph.name], [hg.name])
                            P.op("dve", lambda e, hg=hg, wb=wb, j=j, nt_=nt_, gt=gt: e.tensor_tensor(out=wb[:, 0:nt_], in0=hg[:, 0:nt_], in1=gt[:, j, 0:nt_], op=ALU.mult), [hg.name, gt.name], [wb.name])
                            for k2_, (ti, T) in enumerate(tl):
                                for half in range(2):
                                    P.op("pe", lambda e, k2_=k2_, T=T, half=half, wb=wb, k_=k_, j=j: e.matmul(pY[k2_][:T, half * 512:(half + 1) * 512], lhsT=wb[:, k2_ * 128:k2_ * 128 + T], rhs=vj[k_][:, half * 512:(half + 1) * 512],
                                                                                                       start=(j == 0), stop=(j == NJ - 1)), [wb.name, vj[k_].name], ["pY%d" % k2_])
                            if bi + 1 < len(blocks):
                                if NJ >= 128:
                                    if j % 2 == 0:
                                        g_tokens(bi + 1, 2 * j, 2 * j + 4)
                                elif j == 0:
                                    g_tokens(bi + 1, 0, TB)
                        for k2_, (ti, T) in enumerate(tl):
                            hh_ = h2t[c4["y"] % 2]; yb = ybuf[c4["y"] % 2]; c4["y"] += 1
                            P.dma("sp", lambda e, hh_=hh_, ti=ti, T=T: e.dma_start(out=hh_[:T, :], in_=h2s[ti, :T, :]), [], [hh_.name])
                            P.op("dve", lambda e, hh_=hh_, yb=yb, k2_=k2_, T=T: e.tensor_tensor(out=yb[:T, :], in0=pY[k2_][:T, :], in1=hh_[:T, :], op=ALU.add), ["pY%d" % k2_, hh_.name], [yb.name])
                            if ti == NPB * NT:
                                P.dma("sp", lambda e, yb=yb, T=T: e.dma_start(out=ys, in_=yb[:T, :]), [yb.name], [])
                            else:
                                b_, i_ = ti // NT, ti % NT
                                P.dma("sp", lambda e, yb=yb, b_=b_, i_=i_: e.dma_start(out=yp[b_, i_ * 128:(i_ + 1) * 128, :], in_=yb[:, :]), [yb.name], [])
                    P.barrier()
                    P.emit()
    P.close()
    return nc


def _consts():
    j = np.arange(128)
    triu = (j[:, None] <= j[None, :]).astype(np.float32)
    iota = np.tile(np.arange(128, dtype=np.float32)[None, :], (128, 1))
    t = np.arange(32)
    m32 = ((t[:, None] // 8 == t[None, :] // 8) & (t[:, None] <= t[None, :])).astype(np.float32)
    rowm = (t[:, None] // 8 == np.arange(4)[None, :]).astype(np.float32)
    colm = np.tile((np.arange(4)[:, None] == (t[None, :] // 8)).astype(np.float32).reshape(1, 128), (128, 1))
    reset = np.tile((t % 8 != 0).astype(np.float32)[None, :], (128, 1))
    return dict(c_ident=np.eye(128, dtype=np.float32), c_triu=triu, c_iota=iota, c_m32=m32, c_rowm=rowm,
                c_colm=colm, c_reset=reset, c_ones=np.ones((128, 128), np.float32),
                c_pidx=np.arange(128, dtype=np.float32).reshape(128, 1))


def make_in_maps(inp, cores):
    f = lambda a: np.ascontiguousarray(np.asarray(a))
    n_phys = inp["cache_diff_k"].shape[1]
    ckf = f(inp["cache_diff_k"]).reshape(n_phys * 128, 512)
    cvf = f(inp["cache_diff_v"]).reshape(n_phys * 128, 512)
    shared = dict(ck=ckf, cv=cvf)
    for k in ("attn_norm", "w_in", "diff_q_norm", "diff_k_norm", "lambda_q1", "lambda_k1", "lambda_q2", "lambda_k2",
              "diff_out_norm", "gla_w_gate", "gla_b_gate", "gla_out_norm", "w_o", "cross_norm", "mem_norm", "cross_w_q",
              "cross_w_kv", "cross_q_norm", "cross_k_norm", "cross_w_o", "ffn_norm", "peer_w_q", "peer_q_norm", "peer_u", "peer_v"):
        shared[k] = f(inp[k])[0]
    shared["peer_sub_keys"] = f(inp["peer_sub_keys"])[0].reshape(16, 128, 128)
    shared.update(_consts())
    maps = []
    for c in cores:
        m = dict(shared)
        m["xp"] = f(inp["x_prompt"][NPB * c:NPB * (c + 1)])
        m["xs"] = f(inp["x_sample"][NSB * c:NSB * (c + 1)]).reshape(TS, D)
        m["memp"] = f(inp["mem_prompt"][NPB * c:NPB * (c + 1)])
        m["pt"] = f(inp["page_table"][NSB * c:NSB * (c + 1)]).astype(np.int32)
        m["sgla"] = f(inp["state_gla"][0, NSB * c:NSB * (c + 1)])
        m["cmk"] = f(inp["cache_mem_k"][0, NSB * c:NSB * (c + 1)]).reshape(NSB, 256, D)
        m["cmv"] = f(inp["cache_mem_v"][0, NSB * c:NSB * (c + 1)]).reshape(NSB, 256, D)
        maps.append(m)
    return maps, n_phys


def kernel(**inp):
    cores = list(range(NCORES))
    maps, n_phys = make_in_maps(inp, cores)
    nc = build(n_phys)
    used = set()
    for alloc_name in maps[0]:
        used.add(alloc_name)
    res = run_bass_kernel_spmd(nc, maps, core_ids=cores)
    R = res.results
    cat = lambda k: np.concatenate([r[k] for r in R], axis=0)
    B = NPB * NCORES
    DB = NSB * NCORES
    y_p = cat("yp")
    y_s = cat("ys").reshape(DB, 8, D)
    k_p = cat("okp").reshape(1, B, SEQ, 8, 64)
    v_p = cat("ovp").reshape(1, B, SEQ, 8, 64)
    st_p = cat("ostp").reshape(1, B, 4, 64, 128)
    mk_p = cat("omk").reshape(1, B, 256, 4, 256)
    mv_p = cat("omv").reshape(1, B, 256, 4, 256)
    k_s = cat("oks").reshape(1, DB, 8, 8, 64)
    v_s = cat("ovs").reshape(1, DB, 8, 8, 64)
    st_s = cat("osts").reshape(1, DB, 4, 64, 128)
    return (y_p, y_s, k_p, v_p, st_p, mk_p, mv_p, k_s, v_s, st_s)
```

```python
from contextlib import ExitStack
import math
import numpy as np
import concourse.bass as bass
import concourse.mybir as mybir
from concourse.bass_utils import run_bass_kernel_spmd

F32 = mybir.dt.float32
BF16 = mybir.dt.bfloat16
I32 = mybir.dt.int32
U32 = mybir.dt.uint32
AF = mybir.ActivationFunctionType
ALU = mybir.AluOpType
AX = mybir.AxisListType

EPOCH = 6000
NDMA = 28
NSW = 8
EPS = 1e-6
NCORES = 8
D = 1024
SEQ = 2048
NT = SEQ // 128
NPB = 2
NSB = 4
TS = 32
NPAGES = 128
INW = 3088
LAM_INIT = 0.8 - 0.6 * math.exp(-0.3 * 0)
NTILES = NPB * NT + 1


class Prog:
    COMPUTE = ("pe", "act", "dve", "pool")
    ALL = ("pe", "act", "dve", "pool", "sp")

    def __init__(self, nc):
        self.nc = nc
        self.es = ExitStack()
        self.nsem = 0
        self.cur = {}
        self.cnt = {}
        for e in self.COMPUTE:
            self.cur[e] = self._newsem(e)
            self.cnt[e] = 0
        self.dsem = [self._newsem("d%d" % i) for i in range(NDMA)]
        self.dcnt = [0] * NDMA
        self.drr = 0
        self.wsem = [self._newsem("w%d" % i) for i in range(NSW)]
        self.wcnt = [0] * NSW
        self.wcons = [[] for _ in range(NSW)]
        self.wrr = 0
        self.wids = {id(s_): i for i, s_ in enumerate(self.wsem)}
        self.ops = {e: [] for e in self.ALL}
        self.res = {}
        self.known = {e: {} for e in self.ALL}
        self.last_tok = {}
        self.n_ops = 0

    def _newsem(self, name):
        self.nsem += 1
        return self.es.enter_context(self.nc.semaphore("s_%s_%d" % (name, self.nsem)))

    def _deps(self, eng, reads, writes):
        waits = {}

        def need(tok):
            sem, val, teng = tok
            if teng == eng and eng == "pe":
                return
            k = id(sem)
            if k not in waits or waits[k][1] < val:
                waits[k] = (sem, val)

        for r in reads:
            st = self.res.get(r)
            if st:
                for t in st["w"]:
                    need(t)
        for w in writes:
            st = self.res.get(w)
            if st:
                for t in st["w"]:
                    need(t)
                for t in st["r"]:
                    need(t)
        out = []
        kn = self.known[eng]
        for k, (sem, val) in waits.items():
            if kn.get(k, 0) >= val:
                continue
            kn[k] = val
            out.append((sem, val))
        return out

    def _commit(self, tok, reads, writes):
        for r in reads:
            st = self.res.setdefault(r, {"w": [], "r": []})
            st["r"] = [t for t in st["r"] if t[0] is not tok[0]] + [tok]
        for w in writes:
            st = self.res.setdefault(w, {"w": [], "r": []})
            if st["r"]:
                st["w"] = [tok]
                st["r"] = []
            else:
                st["w"] = [t for t in st["w"] if t[0] is not tok[0]] + [tok]
        self.last_tok[id(tok[0])] = tok

    @staticmethod
    def _excl(reads, writes):
        ex = [r for r in reads if (len(r) > 1 and r[0] == "p" and r[1].isupper()) or r.startswith("accS")]
        if ex:
            reads = [r for r in reads if r not in ex]
            writes = list(writes) + [r for r in ex if r not in writes]
        return reads, writes

    def op(self, eng, fn, reads=(), writes=()):
        reads, writes = self._excl(reads, writes)
        waits = self._deps(eng, reads, writes)
        sem = self.cur[eng]
        self.cnt[eng] += 1
        tok = (sem, self.cnt[eng], eng)
        self.ops[eng].append((waits, fn, sem, 1))
        self._note_sw(waits, tok)
        self._commit(tok, reads, writes)
        if self.cnt[eng] >= EPOCH:
            self.cur[eng] = self._newsem(eng)
            self.cnt[eng] = 0
        self.n_ops += 1
        return tok

    def dma(self, q, fn, reads=(), writes=()):
        j = self.drr
        self.drr = (self.drr + 1) % NDMA
        sem = self.dsem[j]
        waits = self._deps(q, reads, writes)
        if self.dcnt[j] > 0:
            k = id(sem)
            v = 16 * self.dcnt[j]
            if self.known[q].get(k, 0) < v:
                self.known[q][k] = v
                waits.append((sem, v))
        self.dcnt[j] += 1
        tok = (sem, 16 * self.dcnt[j], "dma")
        self.ops[q].append((waits, fn, sem, 16))
        self._note_sw(waits, tok)
        self._commit(tok, reads, writes)
        self.n_ops += 1
        return tok

    def _note_sw(self, waits, tok):
        for (s_, v_) in waits:
            i = self.wids.get(id(s_))
            if i is not None:
                self.wcons[i].append(tok)

    def dma_sw(self, fn, reads=(), writes=()):
        q = "pool"
        i = self.wrr
        self.wrr = (self.wrr + 1) % NSW
        sem = self.wsem[i]
        waits = self._deps(q, reads, writes)
        c = self.wcnt[i]
        if c > 0:
            k = id(sem)
            v = 16 * c
            if self.known[q].get(k, 0) < v:
                self.known[q][k] = v
                waits.append((sem, v))
        self.wcnt[i] = c + 1
        tok = (sem, 16 * (c + 1), "dma")
        self.ops[q].append((waits, fn, sem, 16))
        self._commit(tok, reads, writes)
        self.n_ops += 1
        return tok

    def barrier(self):
        toks = list(self.last_tok.values())
        for e in self.ALL:
            waits = []
            for (sem, val, teng) in toks:
                k = id(sem)
                if self.known[e].get(k, 0) >= val:
                    continue
                self.known[e][k] = val
                waits.append((sem, val))
            if waits:
                self.ops[e].append((waits, None, None, 0))
        self.res = {}

    def emit(self):
        nc = self.nc
        ops = self.ops

        def run(eng_obj, lst):
            for waits, fn, sem, inc in lst:
                for (s, v) in waits:
                    eng_obj.wait_ge(s, v)
                if fn is not None:
                    ins = fn(eng_obj)
                    if sem is not None:
                        ins.then_inc(sem, inc)

        with nc.Block() as blk:
            @blk.tensor
            def _(e):
                run(e, ops["pe"])

            @blk.scalar
            def _(e):
                run(e, ops["act"])

            @blk.vector
            def _(e):
                run(e, ops["dve"])

            @blk.gpsimd
            def _(e):
                run(e, ops["pool"])

            @blk.sync
            def _(e):
                run(e, ops["sp"])
        self.ops = {e: [] for e in self.ALL}

    def close(self):
        self.es.close()


def build(n_phys, cfg=None):
    cfg = cfg or {}
    NT_RUN = cfg.get("nt", NT)
    NPG_RUN = cfg.get("npages", NPAGES)
    PH = cfg.get("phases", "234")
    DBG = cfg.get("dbg", False)
    DO_S = cfg.get("sample", True)
    STOP = cfg.get("stop", 99)
    DO_P = cfg.get("prompt", True)

    nc = bass.Bass("TRN2", target_bir_lowering=False)
    di = lambda n, s, d=F32: nc.dram_tensor(n, list(s), d, kind="ExternalInput").ap()
    do = lambda n, s, d=F32: nc.dram_tensor(n, list(s), d, kind="ExternalOutput").ap()
    dscr = lambda n, s, d=F32: nc.dram_tensor(n, list(s), d, kind="Internal").ap()

    xp = di("xp", [NPB, SEQ, D])
    xs = di("xs", [TS, D])
    memp = di("memp", [NPB, 256, D])
    ck = di("ck", [n_phys * 128, 512])
    cv = di("cv", [n_phys * 128, 512])
    pt = di("pt", [NSB, NPAGES], I32)
    sgla = di("sgla", [NSB, 4, 64, 128])
    cmk = di("cmk", [NSB, 256, D])
    cmv = di("cmv", [NSB, 256, D])
    attn_norm = di("attn_norm", [D])
    w_in = di("w_in", [D, INW])
    diff_q_norm = di("diff_q_norm", [32])
    diff_k_norm = di("diff_k_norm", [32])
    lq1 = di("lambda_q1", [32]); lk1 = di("lambda_k1", [32])
    lq2 = di("lambda_q2", [32]); lk2 = di("lambda_k2", [32])
    diff_out_norm = di("diff_out_norm", [64])
    gla_w_gate = di("gla_w_gate", [16, 256])
    gla_b_gate = di("gla_b_gate", [256])
    gla_out_norm = di("gla_out_norm", [128])
    w_o = di("w_o", [D, D])
    cross_norm = di("cross_norm", [D])
    mem_norm = di("mem_norm", [D])
    cross_w_q = di("cross_w_q", [D, D])
    cross_w_kv = di("cross_w_kv", [D, 2 * D])
    cross_q_norm = di("cross_q_norm", [256])
    cross_k_norm = di("cross_k_norm", [256])
    cross_w_o = di("cross_w_o", [D, D])
    ffn_norm = di("ffn_norm", [D])
    peer_w_q = di("peer_w_q", [D, 2 * D])
    peer_q_norm = di("peer_q_norm", [256])
    peer_sk = di("peer_sub_keys", [16, 128, 128])
    peer_u = di("peer_u", [16384, D])
    peer_v = di("peer_v", [16384, D])
    c_ident = di("c_ident", [128, 128])
    c_triu = di("c_triu", [128, 128])
    c_iota = di("c_iota", [128, 128])
    c_m32 = di("c_m32", [32, 32])
    c_rowm = di("c_rowm", [32, 4])
    c_colm = di("c_colm", [128, 4 * 32])
    c_reset = di("c_reset", [128, 32])
    c_ones = di("c_ones", [128, 128])
    c_pidx = di("c_pidx", [128, 1])

    yp = do("yp", [NPB, SEQ, D])
    ys = do("ys", [TS, D])
    okp = do("okp", [NPB, SEQ, 512])
    ovp = do("ovp", [NPB, SEQ, 512])
    ostp = do("ostp", [NPB, 4, 64, 128])
    omk = do("omk", [NPB, 256, D])
    omv = do("omv", [NPB, 256, D])
    oks = do("oks", [TS, 512])
    ovs = do("ovs", [TS, 512])
    osts = do("osts", [NSB, 4, 64, 128])

    if DBG:
        h1s = do("h1s", [NTILES, 128, D])
        h2s = do("h2s", [NTILES, 128, D])
    else:
        h1s = dscr("h1s", [NTILES, 128, D])
        h2s = dscr("h2s", [NTILES, 128, D])
    hnTs = dscr("hnTs", [NTILES, 128, 8 * 128], BF16)
    r1s = dscr("r1s", [NTILES, 128, 128])
    r2s = dscr("r2s", [NTILES, 128, 128])
    rws = dscr("rws", [NTILES, 128, 128])
    uTs = dscr("uTs", [128, 128, 8 * 128], BF16)
    vbs = dscr("vbs", [128, 128, D], BF16)

    P = Prog(nc)

    def tile_of(kind, b, i):
        return NPB * NT if kind == "s" else b * NT + i

    def act_copy(out, in_, reads, writes, eng="act"):
        if eng == "act":
            P.op("act", lambda e: e.copy(out=out, in_=in_), reads, writes)
        else:
            P.op(eng, lambda e: e.tensor_copy(out=out, in_=in_), reads, writes)

    def rstd_from_ss(ss, n, T, w, rn, tmpn):
        P.op("dve", lambda e: e.tensor_scalar(out=ss, in0=ss, scalar1=1.0 / n, scalar2=EPS,
                                              op0=ALU.mult, op1=ALU.add), [rn], [rn])
        P.op("act", lambda e: e.activation(out=ss, in_=ss, func=AF.Ln), [rn], [rn])
        P.op("act", lambda e: e.activation(out=ss, in_=ss, func=AF.Exp, scale=-0.5), [rn], [rn])

    with ExitStack() as S2:
        sb = lambda n, s, d=F32: S2.enter_context(nc.sbuf_tensor(n, list(s), d))
        ident = sb("ident", [128, 128]); ident_b = sb("ident_b", [128, 128], BF16)
        triu = sb("triu", [128, 128]); triu_b = sb("triu_b", [128, 128], BF16)
        m32 = sb("m32", [32, 32]); m32_b = sb("m32_b", [32, 32], BF16)
        rowm = sb("rowm", [32, 4]); colm = sb("colm", [128, 4, 32]); resetm = sb("resetm", [128, 32])
        ones = sb("ones", [128, 128])
        for (t_, src) in ((ident, c_ident), (triu, c_triu), (m32, c_m32), (rowm, c_rowm),
                          (resetm, c_reset), (ones, c_ones)):
            P.dma("sp", lambda e, t_=t_, src=src: e.dma_start(out=t_[:], in_=src), [], [t_.name])
        P.dma("sp", lambda e: e.dma_start(out=colm[:].rearrange("p s t -> p (s t)"), in_=c_colm), [], ["colm"])
        P.op("dve", lambda e: e.tensor_copy(out=ident_b[:], in_=ident[:]), ["ident"], ["ident_b"])
        P.op("dve", lambda e: e.tensor_copy(out=triu_b[:], in_=triu[:]), ["triu"], ["triu_b"])
        P.op("dve", lambda e: e.tensor_copy(out=m32_b[:], in_=m32[:]), ["m32"], ["m32_b"])

        iota_t = sb("iota_t", [128, 128])
        P.dma("sp", lambda e: e.dma_start(out=iota_t[:], in_=c_iota), [], ["iota_t"])
        iota_b = sb("iota_b", [128, 128], BF16)
        P.op("dve", lambda e: e.tensor_copy(out=iota_b[:], in_=iota_t[:]), ["iota_t"], ["iota_b"])
        S2p = ExitStack()
        sb = lambda n, s, d=F32: S2p.enter_context(nc.sbuf_tensor(n, list(s), d))
        if "2" in PH:
            w_in_b = sb("w_in_b", [128, 8, INW], BF16)
            w_o_b = sb("w_o_b", [128, 8, D], BF16)
            Sg = sb("Sg", [128, 2, 128])
            g_attn = sb("g_attn", [128, 8])
            qn_bc = sb("qn_bc", [128, 32]); kn_bc = sb("kn_bc", [128, 32])
            dn_bc = sb("dn_bc", [128, 64]); gn_bc = sb("gn_bc", [128, 128])
            negb = sb("negb", [128, 2]); wg = sb("wg", [16, 256])
            lamt = sb("lamt", [128, 4, 32]); lamw = sb("lamw", [128, 8]); neglam = sb("neglam", [128, 1])

            P.dma("sp", lambda e: e.dma_start(out=g_attn[:], in_=attn_norm.rearrange("(c p) -> p c", p=128), allow_slow_non_contiguous=True), [], ["g_attn"])
            P.dma("sp", lambda e: e.dma_start(out=negb[:], in_=gla_b_gate.rearrange("(c p) -> p c", p=128), allow_slow_non_contiguous=True), [], ["negb"])
            P.dma("sp", lambda e: e.dma_start(out=qn_bc[:], in_=diff_q_norm.partition_broadcast(128)), [], ["qn_bc"])
            P.dma("sp", lambda e: e.dma_start(out=kn_bc[:], in_=diff_k_norm.partition_broadcast(128)), [], ["kn_bc"])
            P.dma("sp", lambda e: e.dma_start(out=dn_bc[:], in_=diff_out_norm.partition_broadcast(128)), [], ["dn_bc"])
            P.dma("sp", lambda e: e.dma_start(out=gn_bc[:], in_=gla_out_norm.partition_broadcast(128)), [], ["gn_bc"])
            P.dma("sp", lambda e: e.dma_start(out=wg[:], in_=gla_w_gate), [], ["wg"])
            for k_, src in enumerate((lq1, lk1, lq2, lk2)):
                P.dma("sp", lambda e, k_=k_, src=src: e.dma_start(out=lamt[:, k_, :], in_=src.partition_broadcast(128)), [], ["lamt"])
            P.op("dve", lambda e: e.tensor_scalar(out=qn_bc[:], in0=qn_bc[:], scalar1=32.0 ** -0.5, scalar2=None, op0=ALU.mult), ["qn_bc"], ["qn_bc"])
            P.op("dve", lambda e: e.tensor_scalar(out=dn_bc[:], in0=dn_bc[:], scalar1=1.0 - LAM_INIT, scalar2=None, op0=ALU.mult), ["dn_bc"], ["dn_bc"])
            P.op("dve", lambda e: e.tensor_scalar(out=negb[:], in0=negb[:], scalar1=-1.0, scalar2=None, op0=ALU.mult), ["negb"], ["negb"])
            P.op("dve", lambda e: e.tensor_tensor(out=lamt[:, 0, :], in0=lamt[:, 0, :], in1=lamt[:, 1, :], op=ALU.mult), ["lamt"], ["lamt"])
            P.op("dve", lambda e: e.tensor_tensor(out=lamt[:, 2, :], in0=lamt[:, 2, :], in1=lamt[:, 3, :], op=ALU.mult), ["lamt"], ["lamt"])
            P.op("dve", lambda e: e.tensor_reduce(out=lamw[:, 0:1], in_=lamt[:, 0, :], axis=AX.X, op=ALU.add), ["lamt"], ["lamw"])
            P.op("dve", lambda e: e.tensor_reduce(out=lamw[:, 1:2], in_=lamt[:, 2, :], axis=AX.X, op=ALU.add), ["lamt"], ["lamw"])
            P.op("act", lambda e: e.activation(out=lamw[:, 2:4], in_=lamw[:, 0:2], func=AF.Exp), ["lamw"], ["lamw"])
            P.op("dve", lambda e: e.tensor_tensor(out=lamw[:, 4:5], in0=lamw[:, 3:4], in1=lamw[:, 2:3], op=ALU.subtract), ["lamw"], ["lamw"])
            P.op("dve", lambda e: e.tensor_scalar(out=neglam[:], in0=lamw[:, 4:5], scalar1=-LAM_INIT, scalar2=None, op0=ALU.add), ["lamw"], ["neglam"])

            with ExitStack() as S0:
                stg = [S0.enter_context(nc.sbuf_tensor("stg%d" % k_, [128, INW], F32)) for k_ in range(2)]
                for c in range(8):
                    s_ = stg[c % 2]
                    P.dma("sp", lambda e, c=c, s_=s_: e.dma_start(out=s_[:], in_=w_in[c * 128:(c + 1) * 128, :]), [], [s_.name])
                    if c % 2 == 0:
                        P.op("dve", lambda e, c=c, s_=s_: e.tensor_scalar(out=w_in_b[:, c, :], in0=s_[:], scalar1=g_attn[:, c:c + 1], scalar2=None, op0=ALU.mult), [s_.name, "g_attn"], ["w_in_b"])
                    else:
                        P.op("act", lambda e, c=c, s_=s_: e.activation(out=w_in_b[:, c, :], in_=s_[:], func=AF.Copy, scale=g_attn[:, c:c + 1]), [s_.name, "g_attn"], ["w_in_b"])
                for c in range(8):
                    s_ = stg[c % 2]
                    P.dma("sp", lambda e, c=c, s_=s_: e.dma_start(out=s_[:, 0:D], in_=w_o[c * 128:(c + 1) * 128, :]), [], [s_.name])
                    act_copy(w_o_b[:, c, :], s_[:, 0:D], [s_.name], ["w_o_b"], eng=("dve" if c % 2 == 0 else "act"))
                P.barrier()
                P.emit()

            with ExitStack() as SW:
                sw = lambda n, s, d=F32: SW.enter_context(nc.sbuf_tensor(n, list(s), d))
                xt = [sw("xt%d" % k_, [128, D]) for k_ in range(2)]
                xs_b = sw("xs_b", [128, D], BF16)
                xnT = sw("xnT", [128, 8, 128], BF16)
                sq = sw("sq", [128, D])
                tmp = sw("tmp", [128, 512])
                st = sw("st", [128, 64])
                qn_b = sw("qn_b", [128, 512], BF16)
                kn = [sw("kn%d" % k_, [128, 512]) for k_ in range(2)]
                kn_b = sw("kn_b", [128, 512], BF16)
                QTb = sw("QTb", [64, 8, 2, 128], BF16)
                dv_sb = [sw("dv%d" % k_, [128, 512]) for k_ in range(2)]
                gv_sb = sw("gv_sb", [128, 512])
                sr = sw("sr", [128, 512])
                gaT = sw("gaT", [16, 128])
                e_sb = sw("e_sb", [128, 2, 128]); sp_sb = sw("sp_sb", [128, 2, 128])
                bT = sw("bT", [128, 2, 128]); eb = sw("eb", [128, 2, 128]); enb = sw("enb", [128, 2, 128])
                tmpE = sw("tmpE", [128, 2, 128])
                qdz = sw("qdz", [128, 2, 2, 128]); kd = sw("kd", [128, 2, 128])
                kd2T = sw("kd2T", [128, 2, 128]); kd2 = sw("kd2", [128, 2, 128])
                AT = sw("AT", [128, 4, 128])
                o_all = sw("o_all", [128, 8, 64])
                PT = [sw("PT%d" % k_, [128, 2, 128], BF16) for k_ in range(3)]
                cat = sw("cat", [128, D], BF16)
                catT = sw("catT", [128, 8, 128], BF16)
                h1 = [sw("h1_%d" % k_, [128, D]) for k_ in range(2)]
                qdp = sw("qdp", [128, 4, 2, 2, 32])
                S0s = sw("S0s", [128, 4, 2, 128])
                Sns = sw("Sns", [128, 4, 2, 128])
                bl = sw("bl", [128, 8]); ebl = sw("ebl", [128, 8])
                Vnew = sw("Vnew", [32, 8, 65], BF16)
                KTn = sw("KTn", [64, 8, 32], BF16)

                P.op("pool", lambda e: e.memset(QTb[:], 0.0), [], ["QTb"])
                P.op("pool", lambda e: e.memset(qdz[:], 0.0), [], ["qdz"])
                P.op("pool", lambda e: e.memset(Vnew[:], 1.0), [], ["Vnew"])

                with ExitStack() as SA:
                    KT = SA.enter_context(nc.sbuf_tensor("KT", [64, 8, SEQ], BF16))
                    Vaug = SA.enter_context(nc.sbuf_tensor("Vaug", [128, NT, 8, 65], BF16))
                    P.op("pool", lambda e: e.memset(Vaug[:], 1.0), [], ["Vaug"])
                    pA = SA.enter_context(nc.psum_tensor("pA", [128, 1024], F32))
                    pT = SA.enter_context(nc.psum_tensor("pT", [128, 8, 128], BF16))
                    pSb = [SA.enter_context(nc.psum_tensor("pS%d" % k_, [128, 512], F32)) for k_ in range(2)]
                    pAccF = [SA.enter_context(nc.psum_tensor("pAcc%d" % k_, [128, 512], F32)) for k_ in range(2)]
                    pAcc = [t_[:, 0:130].rearrange("p (c d) -> p c d", d=65) for t_ in pAccF]
                    pF = SA.enter_context(nc.psum_tensor("pF", [128, 512], F32))
                    pG = pA[:, 0:512]

                    cnt = {"s": 0, "pt": 0, "x": 0, "kn": 0, "dv": 0, "h1": 0}

                    def load_x(kind, b, i):
                        buf = xt[cnt["x"] % 2]; cnt["x"] += 1
                        T = 128 if kind == "p" else TS
                        src = xp[b, i * 128:(i + 1) * 128, :] if kind == "p" else xs
                        P.dma("sp", lambda e: e.dma_start(out=buf[:T, :], in_=src), [], [buf.name])
                        return buf

                    def transposes_bf(src_tile, T, dstT, n, src_res, dst_res, width=128):
                        for k_ in range(n):
                            P.op("pe", lambda e, k_=k_: e.transpose(out=pT[:width, k_, :T], in_=src_tile[:T, k_ * width:(k_ + 1) * width], identity=ident_b[:T, :T]),
                                 [src_res, "ident_b"], ["pT"])

                    def mixer_front(kind, b, i, xbuf):
                        T = 128 if kind == "p" else TS
                        xr = xbuf.name
                        P.op("act", lambda e: e.activation(out=sq[:T, :], in_=xbuf[:T, :], func=AF.Square, accum_out=st[:T, 0:1]), [xr], ["sq", "st0"])
                        rstd_from_ss(st[:T, 0:1], D, T, 1, "st0", None)
                        P.op("act", lambda e: e.activation(out=xs_b[:T, :], in_=xbuf[:T, :], func=AF.Copy, scale=st[:T, 0:1]), [xr, "st0"], ["xs_b"])
                        transposes_bf(xs_b, T, xnT, 8, "xs_b", "xnT")
                        P.op("dve", lambda e: e.tensor_copy(out=xnT[:, :, :T], in_=pT[:, :, :T]), ["pT"], ["xnT"])

                        if STOP <= 1:
                            return
                        def proj_tok(col0, ncols, half):
                            for c in range(8):
                                P.op("pe", lambda e, c=c: e.matmul(pA[:T, half * 512:half * 512 + ncols], lhsT=xnT[:, c, :T], rhs=w_in_b[:, c, col0:col0 + ncols],
                                                                   start=(c == 0), stop=(c == 7)), ["xnT", "w_in_b"], ["pA%d" % half])
                            return pA[:T, half * 512:half * 512 + ncols], "pA%d" % half

                        def qknorm(src, sres, gain, out_ap, out_res):
                            P.op("act", lambda e: e.activation(out=sq[:T, 0:512], in_=src, func=AF.Square), [sres], ["sq"])
                            P.op("dve", lambda e: e.tensor_reduce(out=st[:T, 8:24], in_=sq[:T, 0:512].rearrange("p (g d) -> p g d", d=32), axis=AX.X, op=ALU.add), ["sq"], ["st8"])
                            rstd_from_ss(st[:T, 8:24], 32, T, 16, "st8", None)
                            P.op("dve", lambda e: e.tensor_tensor(out=tmp[:T, :].rearrange("p (g d) -> p g d", d=32), in0=src.rearrange("p (g d) -> p g d", d=32),
                                                                  in1=st[:T, 8:24].unsqueeze(2).to_broadcast([T, 16, 32]), op=ALU.mult), [sres, "st8"], ["tmp"])
                            P.op("dve", lambda e: e.tensor_tensor(out=out_ap.rearrange("p (g d) -> p g d", d=32), in0=tmp[:T, :].rearrange("p (g d) -> p g d", d=32),
                                                                  in1=gain[:T, :].unsqueeze(1).to_broadcast([T, 16, 32]), op=ALU.mult), ["tmp", gain.name], [out_res])

                        src, sres = proj_tok(0, 512, 0)
                        qknorm(src, sres, qn_bc, qn_b[:T, :], "qn_b")
                        if STOP <= 2:
                            return
                        src, sres = proj_tok(512, 512, 1)
                        knb = kn[cnt["kn"] % 2]; cnt["kn"] += 1
                        qknorm(src, sres, kn_bc, knb[:T, :], knb.name)
                        dstk = okp[b, i * 128:(i + 1) * 128, :] if kind == "p" else oks
                        P.dma("sp", lambda e: e.dma_start(out=dstk, in_=knb[:T, :]), [knb.name], [])
                        P.op("act", lambda e: e.copy(out=kn_b[:T, :], in_=knb[:T, :]), [knb.name], ["kn_b"])
                        transposes_bf(qn_b, T, None, 8, "qn_b", None, width=64)
                        P.op("dve", lambda e: e.tensor_copy(out=QTb[0:32, :, 0, :T], in_=pT[0:32, :, :T]), ["pT"], ["QTb"])
                        P.op("dve", lambda e: e.tensor_copy(out=QTb[32:64, :, 1, :T], in_=pT[32:64, :, :T]), ["pT"], ["QTb"])
                        transposes_bf(kn_b, T, None, 8, "kn_b", None, width=64)
                        if kind == "p":
                            P.op("dve", lambda e: e.tensor_copy(out=KT[:, :, i * 128:(i + 1) * 128], in_=pT[0:64, :, :]), ["pT"], ["KT%d" % i])
                        else:
                            P.op("dve", lambda e: e.tensor_copy(out=KTn[:, :, :], in_=pT[0:64, :, :T]), ["pT"], ["KTn"])
                        if STOP <= 3:
                            return
                        src, sres = proj_tok(1024, 512, 0)
                        dvb = dv_sb[cnt["dv"] % 2]; cnt["dv"] += 1
                        P.op("act", lambda e, src=src: e.copy(out=dvb[:T, :], in_=src), [sres], [dvb.name])
                        dstv = ovp[b, i * 128:(i + 1) * 128, :] if kind == "p" else ovs
                        P.dma("sp", lambda e: e.dma_start(out=dstv, in_=dvb[:T, :]), [dvb.name], [])
                        if kind == "p":
                            P.op("pool", lambda e: e.tensor_copy(out=Vaug[:, i, :, 0:64], in_=dvb[:, :].rearrange("p (h d) -> p h d", d=64)), [dvb.name, "Vaug"], ["Vaug%d" % i])
                        else:
                            P.op("pool", lambda e: e.tensor_copy(out=Vnew[:, :, 0:64], in_=dvb[:T, :].rearrange("p (h d) -> p h d", d=64)), [dvb.name], ["Vnew"])
                        src, sres = proj_tok(2048, 512, 1)
                        P.op("act", lambda e, src=src: e.copy(out=gv_sb[:T, :], in_=src), [sres], ["gv_sb"])
                        src, sres = proj_tok(2576, 512, 0)
                        P.op("act", lambda e, src=src: e.activation(out=sr[:T, :], in_=src, func=AF.Silu), [sres], ["sr"])
                        if STOP <= 4:
                            return
                        for k_, col0 in enumerate((1536, 1664, 1792, 1920)):
                            for c in range(8):
                                P.op("pe", lambda e, c=c, k_=k_, col0=col0: e.matmul(pF[:, k_ * 128:k_ * 128 + T], lhsT=w_in_b[:, c, col0:col0 + 128], rhs=xnT[:, c, :T],
                                                                                     start=(c == 0), stop=(c == 7)), ["xnT", "w_in_b"], ["pF"])
                        for c in range(8):
                            P.op("pe", lambda e, c=c: e.matmul(pG[0:16, 0:T], lhsT=w_in_b[:, c, 2560:2576], rhs=xnT[:, c, :T], start=(c == 0), stop=(c == 7)),
                                 ["xnT", "w_in_b"], ["pA0"])
                        P.op("act", lambda e: e.copy(out=gaT[:, :T], in_=pG[0:16, 0:T]), ["pA0"], ["gaT"])
                        if STOP <= 5:
                            return
                        yield
                        for m in range(2):
                            P.op("pe", lambda e, m=m: e.matmul(pG[:, 128 + m * 128:128 + m * 128 + T], lhsT=wg[:, m * 128:(m + 1) * 128], rhs=gaT[:, :T], start=True, stop=True),
                                 ["wg", "gaT"], ["pA0"])
                        for m in range(2):
                            P.op("act", lambda e, m=m: e.activation(out=e_sb[:, m, :T], in_=pG[:, 128 + m * 128:128 + m * 128 + T], func=AF.Exp, scale=-1.0, bias=negb[:, m:m + 1]),
                                 ["pA0", "negb"], ["e_sb"])
                        P.op("act", lambda e: e.activation(out=sp_sb[:, :, :T], in_=e_sb[:, :, :T], func=AF.Ln, bias=1.0), ["e_sb"], ["sp_sb"])
                        for m in range(2):
                            d0 = ones[:, :T] if kind == "p" else resetm[:, :T]
                            P.op("dve", lambda e, m=m, d0=d0: e.tensor_tensor_scan(out=bT[:, m, :T], data0=d0, data1=sp_sb[:, m, :T], initial=0.0, op0=ALU.mult, op1=ALU.subtract),
                                 ["sp_sb", "ones", "resetm"], ["bT"])
                        P.op("act", lambda e: e.activation(out=eb[:, :, :T], in_=bT[:, :, :T], func=AF.Exp, scale=1.0 / 16), ["bT"], ["eb"])
                        P.op("act", lambda e: e.activation(out=enb[:, :, :T], in_=bT[:, :, :T], func=AF.Exp, scale=-1.0 / 16), ["bT"], ["enb"])
                        for m in range(2):
                            for hh in range(2):
                                rws = slice(hh * 64, (hh + 1) * 64)
                                P.op("dve", lambda e, m=m, hh=hh, rws=rws: e.scalar_tensor_tensor(out=qdz[rws, m, hh, :T], in0=pF[rws, m * 128:m * 128 + T], scalar=0.125, in1=eb[rws, m, :T], op0=ALU.mult, op1=ALU.mult),
                                     ["pF", "eb"], ["qdz"])
                            P.op("dve", lambda e, m=m: e.tensor_tensor(out=kd[:, m, :T], in0=pF[:, 256 + m * 128:256 + m * 128 + T], in1=enb[:, m, :T], op=ALU.mult),
                                 ["pF", "enb"], ["kd"])
                        if STOP <= 6:
                            return
                        yield
                        for h in range(4):
                            m, hh = h // 2, h % 2
                            rows = slice(hh * 64, (hh + 1) * 64)
                            P.op("pe", lambda e, h=h, m=m, hh=hh: e.matmul(pG[:T, h * 128:h * 128 + T], lhsT=kd[:, m, :T], rhs=qdz[:, m, hh, :T], start=True, stop=True),
                                 ["kd", "qdz"], ["pA0"])
                        msk = triu if kind == "p" else m32
                        P.op("dve", lambda e: e.tensor_tensor(out=AT[:T, :, :T], in0=pG[:T, :].rearrange("p (h t) -> p h t", t=128)[:, :, :T],
                                                              in1=msk[:T, :T].unsqueeze(1).to_broadcast([T, 4, T]), op=ALU.mult), ["pA0", msk.name], ["AT"])
                        if STOP <= 7:
                            return
                        yield
                        if kind == "s":
                            for s_ in range(NSB):
                                P.op("dve", lambda e, s_=s_: e.tensor_tensor(out=qdp[:, s_, :, :, :].rearrange("p m h t -> p (m h) t"), in0=qdz[:, :, :, :T].rearrange("p m h t -> p (m h) t"),
                                                                             in1=colm[:, s_, :].unsqueeze(1).to_broadcast([128, 4, TS]), op=ALU.mult),
                                     ["qdz", "colm"], ["qdp"])
                        for h in range(4):
                            m, hh = h // 2, h % 2
                            rows = slice(hh * 64, (hh + 1) * 64)
                            oo = pA[:T, 512 + h * 128:512 + (h + 1) * 128]
                            P.op("pe", lambda e, h=h, oo=oo: e.matmul(oo, lhsT=AT[:T, h, :T], rhs=gv_sb[:T, h * 128:(h + 1) * 128], start=True, stop=False),
                                 ["AT", "gv_sb"], ["pA1"])
                            if kind == "p":
                                P.op("pe", lambda e, m=m, hh=hh, oo=oo: e.matmul(oo, lhsT=qdz[:, m, hh, :T], rhs=Sg[:, m, :], start=False, stop=True),
                                     ["qdz", "Sg"], ["pA1"])
                            else:
                                for s_ in range(NSB):
                                    P.op("pe", lambda e, m=m, hh=hh, oo=oo, s_=s_: e.matmul(oo, lhsT=qdp[:, s_, m, hh, :], rhs=S0s[:, s_, m, :], start=False, stop=(s_ == NSB - 1)),
                                         ["qdp", "S0s"], ["pA1"])
                        if STOP <= 8:
                            return
                        yield
                        nseq = 1 if kind == "p" else NSB
                        for s_ in range(nseq):
                            last = T - 1 if kind == "p" else 8 * s_ + 7
                            P.op("dve", lambda e, s_=s_, last=last: e.tensor_scalar(out=bl[:, 2 * s_:2 * s_ + 2], in0=bT[:, :, last], scalar1=1.0 / 16, scalar2=None, op0=ALU.mult),
                                 ["bT"], ["bl"])
                            P.op("act", lambda e, s_=s_: e.activation(out=ebl[:, 2 * s_:2 * s_ + 2], in_=bl[:, 2 * s_:2 * s_ + 2], func=AF.Exp), ["bl"], ["ebl"])
                            for m in range(2):
                                P.op("act", lambda e, m=m, s_=s_: e.activation(out=tmpE[:, m, :T], in_=bT[:, m, :T], func=AF.Exp, scale=-1.0 / 16, bias=bl[:, 2 * s_ + m:2 * s_ + m + 1]),
                                     ["bT", "bl"], ["tmpE"])
                                P.op("dve", lambda e, m=m: e.tensor_tensor(out=kd2T[:, m, :T], in0=pF[:, 256 + m * 128:256 + m * 128 + T], in1=tmpE[:, m, :T], op=ALU.mult),
                                     ["pF", "tmpE"], ["kd2T"])
                                P.op("pe", lambda e, m=m: e.transpose(out=pG[:T, m * 128:(m + 1) * 128], in_=kd2T[:, m, :T], identity=ident[:, :]), ["kd2T", "ident"], ["pA0"])
                            if kind == "p":
                                P.op("dve", lambda e: e.tensor_copy(out=kd2[:T, :, :], in_=pG[:T, 0:256].rearrange("p (m f) -> p m f", m=2)), ["pA0"], ["kd2"])
                            else:
                                P.op("dve", lambda e, s_=s_: e.tensor_scalar(out=kd2[:T, :, :], in0=pG[:T, 0:256].rearrange("p (m f) -> p m f", m=2), scalar1=rowm[:T, s_:s_ + 1], scalar2=None, op0=ALU.mult),
                                     ["pA0", "rowm"], ["kd2"])
                            for m in range(2):
                                yield
                                P.op("pe", lambda e, m=m: e.matmul(pG[:, 256:512], lhsT=kd2[:T, m, :], rhs=gv_sb[:T, m * 256:(m + 1) * 256], start=True, stop=True),
                                     ["kd2", "gv_sb"], ["pA0"])
                                for hh in range(2):
                                    rows = slice(hh * 64, (hh + 1) * 64)
                                    if kind == "p":
                                        P.op("dve", lambda e, m=m, hh=hh, rows=rows: e.scalar_tensor_tensor(out=Sg[rows, m, :], in0=Sg[rows, m, :], scalar=ebl[rows, m:m + 1],
                                                                                                         in1=pG[rows, 256 + hh * 128:256 + (hh + 1) * 128], op0=ALU.mult, op1=ALU.add),
                                             ["Sg", "ebl", "pA0"], ["Sg"])
                                    else:
                                        P.op("dve", lambda e, m=m, hh=hh, rows=rows, s_=s_: e.scalar_tensor_tensor(out=Sns[rows, s_, m, :], in0=S0s[rows, s_, m, :], scalar=ebl[rows, 2 * s_ + m:2 * s_ + m + 1],
                                                                                                                  in1=pG[rows, 256 + hh * 128:256 + (hh + 1) * 128], op0=ALU.mult, op1=ALU.add),
                                             ["S0s", "ebl", "pA0"], ["Sns"])
                        if STOP <= 9:
                            return
                        yield
                        og = pA[:T, 512:1024]
                        P.op("act", lambda e: e.activation(out=sq[:T, 0:512], in_=og, func=AF.Square), ["pA1"], ["sq"])
                        P.op("dve", lambda e: e.tensor_reduce(out=st[:T, 24:28], in_=sq[:T, 0:512].rearrange("p (g d) -> p g d", d=128), axis=AX.X, op=ALU.add), ["sq"], ["st24"])
                        rstd_from_ss(st[:T, 24:28], 128, T, 4, "st24", None)
                        P.op("dve", lambda e: e.tensor_tensor(out=tmp[:T, :].rearrange("p (g d) -> p g d", d=128), in0=og.rearrange("p (g d) -> p g d", d=128),
                                                              in1=st[:T, 24:28].unsqueeze(2).to_broadcast([T, 4, 128]), op=ALU.mult), ["pA1", "st24"], ["tmp"])
                        P.op("dve", lambda e: e.tensor_tensor(out=tmp[:T, :].rearrange("p (g d) -> p g d", d=128), in0=tmp[:T, :].rearrange("p (g d) -> p g d", d=128),
                                                              in1=gn_bc[:T, :].unsqueeze(1).to_broadcast([T, 4, 128]), op=ALU.mult), ["tmp", "gn_bc"], ["tmp"])
                        P.op("dve", lambda e: e.tensor_tensor(out=cat[:T, 512:1024], in0=tmp[:T, :], in1=sr[:T, :], op=ALU.mult), ["tmp", "sr"], ["cat_g"])

                    def attn_normalize(T, acc, h):
                        P.op("dve", lambda e: e.reciprocal(out=st[:T, 32:34], in_=acc[:T, :, 64]), [acc.name], ["st32"])
                        P.op("dve", lambda e: e.tensor_tensor(out=st[:T, 34:35], in0=st[:T, 33:34], in1=neglam[:T, :], op=ALU.mult), ["st32", "neglam"], ["st34"])
                        P.op("dve", lambda e: e.tensor_scalar(out=o_all[:T, h, :], in0=acc[:T, 0, 0:64], scalar1=st[:T, 32:33], scalar2=None, op0=ALU.mult), [acc.name, "st32"], ["o_all"])
                        P.op("dve", lambda e: e.scalar_tensor_tensor(out=o_all[:T, h, :], in0=acc[:T, 1, 0:64], scalar=st[:T, 34:35], in1=o_all[:T, h, :], op0=ALU.mult, op1=ALU.add),
                             [acc.name, "st34", "o_all"], ["o_all"])

                    def diff_post_and_out(kind, b, i, xbuf, pAo, pTt):
                        T = 128 if kind == "p" else TS
                        P.op("act", lambda e: e.activation(out=sq[:T, 0:512], in_=o_all[:T, :, :].rearrange("p h d -> p (h d)"), func=AF.Square), ["o_all"], ["sq"])
                        P.op("dve", lambda e: e.tensor_reduce(out=st[:T, 36:44], in_=sq[:T, 0:512].rearrange("p (g d) -> p g d", d=64), axis=AX.X, op=ALU.add), ["sq"], ["st36"])
                        rstd_from_ss(st[:T, 36:44], 64, T, 8, "st36", None)
                        P.op("dve", lambda e: e.tensor_tensor(out=o_all[:T, :, :], in0=o_all[:T, :, :], in1=st[:T, 36:44].unsqueeze(2).to_broadcast([T, 8, 64]), op=ALU.mult), ["o_all", "st36"], ["o_all"])
                        P.op("dve", lambda e: e.tensor_tensor(out=cat[:T, 0:512].rearrange("p (h d) -> p h d", d=64), in0=o_all[:T, :, :], in1=dn_bc[:T, :].unsqueeze(1).to_broadcast([T, 8, 64]), op=ALU.mult),
                             ["o_all", "dn_bc"], ["cat_d"])
                        for k_ in range(8):
                            P.op("pe", lambda e, k_=k_: e.transpose(out=pTt[:, k_, :T], in_=cat[:T, k_ * 128:(k_ + 1) * 128], identity=ident_b[:T, :T]), ["cat_d", "cat_g", "ident_b"], ["pT"])
                        P.op("dve", lambda e: e.tensor_copy(out=catT[:, :, :T], in_=pTt[:, :, :T]), ["pT"], ["catT"])
                        hb = h1[cnt["h1"] % 2]; cnt["h1"] += 1
                        for half in range(2):
                            po = pAo[half]
                            for c in range(8):
                                P.op("pe", lambda e, c=c, half=half, po=po: e.matmul(po[0][:T, po[1]:po[1] + 512], lhsT=catT[:, c, :T], rhs=w_o_b[:, c, half * 512:(half + 1) * 512], start=(c == 0), stop=(c == 7)),
                                     ["catT", "w_o_b"], [po[2]])
                            P.op("dve", lambda e, half=half, po=po: e.tensor_tensor(out=hb[:T, half * 512:(half + 1) * 512], in0=po[0][:T, po[1]:po[1] + 512], in1=xbuf[:T, half * 512:(half + 1) * 512], op=ALU.add),
                                 [po[2], xbuf.name], [hb.name])
                        ti = tile_of(kind, b, i)
                        P.dma("sp", lambda e: e.dma_start(out=h1s[ti, :T, :], in_=hb[:T, :]), [hb.name], [])

                    def prompt_attention(i):
                        steps = [(h, j) for h in range(8) for j in range(i + 1)]

                        def issue_S(n):
                            h, j = steps[n]
                            sl = cnt["s"] % 2; cnt["s"] += 1
                            P.op("pe", lambda e, sl=sl, j=j, h=h: e.matmul(pSb[sl][:, 0:256], lhsT=KT[:, h, j * 128:(j + 1) * 128], rhs=QTb[:, h, :, :], start=True, stop=True),
                                 ["KT%d" % j, "QTb"], ["pS%d" % sl])
                            return sl

                        sls = {0: issue_S(0)}
                        for n, (h, j) in enumerate(steps):
                            if n + 1 < len(steps):
                                sls[n + 1] = issue_S(n + 1)
                            sl = sls.pop(n)
                            acc = pAcc[h % 2]
                            ptb = PT[cnt["pt"] % 3]; cnt["pt"] += 1
                            P.op("act", lambda e, sl=sl, ptb=ptb: e.activation(out=ptb[:, :, :].rearrange("p c q -> p (c q)"), in_=pSb[sl][:, 0:256], func=AF.Exp), ["pS%d" % sl], [ptb.name])
                            if j == i:
                                P.op("pool", lambda e, ptb=ptb: e.tensor_tensor(out=ptb[:, :, :], in0=ptb[:, :, :], in1=triu_b[:, :].unsqueeze(1).to_broadcast([128, 2, 128]), op=ALU.mult),
                                     [ptb.name, "triu_b"], [ptb.name])
                            for c in range(2):
                                P.op("pe", lambda e, c=c, j=j, h=h, acc=acc, ptb=ptb: e.matmul(acc[:, c, :], lhsT=ptb[:, c, :], rhs=Vaug[:, j, h, :], start=(j == 0 and c == 0), stop=(j == i),
                                                                                        skip_group_check=True), [ptb.name, "Vaug%d" % j], [acc.name])
                            if j == i:
                                attn_normalize(128, acc, h)
                                yield

                    pAo = [(pA, 0, "pA0"), (pA, 512, "pA1")]
                    for b in range(NPB if DO_P else 0):
                        P.op("pool", lambda e: e.memset(Sg[:], 0.0), ["Sg"], ["Sg"])
                        nxt = load_x("p", b, 0)
                        for i in range(NT_RUN):
                            xbuf = nxt
                            if i + 1 < NT_RUN:
                                nxt = load_x("p", b, i + 1)
                            gm = mixer_front("p", b, i, xbuf)
                            try:
                                next(gm)
                            except StopIteration:
                                gm = None
                            live = [g_ for g_ in (gm, prompt_attention(i) if STOP > 10 else None) if g_ is not None]
                            while live:
                                for g_ in list(live):
                                    try:
                                        next(g_)
                                    except StopIteration:
                                        live.remove(g_)
                            if STOP > 11:
                                diff_post_and_out("p", b, i, xbuf, pAo, pT)
                        P.dma("sp", lambda e, b=b: e.dma_start(out=ostp[b].rearrange("(m hh) d v -> (hh d) m v", hh=2), in_=Sg[:]), ["Sg"], [])
                    if DO_S:
                        P.dma("sp", lambda e: e.dma_start(out=S0s[:], in_=sgla.rearrange("s (m hh) d v -> (hh d) s m v", hh=2)), [], ["S0s"])
                        xsb = load_x("s", 0, 0)
                        for _ in mixer_front("s", 0, 0, xsb):
                            pass
                        P.dma("sp", lambda e: e.dma_start(out=osts.rearrange("s (m hh) d v -> (hh d) s m v", hh=2), in_=Sns[:]), ["Sns"], [])
                    P.barrier()
                    P.emit()

                with ExitStack() as SBk:
                  if DO_S:
                    sbk = lambda n, s, d=F32: SBk.enter_context(nc.sbuf_tensor(n, list(s), d))
                    pKT = SBk.enter_context(nc.psum_tensor("pKT", [128, 4, 128], F32))
                    pSsF = SBk.enter_context(nc.psum_tensor("pSs", [128, 512], F32))
                    pSs = pSsF[:, 0:128].rearrange("p (g q) -> p g q", q=8)
                    accSF = [SBk.enter_context(nc.psum_tensor("accS%d" % k_, [128, 512], F32)) for k_ in range(4)]
                    accS = [t_[0:32, 0:260].rearrange("p (a b) -> p a b", b=65) for t_ in accSF]
                    pT2 = SBk.enter_context(nc.psum_tensor("pT2", [128, 8, 128], BF16))
                    pA2 = SBk.enter_context(nc.psum_tensor("pA2", [128, 512], F32))
                    NB = 3
                    kpg = [sbk("kpg%d" % k_, [128, 512]) for k_ in range(NB)]
                    vpg = [sbk("vpg%d" % k_, [128, 512]) for k_ in range(NB)]
                    kTp = [sbk("kTp%d" % k_, [128, 4, 128], BF16) for k_ in range(2)]
                    vpa = [sbk("vpa%d" % k_, [128, 8, 65], BF16) for k_ in range(2)]
                    PTp = [[sbk("PTp%d_%d" % (s_, k_), [128, 16, 32], BF16) for k_ in range(2)] for s_ in range(NSB)]
                    Qb2 = sbk("Qb2", [128, 4, NSB, 4, 8], BF16)
                    ptf = sbk("ptf", [128, NSB * NPAGES])
                    pti = sbk("pti", [128, NSB * NPAGES], I32)
                    pidx = sbk("pidx", [128, 1])
                    zz = sbk("zz", [1, 512], BF16)
                    for s_ in range(NSB):
                        for k_ in range(2):
                            P.op("pool", lambda e, t_=PTp[s_][k_]: e.memset(t_[:], 0.0), [], [PTp[s_][k_].name])
                    for k_ in range(2):
                        P.op("pool", lambda e, k_=k_: e.memset(vpa[k_][:], 1.0), [], [vpa[k_].name])
                    P.op("pool", lambda e: e.memset(Qb2[:], 0.0), [], ["Qb2"])
                    P.op("pool", lambda e: e.memset(zz[:], 0.0), [], ["zz"])
                    for pr in range(4):
                        for hh in range(2):
                            h = 2 * pr + hh
                            for c in range(2):
                                rows_src = slice(c * 32, (c + 1) * 32)
                                P.dma("sp", lambda e, pr=pr, hh=hh, h=h, c=c, rows_src=rows_src: e.dma_start(
                                    out=Qb2[hh * 64 + c * 32:hh * 64 + (c + 1) * 32, pr, :, 2 * hh + c, :],
                                    in_=QTb[rows_src, h, c, 0:TS].rearrange("p (s q) -> p s q", q=8)), ["QTb"], ["Qb2"])
                    P.dma("sp", lambda e: e.dma_start(out=pti[:], in_=pt.rearrange("s n -> (s n)").partition_broadcast(128)), [], ["pti"])
                    P.op("dve", lambda e: e.tensor_copy(out=ptf[:], in_=pti[:]), ["pti"], ["ptf"])
                    P.dma("sp", lambda e: e.dma_start(out=pidx[:], in_=c_pidx), [], ["pidx"])
                    P.op("dve", lambda e: e.tensor_scalar(out=ptf[:], in0=ptf[:], scalar1=128.0, scalar2=pidx[:, 0:1], op0=ALU.mult, op1=ALU.add), ["ptf", "pidx"], ["ptf"])
                    P.op("dve", lambda e: e.tensor_copy(out=pti[:], in_=ptf[:]), ["ptf"], ["pti"])
                    for k_ in range(4):
                        P.op("pe", lambda e, k_=k_: e.matmul(accSF[k_][:, 0:130], lhsT=zz[0:1, 0:128], rhs=zz[0:1, 0:130], start=True, stop=False, skip_group_check=True),
                             ["zz"], ["accS"])
                    pages = [(s_, n_) for s_ in range(NSB) for n_ in range(NPG_RUN)]

                    def issue_page(k_):
                        s_, n_ = pages[k_]
                        kb, vb = kpg[k_ % NB], vpg[k_ % NB]
                        col = s_ * NPAGES + n_
                        P.dma_sw(lambda e: e.indirect_dma_start(out=kb[:], out_offset=None, in_=ck, in_offset=bass.IndirectOffsetOnAxis(ap=pti[:, col:col + 1], axis=0)), ["pti"], [kb.name])
                        P.dma_sw(lambda e: e.indirect_dma_start(out=vb[:], out_offset=None, in_=cv, in_offset=bass.IndirectOffsetOnAxis(ap=pti[:, col:col + 1], axis=0)), ["pti"], [vb.name])

                    for k_ in range(min(NB - 1, len(pages))):
                        issue_page(k_)
                    for k_, (s_, n_) in enumerate(pages):
                        if k_ + NB - 1 < len(pages):
                            issue_page(k_ + NB - 1)
                        kb, vb = kpg[k_ % NB], vpg[k_ % NB]
                        ktb, vab, ptp = kTp[k_ % 2], vpa[k_ % 2], PTp[s_][k_ % 2]
                        for pr in range(4):
                            P.op("pe", lambda e, pr=pr, kb=kb: e.transpose(out=pKT[:, pr, :], in_=kb[:, pr * 128:(pr + 1) * 128], identity=ident[:, :]), [kb.name, "ident"], ["pKT"])
                        P.op("dve", lambda e, ktb=ktb: e.tensor_copy(out=ktb[:], in_=pKT[:]), ["pKT"], [ktb.name])
                        P.op("act", lambda e, vab=vab, vb=vb: e.copy(out=vab[:, :, 0:64], in_=vb[:, :].rearrange("p (h d) -> p h d", d=64)), [vb.name], [vab.name])
                        for pr in range(4):
                            P.op("pe", lambda e, pr=pr, ktb=ktb, s_=s_: e.matmul(pSs[:, 4 * pr:4 * pr + 4, :], lhsT=ktb[:, pr, :], rhs=Qb2[:, pr, s_, :, :], start=True, stop=True),
                                 [ktb.name, "Qb2"], ["pSs"])
                        P.op("act", lambda e, ptp=ptp, s_=s_: e.activation(out=ptp[:, :, 8 * s_:8 * s_ + 8], in_=pSs[:, :, :], func=AF.Exp), ["pSs"], [ptp.name])
                        for pr in range(4):
                            P.op("pe", lambda e, pr=pr, ptp=ptp, vab=vab: e.matmul(accSF[pr][:, 0:130], lhsT=ptp[:, 4 * pr:4 * pr + 4, :].rearrange("p a b -> p (a b)"),
                                                                                   rhs=vab[:, 2 * pr:2 * pr + 2, :].rearrange("p a b -> p (a b)"), start=False, stop=False, skip_group_check=True),
                                 [ptp.name, vab.name], ["accS"])
                    PTn = sbk("PTn", [32, 8, 2, 32], BF16)
                    for h in range(8):
                        P.op("pe", lambda e, h=h: e.matmul(pA2[0:32, h * 64:(h + 1) * 64], lhsT=KTn[:, h, :], rhs=QTb[:, h, :, 0:TS], start=True, stop=True), ["KTn", "QTb"], ["pA2"])
                    P.op("act", lambda e: e.activation(out=PTn[:, :, :, :].rearrange("p h c q -> p (h c q)"), in_=pA2[0:32, 0:512], func=AF.Exp), ["pA2"], ["PTn"])
                    P.op("dve", lambda e: e.tensor_tensor(out=PTn[:, :, :, :].rearrange("p h c q -> p (h c) q"), in0=PTn[:, :, :, :].rearrange("p h c q -> p (h c) q"),
                                                          in1=m32_b[:, :].unsqueeze(1).to_broadcast([32, 16, 32]), op=ALU.mult), ["PTn", "m32_b"], ["PTn"])
                    for pr in range(4):
                        P.op("pe", lambda e, pr=pr: e.matmul(accSF[pr][:, 0:130], lhsT=PTn[:, 2 * pr:2 * pr + 2, :, :].rearrange("p h c q -> p (h c q)"),
                                                             rhs=Vnew[:, 2 * pr:2 * pr + 2, :].rearrange("p a b -> p (a b)"), start=False, stop=True, skip_group_check=True),
                             ["PTn", "Vnew"], ["accS"])
                    accsb = [sbk("accsb%d" % k_, [128, 130]) for k_ in range(4)]
                    for pr in range(4):
                        P.op("dve", lambda e, pr=pr: e.tensor_copy(out=accsb[pr][:], in_=accSF[pr][:, 0:130]), ["accS"], [accsb[pr].name])
                    for pr in range(4):
                        for k_ in range(4):
                            hh_ = k_ // 2
                            P.op("pe", lambda e, pr=pr, k_=k_, hh_=hh_: e.matmul(accSF[pr][0:32, k_ * 65:(k_ + 1) * 65], lhsT=ident[:, 32 * k_:32 * k_ + 32], rhs=accsb[pr][:, hh_ * 65:(hh_ + 1) * 65], start=True, stop=True,
                                                                             skip_group_check=True), ["ident", accsb[pr].name], ["accS"])
                    for h in range(8):
                        a_ = accS[h // 2]
                        base = 2 * (h % 2)
                        P.op("dve", lambda e, a_=a_, base=base: e.reciprocal(out=st[:TS, 32:34], in_=a_[:, base:base + 2, 64]), ["accS"], ["st32"])
                        P.op("dve", lambda e: e.tensor_tensor(out=st[:TS, 34:35], in0=st[:TS, 33:34], in1=neglam[:TS, :], op=ALU.mult), ["st32", "neglam"], ["st34"])
                        P.op("dve", lambda e, a_=a_, base=base, h=h: e.tensor_scalar(out=o_all[:TS, h, :], in0=a_[:, base, 0:64], scalar1=st[:TS, 32:33], scalar2=None, op0=ALU.mult), ["accS", "st32"], ["o_all"])
                        P.op("dve", lambda e, a_=a_, base=base, h=h: e.scalar_tensor_tensor(out=o_all[:TS, h, :], in0=a_[:, base + 1, 0:64], scalar=st[:TS, 34:35], in1=o_all[:TS, h, :], op0=ALU.mult, op1=ALU.add),
                             ["accS", "st34", "o_all"], ["o_all"])
                    diff_post_and_out("s", 0, 0, xsb, [(pA2, 0, "pA2"), (pA2, 0, "pA2")], pT2)
                    P.barrier()
                    P.emit()

        S2p.close()
        if "3" in PH:
            S3 = ExitStack()
            s3 = lambda n, s, d=F32: S3.enter_context(nc.sbuf_tensor(n, list(s), d))
            cwq_b = s3("cwq_b", [128, 8, D], BF16)
            cwo_b = s3("cwo_b", [128, 8, D], BF16)
            pwq_b = s3("pwq_b", [128, 8, 2 * D], BF16)
            skT_b = s3("skT_b", [128, 16, 128], BF16)
            mkT = s3("mkT", [128, 4, 8, 256], BF16)
            mva = s3("mva", [128, 4, 2, 4, 257], BF16)
            g_cross = s3("g_cross", [128, 8]); g_mem = s3("g_mem", [128, 8]); g_ffn = s3("g_ffn", [128, 8])
            cqn_bc = s3("cqn_bc", [128, 256]); ckn_bc = s3("ckn_bc", [128, 256]); pqn_bc = s3("pqn_bc", [128, 256])
            for (t_, src) in ((g_cross, cross_norm), (g_mem, mem_norm), (g_ffn, ffn_norm)):
                P.dma("sp", lambda e, t_=t_, src=src: e.dma_start(out=t_[:], in_=src.rearrange("(c p) -> p c", p=128), allow_slow_non_contiguous=True), [], [t_.name])
            for (t_, src) in ((cqn_bc, cross_q_norm), (ckn_bc, cross_k_norm), (pqn_bc, peer_q_norm)):
                P.dma("sp", lambda e, t_=t_, src=src: e.dma_start(out=t_[:], in_=src.partition_broadcast(128)), [], [t_.name])
            P.op("dve", lambda e: e.tensor_scalar(out=cqn_bc[:], in0=cqn_bc[:], scalar1=1.0 / 16.0, scalar2=None, op0=ALU.mult), ["cqn_bc"], ["cqn_bc"])
            P.op("pool", lambda e: e.memset(mva[:], 1.0), [], ["mva"])

            with ExitStack() as SP1:
                s1 = lambda n, s, d=F32: SP1.enter_context(nc.sbuf_tensor(n, list(s), d))
                wkv_b = s1("wkv_b", [128, 8, 2 * D], BF16)
                stg = [s1("stg3_%d" % k_, [128, 2 * D]) for k_ in range(2)]
                xm = [s1("xm%d" % k_, [128, D]) for k_ in range(2)]
                xms_b = s1("xms_b", [128, D], BF16)
                xmT = s1("xmT", [128, 8, 128], BF16)
                sq1 = s1("sq1", [128, D])
                tq1 = s1("tq1", [128, D])
                mkf = [s1("mkf%d" % k_, [128, D]) for k_ in range(2)]
                mk_b = s1("mk_b", [128, D], BF16)
                mvf = [s1("mvf%d" % k_, [128, D]) for k_ in range(2)]
                st1 = s1("st1", [128, 16])
                pA = SP1.enter_context(nc.psum_tensor("pA3", [128, 1024], F32))
                pB = SP1.enter_context(nc.psum_tensor("pB3", [128, 1024], F32))
                pT = SP1.enter_context(nc.psum_tensor("pT3", [128, 8, 128], BF16))
                pTf = SP1.enter_context(nc.psum_tensor("pTf3", [128, 512], F32))
                wl = [(cross_w_kv, 2 * D, wkv_b, g_mem), (cross_w_q, D, cwq_b, g_cross), (cross_w_o, D, cwo_b, None), (peer_w_q, 2 * D, pwq_b, g_ffn)]
                k2 = 0
                for (wsrc, wn, wdst, gg) in wl:
                    for c in range(8):
                        s_ = stg[k2 % 2]; k2 += 1
                        P.dma("sp", lambda e, c=c, s_=s_, wsrc=wsrc, wn=wn: e.dma_start(out=s_[:, 0:wn], in_=wsrc[c * 128:(c + 1) * 128, :]), [], [s_.name])
                        if gg is None:
                            act_copy(wdst[:, c, :], s_[:, 0:wn], [s_.name], [wdst.name], eng=("dve" if c % 2 == 0 else "act"))
                        elif c % 2 == 0:
                            P.op("dve", lambda e, c=c, s_=s_, wn=wn, wdst=wdst, gg=gg: e.tensor_scalar(out=wdst[:, c, :], in0=s_[:, 0:wn], scalar1=gg[:, c:c + 1], scalar2=None, op0=ALU.mult),
                                 [s_.name, gg.name], [wdst.name])
                        else:
                            P.op("act", lambda e, c=c, s_=s_, wn=wn, wdst=wdst, gg=gg: e.activation(out=wdst[:, c, :], in_=s_[:, 0:wn], func=AF.Copy, scale=gg[:, c:c + 1]),
                                 [s_.name, gg.name], [wdst.name])
                for g4 in range(4):
                    s_ = stg[k2 % 2]; k2 += 1
                    P.dma("sp", lambda e, g4=g4, s_=s_: e.dma_start(out=s_[:, 0:512].rearrange("p (g d) -> p g d", d=128), in_=peer_sk[4 * g4:4 * g4 + 4].rearrange("g k d -> k g d")), [], [s_.name])
                    for g_ in range(4):
                        P.op("pe", lambda e, g_=g_, s_=s_: e.transpose(out=pTf[:, g_ * 128:(g_ + 1) * 128], in_=s_[:, g_ * 128:(g_ + 1) * 128], identity=ident[:, :]), [s_.name, "ident"], ["pTf"])
                    P.op("dve", lambda e, g4=g4: e.tensor_copy(out=skT_b[:, 4 * g4:4 * g4 + 4, :], in_=pTf[:, :].rearrange("p (g k) -> p g k", k=128)), ["pTf"], ["skT_b"])

                def mem_set(kind, b):
                    slot = b
                    for mt in range(2):
                        rows = slice(mt * 128, (mt + 1) * 128)
                        mkb = mkf[mt]; mvb = mvf[mt]
                        if kind == "p":
                            xb = xm[mt]
                            P.dma("sp", lambda e, xb=xb, rows=rows: e.dma_start(out=xb[:], in_=memp[b, rows, :]), [], [xb.name])
                            P.op("act", lambda e, xb=xb: e.activation(out=sq1[:], in_=xb[:], func=AF.Square, accum_out=st1[:, 0:1]), [xb.name], ["sq1", "st1a"])
                            rstd_from_ss(st1[:, 0:1], D, 128, 1, "st1a", None)
                            P.op("act", lambda e, xb=xb: e.activation(out=xms_b[:], in_=xb[:], func=AF.Copy, scale=st1[:, 0:1]), [xb.name, "st1a"], ["xms_b"])
                            for k_ in range(8):
                                P.op("pe", lambda e, k_=k_: e.transpose(out=pT[:, k_, :], in_=xms_b[:, k_ * 128:(k_ + 1) * 128], identity=ident_b[:, :]), ["xms_b", "ident_b"], ["pT3"])
                            P.op("dve", lambda e: e.tensor_copy(out=xmT[:], in_=pT[:]), ["pT3"], ["xmT"])
                            for grp in range(4):
                                po = pA if grp < 2 else pB
                                pon = "pA3" if grp < 2 else "pB3"
                                for c in range(8):
                                    P.op("pe", lambda e, c=c, grp=grp, po=po: e.matmul(po[:, (grp % 2) * 512:(grp % 2 + 1) * 512], lhsT=xmT[:, c, :], rhs=wkv_b[:, c, grp * 512:(grp + 1) * 512], start=(c == 0), stop=(c == 7)),
                                         ["xmT", "wkv_b"], [pon])
                            P.op("act", lambda e: e.activation(out=sq1[:], in_=pA[:], func=AF.Square), ["pA3"], ["sq1"])
                            P.op("dve", lambda e: e.tensor_reduce(out=st1[:, 4:8], in_=sq1[:].rearrange("p (g d) -> p g d", d=256), axis=AX.X, op=ALU.add), ["sq1"], ["st1b"])
                            rstd_from_ss(st1[:, 4:8], 256, 128, 4, "st1b", None)
                            P.op("dve", lambda e: e.tensor_tensor(out=tq1[:].rearrange("p (g d) -> p g d", d=256), in0=pA[:].rearrange("p (g d) -> p g d", d=256),
                                                                  in1=st1[:, 4:8].unsqueeze(2).to_broadcast([128, 4, 256]), op=ALU.mult), ["pA3", "st1b"], ["tq1"])
                            P.op("dve", lambda e, mkb=mkb: e.tensor_tensor(out=mkb[:].rearrange("p (g d) -> p g d", d=256), in0=tq1[:].rearrange("p (g d) -> p g d", d=256),
                                                                           in1=ckn_bc[:, :].unsqueeze(1).to_broadcast([128, 4, 256]), op=ALU.mult), ["tq1", "ckn_bc"], [mkb.name])
                            P.dma("sp", lambda e, mkb=mkb, rows=rows: e.dma_start(out=omk[b, rows, :], in_=mkb[:]), [mkb.name], [])
                            P.op("act", lambda e, mvb=mvb: e.copy(out=mvb[:], in_=pB[:]), ["pB3"], [mvb.name])
                            P.dma("sp", lambda e, mvb=mvb, rows=rows: e.dma_start(out=omv[b, rows, :], in_=mvb[:]), [mvb.name], [])
                        else:
                            P.dma("sp", lambda e, mkb=mkb, rows=rows: e.dma_start(out=mkb[:], in_=cmk[b, rows, :]), [], [mkb.name])
                            P.dma("sp", lambda e, mvb=mvb, rows=rows: e.dma_start(out=mvb[:], in_=cmv[b, rows, :]), [], [mvb.name])
                        P.op("act", lambda e, mkb=mkb: e.copy(out=mk_b[:], in_=mkb[:]), [mkb.name], ["mk_b"])
                        for k_ in range(8):
                            P.op("pe", lambda e, k_=k_: e.transpose(out=pT[:, k_, :], in_=mk_b[:, k_ * 128:(k_ + 1) * 128], identity=ident_b[:, :]), ["mk_b", "ident_b"], ["pT3"])
                        P.op("dve", lambda e, rows=rows: e.tensor_copy(out=mkT[:, slot, :, rows], in_=pT[:]), ["pT3"], ["mkT%d" % slot])
                        P.op("pool", lambda e, mvb=mvb, mt=mt: e.tensor_copy(out=mva[:, slot, mt, :, 0:256], in_=mvb[:].rearrange("p (h d) -> p h d", d=256)), [mvb.name, "mva"], ["mva%d" % slot])

                for b in range(NPB):
                    mem_set("p", b)
                P.barrier()
                P.emit()

            with ExitStack() as SW3:
                sw = lambda n, s, d=F32: SW3.enter_context(nc.sbuf_tensor(n, list(s), d))
                hx = [sw("hx%d" % k_, [128, D]) for k_ in range(2)]
                hs_b = sw("hs_b", [128, D], BF16)
                hnT = sw("hnT", [128, 8, 128], BF16)
                sq = sw("sq3", [128, D])
                tq = sw("tq3", [128, D])
                qn_b = sw("qn3_b", [128, D], BF16)
                qnT = sw("qnT", [128, 8, 128], BF16)
                PTc = [sw("PTc%d" % k_, [128, 2, 128], BF16) for k_ in range(2)]
                PTz = [sw("PTz%d" % k_, [128, 2, 32], BF16) for k_ in range(NSB)]
                co = sw("co", [128, D], BF16)
                coT = sw("coT", [128, 8, 128], BF16)
                h2b = [sw("h2b%d" % k_, [128, D]) for k_ in range(2)]
                h2s_b = sw("h2s_b", [128, D], BF16)
                hn2T = [sw("hn2T%d" % k_, [128, 8, 128], BF16) for k_ in range(2)]
                pqn_b = sw("pqn_b", [128, 2 * D], BF16)
                pqT = sw("pqT", [128, 16, 128], BF16)
                scs = [sw("sc%d" % k_, [128, 16, 128]) for k_ in range(2)]
                m16 = sw("m16", [128, 16, 16]); ix = sw("ix", [128, 16, 16], U32); ixf = sw("ixf", [128, 16, 16])
                wk = sw("wk", [128, 128]); wk2 = sw("wk2", [128, 256])
                cand = sw("cand", [128, 8, 256])
                cm = sw("cm", [128, 8, 16]); cj = sw("cj", [128, 8, 16], U32)
                ca = sw("ca", [128, 8, 16], U32); cb = sw("cb", [128, 8, 16], U32)
                caf = sw("caf", [128, 8, 16]); cbf = sw("cbf", [128, 8, 16])
                ce = sw("ce", [128, 8, 16])
                oh = sw("oh", [128, 8, 256])
                rtk = sw("rtk", [128, 3, 128])
                rt = [sw("rt%d" % k_, [128, 3, 128]) for k_ in range(2)]
                st = sw("st3", [128, 64])
                mkf2 = [sq, tq]
                mk_b2 = sw("mk_b2", [128, D], BF16)
                for k_ in range(NSB):
                    P.op("pool", lambda e, k_=k_: e.memset(PTz[k_][:], 0.0), [], [PTz[k_].name])

                with ExitStack() as SA3:
                    pA = SA3.enter_context(nc.psum_tensor("pA4", [128, 1024], F32))
                    pB = SA3.enter_context(nc.psum_tensor("pB4", [128, 1024], F32))
                    pT = SA3.enter_context(nc.psum_tensor("pT4", [128, 8, 128], BF16))
                    pS = SA3.enter_context(nc.psum_tensor("pS4", [128, 2, 256], F32))
                    pO = [SA3.enter_context(nc.psum_tensor("pO4_%d" % k_, [128, 512], F32)) for k_ in range(2)]
                    c3 = {"hx": 0, "h2": 0, "hn": 0, "pt": 0, "rt": 0}

                    def load_h1(ti, T):
                        buf = hx[c3["hx"] % 2]; c3["hx"] += 1
                        P.dma("sp", lambda e: e.dma_start(out=buf[:T, :], in_=h1s[ti, :T, :]), [], [buf.name])
                        return buf

                    def norm_T(src, srcn, T, dst_b, dstT, dstn):
                        P.op("act", lambda e: e.activation(out=sq[:T, :], in_=src[:T, :], func=AF.Square, accum_out=st[:T, 0:1]), [srcn], ["sq3", "st0"])
                        rstd_from_ss(st[:T, 0:1], D, T, 1, "st0", None)
                        P.op("act", lambda e: e.activation(out=dst_b[:T, :], in_=src[:T, :], func=AF.Copy, scale=st[:T, 0:1]), [srcn, "st0"], [dst_b.name])
                        tr8(dst_b, T, dstT, dstn, 0)

                    def tr8(src_b, T, dstT, dstn, k0):
                        for k_ in range(8):
                            P.op("pe", lambda e, k_=k_: e.transpose(out=pT[:, k_, :T], in_=src_b[:T, (k0 + k_) * 128:(k0 + k_ + 1) * 128], identity=ident_b[:T, :T]), [src_b.name, "ident_b"], ["pT4"])
                        P.op("dve", lambda e: e.tensor_copy(out=dstT[:, k0:k0 + 8, :T], in_=pT[:, :, :T]), ["pT4"], [dstn])

                    def headnorm(T, src, srcn, nh, gain, out_ap, outn, stoff):
                        w = nh * 256
                        P.op("act", lambda e: e.activation(out=sq[:T, 0:w], in_=src, func=AF.Square), [srcn], ["sq3"])
                        P.op("dve", lambda e: e.tensor_reduce(out=st[:T, stoff:stoff + nh], in_=sq[:T, 0:w].rearrange("p (g d) -> p g d", d=256), axis=AX.X, op=ALU.add), ["sq3"], ["st%d" % stoff])
                        rstd_from_ss(st[:T, stoff:stoff + nh], 256, T, nh, "st%d" % stoff, None)
                        P.op("dve", lambda e: e.tensor_tensor(out=tq[:T, 0:w].rearrange("p (g d) -> p g d", d=256), in0=src.rearrange("p (g d) -> p g d", d=256),
                                                              in1=st[:T, stoff:stoff + nh].unsqueeze(2).to_broadcast([T, nh, 256]), op=ALU.mult), [srcn, "st%d" % stoff], ["tq3"])
                        P.op("dve", lambda e: e.tensor_tensor(out=out_ap.rearrange("p (g d) -> p g d", d=256), in0=tq[:T, 0:w].rearrange("p (g d) -> p g d", d=256),
                                                              in1=gain[:T, :].unsqueeze(1).to_broadcast([T, nh, 256]), op=ALU.mult), ["tq3", gain.name], [outn])

                    def tile3A(kind, ti, T, sets, hxb, sc):
                        hn_ = hxb.name
                        norm_T(hxb, hn_, T, hs_b, hnT, "hnT")
                        yield
                        for half in range(2):
                            for c in range(8):
                                P.op("pe", lambda e, c=c, half=half: e.matmul(pA[:T, half * 512:(half + 1) * 512], lhsT=hnT[:, c, :T], rhs=cwq_b[:, c, half * 512:(half + 1) * 512], start=(c == 0), stop=(c == 7)),
                                     ["hnT", "cwq_b"], ["pA4"])
                        headnorm(T, pA[:T, :], "pA4", 4, cqn_bc, qn_b[:T, :], "qn3_b", 4)
                        tr8(qn_b, T, qnT, "qnT", 0)
                        yield
                        for h in range(4):
                            po = pO[h % 2]
                            pon = "pO4_%d" % (h % 2)
                            nmm = 2 * len(sets)
                            kmm = 0
                            for (slot, c0, c1) in sets:
                                for mb in range(2):
                                    for k2_ in range(2):
                                        P.op("pe", lambda e, mb=mb, k2_=k2_, slot=slot, h=h: e.matmul(pS[:, mb, :T], lhsT=mkT[:, slot, 2 * h + k2_, mb * 128:(mb + 1) * 128], rhs=qnT[:, 2 * h + k2_, :T],
                                                                                                   start=(k2_ == 0), stop=(k2_ == 1)), ["mkT%d" % slot, "qnT"], ["pS4"])
                                if kind == "p":
                                    ptb = PTc[c3["pt"] % 2]; c3["pt"] += 1
                                    P.op("act", lambda e, ptb=ptb: e.activation(out=ptb[:, :, :T], in_=pS[:, :, :T], func=AF.Exp), ["pS4"], [ptb.name])
                                else:
                                    ptb = PTz[slot]
                                    P.op("act", lambda e, ptb=ptb, c0=c0, c1=c1: e.activation(out=ptb[:, :, c0:c1], in_=pS[:, :, c0:c1], func=AF.Exp), ["pS4"], [ptb.name])
                                for mb in range(2):
                                    P.op("pe", lambda e, mb=mb, slot=slot, h=h, ptb=ptb, po=po, kmm=kmm, nmm=nmm: e.matmul(po[:T, 0:257], lhsT=ptb[:, mb, :T], rhs=mva[:, slot, mb, h, :],
                                                                                                                       start=(kmm == 0), stop=(kmm == nmm - 1)), [ptb.name, "mva%d" % slot], [pon])
                                    kmm += 1
                            P.op("dve", lambda e, po=po: e.reciprocal(out=st[:T, 12:13], in_=po[:T, 256:257]), [pon], ["st12"])
                            P.op("dve", lambda e, po=po, h=h: e.tensor_scalar(out=co[:T, h * 256:(h + 1) * 256], in0=po[:T, 0:256], scalar1=st[:T, 12:13], scalar2=None, op0=ALU.mult), [pon, "st12"], ["co"])
                            yield
                        tr8(co, T, coT, "coT", 0)
                        h2 = h2b[c3["h2"] % 2]; c3["h2"] += 1
                        for half in range(2):
                            for c in range(8):
                                P.op("pe", lambda e, c=c, half=half: e.matmul(pA[:T, half * 512:(half + 1) * 512], lhsT=coT[:, c, :T], rhs=cwo_b[:, c, half * 512:(half + 1) * 512], start=(c == 0), stop=(c == 7)),
                                     ["coT", "cwo_b"], ["pA4"])
                        P.op("dve", lambda e: e.tensor_tensor(out=h2[:T, :], in0=pA[:T, :], in1=hxb[:T, :], op=ALU.add), ["pA4", hn_], [h2.name])
                        P.dma("sp", lambda e: e.dma_start(out=h2s[ti, :T, :], in_=h2[:T, :]), [h2.name], [])
                        yield
                        h2T = hn2T[c3["hn"] % 2]; c3["hn"] += 1
                        norm_T(h2, h2.name, T, h2s_b, h2T, h2T.name)
                        P.dma("sp", lambda e: e.dma_start(out=hnTs[ti].rearrange("p (c t) -> p c t", t=128)[:, :, :T], in_=h2T[:, :, :T]), [h2T.name], [])
                        yield
                        for grp in range(4):
                            po2 = pA if grp < 2 else pB
                            pon2 = "pA4" if grp < 2 else "pB4"
                            for c in range(8):
                                P.op("pe", lambda e, c=c, grp=grp, po2=po2: e.matmul(po2[:T, (grp % 2) * 512:(grp % 2 + 1) * 512], lhsT=h2T[:, c, :T], rhs=pwq_b[:, c, grp * 512:(grp + 1) * 512], start=(c == 0), stop=(c == 7)),
                                     [h2T.name, "pwq_b"], [pon2])
                        yield
                        headnorm(T, pA[:T, :], "pA4", 4, pqn_bc, pqn_b[:T, 0:D], "pqn_b", 16)
                        yield
                        headnorm(T, pB[:T, :], "pB4", 4, pqn_bc, pqn_b[:T, D:2 * D], "pqn_b", 20)
                        yield
                        tr8(pqn_b, T, pqT, "pqT", 0)
                        tr8(pqn_b, T, pqT, "pqT", 8)
                        for g_ in range(16):
                            po2 = pA if g_ < 8 else pB
                            pon2 = "pA4" if g_ < 8 else "pB4"
                            P.op("pe", lambda e, g_=g_, po2=po2: e.matmul(po2[:T, (g_ % 8) * 128:(g_ % 8 + 1) * 128], lhsT=pqT[:, g_, :T], rhs=skT_b[:, g_, :], start=True, stop=True), ["pqT", "skT_b"], [pon2])
                        P.op("act", lambda e: e.copy(out=sc[:T, 0:8, :], in_=pA[:T, :].rearrange("p (g k) -> p g k", k=128)), ["pA4"], [sc.name])
                        P.op("act", lambda e: e.copy(out=sc[:T, 8:16, :], in_=pB[:T, :].rearrange("p (g k) -> p g k", k=128)), ["pB4"], [sc.name])
                        yield

                    def tile3B(ti, T, sc):
                        for g_ in range(16):
                            if g_ % 2 == 0 and g_ > 0:
                                yield
                            P.op("dve", lambda e, g_=g_: e.max(out=m16[:T, g_, 0:8], in_=sc[:T, g_, :]), [sc.name], ["m16"])
                            P.op("dve", lambda e, g_=g_: e.max_index(out=ix[:T, g_, 0:8], in_max=m16[:T, g_, 0:8], in_values=sc[:T, g_, :]), [sc.name, "m16"], ["ix"])
                            P.op("dve", lambda e, g_=g_: e.match_replace(out=wk[:T, :], in_to_replace=m16[:T, g_, 0:8], in_values=sc[:T, g_, :], imm_value=-1e30), [sc.name, "m16"], ["wk"])
                            P.op("dve", lambda e, g_=g_: e.max(out=m16[:T, g_, 8:16], in_=wk[:T, :]), ["wk"], ["m16"])
                            P.op("dve", lambda e, g_=g_: e.max_index(out=ix[:T, g_, 8:16], in_max=m16[:T, g_, 8:16], in_values=wk[:T, :]), ["wk", "m16"], ["ix"])
                        yield
                        m16v = m16[:T, :, :].rearrange("p (h c) k -> p h c k", c=2)
                        P.op("dve", lambda e: e.tensor_tensor(out=cand[:T, :, :].rearrange("p h (a b) -> p h a b", b=16), in0=m16v[:, :, 0, :].unsqueeze(3).to_broadcast([T, 8, 16, 16]),
                                                              in1=m16v[:, :, 1, :].unsqueeze(2).to_broadcast([T, 8, 16, 16]), op=ALU.add), ["m16"], ["cand"])
                        for h in range(8):
                            if h % 2 == 0:
                                yield
                            P.op("dve", lambda e, h=h: e.max(out=cm[:T, h, 0:8], in_=cand[:T, h, :]), ["cand"], ["cm"])
                            P.op("dve", lambda e, h=h: e.max_index(out=cj[:T, h, 0:8], in_max=cm[:T, h, 0:8], in_values=cand[:T, h, :]), ["cand", "cm"], ["cj"])
                            P.op("dve", lambda e, h=h: e.match_replace(out=wk2[:T, :], in_to_replace=cm[:T, h, 0:8], in_values=cand[:T, h, :], imm_value=-1e30), ["cand", "cm"], ["wk2"])
                            P.op("dve", lambda e, h=h: e.max(out=cm[:T, h, 8:16], in_=wk2[:T, :]), ["wk2"], ["cm"])
                            P.op("dve", lambda e, h=h: e.max_index(out=cj[:T, h, 8:16], in_max=cm[:T, h, 8:16], in_values=wk2[:T, :]), ["wk2", "cm"], ["cj"])
                        yield
                        P.op("dve", lambda e: e.tensor_tensor(out=ce[:T, :, :], in0=cm[:T, :, :], in1=cm[:T, :, 0:1].to_broadcast([T, 8, 16]), op=ALU.subtract), ["cm"], ["ce"])
                        P.op("act", lambda e: e.activation(out=ce[:T, :, :], in_=ce[:T, :, :], func=AF.Exp), ["ce"], ["ce"])
                        P.op("dve", lambda e: e.tensor_reduce(out=st[:T, 24:32], in_=ce[:T, :, :], axis=AX.X, op=ALU.add), ["ce"], ["st24"])
                        P.op("dve", lambda e: e.reciprocal(out=st[:T, 24:32], in_=st[:T, 24:32]), ["st24"], ["st24"])
                        P.op("dve", lambda e: e.tensor_tensor(out=rtk[:T, 2, :].rearrange("p (h k) -> p h k", k=16), in0=ce[:T, :, :], in1=st[:T, 24:32].unsqueeze(2).to_broadcast([T, 8, 16]), op=ALU.mult),
                             ["ce", "st24"], ["rtk"])
                        yield
                        P.op("dve", lambda e: e.tensor_single_scalar(out=ca[:T, :, :], in_=cj[:T, :, :], scalar=4, op=ALU.logical_shift_right), ["cj"], ["ca"])
                        P.op("dve", lambda e: e.tensor_single_scalar(out=cb[:T, :, :], in_=cj[:T, :, :], scalar=15, op=ALU.bitwise_and), ["cj"], ["cb"])
                        P.op("dve", lambda e: e.tensor_copy(out=caf[:T, :, :], in_=ca[:T, :, :]), ["ca"], ["caf"])
                        P.op("dve", lambda e: e.tensor_copy(out=cbf[:T, :, :], in_=cb[:T, :, :]), ["cb"], ["cbf"])
                        P.op("dve", lambda e: e.tensor_copy(out=ixf[:T, :, :], in_=ix[:T, :, :]), ["ix"], ["ixf"])
                        ixv = ixf[:T, :, :].rearrange("p (h c) k -> p h c k", c=2)
                        io16 = iota_t[:T, 0:16].unsqueeze(1).unsqueeze(1).to_broadcast([T, 8, 16, 16])
                        for (sel, cxf, dsti) in ((0, caf, 0), (1, cbf, 1)):
                            yield
                            P.op("dve", lambda e, cxf=cxf: e.tensor_tensor(out=oh[:T, :, :].rearrange("p h (k a) -> p h k a", a=16), in0=cxf[:T, :, :].unsqueeze(3).to_broadcast([T, 8, 16, 16]), in1=io16, op=ALU.is_equal),
                                 [cxf.name, "iota_t"], ["oh"])
                            P.op("dve", lambda e, sel=sel: e.tensor_tensor(out=oh[:T, :, :].rearrange("p h (k a) -> p h k a", a=16), in0=oh[:T, :, :].rearrange("p h (k a) -> p h k a", a=16),
                                                                           in1=ixv[:, :, sel, :].unsqueeze(2).to_broadcast([T, 8, 16, 16]), op=ALU.mult), ["oh", "ixf"], ["oh"])
                            P.op("dve", lambda e, dsti=dsti: e.tensor_reduce(out=rtk[:T, dsti, :], in_=oh[:T, :, :].rearrange("p h (k a) -> p (h k) a", a=16), axis=AX.X, op=ALU.add), ["oh"], ["rtk"])
                        yield
                        rtb = rt[c3["rt"] % 2]; c3["rt"] += 1
                        pR = pS
                        for k_ in range(3):
                            P.op("pe", lambda e, k_=k_: e.transpose(out=pR[:, :, :].rearrange("p a b -> p (a b)")[:, k_ * 128:k_ * 128 + T], in_=rtk[:T, k_, :], identity=ident[:T, :T]), ["rtk", "ident"], ["pS4"])
                        P.op("dve", lambda e, rtb=rtb: e.tensor_copy(out=rtb[:, :, :T], in_=pR[:, :, :].rearrange("p a b -> p (a b)")[:, 0:384].rearrange("p (k t) -> p k t", t=128)[:, :, :T]), ["pS4"], [rtb.name])
                        for k_, dst in enumerate((r1s, r2s, rws)):
                            P.dma("sp", lambda e, k_=k_, dst=dst, rtb=rtb: e.dma_start(out=dst[ti, :, :T], in_=rtb[:, k_, :T]), [rtb.name], [])

                    tiles = [("p", b, i) for b in range(NPB if DO_P else 0) for i in range(NT_RUN)]
                    nxt = load_h1(tile_of(*tiles[0]), 128) if tiles else None
                    def interleave(ga, gb):
                        live = [g for g in (ga, gb) if g is not None]
                        while live:
                            for g in list(live):
                                try:
                                    next(g)
                                except StopIteration:
                                    live.remove(g)

                    prevB = None
                    nsc = 0
                    for n_, (kind, b, i) in enumerate(tiles):
                        cur = nxt
                        if n_ + 1 < len(tiles):
                            nxt = load_h1(tile_of(*tiles[n_ + 1]), 128)
                        scb = scs[nsc % 2]; nsc += 1
                        interleave(tile3A("p", tile_of(kind, b, i), 128, [(b, 0, 128)], cur, scb), tile3B(*prevB) if prevB else None)
                        prevB = (tile_of(kind, b, i), 128, scb)
                    if DO_S:
                        for s_ in range(NSB):
                            for mt in range(2):
                                rows = slice(mt * 128, (mt + 1) * 128)
                                mkb = mkf2[mt]
                                P.dma("sp", lambda e, mkb=mkb, rows=rows, s_=s_: e.dma_start(out=mkb[:], in_=cmk[s_, rows, :]), [], [mkb.name])
                                P.op("act", lambda e, mkb=mkb: e.copy(out=mk_b2[:], in_=mkb[:]), [mkb.name], ["mk_b2"])
                                for k_ in range(8):
                                    P.op("pe", lambda e, k_=k_: e.transpose(out=pT[:, k_, :], in_=mk_b2[:, k_ * 128:(k_ + 1) * 128], identity=ident_b[:, :]), ["mk_b2", "ident_b"], ["pT4"])
                                P.op("dve", lambda e, rows=rows, s_=s_: e.tensor_copy(out=mkT[:, s_, :, rows], in_=pT[:]), ["pT4"], ["mkT%d" % s_])
                                mvb = hx[mt]
                                P.dma("sp", lambda e, mvb=mvb, rows=rows, s_=s_: e.dma_start(out=mvb[:], in_=cmv[s_, rows, :]), [], [mvb.name])
                                P.op("pool", lambda e, mvb=mvb, mt=mt, s_=s_: e.tensor_copy(out=mva[:, s_, mt, :, 0:256], in_=mvb[:].rearrange("p (h d) -> p h d", d=256)), [mvb.name, "mva"], ["mva%d" % s_])
                        c3["hx"] = 0
                        cur = load_h1(NPB * NT, TS)
                        scb = scs[nsc % 2]; nsc += 1
                        interleave(tile3A("s", NPB * NT, TS, [(s_, 8 * s_, 8 * s_ + 8) for s_ in range(NSB)], cur, scb), tile3B(*prevB) if prevB else None)
                        prevB = (NPB * NT, TS, scb)
                    if prevB:
                        interleave(tile3B(*prevB), None)
                    P.barrier()
                    P.emit()
            S3.close()

        if "4" in PH:
            with ExitStack() as S4:
                s4 = lambda n, s, d=F32: S4.enter_context(nc.sbuf_tensor(n, list(s), d))
                g_ffn4 = s4("g_ffn4", [128, 8])
                P.dma("sp", lambda e: e.dma_start(out=g_ffn4[:], in_=ffn_norm.rearrange("(c p) -> p c", p=128), allow_slow_non_contiguous=True), [], ["g_ffn4"])
                NJ = cfg.get("nj", 128)
                with ExitStack() as S4a:
                    s4a = lambda n, s, d=F32: S4a.enter_context(nc.sbuf_tensor(n, list(s), d))
                    ub = [s4a("ub%d" % k_, [128, D]) for k_ in range(3)]
                    vb4 = [s4a("vb4_%d" % k_, [128, D]) for k_ in range(3)]
                    uTb = [s4a("uTb%d" % k_, [128, 8, 128], BF16) for k_ in range(2)]
                    vbb = [s4a("vbb%d" % k_, [128, D], BF16) for k_ in range(2)]
                    pU = [S4a.enter_context(nc.psum_tensor("pU%d" % k_, [128, 1024], F32)) for k_ in range(2)]

                    def ld4a(j):
                        P.dma("sp", lambda e: e.dma_start(out=ub[j % 3][:], in_=peer_u[j * 128:(j + 1) * 128, :]), [], [ub[j % 3].name])
                        P.dma("sp", lambda e: e.dma_start(out=vb4[j % 3][:], in_=peer_v[j * 128:(j + 1) * 128, :]), [], [vb4[j % 3].name])

                    for j in range(min(2, NJ)):
                        ld4a(j)
                    for j in range(NJ):
                        if j + 2 < NJ:
                            ld4a(j + 2)
                        u_ = ub[j % 3]; v_ = vb4[j % 3]; pu = pU[j % 2]; ut = uTb[j % 2]; vt = vbb[j % 2]
                        for c in range(8):
                            P.op("pe", lambda e, c=c, u_=u_, pu=pu: e.transpose(out=pu[:, c * 128:(c + 1) * 128], in_=u_[:, c * 128:(c + 1) * 128], identity=ident[:, :]), [u_.name, "ident"], [pu.name])
                        P.op("dve", lambda e, pu=pu, ut=ut: e.tensor_tensor(out=ut[:, :, :], in0=pu[:, :].rearrange("p (c e) -> p c e", e=128), in1=g_ffn4[:, :].unsqueeze(2).to_broadcast([128, 8, 128]), op=ALU.mult),
                             [pu.name, "g_ffn4"], [ut.name])
                        P.dma("sp", lambda e, ut=ut, j=j: e.dma_start(out=uTs[j], in_=ut[:, :, :].rearrange("p c e -> p (c e)")), [ut.name], [])
                        P.op("act", lambda e, v_=v_, vt=vt: e.copy(out=vt[:], in_=v_[:]), [v_.name], [vt.name])
                        P.dma("sp", lambda e, vt=vt, j=j: e.dma_start(out=vbs[j], in_=vt[:]), [vt.name], [])
                    P.barrier()
                    P.emit()

                with ExitStack() as S4b:
                    s4b = lambda n, s, d=F32: S4b.enter_context(nc.sbuf_tensor(n, list(s), d))
                    TB = 256
                    GT = [s4b("GT%d" % k_, [128, 128, TB], BF16) for k_ in range(2)]
                    NUV = 5
                    uj = [s4b("uj%d" % k_, [128, 8, 128], BF16) for k_ in range(NUV)]
                    vj = [s4b("vj%d" % k_, [128, D], BF16) for k_ in range(NUV)]
                    hT = [s4b("hT%d" % k_, [128, 8, TB], BF16) for k_ in range(2)]
                    rr = [s4b("rr%d" % k_, [128, 3, TB]) for k_ in range(2)]
                    O1 = [s4b("O1_%d" % k_, [128, 128], BF16) for k_ in range(4)]
                    O2 = [s4b("O2_%d" % k_, [128, 128], BF16) for k_ in range(4)]
                    Hg = [s4b("Hg%d" % k_, [128, TB], BF16) for k_ in range(3)]
                    Wb = [s4b("Wb%d" % k_, [128, TB], BF16) for k_ in range(3)]
                    h2t = [s4b("h2t%d" % k_, [128, D]) for k_ in range(2)]
                    ybuf = [s4b("ybuf%d" % k_, [128, D]) for k_ in range(2)]
                    pY = [S4b.enter_context(nc.psum_tensor("pY%d" % k_, [128, 1024], F32)) for k_ in range(2)]
                    pH = [S4b.enter_context(nc.psum_tensor("pH%d" % k_, [128, 512], F32)) for k_ in range(3)]
                    pGt = [S4b.enter_context(nc.psum_tensor("pGt%d" % k_, [128, 4, 128], F32)) for k_ in range(1)]
                    c4 = {"o": 0, "g": 0, "uv": 0, "hw": 0, "y": 0}
                    blocks = []
                    if DO_P:
                        for b in range(NPB):
                            for i in range(0, NT_RUN, 2):
                                tl = [(b * NT + i, 128)]
                                if i + 1 < NT_RUN:
                                    tl.append((b * NT + i + 1, 128))
                                blocks.append(tl)
                    if DO_S:
                        blocks.append([(NPB * NT, TS)])

                    def load_block(bi):
                        tl = blocks[bi]
                        hb = hT[bi % 2]; rb = rr[bi % 2]
                        for k_, (ti, T) in enumerate(tl):
                            P.dma("sp", lambda e, k_=k_, ti=ti, T=T: e.dma_start(out=hb[:, :, k_ * 128:k_ * 128 + T], in_=hnTs[ti].rearrange("p (c t) -> p c t", t=128)[:, :, :T]), [], [hb.name])
                            for q_, src in enumerate((r1s, r2s, rws)):
                                P.dma("sp", lambda e, k_=k_, ti=ti, T=T, q_=q_, src=src: e.dma_start(out=rb[:, q_, k_ * 128:k_ * 128 + T], in_=src[ti, :, :T]), [], [rb.name])

                    def g_tokens(bi, t0, t1):
                        rb = rr[bi % 2]; gt = GT[bi % 2]
                        nt_ = sum(T for (_, T) in blocks[bi])
                        t1 = min(t1, nt_)
                        t = t0
                        while t < t1:
                            n4 = min(4, t1 - t)
                            pg = pGt[0]
                            for q_ in range(n4):
                                o1 = O1[c4["o"] % 4]; o2 = O2[c4["o"] % 4]; c4["o"] += 1
                                tt = t + q_
                                P.op("dve", lambda e, o1=o1, tt=tt: e.tensor_scalar(out=o1[:], in0=iota_b[:], scalar1=rb[:, 0, tt:tt + 1], scalar2=None, op0=ALU.is_equal), ["iota_b", rb.name], [o1.name])
                                P.op("dve", lambda e, o2=o2, tt=tt: e.tensor_scalar(out=o2[:], in0=iota_b[:], scalar1=rb[:, 1, tt:tt + 1], scalar2=rb[:, 2, tt:tt + 1], op0=ALU.is_equal, op1=ALU.mult), ["iota_b", rb.name], [o2.name])
                                P.op("pe", lambda e, o1=o1, o2=o2, pg=pg, q_=q_: e.matmul(pg[:, q_, :], lhsT=o2[:], rhs=o1[:], start=True, stop=True), [o1.name, o2.name], [pg.name])
                            P.op("act", lambda e, pg=pg, t=t, n4=n4: e.copy(out=gt[:, :, t:t + n4], in_=pg[:, 0:n4, :].rearrange("p q i -> p i q")), [pg.name], [gt.name])
                            t += n4

                    def ld_uv(j):
                        k_ = c4["uv"] % NUV; c4["uv"] += 1
                        P.dma("sp", lambda e: e.dma_start(out=uj[k_][:, :, :].rearrange("p c e -> p (c e)"), in_=uTs[j]), [], [uj[k_].name])
                        P.dma("sp", lambda e: e.dma_start(out=vj[k_][:], in_=vbs[j]), [], [vj[k_].name])
                        return k_

                    if blocks:
                        load_block(0)
                        g_tokens(0, 0, TB)
                    for bi, tl in enumerate(blocks):
                        nt_ = sum(T for (_, T) in tl)
                        hb = hT[bi % 2]; gt = GT[bi % 2]
                        if bi + 1 < len(blocks):
                            load_block(bi + 1)
                        q = [ld_uv(j_) for j_ in range(min(NUV - 1, NJ))]
                        hst = {}

                        def do_H(j, nt_=nt_, hb=hb):
                            k_ = q[j]
                            sel = c4["hw"] % 3; c4["hw"] += 1
                            ph, hg, wb = pH[sel], Hg[sel], Wb[sel]
                            for c in range(8):
                                P.op("pe", lambda e, c=c, k_=k_, ph=ph: e.matmul(ph[:, 0:nt_], lhsT=uj[k_][:, c, :], rhs=hb[:, c, 0:nt_], start=(c == 0), stop=(c == 7)), [uj[k_].name, hb.name], [ph.name])
                            hst[j] = (k_, ph, hg, wb)

                        do_H(0)
                        if NJ > 1:
                            do_H(1)
                        for j in range(NJ):
                            if j + NUV - 1 < NJ:
                                q.append(ld_uv(j + NUV - 1))
                            if j + 2 < NJ:
                                do_H(j + 2)
                            k_, ph, hg, wb = hst.pop(j)
                            P.op("act", lambda e, ph=ph, hg=hg, nt_=nt_: e.activation(out=hg[:, 0:nt_], in_=ph[:, 0:nt_], func=AF.Gelu_apprx_tanh), [ph.name], [hg.name])
                            P.op("dve", lambda e, hg=hg, wb=wb, j=j, nt_=nt_, gt=gt: e.tensor_tensor(out=wb[:, 0:nt_], in0=hg[:, 0:nt_], in1=gt[:, j, 0:nt_], op=ALU.mult), [hg.name, gt.name], [wb.name])
                            for k2_, (ti, T) in enumerate(tl):
                                for half in range(2):
                                    P.op("pe", lambda e, k2_=k2_, T=T, half=half, wb=wb, k_=k_, j=j: e.matmul(pY[k2_][:T, half * 512:(half + 1) * 512], lhsT=wb[:, k2_ * 128:k2_ * 128 + T], rhs=vj[k_][:, half * 512:(half + 1) * 512],
                                                                                                       start=(j == 0), stop=(j == NJ - 1)), [wb.name, vj[k_].name], ["pY%d" % k2_])
                            if bi + 1 < len(blocks):
                                if NJ >= 128:
                                    if j % 2 == 0:
                                        g_tokens(bi + 1, 2 * j, 2 * j + 4)
                                elif j == 0:
                                    g_tokens(bi + 1, 0, TB)
                        for k2_, (ti, T) in enumerate(tl):
                            hh_ = h2t[c4["y"] % 2]; yb = ybuf[c4["y"] % 2]; c4["y"] += 1
                            P.dma("sp", lambda e, hh_=hh_, ti=ti, T=T: e.dma_start(out=hh_[:T, :], in_=h2s[ti, :T, :]), [], [hh_.name])
                            P.op("dve", lambda e, hh_=hh_, yb=yb, k2_=k2_, T=T: e.tensor_tensor(out=yb[:T, :], in0=pY[k2_][:T, :], in1=hh_[:T, :], op=ALU.add), ["pY%d" % k2_, hh_.name], [yb.name])
                            if ti == NPB * NT:
                                P.dma("sp", lambda e, yb=yb, T=T: e.dma_start(out=ys, in_=yb[:T, :]), [yb.name], [])
                            else:
                                b_, i_ = ti // NT, ti % NT
                                P.dma("sp", lambda e, yb=yb, b_=b_, i_=i_: e.dma_start(out=yp[b_, i_ * 128:(i_ + 1) * 128, :], in_=yb[:, :]), [yb.name], [])
                    P.barrier()
                    P.emit()
    P.close()
    return nc


def _consts():
    j = np.arange(128)
    triu = (j[:, None] <= j[None, :]).astype(np.float32)
    iota = np.tile(np.arange(128, dtype=np.float32)[None, :], (128, 1))
    t = np.arange(32)
    m32 = ((t[:, None] // 8 == t[None, :] // 8) & (t[:, None] <= t[None, :])).astype(np.float32)
    rowm = (t[:, None] // 8 == np.arange(4)[None, :]).astype(np.float32)
    colm = np.tile((np.arange(4)[:, None] == (t[None, :] // 8)).astype(np.float32).reshape(1, 128), (128, 1))
    reset = np.tile((t % 8 != 0).astype(np.float32)[None, :], (128, 1))
    return dict(c_ident=np.eye(128, dtype=np.float32), c_triu=triu, c_iota=iota, c_m32=m32, c_rowm=rowm,
                c_colm=colm, c_reset=reset, c_ones=np.ones((128, 128), np.float32),
                c_pidx=np.arange(128, dtype=np.float32).reshape(128, 1))


def make_in_maps(inp, cores):
    f = lambda a: np.ascontiguousarray(np.asarray(a))
    n_phys = inp["cache_diff_k"].shape[1]
    ckf = f(inp["cache_diff_k"]).reshape(n_phys * 128, 512)
    cvf = f(inp["cache_diff_v"]).reshape(n_phys * 128, 512)
    shared = dict(ck=ckf, cv=cvf)
    for k in ("attn_norm", "w_in", "diff_q_norm", "diff_k_norm", "lambda_q1", "lambda_k1", "lambda_q2", "lambda_k2",
              "diff_out_norm", "gla_w_gate", "gla_b_gate", "gla_out_norm", "w_o", "cross_norm", "mem_norm", "cross_w_q",
              "cross_w_kv", "cross_q_norm", "cross_k_norm", "cross_w_o", "ffn_norm", "peer_w_q", "peer_q_norm", "peer_u", "peer_v"):
        shared[k] = f(inp[k])[0]
    shared["peer_sub_keys"] = f(inp["peer_sub_keys"])[0].reshape(16, 128, 128)
    shared.update(_consts())
    maps = []
    for c in cores:
        m = dict(shared)
        m["xp"] = f(inp["x_prompt"][NPB * c:NPB * (c + 1)])
        m["xs"] = f(inp["x_sample"][NSB * c:NSB * (c + 1)]).reshape(TS, D)
        m["memp"] = f(inp["mem_prompt"][NPB * c:NPB * (c + 1)])
        m["pt"] = f(inp["page_table"][NSB * c:NSB * (c + 1)]).astype(np.int32)
        m["sgla"] = f(inp["state_gla"][0, NSB * c:NSB * (c + 1)])
        m["cmk"] = f(inp["cache_mem_k"][0, NSB * c:NSB * (c + 1)]).reshape(NSB, 256, D)
        m["cmv"] = f(inp["cache_mem_v"][0, NSB * c:NSB * (c + 1)]).reshape(NSB, 256, D)
        maps.append(m)
    return maps, n_phys


def kernel(**inp):
    cores = list(range(NCORES))
    maps, n_phys = make_in_maps(inp, cores)
    nc = build(n_phys)
    used = set()
    for alloc_name in maps[0]:
        used.add(alloc_name)
    res = run_bass_kernel_spmd(nc, maps, core_ids=cores)
    R = res.results
    cat = lambda k: np.concatenate([r[k] for r in R], axis=0)
    B = NPB * NCORES
    DB = NSB * NCORES
    y_p = cat("yp")
    y_s = cat("ys").reshape(DB, 8, D)
    k_p = cat("okp").reshape(1, B, SEQ, 8, 64)
    v_p = cat("ovp").reshape(1, B, SEQ, 8, 64)
    st_p = cat("ostp").reshape(1, B, 4, 64, 128)
    mk_p = cat("omk").reshape(1, B, 256, 4, 256)
    mv_p = cat("omv").reshape(1, B, 256, 4, 256)
    k_s = cat("oks").reshape(1, DB, 8, 8, 64)
    v_s = cat("ovs").reshape(1, DB, 8, 8, 64)
    st_s = cat("osts").reshape(1, DB, 4, 64, 128)
    return (y_p, y_s, k_p, v_p, st_p, mk_p, mv_p, k_s, v_s, st_s)
```
